# Optimizing a Trainium2 kernel written in Bass

```python
import math
import jax
import jax.numpy as jnp
from jax import lax
import numpy as np

D_MODEL = 2048
BATCH = 32
SEQ = 256
DEPTH = 2
DEC_BATCH = 4
DEC_SEQ = 2048
PAST_LEN = 512

GRID_W = 64
N_DIR = 2
EPS = 1e-6
RWKV_WIDTH = 1024
RWKV_HEAD = 64
RWKV_HEADS = RWKV_WIDTH // RWKV_HEAD
RWKV_DECAY_RANK = 64
RWKV_AICL_RANK = 64
RWKV_GATE_RANK = 128
RWKV_LN_EPS = 64e-5
RWKV_COLS = 3 * RWKV_WIDTH + RWKV_DECAY_RANK + RWKV_AICL_RANK + RWKV_GATE_RANK
SSD_WIDTH = 1024
SSD_HEAD = 64
SSD_HEADS = SSD_WIDTH // SSD_HEAD
SSD_GROUPS = 2
SSD_STATE = 128
SSD_CONV = 5
SSD_CHUNK = 128
SSD_XBC = SSD_WIDTH + 2 * SSD_GROUPS * SSD_STATE
SSD_COLS = SSD_WIDTH + SSD_XBC + SSD_HEADS
S5_WIDTH = 1024
S5_GROUP = 16
S5_GROUPS = S5_WIDTH // S5_GROUP
S5_STATE = 64
D_FF = 5504
N_MOD = 9
D_IN = RWKV_COLS + SSD_COLS + S5_WIDTH + 3 * D_MODEL

kernel_name = "hybrid_rwkv7_ssd_s5_diffusion_step"


def _split(x, sizes):
    return jnp.split(x, np.cumsum(sizes)[:-1].tolist(), axis=-1)


def _rmsnorm(x, g):
    xf = x.astype(jnp.float32)
    y = xf * lax.rsqrt(jnp.mean(xf * xf, axis=-1, keepdims=True) + EPS)
    return (y * g.astype(jnp.float32)).astype(x.dtype)


def _swiglu(h, w_in, w_out):
    gate, up = jnp.split(h @ w_in, 2, axis=-1)
    return (jax.nn.silu(gate) * up) @ w_out


def _centred_shift(x, grid):
    bsz, t, ch = x.shape
    if grid:
        rows = t // GRID_W
        pg = jnp.pad(x.reshape(bsz, rows, GRID_W, ch), ((0, 0), (1, 1), (1, 1), (0, 0)))
        nb = 0.25 * (pg[:, :-2, 1:-1] + pg[:, 2:, 1:-1] + pg[:, 1:-1, :-2] + pg[:, 1:-1, 2:])
        return nb.reshape(bsz, t, ch)
    ps = jnp.pad(x, ((0, 0), (1, 1), (0, 0)))
    return 0.5 * (ps[:, :-2] + ps[:, 2:])


def _depthwise_conv(x, w, b):
    k, ch = w.shape
    y = lax.conv_general_dilated(x, w[:, None, :].astype(x.dtype), window_strides=(1,),
                                 padding=[(k // 2, k // 2)], dimension_numbers=("NWC", "WIO", "NWC"),
                                 feature_group_count=ch)
    return y + b


def _rwkv7_scan(r, w, k, v, kk, a, s0, reverse):
    def step(s, inp):
        r_t, w_t, k_t, v_t, kk_t, a_t = inp
        s_kk = jnp.einsum("bhvk,bhk->bhv", s, kk_t)
        s = (s * w_t[:, :, None, :] - s_kk[..., None] * (kk_t * a_t)[:, :, None, :]
             + v_t[..., None] * k_t[:, :, None, :])
        return s, jnp.einsum("bhvk,bhk->bhv", s, r_t)
    xs = tuple(jnp.moveaxis(u, 1, 0) for u in (r, w, k, v, kk, a))
    s_fin, out = lax.scan(step, s0, xs, reverse=reverse)
    return jnp.moveaxis(out, 0, 1), s_fin


def _rwkv7_mixer(stream, grid, s0, p, l):
    bsz, t, _ = stream.shape
    f32 = jnp.float32
    xs = stream + p["rwkv_mu"][l] * (_centred_shift(stream, grid) - stream)
    r, k, v, wl, al, gl = _split(xs.astype(f32), [RWKV_WIDTH] * 3 + [RWKV_DECAY_RANK, RWKV_AICL_RANK, RWKV_GATE_RANK])

    def heads(u):
        return u.reshape(bsz, t, RWKV_HEADS, RWKV_HEAD)

    kk = heads(k * p["rwkv_k_k"][l])
    kk = kk * lax.rsqrt(jnp.sum(kk * kk, axis=-1, keepdims=True) + 1e-12)
    s0 = s0.astype(f32)
    outs, finals = [], []
    for d in range(N_DIR):
        w_log = -jax.nn.softplus(-(p["rwkv_w0"][l, d] + jnp.tanh(wl) @ p["rwkv_w2"][l, d])) - 0.5
        decay = jnp.exp(-jnp.exp(w_log))
        a_d = jax.nn.sigmoid(p["rwkv_a0"][l, d] + al @ p["rwkv_a2"][l, d])
        k_d = k * (1.0 + (a_d - 1.0) * p["rwkv_k_a"][l])
        o_d, s_d = _rwkv7_scan(heads(r), heads(decay), heads(k_d), heads(v), kk, heads(a_d), s0[:, d], d == 1)
        outs.append(o_d)
        finals.append(s_d)
    o = outs[0] + outs[1]
    mu = jnp.mean(o, axis=-1, keepdims=True)
    var = jnp.mean(jnp.square(o - mu), axis=-1, keepdims=True)
    o = ((o - mu) * lax.rsqrt(var + RWKV_LN_EPS)).reshape(bsz, t, RWKV_WIDTH) * p["rwkv_ln_g"][l] + p["rwkv_ln_b"][l]
    bonus = jnp.sum(heads(r) * heads(k) * p["rwkv_r_k"][l], axis=-1, keepdims=True) * heads(v)
    o = o + bonus.reshape(bsz, t, RWKV_WIDTH)
    g = jax.nn.sigmoid(gl) @ p["rwkv_g2"][l]
    return (o * g).astype(stream.dtype), jnp.stack(finals, axis=1)


def _ssd_chunked(x, dt, a, bm, cm, s0):
    bsz, t, h, pd = x.shape
    g, n = bm.shape[2], bm.shape[3]
    e = h // g
    nc, lc = t // SSD_CHUNK, SSD_CHUNK
    xc = (x * dt[..., None]).reshape(bsz, nc, lc, g, e, pd)
    da = jnp.moveaxis((dt * a).reshape(bsz, nc, lc, g, e), 2, -1)
    cum = jnp.cumsum(da, axis=-1)
    bc = bm.reshape(bsz, nc, lc, g, n)
    cc = cm.reshape(bsz, nc, lc, g, n)
    seg = cum[..., :, None] - cum[..., None, :]
    lower = jnp.tril(jnp.ones((lc, lc), dtype=bool))
    lmat = jnp.exp(jnp.where(lower, seg, -jnp.inf))
    y_diag = jnp.einsum("bclgn,bcsgn,bcgels,bcsgep->bclgep", cc, bc, lmat, xc)
    decay_states = jnp.exp(cum[..., -1:] - cum)
    states = jnp.einsum("bclgn,bcgel,bclgep->bcgepn", bc, decay_states, xc)
    chunk_decay = jnp.exp(cum[..., -1])

    def step(s, inp):
        st, dec = inp
        return s * dec[..., None, None] + st, s

    s_fin, s_in = lax.scan(step, s0.reshape(bsz, g, e, pd, n),
                           (jnp.moveaxis(states, 1, 0), jnp.moveaxis(chunk_decay, 1, 0)))
    s_in = jnp.moveaxis(s_in, 0, 1)
    y_off = jnp.einsum("bclgn,bcgel,bcgepn->bclgep", cc, jnp.exp(cum), s_in)
    return (y_diag + y_off).reshape(bsz, t, h, pd), s_fin.reshape(bsz, h, pd, n)


def _ssd_mixer(stream, s0, p, l):
    bsz, t, _ = stream.shape
    f32 = jnp.float32
    z, xbc, dt_raw = _split(stream, [SSD_WIDTH, SSD_XBC, SSD_HEADS])
    xbc = jax.nn.silu(_depthwise_conv(xbc, p["ssd_conv_w"][l], p["ssd_conv_b"][l])).astype(f32)
    xs, bm, cm = _split(xbc, [SSD_WIDTH, SSD_GROUPS * SSD_STATE, SSD_GROUPS * SSD_STATE])
    x = xs.reshape(bsz, t, SSD_HEADS, SSD_HEAD)
    bm = bm.reshape(bsz, t, SSD_GROUPS, SSD_STATE)
    cm = cm.reshape(bsz, t, SSD_GROUPS, SSD_STATE)
    s0 = s0.astype(f32)
    y = p["ssd_d"][l][:, None] * x
    finals = []
    for d in range(N_DIR):
        dt = jax.nn.softplus(dt_raw.astype(f32) + p["ssd_dt_bias"][l, d])
        a = -jnp.exp(p["ssd_a_log"][l, d].astype(f32))
        if d == 0:
            y_d, s_d = _ssd_chunked(x, dt, a, bm, cm, s0[:, d])
        else:
            y_d, s_d = _ssd_chunked(jnp.flip(x, 1), jnp.flip(dt, 1), a, jnp.flip(bm, 1), jnp.flip(cm, 1), s0[:, d])
            y_d = jnp.flip(y_d, 1)
        y = y + y_d
        finals.append(s_d)
    y = y.reshape(bsz, t, SSD_WIDTH) * jax.nn.silu(z.astype(f32))
    return _rmsnorm(y, p["ssd_norm_g"][l]).astype(stream.dtype), jnp.stack(finals, axis=1)


def _complex_affine_combine(e1, e2):
    a1r, a1i, b1r, b1i = e1
    a2r, a2i, b2r, b2i = e2
    return (a2r * a1r - a2i * a1i, a2r * a1i + a2i * a1r,
            a2r * b1r - a2i * b1i + b2r, a2r * b1i + a2i * b1r + b2i)


def _s5_mixer(u, s0_re, s0_im, p, l):
    bsz, t, _ = u.shape
    f32 = jnp.float32
    uf = u.astype(f32).reshape(bsz, t, S5_GROUPS, S5_GROUP)
    s0_re, s0_im = s0_re.astype(f32), s0_im.astype(f32)
    y = p["s5_d"][l].reshape(S5_GROUPS, S5_GROUP) * uf
    fin_re, fin_im = [], []
    for d in range(N_DIR):
        lam_re, lam_im = p["s5_lambda_re"][l, d], p["s5_lambda_im"][l, d]
        delta = jnp.exp(p["s5_log_dt"][l, d])[:, None]
        mag = jnp.exp(lam_re * delta)
        lb_re, lb_im = mag * jnp.cos(lam_im * delta), mag * jnp.sin(lam_im * delta)
        den = lam_re * lam_re + lam_im * lam_im
        q_re = ((lb_re - 1.0) * lam_re + lb_im * lam_im) / den
        q_im = (lb_im * lam_re - (lb_re - 1.0) * lam_im) / den
        b_re, b_im = p["s5_b_re"][l, d], p["s5_b_im"][l, d]
        bb_re = q_re[..., None] * b_re - q_im[..., None] * b_im
        bb_im = q_re[..., None] * b_im + q_im[..., None] * b_re
        bu_re = jnp.einsum("gpc,btgc->btgp", bb_re, uf)
        bu_im = jnp.einsum("gpc,btgc->btgp", bb_im, uf)
        first, last = (0, t - 1) if d == 0 else (t - 1, 0)
        s_re0, s_im0 = s0_re[:, d], s0_im[:, d]
        bu_re = bu_re.at[:, first].add(lb_re * s_re0 - lb_im * s_im0)
        bu_im = bu_im.at[:, first].add(lb_re * s_im0 + lb_im * s_re0)
        a_re = jnp.broadcast_to(lb_re[None, None], (1, t, S5_GROUPS, S5_STATE))
        a_im = jnp.broadcast_to(lb_im[None, None], (1, t, S5_GROUPS, S5_STATE))
        _, _, s_re, s_im = lax.associative_scan(_complex_affine_combine, (a_re, a_im, bu_re, bu_im),
                                                reverse=(d == 1), axis=1)
        y = y + (jnp.einsum("gcp,btgp->btgc", p["s5_c_re"][l, d], s_re)
                 - jnp.einsum("gcp,btgp->btgc", p["s5_c_im"][l, d], s_im))
        fin_re.append(s_re[:, last])
        fin_im.append(s_im[:, last])
    y = jax.nn.gelu(y.reshape(bsz, t, S5_WIDTH))
    return y.astype(u.dtype), jnp.stack(fin_re, axis=1), jnp.stack(fin_im, axis=1)


def _mixing(h, grid, s_rwkv, s_ssd, s_re, s_im, p, l):
    z = h @ p["w_in"][l]
    z_a, z_b, z_c, z_g = _split(z, [RWKV_COLS, SSD_COLS, S5_WIDTH, 3 * D_MODEL])
    y_a, st_a = _rwkv7_mixer(z_a, grid, s_rwkv, p, l)
    y_b, st_b = _ssd_mixer(z_b, s_ssd, p, l)
    y_c, st_re, st_im = _s5_mixer(z_c, s_re, s_im, p, l)
    g_a, g_b, g_c = jnp.split(jax.nn.sigmoid(z_g), 3, axis=-1)
    glu_val, glu_gate = jnp.split(y_c @ p["w_proj_c"][l], 2, axis=-1)
    merged = (g_a * (y_a @ p["w_proj_a"][l]) + g_b * (y_b @ p["w_proj_b"][l])
              + g_c * (glu_val * jax.nn.sigmoid(glu_gate)))
    return merged @ p["w_out"][l], st_a, st_b, st_re, st_im


def _trunk(x, cond, grid, st_rwkv, st_ssd, st_re, st_im, p):
    new_a, new_b, new_re, new_im = [], [], [], []
    for l in range(DEPTH):
        mod = (jax.nn.silu(cond) @ p["w_mod"][l] + p["b_mod"][l])[:, None, :]
        sh1, sc1, g1, sh2, sc2, g2, sh3, sc3, g3 = jnp.split(mod, N_MOD, axis=-1)
        h = _rmsnorm(x, p["norm_g"][l, 0]) * (1.0 + sc1) + sh1
        x = x + 0.5 * g1 * _swiglu(h, p["ffn_w_in"][l, 0], p["ffn_w_out"][l, 0])
        h = _rmsnorm(x, p["norm_g"][l, 1]) * (1.0 + sc2) + sh2
        m, sa, sb, sre, sim = _mixing(h, grid, st_rwkv[:, l], st_ssd[:, l], st_re[:, l], st_im[:, l], p, l)
        x = x + g2 * m
        h = _rmsnorm(x, p["norm_g"][l, 2]) * (1.0 + sc3) + sh3
        x = x + 0.5 * g3 * _swiglu(h, p["ffn_w_in"][l, 1], p["ffn_w_out"][l, 1])
        new_a.append(sa)
        new_b.append(sb)
        new_re.append(sre)
        new_im.append(sim)
    y = _rmsnorm(x, p["final_norm_g"])
    return y, jnp.stack(new_a, axis=1), jnp.stack(new_b, axis=1), jnp.stack(new_re, axis=1), jnp.stack(new_im, axis=1)


def setup_inputs(seed: int = 0) -> dict:
    key = jax.random.key(seed)
    ks = iter(jax.random.split(key, 64))

    def nrm(shape, scale):
        return scale * jax.random.normal(next(ks), shape, jnp.float32)

    def uni(shape, lo, hi):
        return jax.random.uniform(next(ks), shape, jnp.float32, lo, hi)

    ld = (DEPTH, N_DIR)
    dt_ssd = jnp.exp(uni(ld + (SSD_HEADS,), math.log(1e-3), math.log(1e-1)))
    lam_shape = ld + (S5_GROUPS, S5_STATE)
    return {
        "x_prompt": nrm((BATCH, SEQ, D_MODEL), 1.0),
        "x_sample": nrm((DEC_BATCH, DEC_SEQ, D_MODEL), 1.0),
        "state_rwkv": nrm((DEC_BATCH, DEPTH, N_DIR, RWKV_HEADS, RWKV_HEAD, RWKV_HEAD), 0.3),
        "state_ssd": nrm((DEC_BATCH, DEPTH, N_DIR, SSD_HEADS, SSD_HEAD, SSD_STATE), 0.1),
        "state_s5_re": nrm((DEC_BATCH, DEPTH, N_DIR, S5_GROUPS, S5_STATE), 0.3),
        "state_s5_im": nrm((DEC_BATCH, DEPTH, N_DIR, S5_GROUPS, S5_STATE), 0.3),
        "c": nrm((DEC_BATCH, D_MODEL), 1.0),
        "c_ctx": nrm((D_MODEL,), 1.0),
        "w_mod": nrm((DEPTH, D_MODEL, N_MOD * D_MODEL), 0.5 * D_MODEL ** -0.5),
        "b_mod": nrm((DEPTH, N_MOD * D_MODEL), 0.01),
        "norm_g": 1.0 + nrm((DEPTH, 3, D_MODEL), 0.02),
        "ffn_w_in": nrm((DEPTH, 2, D_MODEL, 2 * D_FF), D_MODEL ** -0.5),
        "ffn_w_out": nrm((DEPTH, 2, D_FF, D_MODEL), D_FF ** -0.5),
        "w_in": nrm((DEPTH, D_MODEL, D_IN), D_MODEL ** -0.5),
        "rwkv_mu": uni((DEPTH, RWKV_COLS), 0.0, 1.0),
        "rwkv_w0": uni(ld + (RWKV_WIDTH,), -6.0, -1.0),
        "rwkv_w2": nrm(ld + (RWKV_DECAY_RANK, RWKV_WIDTH), 0.1 * RWKV_DECAY_RANK ** -0.5),
        "rwkv_a0": nrm(ld + (RWKV_WIDTH,), 0.1),
        "rwkv_a2": nrm(ld + (RWKV_AICL_RANK, RWKV_WIDTH), 0.5 * RWKV_AICL_RANK ** -0.5),
        "rwkv_g2": nrm((DEPTH, RWKV_GATE_RANK, RWKV_WIDTH), RWKV_GATE_RANK ** -0.5),
        "rwkv_k_k": 0.85 + nrm((DEPTH, RWKV_WIDTH), 0.02),
        "rwkv_k_a": 1.0 + nrm((DEPTH, RWKV_WIDTH), 0.02),
        "rwkv_r_k": nrm((DEPTH, RWKV_HEADS, RWKV_HEAD), 0.1),
        "rwkv_ln_g": 1.0 + nrm((DEPTH, RWKV_WIDTH), 0.02),
        "rwkv_ln_b": nrm((DEPTH, RWKV_WIDTH), 0.01),
        "w_proj_a": nrm((DEPTH, RWKV_WIDTH, D_MODEL), RWKV_WIDTH ** -0.5),
        "ssd_conv_w": nrm((DEPTH, SSD_CONV, SSD_XBC), SSD_CONV ** -0.5),
        "ssd_conv_b": nrm((DEPTH, SSD_XBC), 0.01),
        "ssd_dt_bias": dt_ssd + jnp.log(-jnp.expm1(-dt_ssd)),
        "ssd_a_log": jnp.log(uni(ld + (SSD_HEADS,), 1.0, 16.0)),
        "ssd_d": 1.0 + nrm((DEPTH, SSD_HEADS), 0.1),
        "ssd_norm_g": 1.0 + nrm((DEPTH, SSD_WIDTH), 0.02),
        "w_proj_b": nrm((DEPTH, SSD_WIDTH, D_MODEL), SSD_WIDTH ** -0.5),
        "s5_lambda_re": -0.5 + nrm(lam_shape, 0.01),
        "s5_lambda_im": jnp.pi * jnp.arange(S5_STATE, dtype=jnp.float32) + nrm(lam_shape, 0.01),
        "s5_log_dt": uni(ld + (S5_GROUPS,), math.log(1e-3), math.log(1e-1)),
        "s5_b_re": nrm(ld + (S5_GROUPS, S5_STATE, S5_GROUP), (2 * S5_GROUP) ** -0.5),
        "s5_b_im": nrm(ld + (S5_GROUPS, S5_STATE, S5_GROUP), (2 * S5_GROUP) ** -0.5),
        "s5_c_re": nrm(ld + (S5_GROUPS, S5_GROUP, S5_STATE), (2 * S5_STATE) ** -0.5),
        "s5_c_im": nrm(ld + (S5_GROUPS, S5_GROUP, S5_STATE), (2 * S5_STATE) ** -0.5),
        "s5_d": nrm((DEPTH, S5_WIDTH), 1.0),
        "w_proj_c": nrm((DEPTH, S5_WIDTH, 2 * D_MODEL), S5_WIDTH ** -0.5),
        "w_out": nrm((DEPTH, D_MODEL, D_MODEL), D_MODEL ** -0.5),
        "final_norm_g": 1.0 + nrm((D_MODEL,), 0.02),
    }


def reference(x_prompt, x_sample, state_rwkv, state_ssd, state_s5_re, state_s5_im, c,
              c_ctx, w_mod, b_mod, norm_g, ffn_w_in, ffn_w_out, w_in,
              rwkv_mu, rwkv_w0, rwkv_w2, rwkv_a0, rwkv_a2, rwkv_g2, rwkv_k_k, rwkv_k_a, rwkv_r_k,
              rwkv_ln_g, rwkv_ln_b, w_proj_a,
              ssd_conv_w, ssd_conv_b, ssd_dt_bias, ssd_a_log, ssd_d, ssd_norm_g, w_proj_b,
              s5_lambda_re, s5_lambda_im, s5_log_dt, s5_b_re, s5_b_im, s5_c_re, s5_c_im, s5_d, w_proj_c,
              w_out, final_norm_g):
    p = {
        "w_mod": w_mod, "b_mod": b_mod, "norm_g": norm_g, "ffn_w_in": ffn_w_in, "ffn_w_out": ffn_w_out,
        "w_in": w_in, "rwkv_mu": rwkv_mu, "rwkv_w0": rwkv_w0, "rwkv_w2": rwkv_w2, "rwkv_a0": rwkv_a0,
        "rwkv_a2": rwkv_a2, "rwkv_g2": rwkv_g2, "rwkv_k_k": rwkv_k_k, "rwkv_k_a": rwkv_k_a,
        "rwkv_r_k": rwkv_r_k, "rwkv_ln_g": rwkv_ln_g, "rwkv_ln_b": rwkv_ln_b, "w_proj_a": w_proj_a,
        "ssd_conv_w": ssd_conv_w, "ssd_conv_b": ssd_conv_b, "ssd_dt_bias": ssd_dt_bias,
        "ssd_a_log": ssd_a_log, "ssd_d": ssd_d, "ssd_norm_g": ssd_norm_g, "w_proj_b": w_proj_b,
        "s5_lambda_re": s5_lambda_re, "s5_lambda_im": s5_lambda_im, "s5_log_dt": s5_log_dt,
        "s5_b_re": s5_b_re, "s5_b_im": s5_b_im, "s5_c_re": s5_c_re, "s5_c_im": s5_c_im, "s5_d": s5_d,
        "w_proj_c": w_proj_c, "w_out": w_out, "final_norm_g": final_norm_g,
    }
    bp = x_prompt.shape[0]
    f32 = jnp.float32
    y_prompt, new_rwkv, new_ssd, new_re, new_im = _trunk(
        x_prompt, c_ctx[None, :], False,
        jnp.zeros((bp, DEPTH, N_DIR, RWKV_HEADS, RWKV_HEAD, RWKV_HEAD), f32),
        jnp.zeros((bp, DEPTH, N_DIR, SSD_HEADS, SSD_HEAD, SSD_STATE), f32),
        jnp.zeros((bp, DEPTH, N_DIR, S5_GROUPS, S5_STATE), f32),
        jnp.zeros((bp, DEPTH, N_DIR, S5_GROUPS, S5_STATE), f32), p)
    y_sample, _, _, _, _ = _trunk(x_sample, c, True, state_rwkv, state_ssd, state_s5_re, state_s5_im, p)
    return (y_prompt, y_sample, new_rwkv, new_ssd, new_re, new_im)
```

```python
import numpy as np
from contextlib import ExitStack
import concourse.bass as bass
import concourse.mybir as mybir
from concourse.bass_utils import run_bass_kernel_spmd

F32 = mybir.dt.float32
F32R = mybir.dt.float32r
AF = mybir.ActivationFunctionType
ALU = mybir.AluOpType
P = 128
TT = 512
NDS = 40

RW = 1024
RH = 64
SW = 1024
SXBC = 1536
S5W = 1024
EPS = 1e-6


class Cfg:
    def __init__(self, DM=2048, DFF=5504, DEPTH=2, T=2048, NPC=4, NSC=4, mix=(1, 1, 1)):
        self.DM, self.DFF, self.DEPTH, self.T, self.NPC, self.NSC = DM, DFF, DEPTH, T, NPC, NSC
        self.NK = DM // P
        self.NJ = DFF // P
        self.NT = T // TT
        self.NSEG = T // 256
        self.mix = mix
        self.ZA, self.ZB, self.ZC = 0, 26, 46
        self.ZG = 54
        self.ZDT = 54 + 3 * self.NK
        self.NZ = self.ZDT + 1


class Tok:
    __slots__ = ("w", "r")

    def __init__(self):
        self.w = []
        self.r = {}


class KB:
    def __init__(self, nc):
        self.nc = nc
        self.E = {"pe": nc.tensor, "dve": nc.vector, "act": nc.scalar, "pool": nc.gpsimd, "sp": nc.sync}
        self.tick = {e: 0 for e in self.E}
        self.seen = {e: {} for e in self.E}
        self.stack = ExitStack()
        self.sem = {e: self.stack.enter_context(nc.semaphore("s_" + e)) for e in self.E}
        self.dsem = [self.stack.enter_context(nc.semaphore("d%d" % i)) for i in range(NDS)]
        self.dval = [0] * NDS
        self.drr = 0
        self.nid = 0
        self.ninstr = 0

    def name(self, s):
        self.nid += 1
        return "%s_%d" % (s, self.nid)

    def _wait(self, e, dep):
        kind, key, val = dep
        if kind == "e" and key == e and e == "pe":
            return
        k = (kind, key)
        if self.seen[e].get(k, 0) >= val:
            return
        self.seen[e][k] = val
        sem = self.sem[key] if kind == "e" else self.dsem[key]
        self.E[e].wait_ge(sem, val)
        self.ninstr += 1

    def _sync(self, e, reads, writes):
        for t in reads:
            for d in t.w:
                self._wait(e, d)
        for t in writes:
            for d in t.w:
                self._wait(e, d)
            for k, v in t.r.items():
                self._wait(e, (k[0], k[1], v))

    def _mark(self, me, reads, writes):
        for t in writes:
            t.w = [me]
            t.r = {}
        for t in reads:
            if not (len(t.w) == 1 and t.w[0] is me):
                k = (me[0], me[1])
                if t.r.get(k, 0) < me[2]:
                    t.r[k] = me[2]

    def join(self, dst, srcs):
        w = list(dst.w)
        for s_ in srcs:
            w.extend(s_.w)
        dst.w = w

    def op(self, e, fn, reads=(), writes=()):
        self._sync(e, reads, writes)
        ins = fn(self.E[e])
        self.tick[e] += 1
        ins.then_inc(self.sem[e], 1)
        self.ninstr += 1
        self._mark(("e", e, self.tick[e]), reads, writes)
        return ins

    def dma(self, q, out, in_, reads=(), writes=(), **kw):
        self._sync(q, reads, writes)
        s = self.drr
        self.drr = (self.drr + 1) % NDS
        if self.dval[s]:
            self._wait(q, ("d", s, self.dval[s]))
        ins = self.E[q].dma_start(out=out, in_=in_, **kw)
        self.dval[s] += 16
        ins.then_inc(self.dsem[s], 16)
        self.ninstr += 1
        self._mark(("d", s, self.dval[s]), reads, writes)
        return ins

    def barrier(self):
        for e in self.E:
            for e2 in self.E:
                if e2 != e and self.tick[e2]:
                    self._wait(e, ("e", e2, self.tick[e2]))
            for s in range(NDS):
                if self.dval[s]:
                    self._wait(e, ("d", s, self.dval[s]))


class Tile:
    def __init__(self, kb, stack, shape, dtype, ntok=1, name="t"):
        self.t = stack.enter_context(kb.nc.sbuf_tensor(kb.name(name), list(shape), dtype))
        self.toks = [Tok() for _ in range(ntok)]
        self.dtype = dtype

    @property
    def tok(self):
        return self.toks[0]

    def __getitem__(self, k):
        return self.t[k]

    def f32(self, k):
        return self.t[k].bitcast(F32)


class Rot:
    def __init__(self, kb, stack, n, shape, dtype, name="r"):
        self.tiles = [Tile(kb, stack, shape, dtype, name=name) for _ in range(n)]
        self.i = 0

    def next(self):
        t = self.tiles[self.i]
        self.i = (self.i + 1) % len(self.tiles)
        return t


def blk(W):
    K, M = W.shape
    return np.ascontiguousarray(W.reshape(K // P, P, M // P, P).transpose(2, 1, 0, 3)).reshape(M // P, P, (K // P) * P)


def colv(v):
    return np.ascontiguousarray(v.reshape(-1, P).T)


class Builder:
    def __init__(self, cfg):
        self.cfg = cfg
        self.nc = bass.Bass("TRN2", target_bir_lowering=False)
        self.kb = KB(self.nc)
        self.din = {}
        self.dout = {}
        self.dtok = {}

    def inp(self, name, shape, dtype=F32):
        self.din[name] = self.nc.dram_tensor(name, list(shape), dtype, kind="ExternalInput").ap()
        return self.din[name]

    def outp(self, name, shape, dtype=F32):
        self.dout[name] = self.nc.dram_tensor(name, list(shape), dtype, kind="ExternalOutput").ap()
        return self.dout[name]

    def scratch(self, name, shape, dtype=F32):
        if getattr(self.cfg, "debug", False) and dtype == F32:
            return self.outp(name, shape, dtype)
        return self.nc.dram_tensor(name, list(shape), dtype, kind="Internal").ap()

    def dt(self, *key):
        if key not in self.dtok:
            self.dtok[key] = Tok()
        return self.dtok[key]

    def build(self):
        cfg, nc, kb = self.cfg, self.nc, self.kb
        NK, NJ, T, NT, DEPTH = cfg.NK, cfg.NJ, cfg.T, cfg.NT, cfg.DEPTH
        self.xT = self.inp("xT", [cfg.DM, T], F32R)
        self.cond = self.inp("cond", [P, NK])
        self.wmodn = self.inp("wmodn", [DEPTH, cfg.DM, 9 * cfg.DM])
        self.bmod = self.inp("bmod", [DEPTH, P, 9 * NK])
        self.normg = self.inp("normg", [DEPTH, P, 3 * NK])
        self.fng = self.inp("fng", [P, NK])
        self.w1 = self.inp("w1", [DEPTH, 2, 2 * NJ, P, NK * P], F32R)
        self.w2 = self.inp("w2", [DEPTH, 2, NK, P, NJ * P], F32R)
        self.yT = self.outp("yT", [cfg.DM, T])
        self.xres = self.scratch("xres", [cfg.DM, T], F32R)
        self.mixer_decl()

        with ExitStack() as gs:
            self.gs = gs
            self.ps = [kb.stack.enter_context(nc.psum_tensor(kb.name("ps"), [P, 1024], F32)) for _ in range(4)]
            self.pstok = [[Tok(), Tok()] for _ in range(4)]
            self.psi = 0
            self.ppi = {}
            self.ones = Tile(kb, gs, [P, P], F32, name="ones")
            kb.op("dve", lambda e: e.memset(self.ones[:], 1.0), writes=[self.ones.tok])
            self.mod = Tile(kb, gs, [P, DEPTH, 9 * NK], F32, name="mod")
            self.modA = Tile(kb, gs, [P, DEPTH, 3 * NK], F32, name="modA")
            self.modG = Tile(kb, gs, [P, DEPTH, 3 * NK], F32, name="modG")
            self.fngt = Tile(kb, gs, [P, NK], F32, name="fng")
            kb.dma("sp", self.fngt[:], self.fng[:, :], writes=[self.fngt.tok])
            self.mixer_consts()
            self.phase_mod()
            src = self.xT
            for l in range(DEPTH):
                self.phase_ffn(l, 0, src)
                src = self.xres
                self.phase_mix(l)
                self.phase_ffn(l, 1, src)
            self.phase_final()
            kb.barrier()
        kb.stack.close()
        return nc

    def bank(self, pool=None):
        if pool is None:
            i = self.psi
            self.psi = (self.psi + 1) % 8
        else:
            base = 0 if pool == "A" else 4
            k = self.ppi.get(pool, 0)
            self.ppi[pool] = (k + 1) % 4
            i = base + k
        return self.ps[i // 2][:, (i % 2) * 512:(i % 2) * 512 + 512], self.pstok[i // 2][i % 2]

    def bank2(self, pool=None):
        if pool is None:
            if self.psi % 2:
                self.psi = (self.psi + 1) % 8
            i = self.psi
            self.psi = (self.psi + 2) % 8
        else:
            base = 0 if pool == "A" else 4
            k = self.ppi.get(pool, 0)
            if k % 2:
                k = (k + 1) % 4
            self.ppi[pool] = (k + 2) % 4
            i = base + k
        return self.ps[i // 2], self.pstok[i // 2]

    def phase_mod(self):
        cfg, kb = self.cfg, self.kb
        NK, DEPTH = cfg.NK, cfg.DEPTH
        NM = 9 * NK
        NCOL = NM * P
        CG = 512 if NCOL % 512 == 0 else 256
        with ExitStack() as st:
            cnd = Tile(kb, st, [P, NK], F32, name="cnd")
            sc = Tile(kb, st, [P, NK], F32, name="scnd")
            bm = Tile(kb, st, [P, DEPTH, NM], F32, name="bm")
            rows = Rot(kb, st, 3, [1, CG], F32, name="mrow")
            ng = Tile(kb, st, [P, DEPTH, 3 * NK], F32, name="ng")
            wr = Rot(kb, st, 2, [P, NK, CG], F32, name="wm")
            kb.dma("sp", cnd[:], self.cond[:, :], writes=[cnd.tok])
            for l in range(DEPTH):
                kb.dma("sp", ng[:, l, :], self.normg[l], writes=[ng.tok])
                kb.dma("sp", bm[:, l, :], self.bmod[l], writes=[bm.tok])
            kb.op("act", lambda e: e.activation(out=sc[:], in_=cnd[:], func=AF.Silu), reads=[cnd.tok], writes=[sc.tok])
            MC = CG // P
            for l in range(DEPTH):
                wv = self.wmodn[l].rearrange("(k p) c -> p k c", p=P)
                for cg in range(NCOL // CG):
                    w = wr.next()
                    kb.dma("sp", w[:], wv[:, :, cg * CG:(cg + 1) * CG], writes=[w.tok])
                    pb, pt = self.bank()
                    for kc in range(NK):
                        kb.op("pe", lambda e: e.matmul(pb[0:1, 0:CG], sc[:, kc:kc + 1], w[:, kc, :], start=(kc == 0), stop=(kc == NK - 1)),
                              reads=[w.tok, sc.tok], writes=[pt])
                    row = rows.next()
                    kb.op("act", lambda e: e.activation(out=row[:], in_=pb[0:1, 0:CG], func=AF.Copy), reads=[pt], writes=[row.tok])
                    pb2, pt2 = self.bank()
                    for mm in range(MC):
                        kb.op("pe", lambda e: e.matmul(pb2[:, mm:mm + 1], row[0:1, mm * P:(mm + 1) * P], self.ones[0:1, 0:1], start=True, stop=True),
                              reads=[row.tok, self.ones.tok], writes=[pt2])
                    m0 = cg * MC
                    kb.op("dve", lambda e: e.tensor_tensor(out=self.mod[:, l, m0:m0 + MC], in0=pb2[:, 0:MC], in1=bm[:, l, m0:m0 + MC], op=ALU.add),
                          reads=[pt2, bm.tok], writes=[self.mod.tok])
                for i in range(3):
                    scs = self.mod[:, l, (3 * i + 1) * NK:(3 * i + 2) * NK]
                    kb.op("dve", lambda e: e.scalar_tensor_tensor(
                        out=self.modA[:, l, i * NK:(i + 1) * NK], in0=scs, scalar=1.0, in1=ng[:, l, i * NK:(i + 1) * NK],
                        op0=ALU.add, op1=ALU.mult), reads=[self.mod.tok, ng.tok], writes=[self.modA.tok])
                    gs_ = self.mod[:, l, (3 * i + 2) * NK:(3 * i + 3) * NK]
                    kb.op("dve", lambda e: e.tensor_scalar(
                        out=self.modG[:, l, i * NK:(i + 1) * NK], in0=gs_, scalar1=(1.0 if i == 1 else 0.5), scalar2=None,
                        op0=ALU.mult), reads=[self.mod.tok], writes=[self.modG.tok])
            kb.barrier()

    def norm_tile(self, hb, tmp, rstd, A, SH, out_dtype_r=True):
        cfg, kb = self.cfg, self.kb
        NK = cfg.NK
        pb, pt = self.bank()
        for kc in range(NK):
            t = tmp.next()
            kb.op("act", lambda e, t=t, kc=kc: e.activation(out=t[:], in_=hb.f32((slice(None), kc)), func=AF.Square),
                  reads=[hb.tok], writes=[t.tok])
            kb.op("pe", lambda e, t=t, kc=kc: e.matmul(pb, self.ones[:], t[:], start=(kc == 0), stop=(kc == NK - 1)),
                  reads=[t.tok, self.ones.tok], writes=[pt])
        t = tmp.next()
        kb.op("act", lambda e: e.activation(out=t[:], in_=pb, func=AF.Sqrt, bias=self.epsc[:, 0:1], scale=1.0 / cfg.DM),
              reads=[pt, self.epsc.tok], writes=[t.tok])
        kb.op("dve", lambda e: e.reciprocal(out=rstd[:], in_=t[:]), reads=[t.tok], writes=[rstd.tok])
        for kc in range(NK):
            t = tmp.next()
            kb.op("dve", lambda e, t=t, kc=kc: e.scalar_tensor_tensor(out=t[:], in0=hb.f32((slice(None), kc)), scalar=A[:, kc:kc + 1],
                                                                    in1=rstd[:], op0=ALU.mult, op1=ALU.mult),
                  reads=[hb.tok, rstd.tok, self.modA.tok, self.fngt.tok], writes=[t.tok])
            if SH is not None:
                kb.op("act", lambda e, t=t, kc=kc: e.activation(out=hb[:, kc], in_=t[:], func=AF.Identity, bias=SH[:, kc:kc + 1], scale=1.0),
                      reads=[t.tok, self.mod.tok], writes=[hb.tok])
            else:
                kb.op("act", lambda e, t=t, kc=kc: e.activation(out=hb[:, kc], in_=t[:], func=AF.Copy),
                      reads=[t.tok], writes=[hb.tok])

    def load_xtile(self, hb, src, tt):
        kb, cfg = self.kb, self.cfg
        sv = src.rearrange("(k p) t -> p k t", p=P)[:, :, tt * TT:(tt + 1) * TT]
        kb.dma("pool", hb[:], sv, reads=[self.dt("x", tt)], writes=[hb.tok])

    def phase_ffn(self, l, w, src):
        cfg, kb = self.cfg, self.kb
        NK, NJ, NT = cfg.NK, cfg.NJ, cfg.NT
        JH = (NJ + 1) // 2
        WSZ = max(NK, JH) * P
        A = self.modA[:, l, (2 * w) * NK:(2 * w + 1) * NK]
        SH = self.mod[:, l, (6 * w) * NK:(6 * w + 1) * NK]
        G = self.modG[:, l, (2 * w) * NK:(2 * w + 1) * NK]
        with ExitStack() as st:
            hb = Tile(kb, st, [P, NK, TT], F32R, name="hb")
            act = Tile(kb, st, [P, NJ, TT], F32R, ntok=NJ, name="act")
            wr = Rot(kb, st, 4, [P, WSZ], F32R, name="wf")
            tmp = Rot(kb, st, 3, [P, TT], F32, name="tmp")
            xc = Rot(kb, st, 3, [P, TT], F32, name="xc")
            rstd = Tile(kb, st, [P, TT], F32, name="rstd")
            srcf = src.bitcast(F32)
            xresf = self.xres.bitcast(F32)
            for tt in range(NT):
                self.load_xtile(hb, src, tt)
                self.norm_tile(hb, tmp, rstd, A, SH)
                for j in range(NJ):
                    pbs = []
                    for half in range(2):
                        wt = wr.next()
                        kb.dma("pool", wt[:, 0:NK * P], self.w1[l, w, half * NJ + j], writes=[wt.tok])
                        pb, pt = self.bank()
                        for kc in range(NK):
                            kb.op("pe", lambda e, wt=wt, kc=kc, pb=pb: e.matmul(pb, wt[:, kc * P:(kc + 1) * P], hb[:, kc],
                                                                              start=(kc == 0), stop=(kc == NK - 1)),
                                  reads=[wt.tok, hb.tok], writes=[pt])
                        pbs.append((pb, pt))
                    t = tmp.next()
                    kb.op("act", lambda e, t=t: e.activation(out=t[:], in_=pbs[0][0], func=AF.Silu), reads=[pbs[0][1]], writes=[t.tok])
                    kb.op("dve", lambda e, t=t, j=j: e.tensor_tensor(out=act[:, j], in0=pbs[1][0], in1=t[:], op=ALU.mult),
                          reads=[pbs[1][1], t.tok], writes=[act.toks[j]])
                for n in range(NK):
                    pb, pt = self.bank()
                    for hf in range(2):
                        j0, j1 = (0, JH) if hf == 0 else (JH, NJ)
                        wt = wr.next()
                        kb.dma("pool", wt[:, 0:(j1 - j0) * P], self.w2[l, w, n][:, j0 * P:j1 * P], writes=[wt.tok])
                        for j in range(j0, j1):
                            kb.op("pe", lambda e, wt=wt, j=j, j0=j0: e.matmul(pb, wt[:, (j - j0) * P:(j - j0 + 1) * P], act[:, j],
                                                                            start=(j == 0), stop=(j == NJ - 1)),
                                  reads=[wt.tok, act.toks[j]], writes=[pt])
                    x = xc.next()
                    kb.dma("sp", x[:], srcf[n * P:(n + 1) * P, tt * TT:(tt + 1) * TT], reads=[self.dt("x", tt)], writes=[x.tok])
                    kb.op("dve", lambda e, x=x, n=n: e.scalar_tensor_tensor(out=x[:], in0=pb, scalar=G[:, n:n + 1], in1=x[:],
                                                                          op0=ALU.mult, op1=ALU.add),
                          reads=[pt, x.tok, self.modG.tok], writes=[x.tok])
                    kb.dma("sp", xresf[n * P:(n + 1) * P, tt * TT:(tt + 1) * TT], x[:], reads=[x.tok], writes=[self.dt("xo", tt, n)])
                xt_ = self.dt("x", tt)
                xt_.w = []
                kb.join(xt_, [self.dt("xo", tt, n) for n in range(NK)])
            kb.barrier()

    def phase_final(self):
        cfg, kb = self.cfg, self.kb
        NK, NT = cfg.NK, cfg.NT
        with ExitStack() as st:
            hb = Tile(kb, st, [P, NK, TT], F32R, name="hbf")
            tmp = Rot(kb, st, 3, [P, TT], F32, name="tmpf")
            rstd = Tile(kb, st, [P, TT], F32, name="rstdf")
            for tt in range(NT):
                self.load_xtile(hb, self.xres, tt)
                self.norm_tile(hb, tmp, rstd, self.fngt, None)
                dv = self.yT.rearrange("(k p) t -> p k t", p=P)[:, :, tt * TT:(tt + 1) * TT]
                kb.dma("sp", dv, hb.f32(slice(None)), reads=[hb.tok], writes=[self.dt("y", tt)])
            kb.barrier()

    def mixer_decl(self):
        cfg = self.cfg
        NK, T, DEPTH, NSEG = cfg.NK, cfg.T, cfg.DEPTH, cfg.NSEG
        self.win = self.inp("win", [DEPTH, cfg.NZ, P, NK * P], F32R)
        self.zT = self.scratch("zT", [cfg.NZ * P, T])
        self.wpa = self.inp("wpa", [DEPTH, NK, P, 8 * P], F32R)
        self.wpb = self.inp("wpb", [DEPTH, NK, P, 8 * P], F32R)
        self.wpc = self.inp("wpc", [DEPTH, 2 * NK, P, 8 * P], F32R)
        self.wo = self.inp("wo", [DEPTH, NK, P, NK * P], F32R)
        self.ya = self.scratch("ya", [RW, T], F32R)
        self.yb = self.scratch("yb", [SW, T], F32R)
        self.yc = self.scratch("yc", [S5W, T], F32R)
        self.keepT = self.inp("keepT", [P, TT])
        self.rwp = self.scratch("rwp", [2, 5, RW, T], F32R)
        self.rwbon = self.scratch("rwbon", [RW, T])
        self.rwgc = self.scratch("rwgc", [2, RW, T // 64])
        self.rwsm = self.inp("rwsm", [P, 4, TT])
        self.rwcm = self.inp("rwcm", [P, T])
        self.rwcol = self.inp("rwcol", [DEPTH, P, 5, 8])
        self.rwmu = self.inp("rwmu", [DEPTH, P, 26])
        self.rww0 = self.inp("rww0", [DEPTH, P, 2, 2, 8])
        self.rww2 = self.inp("rww2", [DEPTH, P, 2, RW])
        self.rwg2 = self.inp("rwg2", [DEPTH, P, RW])
        self.hblk = self.inp("hblk", [2, P, P])
        self.rwtri = self.inp("rwtri", [3, P, 64])
        self.rws0 = self.inp("rws0", [DEPTH, 2, 64, 16, 64], F32R)
        self.rwo = self.outp("rwo", [DEPTH, 2, NSEG, 64, 16, 64])
        self.rwoT = self.scratch("rwoT", [2, RW, T])
        self.xcs = self.scratch("xcs", [SXBC, T])
        self.tri = self.inp("tri", [2, P, P])
        self.cmT = self.inp("cmT", [P, 4, TT])
        self.ssdcw = self.inp("ssdcw", [DEPTH, P, 12, 6])
        self.ssdcol = self.inp("ssdcol", [DEPTH, 64, 3])
        self.ssdD = self.inp("ssdD", [DEPTH, P, 8])
        self.ssdg = self.inp("ssdg", [DEPTH, P, 8])
        self.ssds0 = self.inp("ssds0", [DEPTH, 2, 2, P, 512])
        self.ssdkeep = self.inp("ssdkeep", [P, 1])
        self.ssdo = self.outp("ssdo", [DEPTH, 2, NSEG, 2, P, 512])
        self.s5lam = self.inp("s5lam", [DEPTH, 2, P, 3, 32])
        self.s5b = self.inp("s5b", [DEPTH, 2, P, 2, 32, 16])
        self.s5c = self.inp("s5c", [DEPTH, 2, 2, 32, P, P], F32R)
        self.s5d = self.inp("s5d", [DEPTH, P, 8])
        self.s5s0 = self.inp("s5s0", [DEPTH, 2, P, 2, 32])
        self.s5o = self.outp("s5o", [DEPTH, P, 2, 2, 32, NSEG])

    def mixer_consts(self):
        kb = self.kb
        self.epsc = Tile(kb, self.gs, [P, 4], F32, name="epsc")
        kb.op("dve", lambda e: e.memset(self.epsc[:, 0:1], EPS), writes=[self.epsc.tok])
        kb.op("dve", lambda e: e.memset(self.epsc[:, 1:2], float(np.pi / 2)), writes=[self.epsc.tok])
        kb.op("dve", lambda e: e.memset(self.epsc[:, 2:3], 1e-12), writes=[self.epsc.tok])
        kb.op("dve", lambda e: e.memset(self.epsc[:, 3:4], 0.0), writes=[self.epsc.tok])
        self.ident = Tile(kb, self.gs, [P, P], F32, name="ident")
        self.identd = self.inp("identd", [P, P])
        kb.dma("sp", self.ident[:], self.identd[:, :], writes=[self.ident.tok])
        self.keep = Tile(kb, self.gs, [P, TT], F32, name="keep")
        kb.dma("sp", self.keep[:], self.keepT[:, :], writes=[self.keep.tok])

    def phase_mix(self, l):
        cfg = self.cfg
        self.phase_inproj(l)
        self.shared_st = ExitStack()
        g1 = self.phase_rwkv(l) if cfg.mix[0] else iter(())
        g2 = None
        a1 = True
        a2 = bool(cfg.mix[2])
        while a1 or a2:
            if a1:
                try:
                    next(g1)
                except StopIteration:
                    a1 = False
            if a2:
                if g2 is None:
                    g2 = self.phase_s5(l)
                for _ in range(4):
                    try:
                        next(g2)
                    except StopIteration:
                        a2 = False
                        break
        self.kb.barrier()
        self.shared_st.close()
        if cfg.mix[0]:
            self.phase_rwkv_post(l)
        if cfg.mix[1]:
            self.phase_ssd(l)
        self.phase_merge(l)

    def phase_inproj(self, l):
        cfg, kb = self.cfg, self.kb
        NK, NT = cfg.NK, cfg.NT
        A = self.modA[:, l, NK:2 * NK]
        SH = self.mod[:, l, 3 * NK:4 * NK]
        with ExitStack() as st:
            hb = Tile(kb, st, [P, NK, TT], F32R, name="hbi")
            wr = Rot(kb, st, 4, [P, NK * P], F32R, name="wi")
            tmp = Rot(kb, st, 3, [P, TT], F32, name="tmpi")
            stg = Rot(kb, st, 4, [P, TT], F32, name="stg")
            rstd = Tile(kb, st, [P, TT], F32, name="rstdi")
            for tt in range(NT):
                self.load_xtile(hb, self.xres, tt)
                self.norm_tile(hb, tmp, rstd, A, SH)
                for m in range(cfg.NZ):
                    wt = wr.next()
                    kb.dma("pool", wt[:], self.win[l, m], writes=[wt.tok])
                    pb, pt = self.bank()
                    for kc in range(NK):
                        kb.op("pe", lambda e: e.matmul(pb, wt[:, kc * P:(kc + 1) * P], hb[:, kc], start=(kc == 0), stop=(kc == NK - 1)),
                              reads=[wt.tok, hb.tok], writes=[pt])
                    s = stg.next()
                    if m >= cfg.ZG and m < cfg.ZDT:
                        kb.op("act", lambda e: e.activation(out=s[:], in_=pb, func=AF.Sigmoid), reads=[pt], writes=[s.tok])
                    elif m >= cfg.ZB and m < cfg.ZB + 8:
                        kb.op("act", lambda e: e.activation(out=s[:], in_=pb, func=AF.Silu), reads=[pt], writes=[s.tok])
                    elif m % 2:
                        kb.op("act", lambda e: e.activation(out=s[:], in_=pb, func=AF.Copy), reads=[pt], writes=[s.tok])
                    else:
                        kb.op("dve", lambda e: e.tensor_copy(out=s[:], in_=pb), reads=[pt], writes=[s.tok])
                    kb.dma("sp", self.zT[m * P:(m + 1) * P, tt * TT:(tt + 1) * TT], s[:], reads=[s.tok], writes=[self.dt("z", m, tt)])
            kb.barrier()

    def phase_merge(self, l):
        cfg, kb = self.cfg, self.kb
        NK, NT = cfg.NK, cfg.NT
        G = self.modG[:, l, NK:2 * NK]
        xresf = self.xres.bitcast(F32)
        with ExitStack() as st:
            ys = [Tile(kb, st, [P, 8, TT], F32R, name="ym%d" % i) for i in range(3)]
            mg = Tile(kb, st, [P, NK, TT], F32R, ntok=NK, name="mg")
            wr = Rot(kb, st, 4, [P, max(NK, 8) * P], F32R, name="wm")
            gt = Rot(kb, st, 4, [P, TT], F32, name="gt")
            tmp = Rot(kb, st, 6, [P, TT], F32, name="tmpm")
            xc = Rot(kb, st, 3, [P, TT], F32, name="xcm")
            srcs = [self.ya, self.yb, self.yc]
            for tt in range(NT):
                for i in range(3):
                    if cfg.mix[i]:
                        sv = srcs[i].rearrange("(k p) t -> p k t", p=P)[:, :, tt * TT:(tt + 1) * TT]
                        kb.dma("pool", ys[i][:], sv, writes=[ys[i].tok])
                for n in range(NK):
                    terms = []
                    for i, wsrc in ((0, self.wpa), (1, self.wpb)):
                        if not cfg.mix[i]:
                            continue
                        wt = wr.next()
                        kb.dma("pool", wt[:, 0:8 * P], wsrc[l, n], writes=[wt.tok])
                        pb, pt = self.bank()
                        for k in range(8):
                            kb.op("pe", lambda e: e.matmul(pb, wt[:, k * P:(k + 1) * P], ys[i][:, k], start=(k == 0), stop=(k == 7)),
                                  reads=[wt.tok, ys[i].tok], writes=[pt])
                        g = gt.next()
                        kb.dma("sp", g[:], self.zT[(cfg.ZG + i * NK + n) * P:(cfg.ZG + i * NK + n + 1) * P, tt * TT:(tt + 1) * TT], writes=[g.tok])
                        t = tmp.next()
                        kb.op("dve", lambda e: e.tensor_tensor(out=t[:], in0=pb, in1=g[:], op=ALU.mult), reads=[pt, g.tok], writes=[t.tok])
                        terms.append(t)
                    if cfg.mix[2]:
                        pbs = []
                        for hf in range(2):
                            wt = wr.next()
                            kb.dma("pool", wt[:, 0:8 * P], self.wpc[l, hf * NK + n], writes=[wt.tok])
                            pb, pt = self.bank()
                            for k in range(8):
                                kb.op("pe", lambda e: e.matmul(pb, wt[:, k * P:(k + 1) * P], ys[2][:, k], start=(k == 0), stop=(k == 7)),
                                      reads=[wt.tok, ys[2].tok], writes=[pt])
                            pbs.append((pb, pt))
                        g = gt.next()
                        kb.dma("sp", g[:], self.zT[(cfg.ZG + 2 * NK + n) * P:(cfg.ZG + 2 * NK + n + 1) * P, tt * TT:(tt + 1) * TT], writes=[g.tok])
                        sg = tmp.next()
                        kb.op("act", lambda e: e.activation(out=sg[:], in_=pbs[1][0], func=AF.Sigmoid), reads=[pbs[1][1]], writes=[sg.tok])
                        t = tmp.next()
                        kb.op("dve", lambda e: e.tensor_tensor(out=t[:], in0=pbs[0][0], in1=sg[:], op=ALU.mult), reads=[pbs[0][1], sg.tok], writes=[t.tok])
                        kb.op("dve", lambda e: e.tensor_tensor(out=t[:], in0=t[:], in1=g[:], op=ALU.mult), reads=[t.tok, g.tok], writes=[t.tok])
                        terms.append(t)
                    if not terms:
                        kb.op("dve", lambda e: e.memset(mg[:, n], 0.0), writes=[mg.toks[n]])
                    elif len(terms) == 1:
                        kb.op("dve", lambda e: e.tensor_copy(out=mg[:, n], in_=terms[0][:]), reads=[terms[0].tok], writes=[mg.toks[n]])
                    else:
                        for a_ in terms[2:]:
                            kb.op("dve", lambda e: e.tensor_tensor(out=terms[0][:], in0=terms[0][:], in1=a_[:], op=ALU.add),
                                  reads=[terms[0].tok, a_.tok], writes=[terms[0].tok])
                        kb.op("dve", lambda e: e.tensor_tensor(out=mg[:, n], in0=terms[0][:], in1=terms[1][:], op=ALU.add),
                              reads=[terms[0].tok, terms[1].tok], writes=[mg.toks[n]])
                for n in range(NK):
                    wt = wr.next()
                    kb.dma("pool", wt[:, 0:NK * P], self.wo[l, n], writes=[wt.tok])
                    pb, pt = self.bank()
                    for k in range(NK):
                        kb.op("pe", lambda e: e.matmul(pb, wt[:, k * P:(k + 1) * P], mg[:, k], start=(k == 0), stop=(k == NK - 1)),
                              reads=[wt.tok, mg.toks[k]], writes=[pt])
                    x = xc.next()
                    kb.dma("sp", x[:], xresf[n * P:(n + 1) * P, tt * TT:(tt + 1) * TT], reads=[self.dt("x", tt)], writes=[x.tok])
                    kb.op("dve", lambda e: e.scalar_tensor_tensor(out=x[:], in0=pb, scalar=G[:, n:n + 1], in1=x[:], op0=ALU.mult, op1=ALU.add),
                          reads=[pt, x.tok, self.modG.tok], writes=[x.tok])
                    kb.dma("sp", xresf[n * P:(n + 1) * P, tt * TT:(tt + 1) * TT], x[:], reads=[x.tok], writes=[self.dt("xo", tt, n)])
            kb.barrier()

    def phase_s5(self, l):
        cfg, kb = self.cfg, self.kb
        T, NT, NSEG = cfg.T, cfg.NT, cfg.NSEG
        if True:
            st = self.shared_st
            def tl(shape, name, dtype=F32):
                return Tile(kb, st, shape, dtype, name=name)
            so = tl([P, 2, 2, 32, NSEG], "s5so")
            dcol = tl([P, 8], "s5dc")
            kb.dma("sp", dcol[:], self.s5d[l], writes=[dcol.tok])
            yacc = tl([P, T], "yacc")
            uch = tl([P, T], "uch", F32R)
            prm = []
            bq = tl([P, 2, 32, 16], "bq")
            tb = tl([P, 32, 16], "tb")
            for d in range(2):
                lam = tl([P, 3, 32], "lam")
                kb.dma("sp", lam[:], self.s5lam[l, d], writes=[lam.tok])
                kb.dma("sp", bq[:], self.s5b[l, d], writes=[bq.tok])
                s0 = tl([P, 2, 32], "s0")
                kb.dma("sp", s0[:], self.s5s0[l, d], writes=[s0.tok])
                w = tl([P, 16, 32], "s5w")
                def W(i):
                    return w[:, i, :]
                def tt_(o, a, b, op):
                    kb.op("dve", lambda e: e.tensor_tensor(out=o, in0=a, in1=b, op=op), reads=[w.tok, lam.tok], writes=[w.tok])
                def ts_(o, a, s1, op0, s2=None, op1=None):
                    if op1 is None:
                        kb.op("dve", lambda e: e.tensor_scalar(out=o, in0=a, scalar1=s1, scalar2=None, op0=op0), reads=[w.tok, lam.tok], writes=[w.tok])
                    else:
                        kb.op("dve", lambda e: e.tensor_scalar(out=o, in0=a, scalar1=s1, scalar2=s2, op0=op0, op1=op1), reads=[w.tok, lam.tok], writes=[w.tok])
                def ac_(o, a, f, bias=None, scale=1.0):
                    if bias is None:
                        kb.op("act", lambda e: e.activation(out=o, in_=a, func=f, scale=scale), reads=[w.tok, lam.tok], writes=[w.tok])
                    else:
                        kb.op("act", lambda e: e.activation(out=o, in_=a, func=f, bias=bias, scale=scale), reads=[w.tok, lam.tok, self.epsc.tok], writes=[w.tok])
                lre, lim, ldt = lam[:, 0, :], lam[:, 1, :], lam[:, 2, :]
                ac_(W(0), ldt, AF.Exp)
                tt_(W(1), lre, W(0), ALU.mult)
                ac_(W(1), W(1), AF.Exp)
                tt_(W(2), lim, W(0), ALU.mult)
                ac_(W(3), W(2), AF.Sin, scale=1.0 / 16)
                ac_(W(4), W(2), AF.Sin, bias=self.epsc[:, 1:2], scale=1.0 / 16)
                for _ in range(4):
                    tt_(W(5), W(3), W(4), ALU.mult)
                    tt_(W(6), W(4), W(4), ALU.mult)
                    tt_(W(7), W(3), W(3), ALU.mult)
                    tt_(W(4), W(6), W(7), ALU.subtract)
                    ts_(W(3), W(5), 2.0, ALU.mult)
                tt_(W(5), W(1), W(4), ALU.mult)
                tt_(W(6), W(1), W(3), ALU.mult)
                ts_(W(7), W(5), -1.0, ALU.add)
                tt_(W(8), lre, lre, ALU.mult)
                tt_(W(9), lim, lim, ALU.mult)
                tt_(W(8), W(8), W(9), ALU.add)
                kb.op("dve", lambda e: e.reciprocal(out=W(8), in_=W(8)), reads=[w.tok], writes=[w.tok])
                tt_(W(9), W(7), lre, ALU.mult)
                tt_(W(10), W(6), lim, ALU.mult)
                tt_(W(9), W(9), W(10), ALU.add)
                tt_(W(9), W(9), W(8), ALU.mult)
                tt_(W(10), W(6), lre, ALU.mult)
                tt_(W(11), W(7), lim, ALU.mult)
                tt_(W(10), W(10), W(11), ALU.subtract)
                tt_(W(10), W(10), W(8), ALU.mult)
                bb = tl([P, 2, 32, 16], "bb")
                qre_b = W(9).to_broadcast([P, 32, 16]) if False else None
                def bc(i):
                    return w[:, i, :].unsqueeze(2).to_broadcast([P, 32, 16])
                kb.op("dve", lambda e: e.tensor_tensor(out=bb[:, 0], in0=bq[:, 0], in1=bc(9), op=ALU.mult), reads=[bq.tok, w.tok], writes=[bb.tok])
                kb.op("dve", lambda e: e.tensor_tensor(out=tb[:], in0=bq[:, 1], in1=bc(10), op=ALU.mult), reads=[bq.tok, w.tok], writes=[tb.tok])
                kb.op("dve", lambda e: e.tensor_tensor(out=bb[:, 0], in0=bb[:, 0], in1=tb[:], op=ALU.subtract), reads=[bb.tok, tb.tok], writes=[bb.tok])
                kb.op("dve", lambda e: e.tensor_tensor(out=bb[:, 1], in0=bq[:, 1], in1=bc(9), op=ALU.mult), reads=[bq.tok, w.tok], writes=[bb.tok])
                kb.op("dve", lambda e: e.tensor_tensor(out=tb[:], in0=bq[:, 0], in1=bc(10), op=ALU.mult), reads=[bq.tok, w.tok], writes=[tb.tok])
                kb.op("dve", lambda e: e.tensor_tensor(out=bb[:, 1], in0=bb[:, 1], in1=tb[:], op=ALU.add), reads=[bb.tok, tb.tok], writes=[bb.tok])
                pw = tl([P, 10, 2, 32], "pw")
                kb.op("dve", lambda e: e.tensor_copy(out=pw[:, 0, 0], in_=W(4)), reads=[w.tok], writes=[pw.tok])
                kb.op("dve", lambda e: e.tensor_copy(out=pw[:, 0, 1], in_=W(3)), reads=[w.tok], writes=[pw.tok])
                for k in range(1, 10):
                    c_, s_ = pw[:, k - 1, 0], pw[:, k - 1, 1]
                    kb.op("dve", lambda e: e.tensor_tensor(out=W(11), in0=c_, in1=c_, op=ALU.mult), reads=[pw.tok, w.tok], writes=[w.tok])
                    kb.op("dve", lambda e: e.tensor_tensor(out=W(12), in0=s_, in1=s_, op=ALU.mult), reads=[pw.tok, w.tok], writes=[w.tok])
                    kb.op("dve", lambda e: e.tensor_tensor(out=pw[:, k, 0], in0=W(11), in1=W(12), op=ALU.subtract), reads=[w.tok, pw.tok], writes=[pw.tok])
                    kb.op("dve", lambda e: e.tensor_tensor(out=W(11), in0=c_, in1=s_, op=ALU.mult), reads=[pw.tok, w.tok], writes=[w.tok])
                    kb.op("dve", lambda e: e.tensor_scalar(out=pw[:, k, 1], in0=W(11), scalar1=2.0, scalar2=None, op0=ALU.mult), reads=[w.tok, pw.tok], writes=[pw.tok])
                prm.append(dict(w=w, bb=bb, pw=pw, s0=s0))
            Fc, Fs = tl([P, TT], "Fc"), tl([P, TT], "Fs")
            d0 = tl([P, TT], "d0")
            E = [tl([P, P], "Eb%d" % i) for i in range(2)]
            Bp = [tl([P, P], "Bp%d" % i, F32R) for i in range(2)]
            Cp = [tl([P, P], "Cp%d" % i, F32R) for i in range(2)]
            for e_ in E:
                kb.op("dve", lambda e: e.memset(e_[:], 0.0), writes=[e_.tok])
            wk = Rot(kb, st, 6, [P, TT], F32, name="s5wk")
            sreR = Rot(kb, st, 2, [P, TT], F32R, name="sre")
            nsiR = Rot(kb, st, 2, [P, TT], F32R, name="nsi")
            pend = []
            pk = Rot(kb, st, 4, [P, TT], F32, name="s5pk")
            Fsn = tl([P, TT], "Fsn")
            cin = tl([P, 4], "cin")
            tmpc = tl([P, 4], "tmpc")
            for Y in range(8):
                kb.dma("pool", uch[:], self.zT.bitcast(F32R)[(cfg.ZC + Y) * P:(cfg.ZC + Y + 1) * P, :], writes=[uch.tok])
                kb.op("dve", lambda e: e.tensor_scalar(out=yacc[:], in0=uch.f32(slice(None)), scalar1=dcol[:, Y:Y + 1], scalar2=None, op0=ALU.mult),
                      reads=[uch.tok, dcol.tok], writes=[yacc.tok])
                for q in range(4 * Y, 4 * Y + 4):
                    for d in range(2):
                        pr = prm[d]
                        w, bb, pw, s0 = pr["w"], pr["bb"], pr["pw"], pr["s0"]
                        while pend:
                            pend.pop(0)()
                        for ri in range(2):
                            for g2 in range(2):
                                g8 = 2 * (q % 4) + g2
                                kb.op("dve", lambda e: e.tensor_copy(out=E[ri][g2 * 64:(g2 + 1) * 64, g8 * 16:(g8 + 1) * 16],
                                                                   in_=bb[g2 * 64:(g2 + 1) * 64, ri, q, :]),
                                      reads=[bb.tok], writes=[E[ri].tok])
                            pb, pt = self.bank("B")
                            kb.op("pe", lambda e: e.transpose(pb[:, 0:P], E[ri][:], self.ident[:]), reads=[E[ri].tok, self.ident.tok], writes=[pt])
                            kb.op("act", lambda e: e.activation(out=Bp[ri][:], in_=pb[:, 0:P], func=AF.Copy), reads=[pt], writes=[Bp[ri].tok])
                            for g2 in range(2):
                                g8 = 2 * (q % 4) + g2
                                kb.op("dve", lambda e: e.memset(E[ri][g2 * 64:(g2 + 1) * 64, g8 * 16:(g8 + 1) * 16], 0.0),
                                      reads=[], writes=[E[ri].tok])
                            kb.dma("pool", Cp[ri][:], self.s5c[l, d, ri, q], writes=[Cp[ri].tok])
                        kb.op("dve", lambda e: e.memset(Fc[:, 0:1], 1.0), writes=[Fc.tok])
                        kb.op("dve", lambda e: e.memset(Fs[:, 0:1], 0.0), writes=[Fs.tok])
                        for k in range(9):
                            n_ = 1 << k
                            pc_, ps_ = pw[:, k, 0, q:q + 1], pw[:, k, 1, q:q + 1]
                            t1 = wk.next()
                            kb.op("dve", lambda e: e.tensor_scalar(out=t1[:, 0:n_], in0=Fs[:, 0:n_], scalar1=ps_, scalar2=None, op0=ALU.mult),
                                  reads=[Fs.tok, pw.tok], writes=[t1.tok])
                            t2 = wk.next()
                            kb.op("dve", lambda e: e.tensor_scalar(out=t2[:, 0:n_], in0=Fc[:, 0:n_], scalar1=ps_, scalar2=None, op0=ALU.mult),
                                  reads=[Fc.tok, pw.tok], writes=[t2.tok])
                            kb.op("dve", lambda e: e.scalar_tensor_tensor(out=Fc[:, n_:2 * n_], in0=Fc[:, 0:n_], scalar=pc_, in1=t1[:, 0:n_],
                                                                        op0=ALU.mult, op1=ALU.subtract),
                                  reads=[Fc.tok, pw.tok, t1.tok], writes=[Fc.tok])
                            kb.op("dve", lambda e: e.scalar_tensor_tensor(out=Fs[:, n_:2 * n_], in0=Fs[:, 0:n_], scalar=pc_, in1=t2[:, 0:n_],
                                                                        op0=ALU.mult, op1=ALU.add),
                                  reads=[Fs.tok, pw.tok, t2.tok], writes=[Fs.tok])
                        kb.op("dve", lambda e: e.tensor_scalar(out=Fsn[:], in0=Fs[:], scalar1=-1.0, scalar2=None, op0=ALU.mult), reads=[Fs.tok], writes=[Fsn.tok])
                        kb.op("dve", lambda e: e.tensor_scalar(out=d0[:], in0=self.keep[:], scalar1=w[:, 1, q:q + 1], scalar2=None, op0=ALU.mult),
                              reads=[self.keep.tok, w.tok], writes=[d0.tok])
                        def crot(src_re, src_im, cc_, ss_, rd):
                            kb.op("dve", lambda e: e.tensor_scalar(out=tmpc[:, 0:1], in0=src_im, scalar1=ss_, scalar2=None, op0=ALU.mult), reads=rd + [pw.tok], writes=[tmpc.tok])
                            kb.op("dve", lambda e: e.scalar_tensor_tensor(out=cin[:, 2:3], in0=src_re, scalar=cc_, in1=tmpc[:, 0:1], op0=ALU.mult, op1=ALU.subtract),
                                  reads=rd + [tmpc.tok, pw.tok], writes=[cin.tok])
                            kb.op("dve", lambda e: e.tensor_scalar(out=tmpc[:, 1:2], in0=src_re, scalar1=ss_, scalar2=None, op0=ALU.mult), reads=rd + [pw.tok], writes=[tmpc.tok])
                            kb.op("dve", lambda e: e.scalar_tensor_tensor(out=cin[:, 3:4], in0=src_im, scalar=cc_, in1=tmpc[:, 1:2], op0=ALU.mult, op1=ALU.add),
                                  reads=rd + [tmpc.tok, pw.tok], writes=[cin.tok])
                        crot(s0[:, 0, q:q + 1], s0[:, 1, q:q + 1], pw[:, 0, 0, q:q + 1], pw[:, 0, 1, q:q + 1], [s0.tok])
                        for tg in range(NT):
                            if d == 0:
                                usl = uch[:, tg * TT:(tg + 1) * TT]
                                ysl = yacc[:, tg * TT:(tg + 1) * TT]
                            else:
                                hi = T - tg * TT
                                usl = uch[:, hi - TT:hi][:, ::-1]
                                ysl = yacc[:, hi - TT:hi][:, ::-1]
                            pbr, ptr = self.bank("B")
                            kb.op("pe", lambda e: e.matmul(pbr, Bp[0][:], usl, start=True, stop=True), reads=[Bp[0].tok, uch.tok], writes=[ptr])
                            pbi, pti = self.bank("B")
                            kb.op("pe", lambda e: e.matmul(pbi, Bp[1][:], usl, start=True, stop=True), reads=[Bp[1].tok, uch.tok], writes=[pti])
                            a1, a2, a3, a4 = wk.next(), wk.next(), wk.next(), wk.next()
                            kb.op("dve", lambda e: e.tensor_tensor(out=a1[:], in0=pbr, in1=Fc[:], op=ALU.mult), reads=[ptr, Fc.tok], writes=[a1.tok])
                            kb.op("dve", lambda e: e.tensor_tensor(out=a2[:], in0=pbi, in1=Fs[:], op=ALU.mult), reads=[pti, Fs.tok], writes=[a2.tok])
                            kb.op("dve", lambda e: e.tensor_tensor(out=a3[:], in0=pbi, in1=Fc[:], op=ALU.mult), reads=[pti, Fc.tok], writes=[a3.tok])
                            kb.op("dve", lambda e: e.tensor_tensor(out=a4[:], in0=pbr, in1=Fs[:], op=ALU.mult), reads=[ptr, Fs.tok], writes=[a4.tok])
                            kb.op("dve", lambda e: e.tensor_tensor(out=a1[:], in0=a1[:], in1=a2[:], op=ALU.add), reads=[a1.tok, a2.tok], writes=[a1.tok])
                            kb.op("dve", lambda e: e.tensor_tensor(out=a3[:], in0=a3[:], in1=a4[:], op=ALU.subtract), reads=[a3.tok, a4.tok], writes=[a3.tok])
                            kb.op("dve", lambda e: e.tensor_tensor_scan(out=a2[:], data0=d0[:], data1=a1[:], initial=cin[:, 2:3], op0=ALU.mult, op1=ALU.add),
                                  reads=[d0.tok, a1.tok, cin.tok], writes=[a2.tok])
                            kb.op("dve", lambda e: e.tensor_tensor_scan(out=a4[:], data0=d0[:], data1=a3[:], initial=cin[:, 3:4], op0=ALU.mult, op1=ALU.add),
                                  reads=[d0.tok, a3.tok, cin.tok], writes=[a4.tok])
                            while pend:
                                pend.pop(0)()
                            if tg + 1 < NT:
                                crot(a2[:, TT - 1:TT], a4[:, TT - 1:TT], pw[:, 9, 0, q:q + 1], pw[:, 9, 1, q:q + 1], [a2.tok, a4.tok])
                            sre, nsi = sreR.next(), nsiR.next()
                            p1, p2, p3, p4 = pk.next(), pk.next(), pk.next(), pk.next()
                            kb.op("pool", lambda e: e.tensor_tensor(out=p1[:], in0=a2[:], in1=Fc[:], op=ALU.mult), reads=[a2.tok, Fc.tok], writes=[p1.tok])
                            kb.op("pool", lambda e: e.tensor_tensor(out=p2[:], in0=a4[:], in1=Fs[:], op=ALU.mult), reads=[a4.tok, Fs.tok], writes=[p2.tok])
                            kb.op("pool", lambda e: e.tensor_tensor(out=sre[:], in0=p1[:], in1=p2[:], op=ALU.subtract), reads=[p1.tok, p2.tok], writes=[sre.tok])
                            kb.op("pool", lambda e: e.tensor_tensor(out=p3[:], in0=a2[:], in1=Fsn[:], op=ALU.mult), reads=[a2.tok, Fsn.tok], writes=[p3.tok])
                            kb.op("pool", lambda e: e.tensor_tensor(out=p4[:], in0=a4[:], in1=Fc[:], op=ALU.mult), reads=[a4.tok, Fc.tok], writes=[p4.tok])
                            kb.op("pool", lambda e: e.tensor_tensor(out=nsi[:], in0=p3[:], in1=p4[:], op=ALU.subtract), reads=[p3.tok, p4.tok], writes=[nsi.tok])
                            nsg = TT // 256
                            kb.op("act", lambda e: e.activation(out=so[:, d, 0, q, tg * nsg:(tg + 1) * nsg], in_=sre.f32((slice(None), slice(255, None, 256))), func=AF.Identity, scale=1.0), reads=[sre.tok], writes=[so.tok])
                            kb.op("act", lambda e: e.activation(out=so[:, d, 1, q, tg * nsg:(tg + 1) * nsg], in_=nsi.f32((slice(None), slice(255, None, 256))), func=AF.Identity, scale=-1.0), reads=[nsi.tok], writes=[so.tok])
                            pby, pty = self.bank("B")
                            kb.op("pe", lambda e: e.matmul(pby, Cp[0][:], sre[:], start=True, stop=False), reads=[Cp[0].tok, sre.tok], writes=[pty])
                            kb.op("pe", lambda e: e.matmul(pby, Cp[1][:], nsi[:], start=False, stop=True), reads=[Cp[1].tok, nsi.tok], writes=[pty])
                            pend.append(lambda pby=pby, pty=pty, ysl=ysl: kb.op("dve", lambda e: e.tensor_tensor(out=ysl, in0=pby, in1=ysl, op=ALU.add),
                                                                                 reads=[pty, yacc.tok], writes=[yacc.tok]))
                            yield
                while pend:
                    pend.pop(0)()
                for tg in range(NT):
                    ysl = yacc[:, tg * TT:(tg + 1) * TT]
                    a1, a2 = wk.next(), wk.next()
                    kb.op("act", lambda e: e.activation(out=a1[:], in_=ysl, func=AF.Square), reads=[yacc.tok], writes=[a1.tok])
                    kb.op("dve", lambda e: e.tensor_scalar(out=a1[:], in0=a1[:], scalar1=0.044715, scalar2=1.0, op0=ALU.mult, op1=ALU.add),
                          reads=[a1.tok], writes=[a1.tok])
                    kb.op("dve", lambda e: e.tensor_tensor(out=a1[:], in0=a1[:], in1=ysl, op=ALU.mult), reads=[a1.tok, yacc.tok], writes=[a1.tok])
                    kb.op("act", lambda e: e.activation(out=a2[:], in_=a1[:], func=AF.Tanh, scale=0.7978845608028654), reads=[a1.tok], writes=[a2.tok])
                    kb.op("dve", lambda e: e.tensor_scalar(out=a2[:], in0=a2[:], scalar1=0.5, scalar2=0.5, op0=ALU.mult, op1=ALU.add),
                          reads=[a2.tok], writes=[a2.tok])
                    kb.op("dve", lambda e: e.tensor_tensor(out=a2[:], in0=a2[:], in1=ysl, op=ALU.mult), reads=[a2.tok, yacc.tok], writes=[a2.tok])
                    kb.dma("sp", self.yc.bitcast(F32)[Y * P:(Y + 1) * P, tg * TT:(tg + 1) * TT], a2[:], reads=[a2.tok], writes=[self.dt("yc", Y, tg)])
            kb.dma("sp", self.s5o[l], so[:], reads=[so.tok], writes=[self.dt("s5o", l)])

    def phase_rwkv(self, l):
        cfg, kb = self.cfg, self.kb
        T, NT, NSEG = cfg.T, cfg.NT, cfg.NSEG
        NCH = T // 64
        HS = (slice(0, 64), slice(64, 128))
        with ExitStack() as st:
            def tl(shape, name, dtype=F32):
                return Tile(kb, st, shape, dtype, name=name)
            sm = tl([P, 4, TT], "rwsm"); kb.dma("sp", sm[:], self.rwsm[:, :, :], writes=[sm.tok])
            cmk = tl([P, T], "rwcm"); kb.dma("sp", cmk[:], self.rwcm[:, :], writes=[cmk.tok])
            col = tl([P, 5, 8], "rwcol"); kb.dma("sp", col[:], self.rwcol[l], writes=[col.tok])
            mu = tl([P, 26], "rwmu"); kb.dma("sp", mu[:], self.rwmu[l], writes=[mu.tok])
            om = tl([P, 26], "rwom")
            kb.op("dve", lambda e: e.tensor_scalar(out=om[:], in0=mu[:], scalar1=-1.0, scalar2=1.0, op0=ALU.mult, op1=ALU.add), reads=[mu.tok], writes=[om.tok])
            oka = tl([P, 8], "rwoka")
            kb.op("dve", lambda e: e.tensor_scalar(out=oka[:], in0=col[:, 1, :], scalar1=-1.0, scalar2=1.0, op0=ALU.mult, op1=ALU.add), reads=[col.tok], writes=[oka.tok])
            w0 = tl([P, 2, 2, 8], "rww0"); kb.dma("sp", w0[:], self.rww0[l], writes=[w0.tok])
            w2 = tl([P, 2, RW], "rww2"); kb.dma("sp", w2[:], self.rww2[l], writes=[w2.tok])
            hb1 = tl([P, P], "hb1"); kb.dma("sp", hb1[:], self.hblk[0], writes=[hb1.tok])
            xraw = tl([P, T], "xraw")
            sacc = tl([P, T], "sacc")
            stmp = Rot(kb, st, 3, [P, TT], F32, name="stmp")

            def load_shifted(ch, dst):
                kb.dma("sp", xraw[:], self.zT[(cfg.ZA + ch) * P:(cfg.ZA + ch + 1) * P, :], writes=[xraw.tok])
                kb.op("dve", lambda e: e.memset(sacc[:], 0.0), writes=[sacc.tok])
                for oi, o in enumerate((-1, 1, -64, 64)):
                    for tg in range(NT):
                        lo, hi = tg * TT, (tg + 1) * TT
                        slo, shi = max(lo + o, 0), min(hi + o, T)
                        dlo, dhi = slo - o, shi - o
                        n_ = dhi - dlo
                        t = stmp.next()
                        kb.op("dve", lambda e: e.tensor_tensor(out=t[:, 0:n_], in0=xraw[:, slo:shi], in1=sm[:, oi, dlo - lo:dhi - lo], op=ALU.mult),
                              reads=[xraw.tok, sm.tok], writes=[t.tok])
                        kb.op("dve", lambda e: e.tensor_tensor(out=sacc[:, dlo:dhi], in0=sacc[:, dlo:dhi], in1=t[:, 0:n_], op=ALU.add),
                              reads=[sacc.tok, t.tok], writes=[sacc.tok])
                kb.op("dve", lambda e: e.tensor_scalar(out=xraw[:], in0=xraw[:], scalar1=om[:, ch:ch + 1], scalar2=None, op0=ALU.mult), reads=[xraw.tok, om.tok], writes=[xraw.tok])
                kb.op("dve", lambda e: e.scalar_tensor_tensor(out=dst[:], in0=sacc[:], scalar=mu[:, ch:ch + 1], in1=xraw[:], op0=ALU.mult, op1=ALU.add),
                      reads=[sacc.tok, mu.tok, xraw.tok], writes=[dst.tok])

            tw = tl([P, T], "rwtw")
            load_shifted(24, tw)
            kb.op("act", lambda e: e.activation(out=tw[0:64, :], in_=tw[0:64, :], func=AF.Tanh), reads=[tw.tok], writes=[tw.tok])
            rr, kk_, vv_ = tl([P, T], "rwr"), tl([P, T], "rwk"), tl([P, T], "rwv")
            kkn = tl([P, T], "rwkkn")
            ad, ldc, cum = tl([P, T], "rwad"), tl([P, T], "rwld"), tl([P, T], "rwcum")
            t1, t2 = tl([P, T], "rwt1"), tl([P, T], "rwt2")
            outr = Rot(kb, st, 3, [P, T], F32, name="rwout")
            gct = tl([P, NCH], "rwgct")
            for j in range(8):
                load_shifted(j, rr)
                load_shifted(8 + j, kk_)
                load_shifted(16 + j, vv_)
                kb.op("dve", lambda e: e.tensor_scalar(out=kkn[:], in0=kk_[:], scalar1=col[:, 0, j:j + 1], scalar2=None, op0=ALU.mult), reads=[kk_.tok, col.tok], writes=[kkn.tok])
                kb.op("act", lambda e: e.activation(out=t1[:], in_=kkn[:], func=AF.Square), reads=[kkn.tok], writes=[t1.tok])
                for tg in range(NT):
                    tc_ = slice(tg * TT, (tg + 1) * TT)
                    pb, pt = self.bank("A")
                    kb.op("pe", lambda e: e.matmul(pb, hb1[:], t1[:, tc_], start=True, stop=True), reads=[hb1.tok, t1.tok], writes=[pt])
                    kb.op("act", lambda e: e.activation(out=t2[:, tc_], in_=pb, func=AF.Sqrt, bias=self.epsc[:, 2:3], scale=1.0), reads=[pt, self.epsc.tok], writes=[t2.tok])
                kb.op("dve", lambda e: e.reciprocal(out=t2[:], in_=t2[:]), reads=[t2.tok], writes=[t2.tok])
                kb.op("dve", lambda e: e.tensor_tensor(out=kkn[:], in0=kkn[:], in1=t2[:], op=ALU.mult), reads=[kkn.tok, t2.tok], writes=[kkn.tok])
                kb.op("dve", lambda e: e.scalar_tensor_tensor(out=t1[:], in0=rr[:], scalar=col[:, 2, j:j + 1], in1=kk_[:], op0=ALU.mult, op1=ALU.mult),
                      reads=[rr.tok, col.tok, kk_.tok], writes=[t1.tok])
                bo = outr.next()
                for tg in range(NT):
                    tc_ = slice(tg * TT, (tg + 1) * TT)
                    pb, pt = self.bank("A")
                    kb.op("pe", lambda e: e.matmul(pb, hb1[:], t1[:, tc_], start=True, stop=True), reads=[hb1.tok, t1.tok], writes=[pt])
                    kb.op("dve", lambda e: e.tensor_tensor(out=bo[:, tc_], in0=pb, in1=vv_[:, tc_], op=ALU.mult), reads=[pt, vv_.tok], writes=[bo.tok])
                kb.dma("sp", self.rwbon[j * P:(j + 1) * P, :], bo[:], reads=[bo.tok], writes=[self.dt("rwbon", j)])
                for d in range(2):
                    R = (lambda ap: ap) if d == 0 else (lambda ap: ap[:, ::-1])
                    for tg in range(NT):
                        tc_ = slice(tg * TT, (tg + 1) * TT)
                        pb, pt = self.bank("A")
                        kb.op("pe", lambda e: e.matmul(pb, w2[0:64, d, j * P:(j + 1) * P], tw[0:64, tc_], start=True, stop=True), reads=[w2.tok, tw.tok], writes=[pt])
                        kb.op("act", lambda e: e.activation(out=ldc[:, tc_], in_=pb, func=AF.Sigmoid, bias=w0[:, d, 0, j:j + 1], scale=1.0), reads=[pt, w0.tok], writes=[ldc.tok])
                        pb2, pt2 = self.bank("A")
                        kb.op("pe", lambda e: e.matmul(pb2, w2[64:128, d, j * P:(j + 1) * P], tw[64:128, tc_], start=True, stop=True), reads=[w2.tok, tw.tok], writes=[pt2])
                        kb.op("act", lambda e: e.activation(out=ad[:, tc_], in_=pb2, func=AF.Sigmoid, bias=w0[:, d, 1, j:j + 1], scale=1.0), reads=[pt2, w0.tok], writes=[ad.tok])
                    kb.op("dve", lambda e: e.tensor_scalar(out=ldc[:], in0=ldc[:], scalar1=-0.6065306597126334, scalar2=None, op0=ALU.mult), reads=[ldc.tok], writes=[ldc.tok])
                    kb.op("dve", lambda e: e.tensor_tensor_scan(out=cum[:], data0=cmk[:], data1=R(ldc[:]), initial=0.0, op0=ALU.mult, op1=ALU.add),
                          reads=[cmk.tok, ldc.tok], writes=[cum.tok])
                    o_rt = outr.next()
                    kb.op("act", lambda e: e.activation(out=t1[:], in_=cum[:], func=AF.Exp), reads=[cum.tok], writes=[t1.tok])
                    kb.op("dve", lambda e: e.tensor_tensor(out=o_rt[:], in0=t1[:], in1=R(rr[:]), op=ALU.mult), reads=[t1.tok, rr.tok], writes=[o_rt.tok])
                    kb.dma("sp", self.rwp[d, 3, j * P:(j + 1) * P, :].bitcast(F32), o_rt[:], reads=[o_rt.tok], writes=[self.dt("rwp", d, 3, j)])
                    kb.op("dve", lambda e: e.tensor_copy(out=gct[:], in_=t1[:, 63::64]), reads=[t1.tok], writes=[gct.tok])
                    kb.dma("sp", self.rwgc[d, j * P:(j + 1) * P, :], gct[:], reads=[gct.tok], writes=[self.dt("rwgc", d, j)])
                    o_at = outr.next()
                    kb.op("dve", lambda e: e.tensor_tensor(out=t2[:], in0=cum[:], in1=R(ldc[:]), op=ALU.subtract), reads=[cum.tok, ldc.tok], writes=[t2.tok])
                    kb.op("act", lambda e: e.activation(out=t2[:], in_=t2[:], func=AF.Exp), reads=[t2.tok], writes=[t2.tok])
                    kb.op("dve", lambda e: e.scalar_tensor_tensor(out=o_at[:], in0=t2[:], scalar=-1.0, in1=R(kkn[:]), op0=ALU.mult, op1=ALU.mult),
                          reads=[t2.tok, kkn.tok], writes=[o_at.tok])
                    kb.dma("sp", self.rwp[d, 0, j * P:(j + 1) * P, :].bitcast(F32), o_at[:], reads=[o_at.tok], writes=[self.dt("rwp", d, 0, j)])
                    kb.op("act", lambda e: e.activation(out=t1[:], in_=cum[:], func=AF.Exp, scale=-1.0), reads=[cum.tok], writes=[t1.tok])
                    o_bt = outr.next()
                    kb.op("dve", lambda e: e.tensor_tensor(out=t2[:], in0=R(kkn[:]), in1=R(ad[:]), op=ALU.mult), reads=[kkn.tok, ad.tok], writes=[t2.tok])
                    kb.op("dve", lambda e: e.tensor_tensor(out=o_bt[:], in0=t2[:], in1=t1[:], op=ALU.mult), reads=[t2.tok, t1.tok], writes=[o_bt.tok])
                    kb.dma("sp", self.rwp[d, 1, j * P:(j + 1) * P, :].bitcast(F32), o_bt[:], reads=[o_bt.tok], writes=[self.dt("rwp", d, 1, j)])
                    o_kt = outr.next()
                    kb.op("dve", lambda e: e.tensor_scalar(out=t2[:], in0=R(ad[:]), scalar1=col[:, 1, j:j + 1], scalar2=oka[:, j:j + 1], op0=ALU.mult, op1=ALU.add),
                          reads=[ad.tok, col.tok, oka.tok], writes=[t2.tok])
                    kb.op("dve", lambda e: e.tensor_tensor(out=t2[:], in0=t2[:], in1=R(kk_[:]), op=ALU.mult), reads=[t2.tok, kk_.tok], writes=[t2.tok])
                    kb.op("dve", lambda e: e.tensor_tensor(out=o_kt[:], in0=t2[:], in1=t1[:], op=ALU.mult), reads=[t2.tok, t1.tok], writes=[o_kt.tok])
                    kb.dma("sp", self.rwp[d, 2, j * P:(j + 1) * P, :].bitcast(F32), o_kt[:], reads=[o_kt.tok], writes=[self.dt("rwp", d, 2, j)])
                    o_v = outr.next()
                    kb.op("act", lambda e: e.activation(out=o_v[:], in_=R(vv_[:]), func=AF.Copy), reads=[vv_.tok], writes=[o_v.tok])
                    kb.dma("sp", self.rwp[d, 4, j * P:(j + 1) * P, :].bitcast(F32), o_v[:], reads=[o_v.tok], writes=[self.dt("rwp", d, 4, j)])
            kb.barrier()
        if True:
            st = self.shared_st
            def tl(shape, name, dtype=F32):
                return Tile(kb, st, shape, dtype, name=name)
            H = 64
            trm = tl([H, 3, 64], "rwtri")
            for i in range(3):
                kb.dma("sp", trm[:, i, :], self.rwtri[i][0:64, :], writes=[trm.tok])
            kcol = tl([P, 1], "rwkcol"); kb.dma("sp", kcol[:], self.ssdkeep[:, :], writes=[kcol.tok])
            Sst = tl([H, 16, 64], "rwS", F32R)
            gcs = tl([H, 16, NCH], "rwgcs")
            slab = [Rot(kb, st, 2, [H, 16, 64], F32R, name="rwsl%d" % q) for q in range(5)]
            tk = [Rot(kb, st, 1, [H, 16, 64], F32R, name="rwtk%d" % q) for q in range(3)]
            Am = [Rot(kb, st, 1, [H, 16, 64], F32R, name="rwA%d" % q) for q in range(3)]
            Ak = Rot(kb, st, 2, [H, 16, 64], F32R, name="rwAk")
            AkT = Rot(kb, st, 2, [H, 16, 64], F32R, name="rwAkT")
            Tm = Rot(kb, st, 2, [H, 16, 64], F32R, name="rwTm")
            Wt = Rot(kb, st, 1, [H, 16, 64], F32R, name="rwWt")
            Ut = Rot(kb, st, 1, [H, 16, 64], F32R, name="rwUt")
            ost = Rot(kb, st, 1, [H, 16, 64], F32, name="rwost")

            def bc16(ap2):
                return ap2.unsqueeze(1).to_broadcast([H, 16, 64])

            def mm16(psb, ptk, lhs_fn, rhs_fn, reads, first=True, last=True):
                for b_ in range(16):
                    kb.op("pe", lambda e: e.matmul(psb[0:H, b_ * 64:(b_ + 1) * 64], lhs_fn(b_), rhs_fn(b_), start=first, stop=last), reads=reads, writes=ptk)

            def mm16g(psb, ptk, terms):
                n = len(terms)
                for b_ in range(16):
                    for ti, (lt, rt__) in enumerate(terms):
                        kb.op("pe", lambda e: e.matmul(psb[0:H, b_ * 64:(b_ + 1) * 64], lt[:, b_, :], rt__[:, b_, :], start=(ti == 0), stop=(ti == n - 1)),
                              reads=[lt.tok, rt__.tok], writes=ptk)

            def v3(pb):
                return pb[0:H, :].rearrange("p (b t) -> p b t", t=64)

            def dview(ap2d):
                return ap2d.rearrange("(b k) x -> k b x", k=64)

            for d in range(2):
                kb.dma("pool", Sst[:], self.rws0[l, d], writes=[Sst.tok])
                kb.dma("sp", gcs[:], dview(self.rwgc[d]), reads=[self.dt("rwgc", d, 0)], writes=[gcs.tok])
                for c in range(NCH):
                    cs = slice(c * 64, (c + 1) * 64)
                    sl = [r_.next() for r_ in slab]
                    for q in range(5):
                        kb.dma("pool", sl[q][:], dview(self.rwp[d, q])[:, :, cs], writes=[sl[q].tok])
                    at_, bt_, kt_, rt_, vs_ = sl
                    tks = []
                    for q, src in enumerate((bt_, kt_, vs_)):
                        pb, pt = self.bank2("A")
                        for b_ in range(16):
                            kb.op("pe", lambda e: e.transpose(pb[0:H, b_ * 64:(b_ + 1) * 64], src[:, b_, :].bitcast(F32), self.ident[0:H, 0:H]), reads=[src.tok, self.ident.tok], writes=pt)
                        t_ = tk[q].next()
                        if q != 2:
                            kb.op("act", lambda e: e.activation(out=t_[:], in_=v3(pb), func=AF.Copy), reads=pt, writes=[t_.tok])
                        else:
                            kb.op("dve", lambda e: e.tensor_copy(out=t_[:], in_=v3(pb)), reads=pt, writes=[t_.tok])
                        tks.append(t_)
                    Btk, Ktk, Vtk = tks

                    def amat(lhs, rhs, mask_i, dst):
                        pb, pt = self.bank2("A")
                        mm16(pb, pt, lambda b_: lhs[:, b_, :], lambda b_: rhs[:, b_, :], [lhs.tok, rhs.tok])
                        kb.op("dve", lambda e: e.tensor_tensor(out=dst[:], in0=v3(pb), in1=bc16(trm[:, mask_i, :]), op=ALU.mult), reads=pt + [trm.tok], writes=[dst.tok])
                    A0, A0T = Ak.next(), AkT.next()
                    amat(bt_, at_, 0, A0)
                    amat(at_, bt_, 1, A0T)
                    Aak, Arb, Ark = Am[0].next(), Am[1].next(), Am[2].next()
                    amat(kt_, at_, 0, Aak)
                    amat(bt_, rt_, 2, Arb)
                    amat(kt_, rt_, 2, Ark)
                    Tc = Tm.next()
                    kb.op("dve", lambda e: e.tensor_tensor(out=Tc[:], in0=A0.f32(slice(None)), in1=bc16(self.ident[0:H, 0:H]), op=ALU.add), reads=[A0.tok, self.ident.tok], writes=[Tc.tok])
                    Ap, ApT = A0, A0T
                    for lev in range(1, 6):
                        An, AnT = Ak.next(), AkT.next()
                        pb, pt = self.bank2("A")
                        mm16(pb, pt, lambda b_: ApT[:, b_, :], lambda b_: Ap[:, b_, :], [Ap.tok, ApT.tok])
                        pb2, pt2 = self.bank2("A")
                        mm16(pb2, pt2, lambda b_: Ap[:, b_, :], lambda b_: ApT[:, b_, :], [Ap.tok, ApT.tok])
                        kb.op("act", lambda e: e.activation(out=An[:], in_=v3(pb), func=AF.Copy), reads=pt, writes=[An.tok])
                        kb.op("act", lambda e: e.activation(out=AnT[:], in_=v3(pb2), func=AF.Copy), reads=pt2, writes=[AnT.tok])
                        pb3, pt3 = self.bank2("A")
                        mm16(pb3, pt3, lambda b_: AnT[:, b_, :], lambda b_: Tc[:, b_, :], [AnT.tok, Tc.tok])
                        Tn = Tm.next()
                        kb.op("dve", lambda e: e.tensor_tensor(out=Tn[:], in0=v3(pb3), in1=Tc.f32(slice(None)), op=ALU.add), reads=pt3 + [Tc.tok], writes=[Tn.tok])
                        Tc, Ap, ApT = Tn, An, AnT
                    pbw, ptw = self.bank2("A")
                    mm16g(pbw, ptw, [(at_, Sst), (Aak, Vtk)])
                    W_ = Wt.next()
                    kb.op("act", lambda e: e.activation(out=W_[:], in_=v3(pbw), func=AF.Copy), reads=ptw, writes=[W_.tok])
                    pbu, ptu = self.bank2("A")
                    mm16(pbu, ptu, lambda b_: Tc[:, b_, :], lambda b_: W_[:, b_, :], [Tc.tok, W_.tok])
                    U_ = Ut.next()
                    kb.op("act", lambda e: e.activation(out=U_[:], in_=v3(pbu), func=AF.Copy), reads=ptu, writes=[U_.tok])
                    pbo, pto = self.bank2("A")
                    mm16g(pbo, pto, [(Sst, rt_), (U_, Arb), (Vtk, Ark)])
                    pbs, pts = self.bank2("A")
                    mm16g(pbs, pts, [(Btk, U_), (Ktk, Vtk)])
                    o_ = ost.next()
                    if d == 0:
                        kb.op("act", lambda e: e.activation(out=o_[:], in_=v3(pbo), func=AF.Copy), reads=pto, writes=[o_.tok])
                        tcs = cs
                    else:
                        kb.op("dve", lambda e: e.tensor_copy(out=o_[:, :, ::-1], in_=v3(pbo)), reads=pto, writes=[o_.tok])
                        tcs = slice(T - (c + 1) * 64, T - c * 64)
                    kb.dma("sp", dview(self.rwoT[d])[:, :, tcs], o_[:], reads=[o_.tok], writes=[self.dt("rwoT", d, c)])
                    kb.op("dve", lambda e: e.tensor_tensor(out=Sst[:], in0=v3(pbs), in1=Sst.f32(slice(None)), op=ALU.add), reads=pts + [Sst.tok], writes=[Sst.tok])
                    kb.op("dve", lambda e: e.tensor_tensor(out=Sst[:], in0=Sst.f32(slice(None)), in1=gcs[:, :, c:c + 1].to_broadcast([H, 16, 64]), op=ALU.mult),
                          reads=[Sst.tok, gcs.tok], writes=[Sst.tok])
                    if c % 4 == 3:
                        seg = c // 4
                        kb.dma("sp", self.rwo[l, d, seg], Sst.f32(slice(None)), reads=[Sst.tok], writes=[self.dt("rwo", l, d, seg)])
                        kb.op("dve", lambda e: e.tensor_scalar(out=Sst[:], in0=Sst.f32(slice(None)), scalar1=kcol[0:H, 0:1], scalar2=None, op0=ALU.mult), reads=[Sst.tok, kcol.tok], writes=[Sst.tok])
                    yield

    def phase_rwkv_post(self, l):
        cfg, kb = self.cfg, self.kb
        T, NT, NSEG = cfg.T, cfg.NT, cfg.NSEG
        with ExitStack() as st:
            def tl(shape, name, dtype=F32):
                return Tile(kb, st, shape, dtype, name=name)
            hb64 = tl([P, P], "hb64"); kb.dma("sp", hb64[:], self.hblk[1], writes=[hb64.tok])
            col = tl([P, 5, 8], "rwcol2"); kb.dma("sp", col[:], self.rwcol[l], writes=[col.tok])
            g2 = tl([P, RW], "rwg2"); kb.dma("sp", g2[:], self.rwg2[l], writes=[g2.tok])
            sgl = tl([P, T], "rwsgl")
            kb.dma("sp", sgl[:], self.zT[(cfg.ZA + 25) * P:(cfg.ZA + 26) * P, :], writes=[sgl.tok])
            sm = tl([P, 4, TT], "rwsm2"); kb.dma("sp", sm[:], self.rwsm[:, :, :], writes=[sm.tok])
            mu = tl([P, 26], "rwmu2"); kb.dma("sp", mu[:], self.rwmu[l], writes=[mu.tok])
            om = tl([P, 1], "rwom2")
            kb.op("dve", lambda e: e.tensor_scalar(out=om[:], in0=mu[:, 25:26], scalar1=-1.0, scalar2=1.0, op0=ALU.mult, op1=ALU.add), reads=[mu.tok], writes=[om.tok])
            sacc = tl([P, T], "rwsacc2")
            pt_ = Rot(kb, st, 8, [P, TT], F32, name="rwpt")
            kb.op("dve", lambda e: e.memset(sacc[:], 0.0), writes=[sacc.tok])
            for oi, o in enumerate((-1, 1, -64, 64)):
                for tg in range(NT):
                    lo, hi = tg * TT, (tg + 1) * TT
                    slo, shi = max(lo + o, 0), min(hi + o, T)
                    dlo, dhi = slo - o, shi - o
                    n_ = dhi - dlo
                    t = pt_.next()
                    kb.op("dve", lambda e: e.tensor_tensor(out=t[:, 0:n_], in0=sgl[:, slo:shi], in1=sm[:, oi, dlo - lo:dhi - lo], op=ALU.mult), reads=[sgl.tok, sm.tok], writes=[t.tok])
                    kb.op("dve", lambda e: e.tensor_tensor(out=sacc[:, dlo:dhi], in0=sacc[:, dlo:dhi], in1=t[:, 0:n_], op=ALU.add), reads=[sacc.tok, t.tok], writes=[sacc.tok])
            kb.op("dve", lambda e: e.tensor_scalar(out=sgl[:], in0=sgl[:], scalar1=om[:, 0:1], scalar2=None, op0=ALU.mult), reads=[sgl.tok, om.tok], writes=[sgl.tok])
            kb.op("dve", lambda e: e.scalar_tensor_tensor(out=sgl[:], in0=sacc[:], scalar=mu[:, 25:26], in1=sgl[:], op0=ALU.mult, op1=ALU.add), reads=[sacc.tok, mu.tok, sgl.tok], writes=[sgl.tok])
            kb.op("act", lambda e: e.activation(out=sgl[:], in_=sgl[:], func=AF.Sigmoid), reads=[sgl.tok], writes=[sgl.tok])
            lnb = tl([P, 1], "rwlneps")
            kb.op("dve", lambda e: e.memset(lnb[:], 64e-5), writes=[lnb.tok])
            for j in range(8):
                for tg in range(NT):
                    tc_ = slice(tg * TT, (tg + 1) * TT)
                    of_, ob_ = pt_.next(), pt_.next()
                    kb.dma("sp", of_[:], self.rwoT[0, j * P:(j + 1) * P, tc_], writes=[of_.tok])
                    kb.dma("sp", ob_[:], self.rwoT[1, j * P:(j + 1) * P, tc_], writes=[ob_.tok])
                    kb.op("dve", lambda e: e.tensor_tensor(out=of_[:], in0=of_[:], in1=ob_[:], op=ALU.add), reads=[of_.tok, ob_.tok], writes=[of_.tok])
                    pb, pt = self.bank("A")
                    kb.op("pe", lambda e: e.matmul(pb, hb64[:], of_[:], start=True, stop=True), reads=[hb64.tok, of_.tok], writes=[pt])
                    oc = pt_.next()
                    kb.op("dve", lambda e: e.tensor_tensor(out=oc[:], in0=of_[:], in1=pb, op=ALU.subtract), reads=[of_.tok, pt], writes=[oc.tok])
                    sq = pt_.next()
                    kb.op("act", lambda e: e.activation(out=sq[:], in_=oc[:], func=AF.Square), reads=[oc.tok], writes=[sq.tok])
                    pb2, pt2 = self.bank("A")
                    kb.op("pe", lambda e: e.matmul(pb2, hb64[:], sq[:], start=True, stop=True), reads=[hb64.tok, sq.tok], writes=[pt2])
                    kb.op("act", lambda e: e.activation(out=sq[:], in_=pb2, func=AF.Sqrt, bias=lnb[:, 0:1], scale=1.0), reads=[pt2, lnb.tok], writes=[sq.tok])
                    kb.op("dve", lambda e: e.reciprocal(out=sq[:], in_=sq[:]), reads=[sq.tok], writes=[sq.tok])
                    kb.op("dve", lambda e: e.scalar_tensor_tensor(out=oc[:], in0=oc[:], scalar=col[:, 3, j:j + 1], in1=sq[:], op0=ALU.mult, op1=ALU.mult), reads=[oc.tok, col.tok, sq.tok], writes=[oc.tok])
                    bn = pt_.next()
                    kb.dma("sp", bn[:], self.rwbon[j * P:(j + 1) * P, tc_], reads=[self.dt("rwbon", j)], writes=[bn.tok])
                    kb.op("dve", lambda e: e.scalar_tensor_tensor(out=oc[:], in0=oc[:], scalar=col[:, 4, j:j + 1], in1=bn[:], op0=ALU.add, op1=ALU.add), reads=[oc.tok, col.tok, bn.tok], writes=[oc.tok])
                    pb3, pt3 = self.bank("A")
                    kb.op("pe", lambda e: e.matmul(pb3, g2[:, j * P:(j + 1) * P], sgl[:, tc_], start=True, stop=True), reads=[g2.tok, sgl.tok], writes=[pt3])
                    kb.op("dve", lambda e: e.tensor_tensor(out=oc[:], in0=pb3, in1=oc[:], op=ALU.mult), reads=[pt3, oc.tok], writes=[oc.tok])
                    kb.dma("sp", self.ya.bitcast(F32)[j * P:(j + 1) * P, tc_], oc[:], reads=[oc.tok], writes=[self.dt("ya", j, tg)])
            kb.barrier()

    def phase_ssd(self, l):
        cfg, kb = self.cfg, self.kb
        T, NT, NSEG = cfg.T, cfg.NT, cfg.NSEG
        NC = T // P
        with ExitStack() as st:
            cm = Tile(kb, st, [P, 4, TT], F32, name="cm")
            kb.dma("sp", cm[:], self.cmT[:, :, :], writes=[cm.tok])
            cw = Tile(kb, st, [P, 12, 6], F32, name="cw")
            kb.dma("sp", cw[:], self.ssdcw[l], writes=[cw.tok])
            xin = Rot(kb, st, 2, [P, T], F32, name="xin")
            acc = Rot(kb, st, 2, [P, T], F32, name="cacc")
            tmp = Rot(kb, st, 3, [P, TT], F32, name="ctmp")
            for ch in range(12):
                x = xin.next()
                a = acc.next()
                kb.dma("sp", x[:], self.zT[(cfg.ZB + 8 + ch) * P:(cfg.ZB + 9 + ch) * P, :], writes=[x.tok])
                kb.op("dve", lambda e: e.tensor_scalar(out=a[:], in0=x[:], scalar1=cw[:, ch, 2:3], scalar2=cw[:, ch, 5:6], op0=ALU.mult, op1=ALU.add),
                      reads=[x.tok, cw.tok], writes=[a.tok])
                for oi, o in enumerate((-2, -1, 1, 2)):
                    for tg in range(NT):
                        lo, hi = tg * TT, (tg + 1) * TT
                        slo, shi = max(lo + o, 0), min(hi + o, T)
                        dlo, dhi = slo - o, shi - o
                        t = tmp.next()
                        n_ = dhi - dlo
                        kb.op("dve", lambda e: e.tensor_tensor(out=t[:, 0:n_], in0=x[:, slo:shi], in1=cm[:, oi, dlo - lo:dhi - lo], op=ALU.mult),
                              reads=[x.tok, cm.tok], writes=[t.tok])
                        kb.op("dve", lambda e: e.scalar_tensor_tensor(out=a[:, dlo:dhi], in0=t[:, 0:n_], scalar=cw[:, ch, (o + 2):(o + 3)], in1=a[:, dlo:dhi],
                                                                    op0=ALU.mult, op1=ALU.add), reads=[t.tok, a.tok, cw.tok], writes=[a.tok])
                kb.op("act", lambda e: e.activation(out=a[:], in_=a[:], func=AF.Silu), reads=[a.tok], writes=[a.tok])
                kb.dma("sp", self.xcs[ch * P:(ch + 1) * P, :], a[:], reads=[a.tok], writes=[self.dt("xcs", ch)])
            kb.barrier()
        with ExitStack() as st:
            def tl(shape, name, dtype=F32):
                return Tile(kb, st, shape, dtype, name=name)
            yacc = tl([P, 8, T], "ssdy")
            kb.op("dve", lambda e: e.memset(yacc[:], 0.0), writes=[yacc.tok])
            tri = tl([P, 2, P], "tri")
            for d in range(2):
                kb.dma("sp", tri[:, d, :], self.tri[d], writes=[tri.tok])
            dd_T = tl([64, T], "ddT")
            col = tl([64, 3], "ssdcol")
            kb.dma("sp", col[:], self.ssdcol[l], writes=[col.tok])
            for r in range(4):
                kb.dma("sp", dd_T[r * 16:(r + 1) * 16, :], self.zT[cfg.ZDT * P:cfg.ZDT * P + 16, :], writes=[dd_T.tok])
            mcol = tl([64, 1], "mcol")
            kb.op("act", lambda e: e.activation(out=mcol[:], in_=col[:, 1:2], func=AF.Exp), reads=[col.tok], writes=[mcol.tok])
            kb.op("dve", lambda e: e.tensor_tensor(out=mcol[:], in0=mcol[:], in1=col[:, 2:3], op=ALU.mult), reads=[mcol.tok, col.tok], writes=[mcol.tok])
            kb.op("act", lambda e: e.activation(out=dd_T[:], in_=dd_T[:], func=AF.Exp, bias=col[:, 0:1], scale=1.0), reads=[dd_T.tok, col.tok], writes=[dd_T.tok])
            kb.op("dve", lambda e: e.tensor_scalar(out=dd_T[:], in0=dd_T[:], scalar1=1.0, scalar2=None, op0=ALU.add), reads=[dd_T.tok], writes=[dd_T.tok])
            kb.op("act", lambda e: e.activation(out=dd_T[:], in_=dd_T[:], func=AF.Ln), reads=[dd_T.tok], writes=[dd_T.tok])
            kb.op("dve", lambda e: e.tensor_scalar(out=dd_T[:], in0=dd_T[:], scalar1=mcol[:, 0:1], scalar2=None, op0=ALU.mult), reads=[dd_T.tok, mcol.tok], writes=[dd_T.tok])
            kcol = tl([P, 1], "kcol")
            kb.dma("sp", kcol[:], self.ssdkeep[:, :], writes=[kcol.tok])
            S = [tl([P, 512], "ssdS%d" % g) for g in range(2)]
            xsl = Rot(kb, st, 2, [P, 8, P], F32, name="xsl")
            bcl = Rot(kb, st, 2, [P, 4, P], F32, name="bcl")
            xtok = Rot(kb, st, 2, [P, 16, 64], F32, name="xtok")
            btok = Rot(kb, st, 2, [P, 2, P], F32, name="btok")
            ddk = Rot(kb, st, 2, [P, 64], F32, name="ddk")
            sm = Rot(kb, st, 6, [P, 16], F32, name="ssm")
            gm = Rot(kb, st, 2, [P, 2, P], F32, name="gm")
            Mt = Rot(kb, st, 2, [P, 16, P], F32, name="Mt")
            xdt = Rot(kb, st, 2, [P, 16, 64], F32, name="xdt")
            xw = Rot(kb, st, 2, [P, 16, 64], F32, name="xw")
            ecr = Rot(kb, st, 3, [P, P], F32, name="ecr")
            yt = Rot(kb, st, 3, [P, P], F32, name="yt")
            for d in range(2):
                trd = tri[:, d, :]
                for g in range(2):
                    kb.dma("sp", S[g][:], self.ssds0[l, d, g], writes=[S[g].tok])
                order = range(NC) if d == 0 else range(NC - 1, -1, -1)
                for c in order:
                    cols = slice(c * P, (c + 1) * P)
                    xs_, bc_ = xsl.next(), bcl.next()
                    kb.dma("sp", xs_[:], self.xcs[0:1024, :].rearrange("(j p) t -> p j t", p=P)[:, :, cols], reads=[self.dt("xcs", 0)], writes=[xs_.tok])
                    kb.dma("sp", bc_[:], self.xcs[1024:1536, :].rearrange("(j p) t -> p j t", p=P)[:, :, cols], writes=[bc_.tok])
                    xt_, bt_, dk = xtok.next(), btok.next(), ddk.next()
                    for hb_ in range(2):
                        pb2, pt2 = self.bank2()
                        for jj in range(4):
                            j = hb_ * 4 + jj
                            kb.op("pe", lambda e: e.transpose(pb2[:, jj * P:(jj + 1) * P], xs_[:, j, :], self.ident[:]), reads=[xs_.tok, self.ident.tok], writes=[pt2[0], pt2[1]])
                        kb.op("act", lambda e: e.activation(out=xt_[:, hb_ * 8:(hb_ + 1) * 8, :], in_=pb2[:, 0:512].rearrange("p (h q) -> p h q", q=64), func=AF.Copy),
                              reads=[pt2[0], pt2[1]], writes=[xt_.tok])
                    pb, pt = self.bank()
                    for g in range(2):
                        kb.op("pe", lambda e: e.transpose(pb[:, g * P:(g + 1) * P], bc_[:, g, :], self.ident[:]), reads=[bc_.tok, self.ident.tok], writes=[pt])
                    kb.op("pe", lambda e: e.transpose(pb[:, 256:320], dd_T[:, cols], self.ident[0:64, 0:64]), reads=[dd_T.tok, self.ident.tok], writes=[pt])
                    kb.op("dve", lambda e: e.tensor_copy(out=bt_[:], in_=pb[:, 0:256].rearrange("p (g n) -> p g n", n=P)), reads=[pt], writes=[bt_.tok])
                    kb.op("dve", lambda e: e.tensor_copy(out=dk[:], in_=pb[:, 256:320]), reads=[pt], writes=[dk.tok])
                    dtc = dk[:, 32 * d:32 * d + 16]
                    dac = dk[:, 32 * d + 16:32 * d + 32]
                    pbc, ptc = self.bank()
                    kb.op("pe", lambda e: e.matmul(pbc[:, 0:16], trd, dac, start=True, stop=True), reads=[tri.tok, dk.tok], writes=[ptc])
                    kb.op("pe", lambda e: e.matmul(pbc[:, 16:32], self.ones[:], dac, start=True, stop=True), reads=[self.ones.tok, dk.tok], writes=[ptc])
                    cumc, wts, edec = sm.next(), sm.next(), sm.next()
                    kb.op("dve", lambda e: e.tensor_copy(out=cumc[:], in_=pbc[:, 0:16]), reads=[ptc], writes=[cumc.tok])
                    kb.op("dve", lambda e: e.tensor_tensor(out=wts[:], in0=pbc[:, 16:32], in1=cumc[:], op=ALU.subtract), reads=[ptc, cumc.tok], writes=[wts.tok])
                    kb.op("act", lambda e: e.activation(out=wts[:], in_=wts[:], func=AF.Exp), reads=[wts.tok], writes=[wts.tok])
                    kb.op("pool", lambda e: e.tensor_tensor(out=wts[:], in0=wts[:], in1=dtc, op=ALU.mult), reads=[wts.tok, dk.tok], writes=[wts.tok])
                    kb.op("act", lambda e: e.activation(out=edec[:], in_=pbc[:, 16:32], func=AF.Exp), reads=[ptc], writes=[edec.tok])
                    pbg, ptg = self.bank()
                    for g in range(2):
                        kb.op("pe", lambda e: e.matmul(pbg[:, g * P:(g + 1) * P], bc_[:, g, :], bc_[:, 2 + g, :], start=True, stop=True), reads=[bc_.tok], writes=[ptg])
                    gm_ = gm.next()
                    kb.op("dve", lambda e: e.tensor_tensor(out=gm_[:], in0=pbg[:, 0:256].rearrange("p (g t) -> p g t", t=P),
                                                           in1=tri[:, d:d + 1, :].to_broadcast([P, 2, P]), op=ALU.mult), reads=[ptg, tri.tok], writes=[gm_.tok])
                    M_ = Mt.next()
                    for hq in range(2):
                        pb2, pt2 = self.bank2()
                        for hh in range(8):
                            h = hq * 8 + hh
                            kb.op("pe", lambda e: e.matmul(pb2[:, hh * P:(hh + 1) * P], dac[:, h:h + 1].to_broadcast([P, P]), trd, start=True, stop=True),
                                  reads=[dk.tok, tri.tok], writes=[pt2[0], pt2[1]])
                        for hh in range(8):
                            h = hq * 8 + hh
                            kb.op("dve", lambda e: e.tensor_scalar(out=M_[:, h, :], in0=pb2[:, hh * P:(hh + 1) * P], scalar1=cumc[:, h:h + 1], scalar2=0.0,
                                                                   op0=ALU.subtract, op1=ALU.min), reads=[pt2[0], pt2[1], cumc.tok], writes=[M_.tok])
                    kb.op("act", lambda e: e.activation(out=M_[:], in_=M_[:], func=AF.Exp), reads=[M_.tok], writes=[M_.tok])
                    for g in range(2):
                        kb.op("pool", lambda e: e.tensor_tensor(out=M_[:, 8 * g:8 * g + 8, :], in0=M_[:, 8 * g:8 * g + 8, :],
                                                               in1=gm_[:, g:g + 1, :].to_broadcast([P, 8, P]), op=ALU.mult), reads=[M_.tok, gm_.tok], writes=[M_.tok])
                    xd_, xw_ = xdt.next(), xw.next()
                    kb.op("pool", lambda e: e.tensor_tensor(out=xd_[:], in0=xt_[:], in1=dtc.unsqueeze(2).to_broadcast([P, 16, 64]), op=ALU.mult),
                          reads=[xt_.tok, dk.tok], writes=[xd_.tok])
                    kb.op("pool", lambda e: e.tensor_tensor(out=xw_[:], in0=xt_[:], in1=wts[:].unsqueeze(2).to_broadcast([P, 16, 64]), op=ALU.mult),
                          reads=[xt_.tok, wts.tok], writes=[xw_.tok])
                    for j in range(8):
                        g = j // 4
                        pb, pt = self.bank()
                        for h2 in range(2):
                            kb.op("pe", lambda e: e.matmul(pb[h2 * 64:(h2 + 1) * 64, 0:P], dac[:, 2 * j + h2:2 * j + h2 + 1].to_broadcast([P, 64]), trd, start=True, stop=True),
                                  reads=[dk.tok, tri.tok], writes=[pt])
                        kb.op("pe", lambda e: e.matmul(pb[:, P:2 * P], S[g][:, (j % 4) * P:(j % 4 + 1) * P], bc_[:, 2 + g, :], start=True, stop=True),
                              reads=[S[g].tok, bc_.tok], writes=[pt])
                        for h2 in range(2):
                            kb.op("pe", lambda e: e.matmul(pb[h2 * 64:(h2 + 1) * 64, 2 * P:3 * P], xd_[:, 2 * j + h2, :], M_[:, 2 * j + h2, :], start=True, stop=True),
                                  reads=[xd_.tok, M_.tok], writes=[pt])
                        ec = ecr.next()
                        kb.op("act", lambda e: e.activation(out=ec[:], in_=pb[:, 0:P], func=AF.Exp), reads=[pt], writes=[ec.tok])
                        y_ = yt.next()
                        kb.op("dve", lambda e: e.tensor_tensor(out=y_[:], in0=pb[:, P:2 * P], in1=ec[:], op=ALU.mult), reads=[pt, ec.tok], writes=[y_.tok])
                        kb.op("dve", lambda e: e.tensor_tensor(out=y_[:], in0=pb[:, 2 * P:3 * P], in1=y_[:], op=ALU.add), reads=[pt, y_.tok], writes=[y_.tok])
                        kb.op("pool", lambda e: e.tensor_tensor(out=yacc[:, j, cols], in0=yacc[:, j, cols], in1=y_[:], op=ALU.add), reads=[yacc.tok, y_.tok], writes=[yacc.tok])
                    seg_end = (c % 2 == 1) if d == 0 else (c % 2 == 0)
                    for g in range(2):
                        pb, pt = self.bank()
                        kb.op("pe", lambda e: e.matmul(pb, bt_[:, g, :], xw_[:, 8 * g:8 * g + 8, :], start=True, stop=True), reads=[bt_.tok, xw_.tok], writes=[pt])
                        kb.op("pool", lambda e: e.tensor_tensor(out=S[g][:].rearrange("p (h q) -> p h q", q=64), in0=S[g][:].rearrange("p (h q) -> p h q", q=64),
                                                               in1=edec[:, 8 * g:8 * g + 8].unsqueeze(2).to_broadcast([P, 8, 64]), op=ALU.mult),
                              reads=[S[g].tok, edec.tok], writes=[S[g].tok])
                        kb.op("dve", lambda e: e.tensor_tensor(out=S[g][:], in0=pb, in1=S[g][:], op=ALU.add), reads=[pt, S[g].tok], writes=[S[g].tok])
                        if seg_end:
                            seg = c // 2
                            kb.dma("sp", self.ssdo[l, d, seg, g], S[g][:], reads=[S[g].tok], writes=[self.dt("ssdo", l, d, seg, g)])
                            kb.op("dve", lambda e: e.tensor_scalar(out=S[g][:], in0=S[g][:], scalar1=kcol[:, 0:1], scalar2=None, op0=ALU.mult),
                                  reads=[S[g].tok, kcol.tok], writes=[S[g].tok])
            Dc = tl([P, 8], "ssdDc")
            gc = tl([P, 8], "ssdgc")
            kb.dma("sp", Dc[:], self.ssdD[l], writes=[Dc.tok])
            kb.dma("sp", gc[:], self.ssdg[l], writes=[gc.tok])
            ld = Rot(kb, st, 4, [P, TT], F32, name="sld")
            rs = tl([P, TT], "srs")
            for tg in range(NT):
                tc_ = slice(tg * TT, (tg + 1) * TT)
                pbn, ptn = self.bank()
                for j in range(8):
                    xj, zj = ld.next(), ld.next()
                    kb.dma("sp", xj[:], self.xcs[j * P:(j + 1) * P, tc_], writes=[xj.tok])
                    kb.dma("sp", zj[:], self.zT[(cfg.ZB + j) * P:(cfg.ZB + j + 1) * P, tc_], writes=[zj.tok])
                    kb.op("dve", lambda e: e.scalar_tensor_tensor(out=xj[:], in0=xj[:], scalar=Dc[:, j:j + 1], in1=yacc[:, j, tc_], op0=ALU.mult, op1=ALU.add),
                          reads=[xj.tok, Dc.tok, yacc.tok], writes=[xj.tok])
                    kb.op("dve", lambda e: e.tensor_tensor(out=yacc[:, j, tc_], in0=xj[:], in1=zj[:], op=ALU.mult), reads=[xj.tok, zj.tok], writes=[yacc.tok])
                    kb.op("act", lambda e: e.activation(out=zj[:], in_=yacc[:, j, tc_], func=AF.Square), reads=[yacc.tok], writes=[zj.tok])
                    kb.op("pe", lambda e: e.matmul(pbn, self.ones[:], zj[:], start=(j == 0), stop=(j == 7)), reads=[zj.tok, self.ones.tok], writes=[ptn])
                t = ld.next()
                kb.op("act", lambda e: e.activation(out=t[:], in_=pbn, func=AF.Sqrt, bias=self.epsc[:, 0:1], scale=1.0 / SW), reads=[ptn, self.epsc.tok], writes=[t.tok])
                kb.op("dve", lambda e: e.reciprocal(out=rs[:], in_=t[:]), reads=[t.tok], writes=[rs.tok])
                for j in range(8):
                    o = ld.next()
                    kb.op("dve", lambda e: e.scalar_tensor_tensor(out=o[:], in0=yacc[:, j, tc_], scalar=gc[:, j:j + 1], in1=rs[:], op0=ALU.mult, op1=ALU.mult),
                          reads=[yacc.tok, gc.tok, rs.tok], writes=[o.tok])
                    kb.dma("sp", self.yb.bitcast(F32)[j * P:(j + 1) * P, tc_], o[:], reads=[o.tok], writes=[self.dt("yb", j, tg)])
            kb.barrier()


def prep_common(cfg, inp):
    NK, NJ, DEPTH = cfg.NK, cfg.NJ, cfg.DEPTH
    d = {}
    d["wmodn"] = inp["w_mod"]
    d["bmod"] = np.stack([colv(inp["b_mod"][l]) for l in range(DEPTH)])
    d["normg"] = np.stack([np.concatenate([colv(inp["norm_g"][l, i]) for i in range(3)], axis=1) for l in range(DEPTH)])
    d["fng"] = colv(inp["final_norm_g"])
    d["w1"] = np.stack([np.stack([blk(inp["ffn_w_in"][l, w]) for w in range(2)]) for l in range(DEPTH)])
    d["w2"] = np.stack([np.stack([blk(inp["ffn_w_out"][l, w]) for w in range(2)]) for l in range(DEPTH)])
    return d


def gp_layout(a):
    g, p = a.shape[0], a.shape[1]
    rest = a.shape[2:]
    b = a.reshape((32, 2, 64) + rest)
    b = np.moveaxis(b, 0, 2)
    return np.ascontiguousarray(b.reshape((128, 32) + rest))


def prep_mixer(cfg, inp):
    NK, DEPTH = cfg.NK, cfg.DEPTH
    d = {}
    wins = []
    for l in range(DEPTH):
        W = inp["w_in"][l]
        Wn = np.concatenate([W[:, 0:3328], W[:, 3328:3328 + 2560], W[:, 5904:6928], W[:, 6928:], W[:, 5888:5904],
                             np.zeros((W.shape[0], 112), np.float32)], axis=1)
        wins.append(blk(Wn))
    d["win"] = np.stack(wins)
    d["wpa"] = np.stack([blk(inp["w_proj_a"][l]) for l in range(DEPTH)])
    d["wpb"] = np.stack([blk(inp["w_proj_b"][l]) for l in range(DEPTH)])
    d["wpc"] = np.stack([blk(inp["w_proj_c"][l]) for l in range(DEPTH)])
    d["wo"] = np.stack([blk(inp["w_out"][l]) for l in range(DEPTH)])
    d["identd"] = np.eye(P, dtype=np.float32)
    lam = np.zeros((DEPTH, 2, P, 3, 32), np.float32)
    sb = np.zeros((DEPTH, 2, P, 2, 32, 16), np.float32)
    sc = np.zeros((DEPTH, 2, 2, 32, P, P), np.float32)
    for l in range(DEPTH):
        for dd in range(2):
            lam[l, dd, :, 0] = gp_layout(inp["s5_lambda_re"][l, dd])
            lam[l, dd, :, 1] = gp_layout(inp["s5_lambda_im"][l, dd])
            lam[l, dd, :, 2] = gp_layout(np.repeat(inp["s5_log_dt"][l, dd][:, None], 64, axis=1))
            sb[l, dd, :, 0] = gp_layout(inp["s5_b_re"][l, dd])
            sb[l, dd, :, 1] = gp_layout(inp["s5_b_im"][l, dd])
            for ri, key in enumerate(("s5_c_re", "s5_c_im")):
                C = inp[key][l, dd]
                for q in range(32):
                    for g2 in range(2):
                        g8 = 2 * (q % 4) + g2
                        sc[l, dd, ri, q, g2 * 64:(g2 + 1) * 64, g8 * 16:(g8 + 1) * 16] = C[2 * q + g2].T
    d["s5lam"], d["s5b"], d["s5c"] = lam, sb, sc
    d["s5d"] = np.stack([colv(inp["s5_d"][l]) for l in range(DEPTH)])
    rc = np.zeros((DEPTH, P, 5, 8), np.float32)
    rw0 = np.zeros((DEPTH, P, 2, 2, 8), np.float32)
    rw2 = np.zeros((DEPTH, P, 2, RW), np.float32)
    for l in range(DEPTH):
        rc[l, :, 0] = colv(inp["rwkv_k_k"][l])
        rc[l, :, 1] = colv(inp["rwkv_k_a"][l])
        rc[l, :, 2] = colv(inp["rwkv_r_k"][l].reshape(-1))
        rc[l, :, 3] = colv(inp["rwkv_ln_g"][l])
        rc[l, :, 4] = colv(inp["rwkv_ln_b"][l])
        for dd in range(2):
            rw0[l, :, dd, 0] = colv(inp["rwkv_w0"][l, dd])
            rw0[l, :, dd, 1] = colv(inp["rwkv_a0"][l, dd])
            rw2[l, 0:64, dd] = inp["rwkv_w2"][l, dd]
            rw2[l, 64:128, dd] = inp["rwkv_a2"][l, dd]
    d["rwcol"], d["rww0"], d["rww2"] = rc, rw0, rw2
    d["rwmu"] = np.stack([colv(inp["rwkv_mu"][l]) for l in range(DEPTH)])
    d["rwg2"] = np.ascontiguousarray(inp["rwkv_g2"])
    hb = np.zeros((2, P, P), np.float32)
    hb[0, 0:64, 0:64] = 1.0
    hb[0, 64:, 64:] = 1.0
    hb[1] = hb[0] / 64.0
    d["hblk"] = hb
    i_ = np.arange(64)[:, None]
    t_ = np.arange(64)[None, :]
    rt = np.stack([(i_ < t_), (i_ > t_), (i_ <= t_)]).astype(np.float32)
    d["rwtri"] = np.concatenate([rt, rt], axis=1)
    cmk = np.ones((P, cfg.T), np.float32)
    cmk[:, 0::64] = 0.0
    d["rwcm"] = cmk
    tri = np.zeros((2, P, P), np.float32)
    tri[0] = np.triu(np.ones((P, P), np.float32))
    tri[1] = np.tril(np.ones((P, P), np.float32))
    d["tri"] = tri
    cw = np.zeros((DEPTH, P, 12, 6), np.float32)
    scol = np.zeros((DEPTH, 64, 3), np.float32)
    for l in range(DEPTH):
        w = inp["ssd_conv_w"][l]
        for j in range(5):
            cw[l, :, :, j] = colv(w[j])
        cw[l, :, :, 5] = colv(inp["ssd_conv_b"][l])
        for dd in range(2):
            scol[l, 32 * dd:32 * dd + 16, 0] = inp["ssd_dt_bias"][l, dd]
            scol[l, 32 * dd + 16:32 * dd + 32, 0] = inp["ssd_dt_bias"][l, dd]
            scol[l, 32 * dd + 16:32 * dd + 32, 1] = inp["ssd_a_log"][l, dd]
            scol[l, 32 * dd:32 * dd + 16, 2] = 1.0
            scol[l, 32 * dd + 16:32 * dd + 32, 2] = -1.0
    d["ssdcw"], d["ssdcol"] = cw, scol
    d["ssdD"] = np.stack([colv(np.repeat(inp["ssd_d"][l], 64)) for l in range(DEPTH)])
    d["ssdg"] = np.stack([colv(inp["ssd_norm_g"][l]) for l in range(DEPTH)])
    return d


def core_mixer_inputs(cfg, inp, c, d):
    DEPTH = cfg.DEPTH
    prompt = c < cfg.NPC
    keep = np.ones((P, TT), np.float32)
    if prompt:
        keep[:, 0::256] = 0.0
    d["keepT"] = keep
    s0 = np.zeros((DEPTH, 2, P, 2, 32), np.float32)
    if not prompt:
        b = c - cfg.NPC
        for l in range(DEPTH):
            for dd in range(2):
                s0[l, dd, :, 0] = gp_layout(inp["state_s5_re"][b, l, dd])
                s0[l, dd, :, 1] = gp_layout(inp["state_s5_im"][b, l, dd])
    d["s5s0"] = s0
    cm = np.ones((P, 4, TT), np.float32)
    if prompt:
        for oi, o in enumerate((-2, -1, 1, 2)):
            for t in range(TT):
                if (t + o) // 256 != t // 256:
                    cm[:, oi, t] = 0.0
    d["cmT"] = cm
    smk = np.zeros((P, 4, TT), np.float32)
    tt_ = np.arange(TT)
    if prompt:
        smk[:, 0] = np.where(tt_ % 256 != 0, 0.5, 0.0)
        smk[:, 1] = np.where(tt_ % 256 != 255, 0.5, 0.0)
    else:
        smk[:, 0] = np.where(tt_ % 64 != 0, 0.25, 0.0)
        smk[:, 1] = np.where(tt_ % 64 != 63, 0.25, 0.0)
        smk[:, 2] = 0.25
        smk[:, 3] = 0.25
    d["rwsm"] = smk
    rs0 = np.zeros((DEPTH, 2, 64, 16, 64), np.float32)
    if not prompt:
        b = c - cfg.NPC
        sr = inp["state_rwkv"][b]
        for l in range(DEPTH):
            for dd in range(2):
                rs0[l, dd] = sr[l, dd].transpose(2, 0, 1)
    d["rws0"] = rs0
    d["ssdkeep"] = np.full((P, 1), 0.0 if prompt else 1.0, np.float32)
    ss0 = np.zeros((DEPTH, 2, 2, P, 512), np.float32)
    if not prompt:
        b = c - cfg.NPC
        st_ = inp["state_ssd"][b]
        for l in range(DEPTH):
            for dd in range(2):
                for g in range(2):
                    ss0[l, dd, g] = st_[l, dd, 8 * g:8 * g + 8].transpose(2, 0, 1).reshape(P, 512)
    d["ssds0"] = ss0


def core_inputs(cfg, inp, common, c):
    d = dict(common)
    T = cfg.T
    if c < cfg.NPC:
        ns = T // 256
        x = inp["x_prompt"][c * ns:(c + 1) * ns].reshape(T, cfg.DM)
        cond = inp["c_ctx"]
    else:
        b = c - cfg.NPC
        x = inp["x_sample"][b]
        cond = inp["c"][b]
    d["xT"] = np.ascontiguousarray(x.T)
    d["cond"] = colv(cond)
    core_mixer_inputs(cfg, inp, c, d)
    return d


def run(cfg, inp):
    b = Builder(cfg)
    nc = b.build()
    common = prep_common(cfg, inp)
    common.update(prep_mixer(cfg, inp))
    n = cfg.NPC + cfg.NSC
    maps = [core_inputs(cfg, inp, common, c) for c in range(n)]
    for m in maps:
        for k in list(m.keys()):
            if k not in b.din:
                del m[k]
            else:
                m[k] = np.ascontiguousarray(m[k], dtype=np.float32)
    res = run_bass_kernel_spmd(nc, maps, core_ids=list(range(n)))
    return res.results


def assemble(cfg, inp, results):
    T, DEPTH, NSEG = cfg.T, cfg.DEPTH, cfg.NSEG
    ns = T // 256
    yp = np.concatenate([results[c]["yT"].T.reshape(ns, 256, cfg.DM) for c in range(cfg.NPC)], axis=0)
    ys = np.stack([results[cfg.NPC + b]["yT"].T for b in range(cfg.NSC)], axis=0)
    nb = cfg.NPC * NSEG
    st_rwkv = np.zeros((nb, DEPTH, 2, 16, 64, 64), np.float32)
    st_ssd = np.zeros((nb, DEPTH, 2, 16, 64, 128), np.float32)
    s5 = [np.zeros((nb, DEPTH, 2, 64, 64), np.float32) for _ in range(2)]
    for c in range(cfg.NPC):
        r = results[c]
        if "s5o" in r:
            o = r["s5o"]
            o = o.reshape(DEPTH, 2, 64, 2, 2, 32, NSEG)
            o = o.transpose(6, 0, 3, 4, 5, 1, 2)
            o = o.reshape(NSEG, DEPTH, 2, 2, 64, 64).copy()
            o[:, :, 1] = o[::-1, :, 1]
            for ri in range(2):
                s5[ri][c * NSEG:(c + 1) * NSEG] = o[:, :, :, ri]
        if "rwo" in r:
            o = r["rwo"]
            o = o.transpose(2, 0, 1, 4, 5, 3).copy()
            o[:, :, 1] = o[::-1, :, 1]
            st_rwkv[c * NSEG:(c + 1) * NSEG] = o
        if "ssdo" in r:
            o = r["ssdo"]
            o = o.reshape(DEPTH, 2, NSEG, 2, P, 8, 64).transpose(2, 0, 1, 3, 5, 6, 4)
            st_ssd[c * NSEG:(c + 1) * NSEG] = o.reshape(NSEG, DEPTH, 2, 16, 64, 128)
    return yp, ys, st_rwkv, st_ssd, s5[0], s5[1]


def kernel(**inputs):
    cfg = Cfg()
    inp = {k: np.asarray(v) for k, v in inputs.items()}
    results = run(cfg, inp)
    return assemble(cfg, inp, results)
```

```python
import numpy as np
from contextlib import ExitStack
import concourse.bass as bass
import concourse.mybir as mybir
from concourse.bass_utils import run_bass_kernel_spmd

F32 = mybir.dt.float32
F32R = mybir.dt.float32r
AF = mybir.ActivationFunctionType
ALU = mybir.AluOpType
P = 128
TT = 512
NDS = 40

RW = 1024
RH = 64
SW = 1024
SXBC = 1536
S5W = 1024
EPS = 1e-6


class Cfg:
    def __init__(self, DM=2048, DFF=5504, DEPTH=2, T=2048, NPC=4, NSC=4, mix=(1, 1, 1)):
        self.DM, self.DFF, self.DEPTH, self.T, self.NPC, self.NSC = DM, DFF, DEPTH, T, NPC, NSC
        self.NK = DM // P
        self.NJ = DFF // P
        self.NT = T // TT
        self.NSEG = T // 256
        self.mix = mix
        self.ZA, self.ZB, self.ZC = 0, 26, 46
        self.ZG = 54
        self.ZDT = 54 + 3 * self.NK
        self.NZ = self.ZDT + 1


class Tok:
    __slots__ = ("w", "r")

    def __init__(self):
        self.w = []
        self.r = {}


class KB:
    def __init__(self, nc):
        self.nc = nc
        self.E = {"pe": nc.tensor, "dve": nc.vector, "act": nc.scalar, "pool": nc.gpsimd, "sp": nc.sync}
        self.tick = {e: 0 for e in self.E}
        self.seen = {e: {} for e in self.E}
        self.stack = ExitStack()
        self.sem = {e: self.stack.enter_context(nc.semaphore("s_" + e)) for e in self.E}
        self.dsem = [self.stack.enter_context(nc.semaphore("d%d" % i)) for i in range(NDS)]
        self.dval = [0] * NDS
        self.drr = 0
        self.nid = 0
        self.ninstr = 0

    def name(self, s):
        self.nid += 1
        return "%s_%d" % (s, self.nid)

    def _wait(self, e, dep):
        kind, key, val = dep
        if kind == "e" and key == e and e == "pe":
            return
        k = (kind, key)
        if self.seen[e].get(k, 0) >= val:
            return
        self.seen[e][k] = val
        sem = self.sem[key] if kind == "e" else self.dsem[key]
        self.E[e].wait_ge(sem, val)
        self.ninstr += 1

    def _sync(self, e, reads, writes):
        for t in reads:
            for d in t.w:
                self._wait(e, d)
        for t in writes:
            for d in t.w:
                self._wait(e, d)
            for k, v in t.r.items():
                self._wait(e, (k[0], k[1], v))

    def _mark(self, me, reads, writes):
        for t in writes:
            t.w = [me]
            t.r = {}
        for t in reads:
            if not (len(t.w) == 1 and t.w[0] is me):
                k = (me[0], me[1])
                if t.r.get(k, 0) < me[2]:
                    t.r[k] = me[2]

    def join(self, dst, srcs):
        w = list(dst.w)
        for s_ in srcs:
            w.extend(s_.w)
        dst.w = w

    def op(self, e, fn, reads=(), writes=()):
        self._sync(e, reads, writes)
        ins = fn(self.E[e])
        self.tick[e] += 1
        ins.then_inc(self.sem[e], 1)
        self.ninstr += 1
        self._mark(("e", e, self.tick[e]), reads, writes)
        return ins

    def dma(self, q, out, in_, reads=(), writes=(), **kw):
        self._sync(q, reads, writes)
        s = self.drr
        self.drr = (self.drr + 1) % NDS
        if self.dval[s]:
            self._wait(q, ("d", s, self.dval[s]))
        ins = self.E[q].dma_start(out=out, in_=in_, **kw)
        self.dval[s] += 16
        ins.then_inc(self.dsem[s], 16)
        self.ninstr += 1
        self._mark(("d", s, self.dval[s]), reads, writes)
        return ins

    def barrier(self):
        for e in self.E:
            for e2 in self.E:
                if e2 != e and self.tick[e2]:
                    self._wait(e, ("e", e2, self.tick[e2]))
            for s in range(NDS):
                if self.dval[s]:
                    self._wait(e, ("d", s, self.dval[s]))


class Tile:
    def __init__(self, kb, stack, shape, dtype, ntok=1, name="t"):
        self.t = stack.enter_context(kb.nc.sbuf_tensor(kb.name(name), list(shape), dtype))
        self.toks = [Tok() for _ in range(ntok)]
        self.dtype = dtype

    @property
    def tok(self):
        return self.toks[0]

    def __getitem__(self, k):
        return self.t[k]

    def f32(self, k):
        return self.t[k].bitcast(F32)


class Rot:
    def __init__(self, kb, stack, n, shape, dtype, name="r"):
        self.tiles = [Tile(kb, stack, shape, dtype, name=name) for _ in range(n)]
        self.i = 0

    def next(self):
        t = self.tiles[self.i]
        self.i = (self.i + 1) % len(self.tiles)
        return t


def blk(W):
    K, M = W.shape
    return np.ascontiguousarray(W.reshape(K // P, P, M // P, P).transpose(2, 1, 0, 3)).reshape(M // P, P, (K // P) * P)


def colv(v):
    return np.ascontiguousarray(v.reshape(-1, P).T)


class Builder:
    def __init__(self, cfg):
        self.cfg = cfg
        self.nc = bass.Bass("TRN2", target_bir_lowering=False)
        self.kb = KB(self.nc)
        self.din = {}
        self.dout = {}
        self.dtok = {}

    def inp(self, name, shape, dtype=F32):
        self.din[name] = self.nc.dram_tensor(name, list(shape), dtype, kind="ExternalInput").ap()
        return self.din[name]

    def outp(self, name, shape, dtype=F32):
        self.dout[name] = self.nc.dram_tensor(name, list(shape), dtype, kind="ExternalOutput").ap()
        return self.dout[name]

    def scratch(self, name, shape, dtype=F32):
        if getattr(self.cfg, "debug", False) and dtype == F32:
            return self.outp(name, shape, dtype)
        return self.nc.dram_tensor(name, list(shape), dtype, kind="Internal").ap()

    def dt(self, *key):
        if key not in self.dtok:
            self.dtok[key] = Tok()
        return self.dtok[key]

    def build(self):
        cfg, nc, kb = self.cfg, self.nc, self.kb
        NK, NJ, T, NT, DEPTH = cfg.NK, cfg.NJ, cfg.T, cfg.NT, cfg.DEPTH
        self.xT = self.inp("xT", [cfg.DM, T], F32R)
        self.cond = self.inp("cond", [P, NK])
        self.wmodn = self.inp("wmodn", [DEPTH, cfg.DM, 9 * cfg.DM])
        self.bmod = self.inp("bmod", [DEPTH, P, 9 * NK])
        self.normg = self.inp("normg", [DEPTH, P, 3 * NK])
        self.fng = self.inp("fng", [P, NK])
        self.w1 = self.inp("w1", [DEPTH, 2, 2 * NJ, P, NK * P], F32R)
        self.w2 = self.inp("w2", [DEPTH, 2, NK, P, NJ * P], F32R)
        self.yT = self.outp("yT", [cfg.DM, T])
        self.xres = self.scratch("xres", [cfg.DM, T], F32R)
        self.mixer_decl()

        with ExitStack() as gs:
            self.gs = gs
            self.ps = [kb.stack.enter_context(nc.psum_tensor(kb.name("ps"), [P, 1024], F32)) for _ in range(4)]
            self.pstok = [[Tok(), Tok()] for _ in range(4)]
            self.psi = 0
            self.ppi = {}
            self.ones = Tile(kb, gs, [P, P], F32, name="ones")
            kb.op("dve", lambda e: e.memset(self.ones[:], 1.0), writes=[self.ones.tok])
            self.mod = Tile(kb, gs, [P, DEPTH, 9 * NK], F32, name="mod")
            self.modA = Tile(kb, gs, [P, DEPTH, 3 * NK], F32, name="modA")
            self.modG = Tile(kb, gs, [P, DEPTH, 3 * NK], F32, name="modG")
            self.fngt = Tile(kb, gs, [P, NK], F32, name="fng")
            kb.dma("sp", self.fngt[:], self.fng[:, :], writes=[self.fngt.tok])
            self.mixer_consts()
            self.phase_mod()
            src = self.xT
            for l in range(DEPTH):
                self.phase_ffn(l, 0, src)
                src = self.xres
                self.phase_mix(l)
                self.phase_ffn(l, 1, src)
            self.phase_final()
            kb.barrier()
        kb.stack.close()
        return nc

    def bank(self, pool=None):
        if pool is None:
            i = self.psi
            self.psi = (self.psi + 1) % 8
        else:
            base = 0 if pool == "A" else 4
            k = self.ppi.get(pool, 0)
            self.ppi[pool] = (k + 1) % 4
            i = base + k
        return self.ps[i // 2][:, (i % 2) * 512:(i % 2) * 512 + 512], self.pstok[i // 2][i % 2]

    def bank2(self):
        if self.psi % 2:
            self.psi = (self.psi + 1) % 8
        i = self.psi
        self.psi = (self.psi + 2) % 8
        return self.ps[i // 2], self.pstok[i // 2]

    def phase_mod(self):
        cfg, kb = self.cfg, self.kb
        NK, DEPTH = cfg.NK, cfg.DEPTH
        NM = 9 * NK
        NCOL = NM * P
        CG = 512 if NCOL % 512 == 0 else 256
        with ExitStack() as st:
            cnd = Tile(kb, st, [P, NK], F32, name="cnd")
            sc = Tile(kb, st, [P, NK], F32, name="scnd")
            bm = Tile(kb, st, [P, DEPTH, NM], F32, name="bm")
            rows = Rot(kb, st, 3, [1, CG], F32, name="mrow")
            ng = Tile(kb, st, [P, DEPTH, 3 * NK], F32, name="ng")
            wr = Rot(kb, st, 2, [P, NK, CG], F32, name="wm")
            kb.dma("sp", cnd[:], self.cond[:, :], writes=[cnd.tok])
            for l in range(DEPTH):
                kb.dma("sp", ng[:, l, :], self.normg[l], writes=[ng.tok])
                kb.dma("sp", bm[:, l, :], self.bmod[l], writes=[bm.tok])
            kb.op("act", lambda e: e.activation(out=sc[:], in_=cnd[:], func=AF.Silu), reads=[cnd.tok], writes=[sc.tok])
            MC = CG // P
            for l in range(DEPTH):
                wv = self.wmodn[l].rearrange("(k p) c -> p k c", p=P)
                for cg in range(NCOL // CG):
                    w = wr.next()
                    kb.dma("sp", w[:], wv[:, :, cg * CG:(cg + 1) * CG], writes=[w.tok])
                    pb, pt = self.bank()
                    for kc in range(NK):
                        kb.op("pe", lambda e: e.matmul(pb[0:1, 0:CG], sc[:, kc:kc + 1], w[:, kc, :], start=(kc == 0), stop=(kc == NK - 1)),
                              reads=[w.tok, sc.tok], writes=[pt])
                    row = rows.next()
                    kb.op("act", lambda e: e.activation(out=row[:], in_=pb[0:1, 0:CG], func=AF.Copy), reads=[pt], writes=[row.tok])
                    pb2, pt2 = self.bank()
                    for mm in range(MC):
                        kb.op("pe", lambda e: e.matmul(pb2[:, mm:mm + 1], row[0:1, mm * P:(mm + 1) * P], self.ones[0:1, 0:1], start=True, stop=True),
                              reads=[row.tok, self.ones.tok], writes=[pt2])
                    m0 = cg * MC
                    kb.op("dve", lambda e: e.tensor_tensor(out=self.mod[:, l, m0:m0 + MC], in0=pb2[:, 0:MC], in1=bm[:, l, m0:m0 + MC], op=ALU.add),
                          reads=[pt2, bm.tok], writes=[self.mod.tok])
                for i in range(3):
                    scs = self.mod[:, l, (3 * i + 1) * NK:(3 * i + 2) * NK]
                    kb.op("dve", lambda e: e.scalar_tensor_tensor(
                        out=self.modA[:, l, i * NK:(i + 1) * NK], in0=scs, scalar=1.0, in1=ng[:, l, i * NK:(i + 1) * NK],
                        op0=ALU.add, op1=ALU.mult), reads=[self.mod.tok, ng.tok], writes=[self.modA.tok])
                    gs_ = self.mod[:, l, (3 * i + 2) * NK:(3 * i + 3) * NK]
                    kb.op("dve", lambda e: e.tensor_scalar(
                        out=self.modG[:, l, i * NK:(i + 1) * NK], in0=gs_, scalar1=(1.0 if i == 1 else 0.5), scalar2=None,
                        op0=ALU.mult), reads=[self.mod.tok], writes=[self.modG.tok])
            kb.barrier()

    def norm_tile(self, hb, tmp, rstd, A, SH, out_dtype_r=True):
        cfg, kb = self.cfg, self.kb
        NK = cfg.NK
        pb, pt = self.bank()
        for kc in range(NK):
            t = tmp.next()
            kb.op("act", lambda e, t=t, kc=kc: e.activation(out=t[:], in_=hb.f32((slice(None), kc)), func=AF.Square),
                  reads=[hb.tok], writes=[t.tok])
            kb.op("pe", lambda e, t=t, kc=kc: e.matmul(pb, self.ones[:], t[:], start=(kc == 0), stop=(kc == NK - 1)),
                  reads=[t.tok, self.ones.tok], writes=[pt])
        t = tmp.next()
        kb.op("act", lambda e: e.activation(out=t[:], in_=pb, func=AF.Sqrt, bias=self.epsc[:, 0:1], scale=1.0 / cfg.DM),
              reads=[pt, self.epsc.tok], writes=[t.tok])
        kb.op("dve", lambda e: e.reciprocal(out=rstd[:], in_=t[:]), reads=[t.tok], writes=[rstd.tok])
        for kc in range(NK):
            t = tmp.next()
            kb.op("dve", lambda e, t=t, kc=kc: e.scalar_tensor_tensor(out=t[:], in0=hb.f32((slice(None), kc)), scalar=A[:, kc:kc + 1],
                                                                    in1=rstd[:], op0=ALU.mult, op1=ALU.mult),
                  reads=[hb.tok, rstd.tok, self.modA.tok, self.fngt.tok], writes=[t.tok])
            if SH is not None:
                kb.op("act", lambda e, t=t, kc=kc: e.activation(out=hb[:, kc], in_=t[:], func=AF.Identity, bias=SH[:, kc:kc + 1], scale=1.0),
                      reads=[t.tok, self.mod.tok], writes=[hb.tok])
            else:
                kb.op("act", lambda e, t=t, kc=kc: e.activation(out=hb[:, kc], in_=t[:], func=AF.Copy),
                      reads=[t.tok], writes=[hb.tok])

    def load_xtile(self, hb, src, tt):
        kb, cfg = self.kb, self.cfg
        sv = src.rearrange("(k p) t -> p k t", p=P)[:, :, tt * TT:(tt + 1) * TT]
        kb.dma("pool", hb[:], sv, reads=[self.dt("x", tt)], writes=[hb.tok])

    def phase_ffn(self, l, w, src):
        cfg, kb = self.cfg, self.kb
        NK, NJ, NT = cfg.NK, cfg.NJ, cfg.NT
        JH = (NJ + 1) // 2
        WSZ = max(NK, JH) * P
        A = self.modA[:, l, (2 * w) * NK:(2 * w + 1) * NK]
        SH = self.mod[:, l, (6 * w) * NK:(6 * w + 1) * NK]
        G = self.modG[:, l, (2 * w) * NK:(2 * w + 1) * NK]
        with ExitStack() as st:
            hb = Tile(kb, st, [P, NK, TT], F32R, name="hb")
            act = Tile(kb, st, [P, NJ, TT], F32R, ntok=NJ, name="act")
            wr = Rot(kb, st, 4, [P, WSZ], F32R, name="wf")
            tmp = Rot(kb, st, 3, [P, TT], F32, name="tmp")
            xc = Rot(kb, st, 3, [P, TT], F32, name="xc")
            rstd = Tile(kb, st, [P, TT], F32, name="rstd")
            srcf = src.bitcast(F32)
            xresf = self.xres.bitcast(F32)
            for tt in range(NT):
                self.load_xtile(hb, src, tt)
                self.norm_tile(hb, tmp, rstd, A, SH)
                for j in range(NJ):
                    pbs = []
                    for half in range(2):
                        wt = wr.next()
                        kb.dma("pool", wt[:, 0:NK * P], self.w1[l, w, half * NJ + j], writes=[wt.tok])
                        pb, pt = self.bank()
                        for kc in range(NK):
                            kb.op("pe", lambda e, wt=wt, kc=kc, pb=pb: e.matmul(pb, wt[:, kc * P:(kc + 1) * P], hb[:, kc],
                                                                              start=(kc == 0), stop=(kc == NK - 1)),
                                  reads=[wt.tok, hb.tok], writes=[pt])
                        pbs.append((pb, pt))
                    t = tmp.next()
                    kb.op("act", lambda e, t=t: e.activation(out=t[:], in_=pbs[0][0], func=AF.Silu), reads=[pbs[0][1]], writes=[t.tok])
                    kb.op("dve", lambda e, t=t, j=j: e.tensor_tensor(out=act[:, j], in0=pbs[1][0], in1=t[:], op=ALU.mult),
                          reads=[pbs[1][1], t.tok], writes=[act.toks[j]])
                for n in range(NK):
                    pb, pt = self.bank()
                    for hf in range(2):
                        j0, j1 = (0, JH) if hf == 0 else (JH, NJ)
                        wt = wr.next()
                        kb.dma("pool", wt[:, 0:(j1 - j0) * P], self.w2[l, w, n][:, j0 * P:j1 * P], writes=[wt.tok])
                        for j in range(j0, j1):
                            kb.op("pe", lambda e, wt=wt, j=j, j0=j0: e.matmul(pb, wt[:, (j - j0) * P:(j - j0 + 1) * P], act[:, j],
                                                                            start=(j == 0), stop=(j == NJ - 1)),
                                  reads=[wt.tok, act.toks[j]], writes=[pt])
                    x = xc.next()
                    kb.dma("sp", x[:], srcf[n * P:(n + 1) * P, tt * TT:(tt + 1) * TT], reads=[self.dt("x", tt)], writes=[x.tok])
                    kb.op("dve", lambda e, x=x, n=n: e.scalar_tensor_tensor(out=x[:], in0=pb, scalar=G[:, n:n + 1], in1=x[:],
                                                                          op0=ALU.mult, op1=ALU.add),
                          reads=[pt, x.tok, self.modG.tok], writes=[x.tok])
                    kb.dma("sp", xresf[n * P:(n + 1) * P, tt * TT:(tt + 1) * TT], x[:], reads=[x.tok], writes=[self.dt("xo", tt, n)])
                xt_ = self.dt("x", tt)
                xt_.w = []
                kb.join(xt_, [self.dt("xo", tt, n) for n in range(NK)])
            kb.barrier()

    def phase_final(self):
        cfg, kb = self.cfg, self.kb
        NK, NT = cfg.NK, cfg.NT
        with ExitStack() as st:
            hb = Tile(kb, st, [P, NK, TT], F32R, name="hbf")
            tmp = Rot(kb, st, 3, [P, TT], F32, name="tmpf")
            rstd = Tile(kb, st, [P, TT], F32, name="rstdf")
            for tt in range(NT):
                self.load_xtile(hb, self.xres, tt)
                self.norm_tile(hb, tmp, rstd, self.fngt, None)
                dv = self.yT.rearrange("(k p) t -> p k t", p=P)[:, :, tt * TT:(tt + 1) * TT]
                kb.dma("sp", dv, hb.f32(slice(None)), reads=[hb.tok], writes=[self.dt("y", tt)])
            kb.barrier()

    def mixer_decl(self):
        cfg = self.cfg
        NK, T, DEPTH, NSEG = cfg.NK, cfg.T, cfg.DEPTH, cfg.NSEG
        self.win = self.inp("win", [DEPTH, cfg.NZ, P, NK * P], F32R)
        self.zT = self.scratch("zT", [cfg.NZ * P, T])
        self.wpa = self.inp("wpa", [DEPTH, NK, P, 8 * P], F32R)
        self.wpb = self.inp("wpb", [DEPTH, NK, P, 8 * P], F32R)
        self.wpc = self.inp("wpc", [DEPTH, 2 * NK, P, 8 * P], F32R)
        self.wo = self.inp("wo", [DEPTH, NK, P, NK * P], F32R)
        self.ya = self.scratch("ya", [RW, T], F32R)
        self.yb = self.scratch("yb", [SW, T], F32R)
        self.yc = self.scratch("yc", [S5W, T], F32R)
        self.keepT = self.inp("keepT", [P, TT])
        self.rwp = self.scratch("rwp", [2, 5, RW, T], F32R)
        self.rwbon = self.scratch("rwbon", [RW, T])
        self.rwgc = self.scratch("rwgc", [2, RW, T // 64])
        self.rwsm = self.inp("rwsm", [P, 4, TT])
        self.rwcm = self.inp("rwcm", [P, T])
        self.rwcol = self.inp("rwcol", [DEPTH, P, 5, 8])
        self.rwmu = self.inp("rwmu", [DEPTH, P, 26])
        self.rww0 = self.inp("rww0", [DEPTH, P, 2, 2, 8])
        self.rww2 = self.inp("rww2", [DEPTH, P, 2, RW])
        self.rwg2 = self.inp("rwg2", [DEPTH, P, RW])
        self.hblk = self.inp("hblk", [2, P, P])
        self.rwtri = self.inp("rwtri", [3, P, 64])
        self.rws0 = self.inp("rws0", [DEPTH, 2, 64, 16, 64], F32R)
        self.rwo = self.outp("rwo", [DEPTH, 2, NSEG, 64, 16, 64])
        self.rwoT = self.scratch("rwoT", [2, RW, T])
        self.xcs = self.scratch("xcs", [SXBC, T])
        self.tri = self.inp("tri", [2, P, P])
        self.cmT = self.inp("cmT", [P, 4, TT])
        self.ssdcw = self.inp("ssdcw", [DEPTH, P, 12, 6])
        self.ssdcol = self.inp("ssdcol", [DEPTH, 64, 3])
        self.ssdD = self.inp("ssdD", [DEPTH, P, 8])
        self.ssdg = self.inp("ssdg", [DEPTH, P, 8])
        self.ssds0 = self.inp("ssds0", [DEPTH, 2, 2, P, 512])
        self.ssdkeep = self.inp("ssdkeep", [P, 1])
        self.ssdo = self.outp("ssdo", [DEPTH, 2, NSEG, 2, P, 512])
        self.s5lam = self.inp("s5lam", [DEPTH, 2, P, 3, 32])
        self.s5b = self.inp("s5b", [DEPTH, 2, P, 2, 32, 16])
        self.s5c = self.inp("s5c", [DEPTH, 2, 2, 32, P, P], F32R)
        self.s5d = self.inp("s5d", [DEPTH, P, 8])
        self.s5s0 = self.inp("s5s0", [DEPTH, 2, P, 2, 32])
        self.s5o = self.outp("s5o", [DEPTH, P, 2, 2, 32, NSEG])

    def mixer_consts(self):
        kb = self.kb
        self.epsc = Tile(kb, self.gs, [P, 4], F32, name="epsc")
        kb.op("dve", lambda e: e.memset(self.epsc[:, 0:1], EPS), writes=[self.epsc.tok])
        kb.op("dve", lambda e: e.memset(self.epsc[:, 1:2], float(np.pi / 2)), writes=[self.epsc.tok])
        kb.op("dve", lambda e: e.memset(self.epsc[:, 2:3], 1e-12), writes=[self.epsc.tok])
        kb.op("dve", lambda e: e.memset(self.epsc[:, 3:4], 0.0), writes=[self.epsc.tok])
        self.ident = Tile(kb, self.gs, [P, P], F32, name="ident")
        self.identd = self.inp("identd", [P, P])
        kb.dma("sp", self.ident[:], self.identd[:, :], writes=[self.ident.tok])
        self.keep = Tile(kb, self.gs, [P, TT], F32, name="keep")
        kb.dma("sp", self.keep[:], self.keepT[:, :], writes=[self.keep.tok])

    def phase_mix(self, l):
        cfg = self.cfg
        self.phase_inproj(l)
        if cfg.mix[0]:
            self.phase_rwkv(l)
        if cfg.mix[1]:
            self.phase_ssd(l)
        if cfg.mix[2]:
            self.phase_s5(l)
        self.phase_merge(l)

    def phase_inproj(self, l):
        cfg, kb = self.cfg, self.kb
        NK, NT = cfg.NK, cfg.NT
        A = self.modA[:, l, NK:2 * NK]
        SH = self.mod[:, l, 3 * NK:4 * NK]
        with ExitStack() as st:
            hb = Tile(kb, st, [P, NK, TT], F32R, name="hbi")
            wr = Rot(kb, st, 4, [P, NK * P], F32R, name="wi")
            tmp = Rot(kb, st, 3, [P, TT], F32, name="tmpi")
            stg = Rot(kb, st, 4, [P, TT], F32, name="stg")
            rstd = Tile(kb, st, [P, TT], F32, name="rstdi")
            for tt in range(NT):
                self.load_xtile(hb, self.xres, tt)
                self.norm_tile(hb, tmp, rstd, A, SH)
                for m in range(cfg.NZ):
                    wt = wr.next()
                    kb.dma("pool", wt[:], self.win[l, m], writes=[wt.tok])
                    pb, pt = self.bank()
                    for kc in range(NK):
                        kb.op("pe", lambda e: e.matmul(pb, wt[:, kc * P:(kc + 1) * P], hb[:, kc], start=(kc == 0), stop=(kc == NK - 1)),
                              reads=[wt.tok, hb.tok], writes=[pt])
                    s = stg.next()
                    if m >= cfg.ZG and m < cfg.ZDT:
                        kb.op("act", lambda e: e.activation(out=s[:], in_=pb, func=AF.Sigmoid), reads=[pt], writes=[s.tok])
                    elif m >= cfg.ZB and m < cfg.ZB + 8:
                        kb.op("act", lambda e: e.activation(out=s[:], in_=pb, func=AF.Silu), reads=[pt], writes=[s.tok])
                    elif m % 2:
                        kb.op("act", lambda e: e.activation(out=s[:], in_=pb, func=AF.Copy), reads=[pt], writes=[s.tok])
                    else:
                        kb.op("dve", lambda e: e.tensor_copy(out=s[:], in_=pb), reads=[pt], writes=[s.tok])
                    kb.dma("sp", self.zT[m * P:(m + 1) * P, tt * TT:(tt + 1) * TT], s[:], reads=[s.tok], writes=[self.dt("z", m, tt)])
            kb.barrier()

    def phase_merge(self, l):
        cfg, kb = self.cfg, self.kb
        NK, NT = cfg.NK, cfg.NT
        G = self.modG[:, l, NK:2 * NK]
        xresf = self.xres.bitcast(F32)
        with ExitStack() as st:
            ys = [Tile(kb, st, [P, 8, TT], F32R, name="ym%d" % i) for i in range(3)]
            mg = Tile(kb, st, [P, NK, TT], F32R, ntok=NK, name="mg")
            wr = Rot(kb, st, 4, [P, max(NK, 8) * P], F32R, name="wm")
            gt = Rot(kb, st, 4, [P, TT], F32, name="gt")
            tmp = Rot(kb, st, 6, [P, TT], F32, name="tmpm")
            xc = Rot(kb, st, 3, [P, TT], F32, name="xcm")
            srcs = [self.ya, self.yb, self.yc]
            for tt in range(NT):
                for i in range(3):
                    if cfg.mix[i]:
                        sv = srcs[i].rearrange("(k p) t -> p k t", p=P)[:, :, tt * TT:(tt + 1) * TT]
                        kb.dma("pool", ys[i][:], sv, writes=[ys[i].tok])
                for n in range(NK):
                    terms = []
                    for i, wsrc in ((0, self.wpa), (1, self.wpb)):
                        if not cfg.mix[i]:
                            continue
                        wt = wr.next()
                        kb.dma("pool", wt[:, 0:8 * P], wsrc[l, n], writes=[wt.tok])
                        pb, pt = self.bank()
                        for k in range(8):
                            kb.op("pe", lambda e: e.matmul(pb, wt[:, k * P:(k + 1) * P], ys[i][:, k], start=(k == 0), stop=(k == 7)),
                                  reads=[wt.tok, ys[i].tok], writes=[pt])
                        g = gt.next()
                        kb.dma("sp", g[:], self.zT[(cfg.ZG + i * NK + n) * P:(cfg.ZG + i * NK + n + 1) * P, tt * TT:(tt + 1) * TT], writes=[g.tok])
                        t = tmp.next()
                        kb.op("dve", lambda e: e.tensor_tensor(out=t[:], in0=pb, in1=g[:], op=ALU.mult), reads=[pt, g.tok], writes=[t.tok])
                        terms.append(t)
                    if cfg.mix[2]:
                        pbs = []
                        for hf in range(2):
                            wt = wr.next()
                            kb.dma("pool", wt[:, 0:8 * P], self.wpc[l, hf * NK + n], writes=[wt.tok])
                            pb, pt = self.bank()
                            for k in range(8):
                                kb.op("pe", lambda e: e.matmul(pb, wt[:, k * P:(k + 1) * P], ys[2][:, k], start=(k == 0), stop=(k == 7)),
                                      reads=[wt.tok, ys[2].tok], writes=[pt])
                            pbs.append((pb, pt))
                        g = gt.next()
                        kb.dma("sp", g[:], self.zT[(cfg.ZG + 2 * NK + n) * P:(cfg.ZG + 2 * NK + n + 1) * P, tt * TT:(tt + 1) * TT], writes=[g.tok])
                        sg = tmp.next()
                        kb.op("act", lambda e: e.activation(out=sg[:], in_=pbs[1][0], func=AF.Sigmoid), reads=[pbs[1][1]], writes=[sg.tok])
                        t = tmp.next()
                        kb.op("dve", lambda e: e.tensor_tensor(out=t[:], in0=pbs[0][0], in1=sg[:], op=ALU.mult), reads=[pbs[0][1], sg.tok], writes=[t.tok])
                        kb.op("dve", lambda e: e.tensor_tensor(out=t[:], in0=t[:], in1=g[:], op=ALU.mult), reads=[t.tok, g.tok], writes=[t.tok])
                        terms.append(t)
                    if not terms:
                        kb.op("dve", lambda e: e.memset(mg[:, n], 0.0), writes=[mg.toks[n]])
                    elif len(terms) == 1:
                        kb.op("dve", lambda e: e.tensor_copy(out=mg[:, n], in_=terms[0][:]), reads=[terms[0].tok], writes=[mg.toks[n]])
                    else:
                        for a_ in terms[2:]:
                            kb.op("dve", lambda e: e.tensor_tensor(out=terms[0][:], in0=terms[0][:], in1=a_[:], op=ALU.add),
                                  reads=[terms[0].tok, a_.tok], writes=[terms[0].tok])
                        kb.op("dve", lambda e: e.tensor_tensor(out=mg[:, n], in0=terms[0][:], in1=terms[1][:], op=ALU.add),
                              reads=[terms[0].tok, terms[1].tok], writes=[mg.toks[n]])
                for n in range(NK):
                    wt = wr.next()
                    kb.dma("pool", wt[:, 0:NK * P], self.wo[l, n], writes=[wt.tok])
                    pb, pt = self.bank()
                    for k in range(NK):
                        kb.op("pe", lambda e: e.matmul(pb, wt[:, k * P:(k + 1) * P], mg[:, k], start=(k == 0), stop=(k == NK - 1)),
                              reads=[wt.tok, mg.toks[k]], writes=[pt])
                    x = xc.next()
                    kb.dma("sp", x[:], xresf[n * P:(n + 1) * P, tt * TT:(tt + 1) * TT], reads=[self.dt("x", tt)], writes=[x.tok])
                    kb.op("dve", lambda e: e.scalar_tensor_tensor(out=x[:], in0=pb, scalar=G[:, n:n + 1], in1=x[:], op0=ALU.mult, op1=ALU.add),
                          reads=[pt, x.tok, self.modG.tok], writes=[x.tok])
                    kb.dma("sp", xresf[n * P:(n + 1) * P, tt * TT:(tt + 1) * TT], x[:], reads=[x.tok], writes=[self.dt("xo", tt, n)])
            kb.barrier()

    def phase_s5(self, l):
        cfg, kb = self.cfg, self.kb
        T, NT, NSEG = cfg.T, cfg.NT, cfg.NSEG
        V = lambda e: e
        with ExitStack() as st:
            def tl(shape, name, dtype=F32):
                return Tile(kb, st, shape, dtype, name=name)
            so = tl([P, 2, 2, 32, NSEG], "s5so")
            dcol = tl([P, 8], "s5dc")
            kb.dma("sp", dcol[:], self.s5d[l], writes=[dcol.tok])
            yacc = tl([P, T], "yacc")
            uch = tl([P, T], "uch", F32R)
            prm = []
            for d in range(2):
                lam = tl([P, 3, 32], "lam")
                kb.dma("sp", lam[:], self.s5lam[l, d], writes=[lam.tok])
                bq = tl([P, 2, 32, 16], "bq")
                kb.dma("sp", bq[:], self.s5b[l, d], writes=[bq.tok])
                s0 = tl([P, 2, 32], "s0")
                kb.dma("sp", s0[:], self.s5s0[l, d], writes=[s0.tok])
                w = tl([P, 16, 32], "s5w")
                def W(i):
                    return w[:, i, :]
                def tt_(o, a, b, op):
                    kb.op("dve", lambda e: e.tensor_tensor(out=o, in0=a, in1=b, op=op), reads=[w.tok, lam.tok], writes=[w.tok])
                def ts_(o, a, s1, op0, s2=None, op1=None):
                    if op1 is None:
                        kb.op("dve", lambda e: e.tensor_scalar(out=o, in0=a, scalar1=s1, scalar2=None, op0=op0), reads=[w.tok, lam.tok], writes=[w.tok])
                    else:
                        kb.op("dve", lambda e: e.tensor_scalar(out=o, in0=a, scalar1=s1, scalar2=s2, op0=op0, op1=op1), reads=[w.tok, lam.tok], writes=[w.tok])
                def ac_(o, a, f, bias=None, scale=1.0):
                    if bias is None:
                        kb.op("act", lambda e: e.activation(out=o, in_=a, func=f, scale=scale), reads=[w.tok, lam.tok], writes=[w.tok])
                    else:
                        kb.op("act", lambda e: e.activation(out=o, in_=a, func=f, bias=bias, scale=scale), reads=[w.tok, lam.tok, self.epsc.tok], writes=[w.tok])
                lre, lim, ldt = lam[:, 0, :], lam[:, 1, :], lam[:, 2, :]
                ac_(W(0), ldt, AF.Exp)
                tt_(W(1), lre, W(0), ALU.mult)
                ac_(W(1), W(1), AF.Exp)
                tt_(W(2), lim, W(0), ALU.mult)
                ac_(W(3), W(2), AF.Sin, scale=1.0 / 16)
                ac_(W(4), W(2), AF.Sin, bias=self.epsc[:, 1:2], scale=1.0 / 16)
                for _ in range(4):
                    tt_(W(5), W(3), W(4), ALU.mult)
                    tt_(W(6), W(4), W(4), ALU.mult)
                    tt_(W(7), W(3), W(3), ALU.mult)
                    tt_(W(4), W(6), W(7), ALU.subtract)
                    ts_(W(3), W(5), 2.0, ALU.mult)
                tt_(W(5), W(1), W(4), ALU.mult)
                tt_(W(6), W(1), W(3), ALU.mult)
                ts_(W(7), W(5), -1.0, ALU.add)
                tt_(W(8), lre, lre, ALU.mult)
                tt_(W(9), lim, lim, ALU.mult)
                tt_(W(8), W(8), W(9), ALU.add)
                kb.op("dve", lambda e: e.reciprocal(out=W(8), in_=W(8)), reads=[w.tok], writes=[w.tok])
                tt_(W(9), W(7), lre, ALU.mult)
                tt_(W(10), W(6), lim, ALU.mult)
                tt_(W(9), W(9), W(10), ALU.add)
                tt_(W(9), W(9), W(8), ALU.mult)
                tt_(W(10), W(6), lre, ALU.mult)
                tt_(W(11), W(7), lim, ALU.mult)
                tt_(W(10), W(10), W(11), ALU.subtract)
                tt_(W(10), W(10), W(8), ALU.mult)
                bb = tl([P, 2, 32, 16], "bb")
                tb = tl([P, 32, 16], "tb")
                qre_b = W(9).to_broadcast([P, 32, 16]) if False else None
                def bc(i):
                    return w[:, i, :].unsqueeze(2).to_broadcast([P, 32, 16])
                kb.op("dve", lambda e: e.tensor_tensor(out=bb[:, 0], in0=bq[:, 0], in1=bc(9), op=ALU.mult), reads=[bq.tok, w.tok], writes=[bb.tok])
                kb.op("dve", lambda e: e.tensor_tensor(out=tb[:], in0=bq[:, 1], in1=bc(10), op=ALU.mult), reads=[bq.tok, w.tok], writes=[tb.tok])
                kb.op("dve", lambda e: e.tensor_tensor(out=bb[:, 0], in0=bb[:, 0], in1=tb[:], op=ALU.subtract), reads=[bb.tok, tb.tok], writes=[bb.tok])
                kb.op("dve", lambda e: e.tensor_tensor(out=bb[:, 1], in0=bq[:, 1], in1=bc(9), op=ALU.mult), reads=[bq.tok, w.tok], writes=[bb.tok])
                kb.op("dve", lambda e: e.tensor_tensor(out=tb[:], in0=bq[:, 0], in1=bc(10), op=ALU.mult), reads=[bq.tok, w.tok], writes=[tb.tok])
                kb.op("dve", lambda e: e.tensor_tensor(out=bb[:, 1], in0=bb[:, 1], in1=tb[:], op=ALU.add), reads=[bb.tok, tb.tok], writes=[bb.tok])
                pw = tl([P, 10, 2, 32], "pw")
                kb.op("dve", lambda e: e.tensor_copy(out=pw[:, 0, 0], in_=W(4)), reads=[w.tok], writes=[pw.tok])
                kb.op("dve", lambda e: e.tensor_copy(out=pw[:, 0, 1], in_=W(3)), reads=[w.tok], writes=[pw.tok])
                for k in range(1, 10):
                    c_, s_ = pw[:, k - 1, 0], pw[:, k - 1, 1]
                    kb.op("dve", lambda e: e.tensor_tensor(out=W(11), in0=c_, in1=c_, op=ALU.mult), reads=[pw.tok, w.tok], writes=[w.tok])
                    kb.op("dve", lambda e: e.tensor_tensor(out=W(12), in0=s_, in1=s_, op=ALU.mult), reads=[pw.tok, w.tok], writes=[w.tok])
                    kb.op("dve", lambda e: e.tensor_tensor(out=pw[:, k, 0], in0=W(11), in1=W(12), op=ALU.subtract), reads=[w.tok, pw.tok], writes=[pw.tok])
                    kb.op("dve", lambda e: e.tensor_tensor(out=W(11), in0=c_, in1=s_, op=ALU.mult), reads=[pw.tok, w.tok], writes=[w.tok])
                    kb.op("dve", lambda e: e.tensor_scalar(out=pw[:, k, 1], in0=W(11), scalar1=2.0, scalar2=None, op0=ALU.mult), reads=[w.tok, pw.tok], writes=[pw.tok])
                prm.append(dict(w=w, bb=bb, pw=pw, s0=s0))
            class Res:
                pass
            RS = []
            for d in range(2):
                r_ = Res()
                r_.pool = "A" if d == 0 else "B"
                r_.Fc, r_.Fs, r_.Fsn, r_.d0 = tl([P, TT], "Fc"), tl([P, TT], "Fs"), tl([P, TT], "Fsn"), tl([P, TT], "d0")
                r_.E = [tl([P, P], "Eb%d" % i) for i in range(2)]
                r_.Bp = [tl([P, P], "Bp%d" % i, F32R) for i in range(2)]
                r_.Cp = [tl([P, P], "Cp%d" % i, F32R) for i in range(2)]
                for e_ in r_.E:
                    kb.op("dve", lambda e: e.memset(e_[:], 0.0), writes=[e_.tok])
                r_.wk = Rot(kb, st, 6, [P, TT], F32, name="s5wk")
                r_.pk = Rot(kb, st, 4, [P, TT], F32, name="s5pk")
                r_.sreR = Rot(kb, st, 2, [P, TT], F32R, name="sre")
                r_.nsiR = Rot(kb, st, 2, [P, TT], F32R, name="nsi")
                r_.cin = tl([P, 4], "cin")
                r_.tmpc = tl([P, 4], "tmpc")
                r_.pend = []
                RS.append(r_)

            def build_stream(q, d, Y):
                R_ = RS[d]
                pr = prm[d]
                w, bb, pw, s0 = pr["w"], pr["bb"], pr["pw"], pr["s0"]
                Fc, Fs, Fsn, d0, E, Bp, Cp = R_.Fc, R_.Fs, R_.Fsn, R_.d0, R_.E, R_.Bp, R_.Cp
                cin, tmpc, wk, pk, pend = R_.cin, R_.tmpc, R_.wk, R_.pk, R_.pend
                L = []

                def OP(eng, fn, reads=(), writes=()):
                    L.append(lambda: kb.op(eng, fn, reads=reads, writes=writes))

                def DMA(q_, out, in_, reads=(), writes=()):
                    L.append(lambda: kb.dma(q_, out, in_, reads=reads, writes=writes))

                def flush():
                    def f():
                        while pend:
                            pend.pop(0)()
                    L.append(f)

                flush()
                for ri in range(2):
                    for g2 in range(2):
                        g8 = 2 * (q % 4) + g2
                        OP("dve", lambda e, ri=ri, g2=g2, g8=g8: e.tensor_copy(out=E[ri][g2 * 64:(g2 + 1) * 64, g8 * 16:(g8 + 1) * 16],
                                                                             in_=bb[g2 * 64:(g2 + 1) * 64, ri, q, :]), reads=[bb.tok], writes=[E[ri].tok])
                    pb, pt = self.bank(R_.pool)
                    OP("pe", lambda e, ri=ri, pb=pb: e.transpose(pb[:, 0:P], E[ri][:], self.ident[:]), reads=[E[ri].tok, self.ident.tok], writes=[pt])
                    OP("act", lambda e, ri=ri, pb=pb: e.activation(out=Bp[ri][:], in_=pb[:, 0:P], func=AF.Copy), reads=[pt], writes=[Bp[ri].tok])
                    for g2 in range(2):
                        g8 = 2 * (q % 4) + g2
                        OP("dve", lambda e, ri=ri, g2=g2, g8=g8: e.memset(E[ri][g2 * 64:(g2 + 1) * 64, g8 * 16:(g8 + 1) * 16], 0.0), writes=[E[ri].tok])
                    DMA("pool", Cp[ri][:], self.s5c[l, d, ri, q], writes=[Cp[ri].tok])
                OP("dve", lambda e: e.memset(Fc[:, 0:1], 1.0), writes=[Fc.tok])
                OP("dve", lambda e: e.memset(Fs[:, 0:1], 0.0), writes=[Fs.tok])
                for k in range(9):
                    n_ = 1 << k
                    pc_, ps_ = pw[:, k, 0, q:q + 1], pw[:, k, 1, q:q + 1]
                    t1, t2 = wk.next(), wk.next()
                    OP("dve", lambda e, t1=t1, n_=n_, ps_=ps_: e.tensor_scalar(out=t1[:, 0:n_], in0=Fs[:, 0:n_], scalar1=ps_, scalar2=None, op0=ALU.mult),
                       reads=[Fs.tok, pw.tok], writes=[t1.tok])
                    OP("dve", lambda e, t2=t2, n_=n_, ps_=ps_: e.tensor_scalar(out=t2[:, 0:n_], in0=Fc[:, 0:n_], scalar1=ps_, scalar2=None, op0=ALU.mult),
                       reads=[Fc.tok, pw.tok], writes=[t2.tok])
                    OP("dve", lambda e, t1=t1, n_=n_, pc_=pc_: e.scalar_tensor_tensor(out=Fc[:, n_:2 * n_], in0=Fc[:, 0:n_], scalar=pc_, in1=t1[:, 0:n_],
                                                                                  op0=ALU.mult, op1=ALU.subtract), reads=[Fc.tok, pw.tok, t1.tok], writes=[Fc.tok])
                    OP("dve", lambda e, t2=t2, n_=n_, pc_=pc_: e.scalar_tensor_tensor(out=Fs[:, n_:2 * n_], in0=Fs[:, 0:n_], scalar=pc_, in1=t2[:, 0:n_],
                                                                                  op0=ALU.mult, op1=ALU.add), reads=[Fs.tok, pw.tok, t2.tok], writes=[Fs.tok])
                OP("dve", lambda e: e.tensor_scalar(out=Fsn[:], in0=Fs[:], scalar1=-1.0, scalar2=None, op0=ALU.mult), reads=[Fs.tok], writes=[Fsn.tok])
                OP("dve", lambda e: e.tensor_scalar(out=d0[:], in0=self.keep[:], scalar1=w[:, 1, q:q + 1], scalar2=None, op0=ALU.mult),
                   reads=[self.keep.tok, w.tok], writes=[d0.tok])

                def crot(src_re, src_im, cc_, ss_, rd):
                    OP("dve", lambda e: e.tensor_scalar(out=tmpc[:, 0:1], in0=src_im, scalar1=ss_, scalar2=None, op0=ALU.mult), reads=rd + [pw.tok], writes=[tmpc.tok])
                    OP("dve", lambda e: e.scalar_tensor_tensor(out=cin[:, 2:3], in0=src_re, scalar=cc_, in1=tmpc[:, 0:1], op0=ALU.mult, op1=ALU.subtract),
                       reads=rd + [tmpc.tok, pw.tok], writes=[cin.tok])
                    OP("dve", lambda e: e.tensor_scalar(out=tmpc[:, 1:2], in0=src_re, scalar1=ss_, scalar2=None, op0=ALU.mult), reads=rd + [pw.tok], writes=[tmpc.tok])
                    OP("dve", lambda e: e.scalar_tensor_tensor(out=cin[:, 3:4], in0=src_im, scalar=cc_, in1=tmpc[:, 1:2], op0=ALU.mult, op1=ALU.add),
                       reads=rd + [tmpc.tok, pw.tok], writes=[cin.tok])
                crot(s0[:, 0, q:q + 1], s0[:, 1, q:q + 1], pw[:, 0, 0, q:q + 1], pw[:, 0, 1, q:q + 1], [s0.tok])
                for tg in range(NT):
                    if d == 0:
                        usl = uch[:, tg * TT:(tg + 1) * TT]
                        ysl = yacc[:, tg * TT:(tg + 1) * TT]
                    else:
                        hi = T - tg * TT
                        usl = uch[:, hi - TT:hi][:, ::-1]
                        ysl = yacc[:, hi - TT:hi][:, ::-1]
                    pbr, ptr = self.bank(R_.pool)
                    pbi, pti = self.bank(R_.pool)
                    OP("pe", lambda e, pbr=pbr, usl=usl: e.matmul(pbr, Bp[0][:], usl, start=True, stop=True), reads=[Bp[0].tok, uch.tok], writes=[ptr])
                    OP("pe", lambda e, pbi=pbi, usl=usl: e.matmul(pbi, Bp[1][:], usl, start=True, stop=True), reads=[Bp[1].tok, uch.tok], writes=[pti])
                    a1, a2, a3, a4 = wk.next(), wk.next(), wk.next(), wk.next()
                    OP("dve", lambda e, a1=a1, pbr=pbr: e.tensor_tensor(out=a1[:], in0=pbr, in1=Fc[:], op=ALU.mult), reads=[ptr, Fc.tok], writes=[a1.tok])
                    OP("dve", lambda e, a2=a2, pbi=pbi: e.tensor_tensor(out=a2[:], in0=pbi, in1=Fs[:], op=ALU.mult), reads=[pti, Fs.tok], writes=[a2.tok])
                    OP("dve", lambda e, a3=a3, pbi=pbi: e.tensor_tensor(out=a3[:], in0=pbi, in1=Fc[:], op=ALU.mult), reads=[pti, Fc.tok], writes=[a3.tok])
                    OP("dve", lambda e, a4=a4, pbr=pbr: e.tensor_tensor(out=a4[:], in0=pbr, in1=Fs[:], op=ALU.mult), reads=[ptr, Fs.tok], writes=[a4.tok])
                    OP("dve", lambda e, a1=a1, a2=a2: e.tensor_tensor(out=a1[:], in0=a1[:], in1=a2[:], op=ALU.add), reads=[a1.tok, a2.tok], writes=[a1.tok])
                    OP("dve", lambda e, a3=a3, a4=a4: e.tensor_tensor(out=a3[:], in0=a3[:], in1=a4[:], op=ALU.subtract), reads=[a3.tok, a4.tok], writes=[a3.tok])
                    OP("dve", lambda e, a1=a1, a2=a2: e.tensor_tensor_scan(out=a2[:], data0=d0[:], data1=a1[:], initial=cin[:, 2:3], op0=ALU.mult, op1=ALU.add),
                       reads=[d0.tok, a1.tok, cin.tok], writes=[a2.tok])
                    OP("dve", lambda e, a3=a3, a4=a4: e.tensor_tensor_scan(out=a4[:], data0=d0[:], data1=a3[:], initial=cin[:, 3:4], op0=ALU.mult, op1=ALU.add),
                       reads=[d0.tok, a3.tok, cin.tok], writes=[a4.tok])
                    flush()
                    if tg + 1 < NT:
                        crot(a2[:, TT - 1:TT], a4[:, TT - 1:TT], pw[:, 9, 0, q:q + 1], pw[:, 9, 1, q:q + 1], [a2.tok, a4.tok])
                    sre, nsi = R_.sreR.next(), R_.nsiR.next()
                    p1, p2, p3, p4 = pk.next(), pk.next(), pk.next(), pk.next()
                    OP("pool", lambda e, p1=p1, a2=a2: e.tensor_tensor(out=p1[:], in0=a2[:], in1=Fc[:], op=ALU.mult), reads=[a2.tok, Fc.tok], writes=[p1.tok])
                    OP("pool", lambda e, p2=p2, a4=a4: e.tensor_tensor(out=p2[:], in0=a4[:], in1=Fs[:], op=ALU.mult), reads=[a4.tok, Fs.tok], writes=[p2.tok])
                    OP("pool", lambda e, sre=sre, p1=p1, p2=p2: e.tensor_tensor(out=sre[:], in0=p1[:], in1=p2[:], op=ALU.subtract), reads=[p1.tok, p2.tok], writes=[sre.tok])
                    OP("pool", lambda e, p3=p3, a2=a2: e.tensor_tensor(out=p3[:], in0=a2[:], in1=Fsn[:], op=ALU.mult), reads=[a2.tok, Fsn.tok], writes=[p3.tok])
                    OP("pool", lambda e, p4=p4, a4=a4: e.tensor_tensor(out=p4[:], in0=a4[:], in1=Fc[:], op=ALU.mult), reads=[a4.tok, Fc.tok], writes=[p4.tok])
                    OP("pool", lambda e, nsi=nsi, p3=p3, p4=p4: e.tensor_tensor(out=nsi[:], in0=p3[:], in1=p4[:], op=ALU.subtract), reads=[p3.tok, p4.tok], writes=[nsi.tok])
                    nsg = TT // 256
                    OP("act", lambda e, sre=sre, tg=tg: e.activation(out=so[:, d, 0, q, tg * nsg:(tg + 1) * nsg], in_=sre.f32((slice(None), slice(255, None, 256))),
                                                                   func=AF.Identity, scale=1.0), reads=[sre.tok], writes=[so.tok])
                    OP("act", lambda e, nsi=nsi, tg=tg: e.activation(out=so[:, d, 1, q, tg * nsg:(tg + 1) * nsg], in_=nsi.f32((slice(None), slice(255, None, 256))),
                                                                   func=AF.Identity, scale=-1.0), reads=[nsi.tok], writes=[so.tok])
                    pby, pty = self.bank(R_.pool)
                    OP("pe", lambda e, pby=pby, sre=sre: e.matmul(pby, Cp[0][:], sre[:], start=True, stop=False), reads=[Cp[0].tok, sre.tok], writes=[pty])
                    OP("pe", lambda e, pby=pby, nsi=nsi: e.matmul(pby, Cp[1][:], nsi[:], start=False, stop=True), reads=[Cp[1].tok, nsi.tok], writes=[pty])
                    L.append(lambda pby=pby, pty=pty, ysl=ysl: pend.append(
                        lambda: kb.op("dve", lambda e: e.tensor_tensor(out=ysl, in0=pby, in1=ysl, op=ALU.add), reads=[pty, yacc.tok], writes=[yacc.tok])))
                return L

            for Y in range(8):
                kb.dma("pool", uch[:], self.zT.bitcast(F32R)[(cfg.ZC + Y) * P:(cfg.ZC + Y + 1) * P, :], writes=[uch.tok])
                kb.op("dve", lambda e: e.tensor_scalar(out=yacc[:], in0=uch.f32(slice(None)), scalar1=dcol[:, Y:Y + 1], scalar2=None, op0=ALU.mult),
                      reads=[uch.tok, dcol.tok], writes=[yacc.tok])
                for q in range(4 * Y, 4 * Y + 4):
                    LA, LB = build_stream(q, 0, Y), build_stream(q, 1, Y)
                    for i in range(max(len(LA), len(LB))):
                        if i < len(LA):
                            LA[i]()
                        if i < len(LB):
                            LB[i]()
                for r_ in RS:
                    while r_.pend:
                        r_.pend.pop(0)()
                for tg in range(NT):
                    ysl = yacc[:, tg * TT:(tg + 1) * TT]
                    a1, a2 = RS[0].wk.next(), RS[0].wk.next()
                    kb.op("act", lambda e: e.activation(out=a1[:], in_=ysl, func=AF.Square), reads=[yacc.tok], writes=[a1.tok])
                    kb.op("dve", lambda e: e.tensor_scalar(out=a1[:], in0=a1[:], scalar1=0.044715, scalar2=1.0, op0=ALU.mult, op1=ALU.add),
                          reads=[a1.tok], writes=[a1.tok])
                    kb.op("dve", lambda e: e.tensor_tensor(out=a1[:], in0=a1[:], in1=ysl, op=ALU.mult), reads=[a1.tok, yacc.tok], writes=[a1.tok])
                    kb.op("act", lambda e: e.activation(out=a2[:], in_=a1[:], func=AF.Tanh, scale=0.7978845608028654), reads=[a1.tok], writes=[a2.tok])
                    kb.op("dve", lambda e: e.tensor_scalar(out=a2[:], in0=a2[:], scalar1=0.5, scalar2=0.5, op0=ALU.mult, op1=ALU.add),
                          reads=[a2.tok], writes=[a2.tok])
                    kb.op("dve", lambda e: e.tensor_tensor(out=a2[:], in0=a2[:], in1=ysl, op=ALU.mult), reads=[a2.tok, yacc.tok], writes=[a2.tok])
                    kb.dma("sp", self.yc.bitcast(F32)[Y * P:(Y + 1) * P, tg * TT:(tg + 1) * TT], a2[:], reads=[a2.tok], writes=[self.dt("yc", Y, tg)])
            kb.dma("sp", self.s5o[l], so[:], reads=[so.tok], writes=[self.dt("s5o", l)])
            kb.barrier()

    def phase_rwkv(self, l):
        cfg, kb = self.cfg, self.kb
        T, NT, NSEG = cfg.T, cfg.NT, cfg.NSEG
        NCH = T // 64
        HS = (slice(0, 64), slice(64, 128))
        with ExitStack() as st:
            def tl(shape, name, dtype=F32):
                return Tile(kb, st, shape, dtype, name=name)
            sm = tl([P, 4, TT], "rwsm"); kb.dma("sp", sm[:], self.rwsm[:, :, :], writes=[sm.tok])
            cmk = tl([P, T], "rwcm"); kb.dma("sp", cmk[:], self.rwcm[:, :], writes=[cmk.tok])
            col = tl([P, 5, 8], "rwcol"); kb.dma("sp", col[:], self.rwcol[l], writes=[col.tok])
            mu = tl([P, 26], "rwmu"); kb.dma("sp", mu[:], self.rwmu[l], writes=[mu.tok])
            om = tl([P, 26], "rwom")
            kb.op("dve", lambda e: e.tensor_scalar(out=om[:], in0=mu[:], scalar1=-1.0, scalar2=1.0, op0=ALU.mult, op1=ALU.add), reads=[mu.tok], writes=[om.tok])
            oka = tl([P, 8], "rwoka")
            kb.op("dve", lambda e: e.tensor_scalar(out=oka[:], in0=col[:, 1, :], scalar1=-1.0, scalar2=1.0, op0=ALU.mult, op1=ALU.add), reads=[col.tok], writes=[oka.tok])
            w0 = tl([P, 2, 2, 8], "rww0"); kb.dma("sp", w0[:], self.rww0[l], writes=[w0.tok])
            w2 = tl([P, 2, RW], "rww2"); kb.dma("sp", w2[:], self.rww2[l], writes=[w2.tok])
            hb1 = tl([P, P], "hb1"); kb.dma("sp", hb1[:], self.hblk[0], writes=[hb1.tok])
            xraw = tl([P, T], "xraw")
            sacc = tl([P, T], "sacc")
            stmp = Rot(kb, st, 3, [P, TT], F32, name="stmp")

            def load_shifted(ch, dst):
                kb.dma("sp", xraw[:], self.zT[(cfg.ZA + ch) * P:(cfg.ZA + ch + 1) * P, :], writes=[xraw.tok])
                kb.op("dve", lambda e: e.memset(sacc[:], 0.0), writes=[sacc.tok])
                for oi, o in enumerate((-1, 1, -64, 64)):
                    for tg in range(NT):
                        lo, hi = tg * TT, (tg + 1) * TT
                        slo, shi = max(lo + o, 0), min(hi + o, T)
                        dlo, dhi = slo - o, shi - o
                        n_ = dhi - dlo
                        t = stmp.next()
                        kb.op("dve", lambda e: e.tensor_tensor(out=t[:, 0:n_], in0=xraw[:, slo:shi], in1=sm[:, oi, dlo - lo:dhi - lo], op=ALU.mult),
                              reads=[xraw.tok, sm.tok], writes=[t.tok])
                        kb.op("dve", lambda e: e.tensor_tensor(out=sacc[:, dlo:dhi], in0=sacc[:, dlo:dhi], in1=t[:, 0:n_], op=ALU.add),
                              reads=[sacc.tok, t.tok], writes=[sacc.tok])
                kb.op("dve", lambda e: e.tensor_scalar(out=xraw[:], in0=xraw[:], scalar1=om[:, ch:ch + 1], scalar2=None, op0=ALU.mult), reads=[xraw.tok, om.tok], writes=[xraw.tok])
                kb.op("dve", lambda e: e.scalar_tensor_tensor(out=dst[:], in0=sacc[:], scalar=mu[:, ch:ch + 1], in1=xraw[:], op0=ALU.mult, op1=ALU.add),
                      reads=[sacc.tok, mu.tok, xraw.tok], writes=[dst.tok])

            tw = tl([P, T], "rwtw")
            load_shifted(24, tw)
            kb.op("act", lambda e: e.activation(out=tw[0:64, :], in_=tw[0:64, :], func=AF.Tanh), reads=[tw.tok], writes=[tw.tok])
            rr, kk_, vv_ = tl([P, T], "rwr"), tl([P, T], "rwk"), tl([P, T], "rwv")
            kkn = tl([P, T], "rwkkn")
            ad, ldc, cum = tl([P, T], "rwad"), tl([P, T], "rwld"), tl([P, T], "rwcum")
            t1, t2 = tl([P, T], "rwt1"), tl([P, T], "rwt2")
            outr = Rot(kb, st, 3, [P, T], F32, name="rwout")
            gct = tl([P, NCH], "rwgct")
            for j in range(8):
                load_shifted(j, rr)
                load_shifted(8 + j, kk_)
                load_shifted(16 + j, vv_)
                kb.op("dve", lambda e: e.tensor_scalar(out=kkn[:], in0=kk_[:], scalar1=col[:, 0, j:j + 1], scalar2=None, op0=ALU.mult), reads=[kk_.tok, col.tok], writes=[kkn.tok])
                kb.op("act", lambda e: e.activation(out=t1[:], in_=kkn[:], func=AF.Square), reads=[kkn.tok], writes=[t1.tok])
                for tg in range(NT):
                    tc_ = slice(tg * TT, (tg + 1) * TT)
                    pb, pt = self.bank()
                    kb.op("pe", lambda e: e.matmul(pb, hb1[:], t1[:, tc_], start=True, stop=True), reads=[hb1.tok, t1.tok], writes=[pt])
                    kb.op("act", lambda e: e.activation(out=t2[:, tc_], in_=pb, func=AF.Sqrt, bias=self.epsc[:, 2:3], scale=1.0), reads=[pt, self.epsc.tok], writes=[t2.tok])
                kb.op("dve", lambda e: e.reciprocal(out=t2[:], in_=t2[:]), reads=[t2.tok], writes=[t2.tok])
                kb.op("dve", lambda e: e.tensor_tensor(out=kkn[:], in0=kkn[:], in1=t2[:], op=ALU.mult), reads=[kkn.tok, t2.tok], writes=[kkn.tok])
                kb.op("dve", lambda e: e.scalar_tensor_tensor(out=t1[:], in0=rr[:], scalar=col[:, 2, j:j + 1], in1=kk_[:], op0=ALU.mult, op1=ALU.mult),
                      reads=[rr.tok, col.tok, kk_.tok], writes=[t1.tok])
                bo = outr.next()
                for tg in range(NT):
                    tc_ = slice(tg * TT, (tg + 1) * TT)
                    pb, pt = self.bank()
                    kb.op("pe", lambda e: e.matmul(pb, hb1[:], t1[:, tc_], start=True, stop=True), reads=[hb1.tok, t1.tok], writes=[pt])
                    kb.op("dve", lambda e: e.tensor_tensor(out=bo[:, tc_], in0=pb, in1=vv_[:, tc_], op=ALU.mult), reads=[pt, vv_.tok], writes=[bo.tok])
                kb.dma("sp", self.rwbon[j * P:(j + 1) * P, :], bo[:], reads=[bo.tok], writes=[self.dt("rwbon", j)])
                for d in range(2):
                    R = (lambda ap: ap) if d == 0 else (lambda ap: ap[:, ::-1])
                    for tg in range(NT):
                        tc_ = slice(tg * TT, (tg + 1) * TT)
                        pb, pt = self.bank()
                        kb.op("pe", lambda e: e.matmul(pb, w2[0:64, d, j * P:(j + 1) * P], tw[0:64, tc_], start=True, stop=True), reads=[w2.tok, tw.tok], writes=[pt])
                        kb.op("act", lambda e: e.activation(out=ldc[:, tc_], in_=pb, func=AF.Sigmoid, bias=w0[:, d, 0, j:j + 1], scale=1.0), reads=[pt, w0.tok], writes=[ldc.tok])
                        pb2, pt2 = self.bank()
                        kb.op("pe", lambda e: e.matmul(pb2, w2[64:128, d, j * P:(j + 1) * P], tw[64:128, tc_], start=True, stop=True), reads=[w2.tok, tw.tok], writes=[pt2])
                        kb.op("act", lambda e: e.activation(out=ad[:, tc_], in_=pb2, func=AF.Sigmoid, bias=w0[:, d, 1, j:j + 1], scale=1.0), reads=[pt2, w0.tok], writes=[ad.tok])
                    kb.op("dve", lambda e: e.tensor_scalar(out=ldc[:], in0=ldc[:], scalar1=-0.6065306597126334, scalar2=None, op0=ALU.mult), reads=[ldc.tok], writes=[ldc.tok])
                    kb.op("dve", lambda e: e.tensor_tensor_scan(out=cum[:], data0=cmk[:], data1=R(ldc[:]), initial=0.0, op0=ALU.mult, op1=ALU.add),
                          reads=[cmk.tok, ldc.tok], writes=[cum.tok])
                    o_rt = outr.next()
                    kb.op("act", lambda e: e.activation(out=t1[:], in_=cum[:], func=AF.Exp), reads=[cum.tok], writes=[t1.tok])
                    kb.op("dve", lambda e: e.tensor_tensor(out=o_rt[:], in0=t1[:], in1=R(rr[:]), op=ALU.mult), reads=[t1.tok, rr.tok], writes=[o_rt.tok])
                    kb.dma("sp", self.rwp[d, 3, j * P:(j + 1) * P, :].bitcast(F32), o_rt[:], reads=[o_rt.tok], writes=[self.dt("rwp", d, 3, j)])
                    kb.op("dve", lambda e: e.tensor_copy(out=gct[:], in_=t1[:, 63::64]), reads=[t1.tok], writes=[gct.tok])
                    kb.dma("sp", self.rwgc[d, j * P:(j + 1) * P, :], gct[:], reads=[gct.tok], writes=[self.dt("rwgc", d, j)])
                    o_at = outr.next()
                    kb.op("dve", lambda e: e.tensor_tensor(out=t2[:], in0=cum[:], in1=R(ldc[:]), op=ALU.subtract), reads=[cum.tok, ldc.tok], writes=[t2.tok])
                    kb.op("act", lambda e: e.activation(out=t2[:], in_=t2[:], func=AF.Exp), reads=[t2.tok], writes=[t2.tok])
                    kb.op("dve", lambda e: e.scalar_tensor_tensor(out=o_at[:], in0=t2[:], scalar=-1.0, in1=R(kkn[:]), op0=ALU.mult, op1=ALU.mult),
                          reads=[t2.tok, kkn.tok], writes=[o_at.tok])
                    kb.dma("sp", self.rwp[d, 0, j * P:(j + 1) * P, :].bitcast(F32), o_at[:], reads=[o_at.tok], writes=[self.dt("rwp", d, 0, j)])
                    kb.op("act", lambda e: e.activation(out=t1[:], in_=cum[:], func=AF.Exp, scale=-1.0), reads=[cum.tok], writes=[t1.tok])
                    o_bt = outr.next()
                    kb.op("dve", lambda e: e.tensor_tensor(out=t2[:], in0=R(kkn[:]), in1=R(ad[:]), op=ALU.mult), reads=[kkn.tok, ad.tok], writes=[t2.tok])
                    kb.op("dve", lambda e: e.tensor_tensor(out=o_bt[:], in0=t2[:], in1=t1[:], op=ALU.mult), reads=[t2.tok, t1.tok], writes=[o_bt.tok])
                    kb.dma("sp", self.rwp[d, 1, j * P:(j + 1) * P, :].bitcast(F32), o_bt[:], reads=[o_bt.tok], writes=[self.dt("rwp", d, 1, j)])
                    o_kt = outr.next()
                    kb.op("dve", lambda e: e.tensor_scalar(out=t2[:], in0=R(ad[:]), scalar1=col[:, 1, j:j + 1], scalar2=oka[:, j:j + 1], op0=ALU.mult, op1=ALU.add),
                          reads=[ad.tok, col.tok, oka.tok], writes=[t2.tok])
                    kb.op("dve", lambda e: e.tensor_tensor(out=t2[:], in0=t2[:], in1=R(kk_[:]), op=ALU.mult), reads=[t2.tok, kk_.tok], writes=[t2.tok])
                    kb.op("dve", lambda e: e.tensor_tensor(out=o_kt[:], in0=t2[:], in1=t1[:], op=ALU.mult), reads=[t2.tok, t1.tok], writes=[o_kt.tok])
                    kb.dma("sp", self.rwp[d, 2, j * P:(j + 1) * P, :].bitcast(F32), o_kt[:], reads=[o_kt.tok], writes=[self.dt("rwp", d, 2, j)])
                    o_v = outr.next()
                    kb.op("act", lambda e: e.activation(out=o_v[:], in_=R(vv_[:]), func=AF.Copy), reads=[vv_.tok], writes=[o_v.tok])
                    kb.dma("sp", self.rwp[d, 4, j * P:(j + 1) * P, :].bitcast(F32), o_v[:], reads=[o_v.tok], writes=[self.dt("rwp", d, 4, j)])
            kb.barrier()
        with ExitStack() as st:
            def tl(shape, name, dtype=F32):
                return Tile(kb, st, shape, dtype, name=name)
            H = 64
            trm = tl([H, 3, 64], "rwtri")
            for i in range(3):
                kb.dma("sp", trm[:, i, :], self.rwtri[i][0:64, :], writes=[trm.tok])
            kcol = tl([P, 1], "rwkcol"); kb.dma("sp", kcol[:], self.ssdkeep[:, :], writes=[kcol.tok])
            Sst = tl([H, 16, 64], "rwS", F32R)
            gcs = tl([H, 16, NCH], "rwgcs")
            slab = [Rot(kb, st, 2, [H, 16, 64], F32R, name="rwsl%d" % q) for q in range(5)]
            tk = [Rot(kb, st, 2, [H, 16, 64], F32R, name="rwtk%d" % q) for q in range(3)]
            Am = [Rot(kb, st, 2, [H, 16, 64], F32R, name="rwA%d" % q) for q in range(3)]
            Ak = Rot(kb, st, 2, [H, 16, 64], F32R, name="rwAk")
            AkT = Rot(kb, st, 2, [H, 16, 64], F32R, name="rwAkT")
            Tm = Rot(kb, st, 2, [H, 16, 64], F32R, name="rwTm")
            Wt = Rot(kb, st, 2, [H, 16, 64], F32R, name="rwWt")
            Ut = Rot(kb, st, 2, [H, 16, 64], F32R, name="rwUt")
            ost = Rot(kb, st, 2, [H, 16, 64], F32, name="rwost")

            def bc16(ap2):
                return ap2.unsqueeze(1).to_broadcast([H, 16, 64])

            def mm16(psb, ptk, lhs_fn, rhs_fn, reads, first=True, last=True):
                for b_ in range(16):
                    kb.op("pe", lambda e: e.matmul(psb[0:H, b_ * 64:(b_ + 1) * 64], lhs_fn(b_), rhs_fn(b_), start=first, stop=last), reads=reads, writes=ptk)

            def mm16g(psb, ptk, terms):
                n = len(terms)
                for b_ in range(16):
                    for ti, (lt, rt__) in enumerate(terms):
                        kb.op("pe", lambda e: e.matmul(psb[0:H, b_ * 64:(b_ + 1) * 64], lt[:, b_, :], rt__[:, b_, :], start=(ti == 0), stop=(ti == n - 1)),
                              reads=[lt.tok, rt__.tok], writes=ptk)

            def v3(pb):
                return pb[0:H, :].rearrange("p (b t) -> p b t", t=64)

            def dview(ap2d):
                return ap2d.rearrange("(b k) x -> k b x", k=64)

            for d in range(2):
                kb.dma("pool", Sst[:], self.rws0[l, d], writes=[Sst.tok])
                kb.dma("sp", gcs[:], dview(self.rwgc[d]), reads=[self.dt("rwgc", d, 0)], writes=[gcs.tok])
                for c in range(NCH):
                    cs = slice(c * 64, (c + 1) * 64)
                    sl = [r_.next() for r_ in slab]
                    for q in range(5):
                        kb.dma("pool", sl[q][:], dview(self.rwp[d, q])[:, :, cs], writes=[sl[q].tok])
                    at_, bt_, kt_, rt_, vs_ = sl
                    tks = []
                    for q, src in enumerate((bt_, kt_, vs_)):
                        pb, pt = self.bank2()
                        for b_ in range(16):
                            kb.op("pe", lambda e: e.transpose(pb[0:H, b_ * 64:(b_ + 1) * 64], src[:, b_, :].bitcast(F32), self.ident[0:H, 0:H]), reads=[src.tok, self.ident.tok], writes=pt)
                        t_ = tk[q].next()
                        if q % 2:
                            kb.op("act", lambda e: e.activation(out=t_[:], in_=v3(pb), func=AF.Copy), reads=pt, writes=[t_.tok])
                        else:
                            kb.op("dve", lambda e: e.tensor_copy(out=t_[:], in_=v3(pb)), reads=pt, writes=[t_.tok])
                        tks.append(t_)
                    Btk, Ktk, Vtk = tks

                    def amat(lhs, rhs, mask_i, dst):
                        pb, pt = self.bank2()
                        mm16(pb, pt, lambda b_: lhs[:, b_, :], lambda b_: rhs[:, b_, :], [lhs.tok, rhs.tok])
                        kb.op("dve", lambda e: e.tensor_tensor(out=dst[:], in0=v3(pb), in1=bc16(trm[:, mask_i, :]), op=ALU.mult), reads=pt + [trm.tok], writes=[dst.tok])
                    A0, A0T = Ak.next(), AkT.next()
                    amat(bt_, at_, 0, A0)
                    amat(at_, bt_, 1, A0T)
                    Aak, Arb, Ark = Am[0].next(), Am[1].next(), Am[2].next()
                    amat(kt_, at_, 0, Aak)
                    amat(bt_, rt_, 2, Arb)
                    amat(kt_, rt_, 2, Ark)
                    Tc = Tm.next()
                    kb.op("dve", lambda e: e.tensor_tensor(out=Tc[:], in0=A0.f32(slice(None)), in1=bc16(self.ident[0:H, 0:H]), op=ALU.add), reads=[A0.tok, self.ident.tok], writes=[Tc.tok])
                    Ap, ApT = A0, A0T
                    for lev in range(1, 6):
                        An, AnT = Ak.next(), AkT.next()
                        pb, pt = self.bank2()
                        mm16(pb, pt, lambda b_: ApT[:, b_, :], lambda b_: Ap[:, b_, :], [Ap.tok, ApT.tok])
                        pb2, pt2 = self.bank2()
                        mm16(pb2, pt2, lambda b_: Ap[:, b_, :], lambda b_: ApT[:, b_, :], [Ap.tok, ApT.tok])
                        kb.op("dve", lambda e: e.tensor_copy(out=An[:], in_=v3(pb)), reads=pt, writes=[An.tok])
                        kb.op("act", lambda e: e.activation(out=AnT[:], in_=v3(pb2), func=AF.Copy), reads=pt2, writes=[AnT.tok])
                        pb3, pt3 = self.bank2()
                        mm16(pb3, pt3, lambda b_: AnT[:, b_, :], lambda b_: Tc[:, b_, :], [AnT.tok, Tc.tok])
                        Tn = Tm.next()
                        kb.op("dve", lambda e: e.tensor_tensor(out=Tn[:], in0=v3(pb3), in1=Tc.f32(slice(None)), op=ALU.add), reads=pt3 + [Tc.tok], writes=[Tn.tok])
                        Tc, Ap, ApT = Tn, An, AnT
                    pbw, ptw = self.bank2()
                    mm16g(pbw, ptw, [(at_, Sst), (Aak, Vtk)])
                    W_ = Wt.next()
                    kb.op("act", lambda e: e.activation(out=W_[:], in_=v3(pbw), func=AF.Copy), reads=ptw, writes=[W_.tok])
                    pbu, ptu = self.bank2()
                    mm16(pbu, ptu, lambda b_: Tc[:, b_, :], lambda b_: W_[:, b_, :], [Tc.tok, W_.tok])
                    U_ = Ut.next()
                    kb.op("dve", lambda e: e.tensor_copy(out=U_[:], in_=v3(pbu)), reads=ptu, writes=[U_.tok])
                    pbo, pto = self.bank2()
                    mm16g(pbo, pto, [(Sst, rt_), (U_, Arb), (Vtk, Ark)])
                    pbs, pts = self.bank2()
                    mm16g(pbs, pts, [(Btk, U_), (Ktk, Vtk)])
                    o_ = ost.next()
                    if d == 0:
                        kb.op("act", lambda e: e.activation(out=o_[:], in_=v3(pbo), func=AF.Copy), reads=pto, writes=[o_.tok])
                        tcs = cs
                    else:
                        kb.op("dve", lambda e: e.tensor_copy(out=o_[:, :, ::-1], in_=v3(pbo)), reads=pto, writes=[o_.tok])
                        tcs = slice(T - (c + 1) * 64, T - c * 64)
                    kb.dma("sp", dview(self.rwoT[d])[:, :, tcs], o_[:], reads=[o_.tok], writes=[self.dt("rwoT", d, c)])
                    kb.op("dve", lambda e: e.tensor_tensor(out=Sst[:], in0=v3(pbs), in1=Sst.f32(slice(None)), op=ALU.add), reads=pts + [Sst.tok], writes=[Sst.tok])
                    kb.op("dve", lambda e: e.tensor_tensor(out=Sst[:], in0=Sst.f32(slice(None)), in1=gcs[:, :, c:c + 1].to_broadcast([H, 16, 64]), op=ALU.mult),
                          reads=[Sst.tok, gcs.tok], writes=[Sst.tok])
                    if c % 4 == 3:
                        seg = c // 4
                        kb.dma("sp", self.rwo[l, d, seg], Sst.f32(slice(None)), reads=[Sst.tok], writes=[self.dt("rwo", l, d, seg)])
                        kb.op("dve", lambda e: e.tensor_scalar(out=Sst[:], in0=Sst.f32(slice(None)), scalar1=kcol[0:H, 0:1], scalar2=None, op0=ALU.mult), reads=[Sst.tok, kcol.tok], writes=[Sst.tok])
            kb.barrier()
        with ExitStack() as st:
            def tl(shape, name, dtype=F32):
                return Tile(kb, st, shape, dtype, name=name)
            hb64 = tl([P, P], "hb64"); kb.dma("sp", hb64[:], self.hblk[1], writes=[hb64.tok])
            col = tl([P, 5, 8], "rwcol2"); kb.dma("sp", col[:], self.rwcol[l], writes=[col.tok])
            g2 = tl([P, RW], "rwg2"); kb.dma("sp", g2[:], self.rwg2[l], writes=[g2.tok])
            sgl = tl([P, T], "rwsgl")
            kb.dma("sp", sgl[:], self.zT[(cfg.ZA + 25) * P:(cfg.ZA + 26) * P, :], writes=[sgl.tok])
            sm = tl([P, 4, TT], "rwsm2"); kb.dma("sp", sm[:], self.rwsm[:, :, :], writes=[sm.tok])
            mu = tl([P, 26], "rwmu2"); kb.dma("sp", mu[:], self.rwmu[l], writes=[mu.tok])
            om = tl([P, 1], "rwom2")
            kb.op("dve", lambda e: e.tensor_scalar(out=om[:], in0=mu[:, 25:26], scalar1=-1.0, scalar2=1.0, op0=ALU.mult, op1=ALU.add), reads=[mu.tok], writes=[om.tok])
            sacc = tl([P, T], "rwsacc2")
            pt_ = Rot(kb, st, 8, [P, TT], F32, name="rwpt")
            kb.op("dve", lambda e: e.memset(sacc[:], 0.0), writes=[sacc.tok])
            for oi, o in enumerate((-1, 1, -64, 64)):
                for tg in range(NT):
                    lo, hi = tg * TT, (tg + 1) * TT
                    slo, shi = max(lo + o, 0), min(hi + o, T)
                    dlo, dhi = slo - o, shi - o
                    n_ = dhi - dlo
                    t = pt_.next()
                    kb.op("dve", lambda e: e.tensor_tensor(out=t[:, 0:n_], in0=sgl[:, slo:shi], in1=sm[:, oi, dlo - lo:dhi - lo], op=ALU.mult), reads=[sgl.tok, sm.tok], writes=[t.tok])
                    kb.op("dve", lambda e: e.tensor_tensor(out=sacc[:, dlo:dhi], in0=sacc[:, dlo:dhi], in1=t[:, 0:n_], op=ALU.add), reads=[sacc.tok, t.tok], writes=[sacc.tok])
            kb.op("dve", lambda e: e.tensor_scalar(out=sgl[:], in0=sgl[:], scalar1=om[:, 0:1], scalar2=None, op0=ALU.mult), reads=[sgl.tok, om.tok], writes=[sgl.tok])
            kb.op("dve", lambda e: e.scalar_tensor_tensor(out=sgl[:], in0=sacc[:], scalar=mu[:, 25:26], in1=sgl[:], op0=ALU.mult, op1=ALU.add), reads=[sacc.tok, mu.tok, sgl.tok], writes=[sgl.tok])
            kb.op("act", lambda e: e.activation(out=sgl[:], in_=sgl[:], func=AF.Sigmoid), reads=[sgl.tok], writes=[sgl.tok])
            lnb = tl([P, 1], "rwlneps")
            kb.op("dve", lambda e: e.memset(lnb[:], 64e-5), writes=[lnb.tok])
            for j in range(8):
                for tg in range(NT):
                    tc_ = slice(tg * TT, (tg + 1) * TT)
                    of_, ob_ = pt_.next(), pt_.next()
                    kb.dma("sp", of_[:], self.rwoT[0, j * P:(j + 1) * P, tc_], writes=[of_.tok])
                    kb.dma("sp", ob_[:], self.rwoT[1, j * P:(j + 1) * P, tc_], writes=[ob_.tok])
                    kb.op("dve", lambda e: e.tensor_tensor(out=of_[:], in0=of_[:], in1=ob_[:], op=ALU.add), reads=[of_.tok, ob_.tok], writes=[of_.tok])
                    pb, pt = self.bank()
                    kb.op("pe", lambda e: e.matmul(pb, hb64[:], of_[:], start=True, stop=True), reads=[hb64.tok, of_.tok], writes=[pt])
                    oc = pt_.next()
                    kb.op("dve", lambda e: e.tensor_tensor(out=oc[:], in0=of_[:], in1=pb, op=ALU.subtract), reads=[of_.tok, pt], writes=[oc.tok])
                    sq = pt_.next()
                    kb.op("act", lambda e: e.activation(out=sq[:], in_=oc[:], func=AF.Square), reads=[oc.tok], writes=[sq.tok])
                    pb2, pt2 = self.bank()
                    kb.op("pe", lambda e: e.matmul(pb2, hb64[:], sq[:], start=True, stop=True), reads=[hb64.tok, sq.tok], writes=[pt2])
                    kb.op("act", lambda e: e.activation(out=sq[:], in_=pb2, func=AF.Sqrt, bias=lnb[:, 0:1], scale=1.0), reads=[pt2, lnb.tok], writes=[sq.tok])
                    kb.op("dve", lambda e: e.reciprocal(out=sq[:], in_=sq[:]), reads=[sq.tok], writes=[sq.tok])
                    kb.op("dve", lambda e: e.scalar_tensor_tensor(out=oc[:], in0=oc[:], scalar=col[:, 3, j:j + 1], in1=sq[:], op0=ALU.mult, op1=ALU.mult), reads=[oc.tok, col.tok, sq.tok], writes=[oc.tok])
                    bn = pt_.next()
                    kb.dma("sp", bn[:], self.rwbon[j * P:(j + 1) * P, tc_], reads=[self.dt("rwbon", j)], writes=[bn.tok])
                    kb.op("dve", lambda e: e.scalar_tensor_tensor(out=oc[:], in0=oc[:], scalar=col[:, 4, j:j + 1], in1=bn[:], op0=ALU.add, op1=ALU.add), reads=[oc.tok, col.tok, bn.tok], writes=[oc.tok])
                    pb3, pt3 = self.bank()
                    kb.op("pe", lambda e: e.matmul(pb3, g2[:, j * P:(j + 1) * P], sgl[:, tc_], start=True, stop=True), reads=[g2.tok, sgl.tok], writes=[pt3])
                    kb.op("dve", lambda e: e.tensor_tensor(out=oc[:], in0=pb3, in1=oc[:], op=ALU.mult), reads=[pt3, oc.tok], writes=[oc.tok])
                    kb.dma("sp", self.ya.bitcast(F32)[j * P:(j + 1) * P, tc_], oc[:], reads=[oc.tok], writes=[self.dt("ya", j, tg)])
            kb.barrier()

    def phase_ssd(self, l):
        cfg, kb = self.cfg, self.kb
        T, NT, NSEG = cfg.T, cfg.NT, cfg.NSEG
        NC = T // P
        with ExitStack() as st:
            cm = Tile(kb, st, [P, 4, TT], F32, name="cm")
            kb.dma("sp", cm[:], self.cmT[:, :, :], writes=[cm.tok])
            cw = Tile(kb, st, [P, 12, 6], F32, name="cw")
            kb.dma("sp", cw[:], self.ssdcw[l], writes=[cw.tok])
            xin = Rot(kb, st, 2, [P, T], F32, name="xin")
            acc = Rot(kb, st, 2, [P, T], F32, name="cacc")
            tmp = Rot(kb, st, 3, [P, TT], F32, name="ctmp")
            for ch in range(12):
                x = xin.next()
                a = acc.next()
                kb.dma("sp", x[:], self.zT[(cfg.ZB + 8 + ch) * P:(cfg.ZB + 9 + ch) * P, :], writes=[x.tok])
                kb.op("dve", lambda e: e.tensor_scalar(out=a[:], in0=x[:], scalar1=cw[:, ch, 2:3], scalar2=cw[:, ch, 5:6], op0=ALU.mult, op1=ALU.add),
                      reads=[x.tok, cw.tok], writes=[a.tok])
                for oi, o in enumerate((-2, -1, 1, 2)):
                    for tg in range(NT):
                        lo, hi = tg * TT, (tg + 1) * TT
                        slo, shi = max(lo + o, 0), min(hi + o, T)
                        dlo, dhi = slo - o, shi - o
                        t = tmp.next()
                        n_ = dhi - dlo
                        kb.op("dve", lambda e: e.tensor_tensor(out=t[:, 0:n_], in0=x[:, slo:shi], in1=cm[:, oi, dlo - lo:dhi - lo], op=ALU.mult),
                              reads=[x.tok, cm.tok], writes=[t.tok])
                        kb.op("dve", lambda e: e.scalar_tensor_tensor(out=a[:, dlo:dhi], in0=t[:, 0:n_], scalar=cw[:, ch, (o + 2):(o + 3)], in1=a[:, dlo:dhi],
                                                                    op0=ALU.mult, op1=ALU.add), reads=[t.tok, a.tok, cw.tok], writes=[a.tok])
                kb.op("act", lambda e: e.activation(out=a[:], in_=a[:], func=AF.Silu), reads=[a.tok], writes=[a.tok])
                kb.dma("sp", self.xcs[ch * P:(ch + 1) * P, :], a[:], reads=[a.tok], writes=[self.dt("xcs", ch)])
            kb.barrier()
        with ExitStack() as st:
            def tl(shape, name, dtype=F32):
                return Tile(kb, st, shape, dtype, name=name)
            yacc = tl([P, 8, T], "ssdy")
            kb.op("dve", lambda e: e.memset(yacc[:], 0.0), writes=[yacc.tok])
            tri = tl([P, 2, P], "tri")
            for d in range(2):
                kb.dma("sp", tri[:, d, :], self.tri[d], writes=[tri.tok])
            dd_T = tl([64, T], "ddT")
            col = tl([64, 3], "ssdcol")
            kb.dma("sp", col[:], self.ssdcol[l], writes=[col.tok])
            for r in range(4):
                kb.dma("sp", dd_T[r * 16:(r + 1) * 16, :], self.zT[cfg.ZDT * P:cfg.ZDT * P + 16, :], writes=[dd_T.tok])
            mcol = tl([64, 1], "mcol")
            kb.op("act", lambda e: e.activation(out=mcol[:], in_=col[:, 1:2], func=AF.Exp), reads=[col.tok], writes=[mcol.tok])
            kb.op("dve", lambda e: e.tensor_tensor(out=mcol[:], in0=mcol[:], in1=col[:, 2:3], op=ALU.mult), reads=[mcol.tok, col.tok], writes=[mcol.tok])
            kb.op("act", lambda e: e.activation(out=dd_T[:], in_=dd_T[:], func=AF.Exp, bias=col[:, 0:1], scale=1.0), reads=[dd_T.tok, col.tok], writes=[dd_T.tok])
            kb.op("dve", lambda e: e.tensor_scalar(out=dd_T[:], in0=dd_T[:], scalar1=1.0, scalar2=None, op0=ALU.add), reads=[dd_T.tok], writes=[dd_T.tok])
            kb.op("act", lambda e: e.activation(out=dd_T[:], in_=dd_T[:], func=AF.Ln), reads=[dd_T.tok], writes=[dd_T.tok])
            kb.op("dve", lambda e: e.tensor_scalar(out=dd_T[:], in0=dd_T[:], scalar1=mcol[:, 0:1], scalar2=None, op0=ALU.mult), reads=[dd_T.tok, mcol.tok], writes=[dd_T.tok])
            kcol = tl([P, 1], "kcol")
            kb.dma("sp", kcol[:], self.ssdkeep[:, :], writes=[kcol.tok])
            S = [tl([P, 512], "ssdS%d" % g) for g in range(2)]
            xsl = Rot(kb, st, 2, [P, 8, P], F32, name="xsl")
            bcl = Rot(kb, st, 2, [P, 4, P], F32, name="bcl")
            xtok = Rot(kb, st, 2, [P, 16, 64], F32, name="xtok")
            btok = Rot(kb, st, 2, [P, 2, P], F32, name="btok")
            ddk = Rot(kb, st, 2, [P, 64], F32, name="ddk")
            sm = Rot(kb, st, 6, [P, 16], F32, name="ssm")
            gm = Rot(kb, st, 2, [P, 2, P], F32, name="gm")
            Mt = Rot(kb, st, 2, [P, 16, P], F32, name="Mt")
            xdt = Rot(kb, st, 2, [P, 16, 64], F32, name="xdt")
            xw = Rot(kb, st, 2, [P, 16, 64], F32, name="xw")
            ecr = Rot(kb, st, 3, [P, P], F32, name="ecr")
            yt = Rot(kb, st, 3, [P, P], F32, name="yt")
            for d in range(2):
                trd = tri[:, d, :]
                for g in range(2):
                    kb.dma("sp", S[g][:], self.ssds0[l, d, g], writes=[S[g].tok])
                order = range(NC) if d == 0 else range(NC - 1, -1, -1)
                for c in order:
                    cols = slice(c * P, (c + 1) * P)
                    xs_, bc_ = xsl.next(), bcl.next()
                    kb.dma("sp", xs_[:], self.xcs[0:1024, :].rearrange("(j p) t -> p j t", p=P)[:, :, cols], reads=[self.dt("xcs", 0)], writes=[xs_.tok])
                    kb.dma("sp", bc_[:], self.xcs[1024:1536, :].rearrange("(j p) t -> p j t", p=P)[:, :, cols], writes=[bc_.tok])
                    xt_, bt_, dk = xtok.next(), btok.next(), ddk.next()
                    for hb_ in range(2):
                        pb2, pt2 = self.bank2()
                        for jj in range(4):
                            j = hb_ * 4 + jj
                            kb.op("pe", lambda e: e.transpose(pb2[:, jj * P:(jj + 1) * P], xs_[:, j, :], self.ident[:]), reads=[xs_.tok, self.ident.tok], writes=[pt2[0], pt2[1]])
                        kb.op("act", lambda e: e.activation(out=xt_[:, hb_ * 8:(hb_ + 1) * 8, :], in_=pb2[:, 0:512].rearrange("p (h q) -> p h q", q=64), func=AF.Copy),
                              reads=[pt2[0], pt2[1]], writes=[xt_.tok])
                    pb, pt = self.bank()
                    for g in range(2):
                        kb.op("pe", lambda e: e.transpose(pb[:, g * P:(g + 1) * P], bc_[:, g, :], self.ident[:]), reads=[bc_.tok, self.ident.tok], writes=[pt])
                    kb.op("pe", lambda e: e.transpose(pb[:, 256:320], dd_T[:, cols], self.ident[0:64, 0:64]), reads=[dd_T.tok, self.ident.tok], writes=[pt])
                    kb.op("dve", lambda e: e.tensor_copy(out=bt_[:], in_=pb[:, 0:256].rearrange("p (g n) -> p g n", n=P)), reads=[pt], writes=[bt_.tok])
                    kb.op("dve", lambda e: e.tensor_copy(out=dk[:], in_=pb[:, 256:320]), reads=[pt], writes=[dk.tok])
                    dtc = dk[:, 32 * d:32 * d + 16]
                    dac = dk[:, 32 * d + 16:32 * d + 32]
                    pbc, ptc = self.bank()
                    kb.op("pe", lambda e: e.matmul(pbc[:, 0:16], trd, dac, start=True, stop=True), reads=[tri.tok, dk.tok], writes=[ptc])
                    kb.op("pe", lambda e: e.matmul(pbc[:, 16:32], self.ones[:], dac, start=True, stop=True), reads=[self.ones.tok, dk.tok], writes=[ptc])
                    cumc, wts, edec = sm.next(), sm.next(), sm.next()
                    kb.op("dve", lambda e: e.tensor_copy(out=cumc[:], in_=pbc[:, 0:16]), reads=[ptc], writes=[cumc.tok])
                    kb.op("dve", lambda e: e.tensor_tensor(out=wts[:], in0=pbc[:, 16:32], in1=cumc[:], op=ALU.subtract), reads=[ptc, cumc.tok], writes=[wts.tok])
                    kb.op("act", lambda e: e.activation(out=wts[:], in_=wts[:], func=AF.Exp), reads=[wts.tok], writes=[wts.tok])
                    kb.op("pool", lambda e: e.tensor_tensor(out=wts[:], in0=wts[:], in1=dtc, op=ALU.mult), reads=[wts.tok, dk.tok], writes=[wts.tok])
                    kb.op("act", lambda e: e.activation(out=edec[:], in_=pbc[:, 16:32], func=AF.Exp), reads=[ptc], writes=[edec.tok])
                    pbg, ptg = self.bank()
                    for g in range(2):
                        kb.op("pe", lambda e: e.matmul(pbg[:, g * P:(g + 1) * P], bc_[:, g, :], bc_[:, 2 + g, :], start=True, stop=True), reads=[bc_.tok], writes=[ptg])
                    gm_ = gm.next()
                    kb.op("dve", lambda e: e.tensor_tensor(out=gm_[:], in0=pbg[:, 0:256].rearrange("p (g t) -> p g t", t=P),
                                                           in1=tri[:, d:d + 1, :].to_broadcast([P, 2, P]), op=ALU.mult), reads=[ptg, tri.tok], writes=[gm_.tok])
                    M_ = Mt.next()
                    for hq in range(2):
                        pb2, pt2 = self.bank2()
                        for hh in range(8):
                            h = hq * 8 + hh
                            kb.op("pe", lambda e: e.matmul(pb2[:, hh * P:(hh + 1) * P], dac[:, h:h + 1].to_broadcast([P, P]), trd, start=True, stop=True),
                                  reads=[dk.tok, tri.tok], writes=[pt2[0], pt2[1]])
                        for hh in range(8):
                            h = hq * 8 + hh
                            kb.op("dve", lambda e: e.tensor_scalar(out=M_[:, h, :], in0=pb2[:, hh * P:(hh + 1) * P], scalar1=cumc[:, h:h + 1], scalar2=0.0,
                                                                   op0=ALU.subtract, op1=ALU.min), reads=[pt2[0], pt2[1], cumc.tok], writes=[M_.tok])
                    kb.op("act", lambda e: e.activation(out=M_[:], in_=M_[:], func=AF.Exp), reads=[M_.tok], writes=[M_.tok])
                    for g in range(2):
                        kb.op("pool", lambda e: e.tensor_tensor(out=M_[:, 8 * g:8 * g + 8, :], in0=M_[:, 8 * g:8 * g + 8, :],
                                                               in1=gm_[:, g:g + 1, :].to_broadcast([P, 8, P]), op=ALU.mult), reads=[M_.tok, gm_.tok], writes=[M_.tok])
                    xd_, xw_ = xdt.next(), xw.next()
                    kb.op("pool", lambda e: e.tensor_tensor(out=xd_[:], in0=xt_[:], in1=dtc.unsqueeze(2).to_broadcast([P, 16, 64]), op=ALU.mult),
                          reads=[xt_.tok, dk.tok], writes=[xd_.tok])
                    kb.op("pool", lambda e: e.tensor_tensor(out=xw_[:], in0=xt_[:], in1=wts[:].unsqueeze(2).to_broadcast([P, 16, 64]), op=ALU.mult),
                          reads=[xt_.tok, wts.tok], writes=[xw_.tok])
                    for j in range(8):
                        g = j // 4
                        pb, pt = self.bank()
                        for h2 in range(2):
                            kb.op("pe", lambda e: e.matmul(pb[h2 * 64:(h2 + 1) * 64, 0:P], dac[:, 2 * j + h2:2 * j + h2 + 1].to_broadcast([P, 64]), trd, start=True, stop=True),
                                  reads=[dk.tok, tri.tok], writes=[pt])
                        kb.op("pe", lambda e: e.matmul(pb[:, P:2 * P], S[g][:, (j % 4) * P:(j % 4 + 1) * P], bc_[:, 2 + g, :], start=True, stop=True),
                              reads=[S[g].tok, bc_.tok], writes=[pt])
                        for h2 in range(2):
                            kb.op("pe", lambda e: e.matmul(pb[h2 * 64:(h2 + 1) * 64, 2 * P:3 * P], xd_[:, 2 * j + h2, :], M_[:, 2 * j + h2, :], start=True, stop=True),
                                  reads=[xd_.tok, M_.tok], writes=[pt])
                        ec = ecr.next()
                        kb.op("act", lambda e: e.activation(out=ec[:], in_=pb[:, 0:P], func=AF.Exp), reads=[pt], writes=[ec.tok])
                        y_ = yt.next()
                        kb.op("dve", lambda e: e.tensor_tensor(out=y_[:], in0=pb[:, P:2 * P], in1=ec[:], op=ALU.mult), reads=[pt, ec.tok], writes=[y_.tok])
                        kb.op("dve", lambda e: e.tensor_tensor(out=y_[:], in0=pb[:, 2 * P:3 * P], in1=y_[:], op=ALU.add), reads=[pt, y_.tok], writes=[y_.tok])
                        kb.op("pool", lambda e: e.tensor_tensor(out=yacc[:, j, cols], in0=yacc[:, j, cols], in1=y_[:], op=ALU.add), reads=[yacc.tok, y_.tok], writes=[yacc.tok])
                    seg_end = (c % 2 == 1) if d == 0 else (c % 2 == 0)
                    for g in range(2):
                        pb, pt = self.bank()
                        kb.op("pe", lambda e: e.matmul(pb, bt_[:, g, :], xw_[:, 8 * g:8 * g + 8, :], start=True, stop=True), reads=[bt_.tok, xw_.tok], writes=[pt])
                        kb.op("pool", lambda e: e.tensor_tensor(out=S[g][:].rearrange("p (h q) -> p h q", q=64), in0=S[g][:].rearrange("p (h q) -> p h q", q=64),
                                                               in1=edec[:, 8 * g:8 * g + 8].unsqueeze(2).to_broadcast([P, 8, 64]), op=ALU.mult),
                              reads=[S[g].tok, edec.tok], writes=[S[g].tok])
                        kb.op("dve", lambda e: e.tensor_tensor(out=S[g][:], in0=pb, in1=S[g][:], op=ALU.add), reads=[pt, S[g].tok], writes=[S[g].tok])
                        if seg_end:
                            seg = c // 2
                            kb.dma("sp", self.ssdo[l, d, seg, g], S[g][:], reads=[S[g].tok], writes=[self.dt("ssdo", l, d, seg, g)])
                            kb.op("dve", lambda e: e.tensor_scalar(out=S[g][:], in0=S[g][:], scalar1=kcol[:, 0:1], scalar2=None, op0=ALU.mult),
                                  reads=[S[g].tok, kcol.tok], writes=[S[g].tok])
            Dc = tl([P, 8], "ssdDc")
            gc = tl([P, 8], "ssdgc")
            kb.dma("sp", Dc[:], self.ssdD[l], writes=[Dc.tok])
            kb.dma("sp", gc[:], self.ssdg[l], writes=[gc.tok])
            ld = Rot(kb, st, 4, [P, TT], F32, name="sld")
            rs = tl([P, TT], "srs")
            for tg in range(NT):
                tc_ = slice(tg * TT, (tg + 1) * TT)
                pbn, ptn = self.bank()
                for j in range(8):
                    xj, zj = ld.next(), ld.next()
                    kb.dma("sp", xj[:], self.xcs[j * P:(j + 1) * P, tc_], writes=[xj.tok])
                    kb.dma("sp", zj[:], self.zT[(cfg.ZB + j) * P:(cfg.ZB + j + 1) * P, tc_], writes=[zj.tok])
                    kb.op("dve", lambda e: e.scalar_tensor_tensor(out=xj[:], in0=xj[:], scalar=Dc[:, j:j + 1], in1=yacc[:, j, tc_], op0=ALU.mult, op1=ALU.add),
                          reads=[xj.tok, Dc.tok, yacc.tok], writes=[xj.tok])
                    kb.op("dve", lambda e: e.tensor_tensor(out=yacc[:, j, tc_], in0=xj[:], in1=zj[:], op=ALU.mult), reads=[xj.tok, zj.tok], writes=[yacc.tok])
                    kb.op("act", lambda e: e.activation(out=zj[:], in_=yacc[:, j, tc_], func=AF.Square), reads=[yacc.tok], writes=[zj.tok])
                    kb.op("pe", lambda e: e.matmul(pbn, self.ones[:], zj[:], start=(j == 0), stop=(j == 7)), reads=[zj.tok, self.ones.tok], writes=[ptn])
                t = ld.next()
                kb.op("act", lambda e: e.activation(out=t[:], in_=pbn, func=AF.Sqrt, bias=self.epsc[:, 0:1], scale=1.0 / SW), reads=[ptn, self.epsc.tok], writes=[t.tok])
                kb.op("dve", lambda e: e.reciprocal(out=rs[:], in_=t[:]), reads=[t.tok], writes=[rs.tok])
                for j in range(8):
                    o = ld.next()
                    kb.op("dve", lambda e: e.scalar_tensor_tensor(out=o[:], in0=yacc[:, j, tc_], scalar=gc[:, j:j + 1], in1=rs[:], op0=ALU.mult, op1=ALU.mult),
                          reads=[yacc.tok, gc.tok, rs.tok], writes=[o.tok])
                    kb.dma("sp", self.yb.bitcast(F32)[j * P:(j + 1) * P, tc_], o[:], reads=[o.tok], writes=[self.dt("yb", j, tg)])
            kb.barrier()


def prep_common(cfg, inp):
    NK, NJ, DEPTH = cfg.NK, cfg.NJ, cfg.DEPTH
    d = {}
    d["wmodn"] = inp["w_mod"]
    d["bmod"] = np.stack([colv(inp["b_mod"][l]) for l in range(DEPTH)])
    d["normg"] = np.stack([np.concatenate([colv(inp["norm_g"][l, i]) for i in range(3)], axis=1) for l in range(DEPTH)])
    d["fng"] = colv(inp["final_norm_g"])
    d["w1"] = np.stack([np.stack([blk(inp["ffn_w_in"][l, w]) for w in range(2)]) for l in range(DEPTH)])
    d["w2"] = np.stack([np.stack([blk(inp["ffn_w_out"][l, w]) for w in range(2)]) for l in range(DEPTH)])
    return d


def gp_layout(a):
    g, p = a.shape[0], a.shape[1]
    rest = a.shape[2:]
    b = a.reshape((32, 2, 64) + rest)
    b = np.moveaxis(b, 0, 2)
    return np.ascontiguousarray(b.reshape((128, 32) + rest))


def prep_mixer(cfg, inp):
    NK, DEPTH = cfg.NK, cfg.DEPTH
    d = {}
    wins = []
    for l in range(DEPTH):
        W = inp["w_in"][l]
        Wn = np.concatenate([W[:, 0:3328], W[:, 3328:3328 + 2560], W[:, 5904:6928], W[:, 6928:], W[:, 5888:5904],
                             np.zeros((W.shape[0], 112), np.float32)], axis=1)
        wins.append(blk(Wn))
    d["win"] = np.stack(wins)
    d["wpa"] = np.stack([blk(inp["w_proj_a"][l]) for l in range(DEPTH)])
    d["wpb"] = np.stack([blk(inp["w_proj_b"][l]) for l in range(DEPTH)])
    d["wpc"] = np.stack([blk(inp["w_proj_c"][l]) for l in range(DEPTH)])
    d["wo"] = np.stack([blk(inp["w_out"][l]) for l in range(DEPTH)])
    d["identd"] = np.eye(P, dtype=np.float32)
    lam = np.zeros((DEPTH, 2, P, 3, 32), np.float32)
    sb = np.zeros((DEPTH, 2, P, 2, 32, 16), np.float32)
    sc = np.zeros((DEPTH, 2, 2, 32, P, P), np.float32)
    for l in range(DEPTH):
        for dd in range(2):
            lam[l, dd, :, 0] = gp_layout(inp["s5_lambda_re"][l, dd])
            lam[l, dd, :, 1] = gp_layout(inp["s5_lambda_im"][l, dd])
            lam[l, dd, :, 2] = gp_layout(np.repeat(inp["s5_log_dt"][l, dd][:, None], 64, axis=1))
            sb[l, dd, :, 0] = gp_layout(inp["s5_b_re"][l, dd])
            sb[l, dd, :, 1] = gp_layout(inp["s5_b_im"][l, dd])
            for ri, key in enumerate(("s5_c_re", "s5_c_im")):
                C = inp[key][l, dd]
                for q in range(32):
                    for g2 in range(2):
                        g8 = 2 * (q % 4) + g2
                        sc[l, dd, ri, q, g2 * 64:(g2 + 1) * 64, g8 * 16:(g8 + 1) * 16] = C[2 * q + g2].T
    d["s5lam"], d["s5b"], d["s5c"] = lam, sb, sc
    d["s5d"] = np.stack([colv(inp["s5_d"][l]) for l in range(DEPTH)])
    rc = np.zeros((DEPTH, P, 5, 8), np.float32)
    rw0 = np.zeros((DEPTH, P, 2, 2, 8), np.float32)
    rw2 = np.zeros((DEPTH, P, 2, RW), np.float32)
    for l in range(DEPTH):
        rc[l, :, 0] = colv(inp["rwkv_k_k"][l])
        rc[l, :, 1] = colv(inp["rwkv_k_a"][l])
        rc[l, :, 2] = colv(inp["rwkv_r_k"][l].reshape(-1))
        rc[l, :, 3] = colv(inp["rwkv_ln_g"][l])
        rc[l, :, 4] = colv(inp["rwkv_ln_b"][l])
        for dd in range(2):
            rw0[l, :, dd, 0] = colv(inp["rwkv_w0"][l, dd])
            rw0[l, :, dd, 1] = colv(inp["rwkv_a0"][l, dd])
            rw2[l, 0:64, dd] = inp["rwkv_w2"][l, dd]
            rw2[l, 64:128, dd] = inp["rwkv_a2"][l, dd]
    d["rwcol"], d["rww0"], d["rww2"] = rc, rw0, rw2
    d["rwmu"] = np.stack([colv(inp["rwkv_mu"][l]) for l in range(DEPTH)])
    d["rwg2"] = np.ascontiguousarray(inp["rwkv_g2"])
    hb = np.zeros((2, P, P), np.float32)
    hb[0, 0:64, 0:64] = 1.0
    hb[0, 64:, 64:] = 1.0
    hb[1] = hb[0] / 64.0
    d["hblk"] = hb
    i_ = np.arange(64)[:, None]
    t_ = np.arange(64)[None, :]
    rt = np.stack([(i_ < t_), (i_ > t_), (i_ <= t_)]).astype(np.float32)
    d["rwtri"] = np.concatenate([rt, rt], axis=1)
    cmk = np.ones((P, cfg.T), np.float32)
    cmk[:, 0::64] = 0.0
    d["rwcm"] = cmk
    tri = np.zeros((2, P, P), np.float32)
    tri[0] = np.triu(np.ones((P, P), np.float32))
    tri[1] = np.tril(np.ones((P, P), np.float32))
    d["tri"] = tri
    cw = np.zeros((DEPTH, P, 12, 6), np.float32)
    scol = np.zeros((DEPTH, 64, 3), np.float32)
    for l in range(DEPTH):
        w = inp["ssd_conv_w"][l]
        for j in range(5):
            cw[l, :, :, j] = colv(w[j])
        cw[l, :, :, 5] = colv(inp["ssd_conv_b"][l])
        for dd in range(2):
            scol[l, 32 * dd:32 * dd + 16, 0] = inp["ssd_dt_bias"][l, dd]
            scol[l, 32 * dd + 16:32 * dd + 32, 0] = inp["ssd_dt_bias"][l, dd]
            scol[l, 32 * dd + 16:32 * dd + 32, 1] = inp["ssd_a_log"][l, dd]
            scol[l, 32 * dd:32 * dd + 16, 2] = 1.0
            scol[l, 32 * dd + 16:32 * dd + 32, 2] = -1.0
    d["ssdcw"], d["ssdcol"] = cw, scol
    d["ssdD"] = np.stack([colv(np.repeat(inp["ssd_d"][l], 64)) for l in range(DEPTH)])
    d["ssdg"] = np.stack([colv(inp["ssd_norm_g"][l]) for l in range(DEPTH)])
    return d


def core_mixer_inputs(cfg, inp, c, d):
    DEPTH = cfg.DEPTH
    prompt = c < cfg.NPC
    keep = np.ones((P, TT), np.float32)
    if prompt:
        keep[:, 0::256] = 0.0
    d["keepT"] = keep
    s0 = np.zeros((DEPTH, 2, P, 2, 32), np.float32)
    if not prompt:
        b = c - cfg.NPC
        for l in range(DEPTH):
            for dd in range(2):
                s0[l, dd, :, 0] = gp_layout(inp["state_s5_re"][b, l, dd])
                s0[l, dd, :, 1] = gp_layout(inp["state_s5_im"][b, l, dd])
    d["s5s0"] = s0
    cm = np.ones((P, 4, TT), np.float32)
    if prompt:
        for oi, o in enumerate((-2, -1, 1, 2)):
            for t in range(TT):
                if (t + o) // 256 != t // 256:
                    cm[:, oi, t] = 0.0
    d["cmT"] = cm
    smk = np.zeros((P, 4, TT), np.float32)
    tt_ = np.arange(TT)
    if prompt:
        smk[:, 0] = np.where(tt_ % 256 != 0, 0.5, 0.0)
        smk[:, 1] = np.where(tt_ % 256 != 255, 0.5, 0.0)
    else:
        smk[:, 0] = np.where(tt_ % 64 != 0, 0.25, 0.0)
        smk[:, 1] = np.where(tt_ % 64 != 63, 0.25, 0.0)
        smk[:, 2] = 0.25
        smk[:, 3] = 0.25
    d["rwsm"] = smk
    rs0 = np.zeros((DEPTH, 2, 64, 16, 64), np.float32)
    if not prompt:
        b = c - cfg.NPC
        sr = inp["state_rwkv"][b]
        for l in range(DEPTH):
            for dd in range(2):
                rs0[l, dd] = sr[l, dd].transpose(2, 0, 1)
    d["rws0"] = rs0
    d["ssdkeep"] = np.full((P, 1), 0.0 if prompt else 1.0, np.float32)
    ss0 = np.zeros((DEPTH, 2, 2, P, 512), np.float32)
    if not prompt:
        b = c - cfg.NPC
        st_ = inp["state_ssd"][b]
        for l in range(DEPTH):
            for dd in range(2):
                for g in range(2):
                    ss0[l, dd, g] = st_[l, dd, 8 * g:8 * g + 8].transpose(2, 0, 1).reshape(P, 512)
    d["ssds0"] = ss0


def core_inputs(cfg, inp, common, c):
    d = dict(common)
    T = cfg.T
    if c < cfg.NPC:
        ns = T // 256
        x = inp["x_prompt"][c * ns:(c + 1) * ns].reshape(T, cfg.DM)
        cond = inp["c_ctx"]
    else:
        b = c - cfg.NPC
        x = inp["x_sample"][b]
        cond = inp["c"][b]
    d["xT"] = np.ascontiguousarray(x.T)
    d["cond"] = colv(cond)
    core_mixer_inputs(cfg, inp, c, d)
    return d


def run(cfg, inp):
    b = Builder(cfg)
    nc = b.build()
    common = prep_common(cfg, inp)
    common.update(prep_mixer(cfg, inp))
    n = cfg.NPC + cfg.NSC
    maps = [core_inputs(cfg, inp, common, c) for c in range(n)]
    for m in maps:
        for k in list(m.keys()):
            if k not in b.din:
                del m[k]
            else:
                m[k] = np.ascontiguousarray(m[k], dtype=np.float32)
    res = run_bass_kernel_spmd(nc, maps, core_ids=list(range(n)))
    return res.results


def assemble(cfg, inp, results):
    T, DEPTH, NSEG = cfg.T, cfg.DEPTH, cfg.NSEG
    ns = T // 256
    yp = np.concatenate([results[c]["yT"].T.reshape(ns, 256, cfg.DM) for c in range(cfg.NPC)], axis=0)
    ys = np.stack([results[cfg.NPC + b]["yT"].T for b in range(cfg.NSC)], axis=0)
    nb = cfg.NPC * NSEG
    st_rwkv = np.zeros((nb, DEPTH, 2, 16, 64, 64), np.float32)
    st_ssd = np.zeros((nb, DEPTH, 2, 16, 64, 128), np.float32)
    s5 = [np.zeros((nb, DEPTH, 2, 64, 64), np.float32) for _ in range(2)]
    for c in range(cfg.NPC):
        r = results[c]
        if "s5o" in r:
            o = r["s5o"]
            o = o.reshape(DEPTH, 2, 64, 2, 2, 32, NSEG)
            o = o.transpose(6, 0, 3, 4, 5, 1, 2)
            o = o.reshape(NSEG, DEPTH, 2, 2, 64, 64).copy()
            o[:, :, 1] = o[::-1, :, 1]
            for ri in range(2):
                s5[ri][c * NSEG:(c + 1) * NSEG] = o[:, :, :, ri]
        if "rwo" in r:
            o = r["rwo"]
            o = o.transpose(2, 0, 1, 4, 5, 3).copy()
            o[:, :, 1] = o[::-1, :, 1]
            st_rwkv[c * NSEG:(c + 1) * NSEG] = o
        if "ssdo" in r:
            o = r["ssdo"]
            o = o.reshape(DEPTH, 2, NSEG, 2, P, 8, 64).transpose(2, 0, 1, 3, 5, 6, 4)
            st_ssd[c * NSEG:(c + 1) * NSEG] = o.reshape(NSEG, DEPTH, 2, 16, 64, 128)
    return yp, ys, st_rwkv, st_ssd, s5[0], s5[1]


def kernel(**inputs):
    cfg = Cfg()
    inp = {k: np.asarray(v) for k, v in inputs.items()}
    results = run(cfg, inp)
    return assemble(cfg, inp, results)
```

```python
import numpy as np
from contextlib import ExitStack
import concourse.bass as bass
import concourse.mybir as mybir
from concourse.bass_utils import run_bass_kernel_spmd

F32 = mybir.dt.float32
F32R = mybir.dt.float32r
AF = mybir.ActivationFunctionType
ALU = mybir.AluOpType
P = 128
TT = 512
NDS = 40

RW = 1024
RH = 64
SW = 1024
SXBC = 1536
S5W = 1024
EPS = 1e-6


class Cfg:
    def __init__(self, DM=2048, DFF=5504, DEPTH=2, T=2048, NPC=4, NSC=4, mix=(1, 1, 1)):
        self.DM, self.DFF, self.DEPTH, self.T, self.NPC, self.NSC = DM, DFF, DEPTH, T, NPC, NSC
        self.NK = DM // P
        self.NJ = DFF // P
        self.NT = T // TT
        self.NSEG = T // 256
        self.mix = mix
        self.ZA, self.ZB, self.ZC = 0, 26, 46
        self.ZG = 54
        self.ZDT = 54 + 3 * self.NK
        self.NZ = self.ZDT + 1


class Tok:
    __slots__ = ("w", "r")

    def __init__(self):
        self.w = []
        self.r = {}


class KB:
    def __init__(self, nc):
        self.nc = nc
        self.E = {"pe": nc.tensor, "dve": nc.vector, "act": nc.scalar, "pool": nc.gpsimd, "sp": nc.sync}
        self.tick = {e: 0 for e in self.E}
        self.seen = {e: {} for e in self.E}
        self.stack = ExitStack()
        self.sem = {e: self.stack.enter_context(nc.semaphore("s_" + e)) for e in self.E}
        self.dsem = [self.stack.enter_context(nc.semaphore("d%d" % i)) for i in range(NDS)]
        self.dval = [0] * NDS
        self.drr = 0
        self.nid = 0
        self.ninstr = 0

    def name(self, s):
        self.nid += 1
        return "%s_%d" % (s, self.nid)

    def _wait(self, e, dep):
        kind, key, val = dep
        if kind == "e" and key == e and e == "pe":
            return
        k = (kind, key)
        if self.seen[e].get(k, 0) >= val:
            return
        self.seen[e][k] = val
        sem = self.sem[key] if kind == "e" else self.dsem[key]
        self.E[e].wait_ge(sem, val)
        self.ninstr += 1

    def _sync(self, e, reads, writes):
        for t in reads:
            for d in t.w:
                self._wait(e, d)
        for t in writes:
            for d in t.w:
                self._wait(e, d)
            for k, v in t.r.items():
                self._wait(e, (k[0], k[1], v))

    def _mark(self, me, reads, writes):
        for t in writes:
            t.w = [me]
            t.r = {}
        for t in reads:
            if not (len(t.w) == 1 and t.w[0] is me):
                k = (me[0], me[1])
                if t.r.get(k, 0) < me[2]:
                    t.r[k] = me[2]

    def join(self, dst, srcs):
        w = list(dst.w)
        for s_ in srcs:
            w.extend(s_.w)
        dst.w = w

    def op(self, e, fn, reads=(), writes=()):
        self._sync(e, reads, writes)
        ins = fn(self.E[e])
        self.tick[e] += 1
        ins.then_inc(self.sem[e], 1)
        self.ninstr += 1
        self._mark(("e", e, self.tick[e]), reads, writes)
        return ins

    def dma(self, q, out, in_, reads=(), writes=(), **kw):
        self._sync(q, reads, writes)
        s = self.drr
        self.drr = (self.drr + 1) % NDS
        if self.dval[s]:
            self._wait(q, ("d", s, self.dval[s]))
        ins = self.E[q].dma_start(out=out, in_=in_, **kw)
        self.dval[s] += 16
        ins.then_inc(self.dsem[s], 16)
        self.ninstr += 1
        self._mark(("d", s, self.dval[s]), reads, writes)
        return ins

    def barrier(self):
        for e in self.E:
            for e2 in self.E:
                if e2 != e and self.tick[e2]:
                    self._wait(e, ("e", e2, self.tick[e2]))
            for s in range(NDS):
                if self.dval[s]:
                    self._wait(e, ("d", s, self.dval[s]))


class Tile:
    def __init__(self, kb, stack, shape, dtype, ntok=1, name="t"):
        self.t = stack.enter_context(kb.nc.sbuf_tensor(kb.name(name), list(shape), dtype))
        self.toks = [Tok() for _ in range(ntok)]
        self.dtype = dtype

    @property
    def tok(self):
        return self.toks[0]

    def __getitem__(self, k):
        return self.t[k]

    def f32(self, k):
        return self.t[k].bitcast(F32)


class Rot:
    def __init__(self, kb, stack, n, shape, dtype, name="r"):
        self.tiles = [Tile(kb, stack, shape, dtype, name=name) for _ in range(n)]
        self.i = 0

    def next(self):
        t = self.tiles[self.i]
        self.i = (self.i + 1) % len(self.tiles)
        return t


def blk(W):
    K, M = W.shape
    return np.ascontiguousarray(W.reshape(K // P, P, M // P, P).transpose(2, 1, 0, 3)).reshape(M // P, P, (K // P) * P)


def colv(v):
    return np.ascontiguousarray(v.reshape(-1, P).T)


class Builder:
    def __init__(self, cfg):
        self.cfg = cfg
        self.nc = bass.Bass("TRN2", target_bir_lowering=False)
        self.kb = KB(self.nc)
        self.din = {}
        self.dout = {}
        self.dtok = {}

    def inp(self, name, shape, dtype=F32):
        self.din[name] = self.nc.dram_tensor(name, list(shape), dtype, kind="ExternalInput").ap()
        return self.din[name]

    def outp(self, name, shape, dtype=F32):
        self.dout[name] = self.nc.dram_tensor(name, list(shape), dtype, kind="ExternalOutput").ap()
        return self.dout[name]

    def scratch(self, name, shape, dtype=F32):
        if getattr(self.cfg, "debug", False) and dtype == F32:
            return self.outp(name, shape, dtype)
        return self.nc.dram_tensor(name, list(shape), dtype, kind="Internal").ap()

    def dt(self, *key):
        if key not in self.dtok:
            self.dtok[key] = Tok()
        return self.dtok[key]

    def build(self):
        cfg, nc, kb = self.cfg, self.nc, self.kb
        NK, NJ, T, NT, DEPTH = cfg.NK, cfg.NJ, cfg.T, cfg.NT, cfg.DEPTH
        self.xT = self.inp("xT", [cfg.DM, T], F32R)
        self.cond = self.inp("cond", [P, NK])
        self.wmodn = self.inp("wmodn", [DEPTH, cfg.DM, 9 * cfg.DM])
        self.bmod = self.inp("bmod", [DEPTH, P, 9 * NK])
        self.normg = self.inp("normg", [DEPTH, P, 3 * NK])
        self.fng = self.inp("fng", [P, NK])
        self.w1 = self.inp("w1", [DEPTH, 2, 2 * NJ, P, NK * P], F32R)
        self.w2 = self.inp("w2", [DEPTH, 2, NK, P, NJ * P], F32R)
        self.yT = self.outp("yT", [cfg.DM, T])
        self.xres = self.scratch("xres", [cfg.DM, T], F32R)
        self.mixer_decl()

        with ExitStack() as gs:
            self.gs = gs
            self.ps = [kb.stack.enter_context(nc.psum_tensor(kb.name("ps"), [P, 1024], F32)) for _ in range(4)]
            self.pstok = [[Tok(), Tok()] for _ in range(4)]
            self.psi = 0
            self.ppi = {}
            self.ones = Tile(kb, gs, [P, P], F32, name="ones")
            kb.op("dve", lambda e: e.memset(self.ones[:], 1.0), writes=[self.ones.tok])
            self.mod = Tile(kb, gs, [P, DEPTH, 9 * NK], F32, name="mod")
            self.modA = Tile(kb, gs, [P, DEPTH, 3 * NK], F32, name="modA")
            self.modG = Tile(kb, gs, [P, DEPTH, 3 * NK], F32, name="modG")
            self.fngt = Tile(kb, gs, [P, NK], F32, name="fng")
            kb.dma("sp", self.fngt[:], self.fng[:, :], writes=[self.fngt.tok])
            self.mixer_consts()
            self.phase_mod()
            src = self.xT
            for l in range(DEPTH):
                self.phase_ffn(l, 0, src)
                src = self.xres
                self.phase_mix(l)
                self.phase_ffn(l, 1, src)
            self.phase_final()
            kb.barrier()
        kb.stack.close()
        return nc

    def bank(self, pool=None):
        if pool is None:
            i = self.psi
            self.psi = (self.psi + 1) % 8
        else:
            base = 0 if pool == "A" else 4
            k = self.ppi.get(pool, 0)
            self.ppi[pool] = (k + 1) % 4
            i = base + k
        return self.ps[i // 2][:, (i % 2) * 512:(i % 2) * 512 + 512], self.pstok[i // 2][i % 2]

    def bank2(self):
        if self.psi % 2:
            self.psi = (self.psi + 1) % 8
        i = self.psi
        self.psi = (self.psi + 2) % 8
        return self.ps[i // 2], self.pstok[i // 2]

    def phase_mod(self):
        cfg, kb = self.cfg, self.kb
        NK, DEPTH = cfg.NK, cfg.DEPTH
        NM = 9 * NK
        NCOL = NM * P
        CG = 512 if NCOL % 512 == 0 else 256
        with ExitStack() as st:
            cnd = Tile(kb, st, [P, NK], F32, name="cnd")
            sc = Tile(kb, st, [P, NK], F32, name="scnd")
            bm = Tile(kb, st, [P, DEPTH, NM], F32, name="bm")
            rows = Rot(kb, st, 3, [1, CG], F32, name="mrow")
            ng = Tile(kb, st, [P, DEPTH, 3 * NK], F32, name="ng")
            wr = Rot(kb, st, 2, [P, NK, CG], F32, name="wm")
            kb.dma("sp", cnd[:], self.cond[:, :], writes=[cnd.tok])
            for l in range(DEPTH):
                kb.dma("sp", ng[:, l, :], self.normg[l], writes=[ng.tok])
                kb.dma("sp", bm[:, l, :], self.bmod[l], writes=[bm.tok])
            kb.op("act", lambda e: e.activation(out=sc[:], in_=cnd[:], func=AF.Silu), reads=[cnd.tok], writes=[sc.tok])
            MC = CG // P
            for l in range(DEPTH):
                wv = self.wmodn[l].rearrange("(k p) c -> p k c", p=P)
                for cg in range(NCOL // CG):
                    w = wr.next()
                    kb.dma("sp", w[:], wv[:, :, cg * CG:(cg + 1) * CG], writes=[w.tok])
                    pb, pt = self.bank()
                    for kc in range(NK):
                        kb.op("pe", lambda e: e.matmul(pb[0:1, 0:CG], sc[:, kc:kc + 1], w[:, kc, :], start=(kc == 0), stop=(kc == NK - 1)),
                              reads=[w.tok, sc.tok], writes=[pt])
                    row = rows.next()
                    kb.op("act", lambda e: e.activation(out=row[:], in_=pb[0:1, 0:CG], func=AF.Copy), reads=[pt], writes=[row.tok])
                    pb2, pt2 = self.bank()
                    for mm in range(MC):
                        kb.op("pe", lambda e: e.matmul(pb2[:, mm:mm + 1], row[0:1, mm * P:(mm + 1) * P], self.ones[0:1, 0:1], start=True, stop=True),
                              reads=[row.tok, self.ones.tok], writes=[pt2])
                    m0 = cg * MC
                    kb.op("dve", lambda e: e.tensor_tensor(out=self.mod[:, l, m0:m0 + MC], in0=pb2[:, 0:MC], in1=bm[:, l, m0:m0 + MC], op=ALU.add),
                          reads=[pt2, bm.tok], writes=[self.mod.tok])
                for i in range(3):
                    scs = self.mod[:, l, (3 * i + 1) * NK:(3 * i + 2) * NK]
                    kb.op("dve", lambda e: e.scalar_tensor_tensor(
                        out=self.modA[:, l, i * NK:(i + 1) * NK], in0=scs, scalar=1.0, in1=ng[:, l, i * NK:(i + 1) * NK],
                        op0=ALU.add, op1=ALU.mult), reads=[self.mod.tok, ng.tok], writes=[self.modA.tok])
                    gs_ = self.mod[:, l, (3 * i + 2) * NK:(3 * i + 3) * NK]
                    kb.op("dve", lambda e: e.tensor_scalar(
                        out=self.modG[:, l, i * NK:(i + 1) * NK], in0=gs_, scalar1=(1.0 if i == 1 else 0.5), scalar2=None,
                        op0=ALU.mult), reads=[self.mod.tok], writes=[self.modG.tok])
            kb.barrier()

    def norm_tile(self, hb, tmp, rstd, A, SH, out_dtype_r=True):
        cfg, kb = self.cfg, self.kb
        NK = cfg.NK
        pb, pt = self.bank()
        for kc in range(NK):
            t = tmp.next()
            kb.op("act", lambda e, t=t, kc=kc: e.activation(out=t[:], in_=hb.f32((slice(None), kc)), func=AF.Square),
                  reads=[hb.tok], writes=[t.tok])
            kb.op("pe", lambda e, t=t, kc=kc: e.matmul(pb, self.ones[:], t[:], start=(kc == 0), stop=(kc == NK - 1)),
                  reads=[t.tok, self.ones.tok], writes=[pt])
        t = tmp.next()
        kb.op("act", lambda e: e.activation(out=t[:], in_=pb, func=AF.Sqrt, bias=self.epsc[:, 0:1], scale=1.0 / cfg.DM),
              reads=[pt, self.epsc.tok], writes=[t.tok])
        kb.op("dve", lambda e: e.reciprocal(out=rstd[:], in_=t[:]), reads=[t.tok], writes=[rstd.tok])
        for kc in range(NK):
            t = tmp.next()
            kb.op("dve", lambda e, t=t, kc=kc: e.scalar_tensor_tensor(out=t[:], in0=hb.f32((slice(None), kc)), scalar=A[:, kc:kc + 1],
                                                                    in1=rstd[:], op0=ALU.mult, op1=ALU.mult),
                  reads=[hb.tok, rstd.tok, self.modA.tok, self.fngt.tok], writes=[t.tok])
            if SH is not None:
                kb.op("act", lambda e, t=t, kc=kc: e.activation(out=hb[:, kc], in_=t[:], func=AF.Identity, bias=SH[:, kc:kc + 1], scale=1.0),
                      reads=[t.tok, self.mod.tok], writes=[hb.tok])
            else:
                kb.op("act", lambda e, t=t, kc=kc: e.activation(out=hb[:, kc], in_=t[:], func=AF.Copy),
                      reads=[t.tok], writes=[hb.tok])

    def load_xtile(self, hb, src, tt):
        kb, cfg = self.kb, self.cfg
        sv = src.rearrange("(k p) t -> p k t", p=P)[:, :, tt * TT:(tt + 1) * TT]
        kb.dma("pool", hb[:], sv, reads=[self.dt("x", tt)], writes=[hb.tok])

    def phase_ffn(self, l, w, src):
        cfg, kb = self.cfg, self.kb
        NK, NJ, NT = cfg.NK, cfg.NJ, cfg.NT
        JH = (NJ + 1) // 2
        WSZ = max(NK, JH) * P
        A = self.modA[:, l, (2 * w) * NK:(2 * w + 1) * NK]
        SH = self.mod[:, l, (6 * w) * NK:(6 * w + 1) * NK]
        G = self.modG[:, l, (2 * w) * NK:(2 * w + 1) * NK]
        with ExitStack() as st:
            hb = Tile(kb, st, [P, NK, TT], F32R, name="hb")
            act = Tile(kb, st, [P, NJ, TT], F32R, ntok=NJ, name="act")
            wr = Rot(kb, st, 4, [P, WSZ], F32R, name="wf")
            tmp = Rot(kb, st, 3, [P, TT], F32, name="tmp")
            xc = Rot(kb, st, 3, [P, TT], F32, name="xc")
            rstd = Tile(kb, st, [P, TT], F32, name="rstd")
            srcf = src.bitcast(F32)
            xresf = self.xres.bitcast(F32)
            for tt in range(NT):
                self.load_xtile(hb, src, tt)
                self.norm_tile(hb, tmp, rstd, A, SH)
                for j in range(NJ):
                    pbs = []
                    for half in range(2):
                        wt = wr.next()
                        kb.dma("pool", wt[:, 0:NK * P], self.w1[l, w, half * NJ + j], writes=[wt.tok])
                        pb, pt = self.bank()
                        for kc in range(NK):
                            kb.op("pe", lambda e, wt=wt, kc=kc, pb=pb: e.matmul(pb, wt[:, kc * P:(kc + 1) * P], hb[:, kc],
                                                                              start=(kc == 0), stop=(kc == NK - 1)),
                                  reads=[wt.tok, hb.tok], writes=[pt])
                        pbs.append((pb, pt))
                    t = tmp.next()
                    kb.op("act", lambda e, t=t: e.activation(out=t[:], in_=pbs[0][0], func=AF.Silu), reads=[pbs[0][1]], writes=[t.tok])
                    kb.op("dve", lambda e, t=t, j=j: e.tensor_tensor(out=act[:, j], in0=pbs[1][0], in1=t[:], op=ALU.mult),
                          reads=[pbs[1][1], t.tok], writes=[act.toks[j]])
                for n in range(NK):
                    pb, pt = self.bank()
                    for hf in range(2):
                        j0, j1 = (0, JH) if hf == 0 else (JH, NJ)
                        wt = wr.next()
                        kb.dma("pool", wt[:, 0:(j1 - j0) * P], self.w2[l, w, n][:, j0 * P:j1 * P], writes=[wt.tok])
                        for j in range(j0, j1):
                            kb.op("pe", lambda e, wt=wt, j=j, j0=j0: e.matmul(pb, wt[:, (j - j0) * P:(j - j0 + 1) * P], act[:, j],
                                                                            start=(j == 0), stop=(j == NJ - 1)),
                                  reads=[wt.tok, act.toks[j]], writes=[pt])
                    x = xc.next()
                    kb.dma("sp", x[:], srcf[n * P:(n + 1) * P, tt * TT:(tt + 1) * TT], reads=[self.dt("x", tt)], writes=[x.tok])
                    kb.op("dve", lambda e, x=x, n=n: e.scalar_tensor_tensor(out=x[:], in0=pb, scalar=G[:, n:n + 1], in1=x[:],
                                                                          op0=ALU.mult, op1=ALU.add),
                          reads=[pt, x.tok, self.modG.tok], writes=[x.tok])
                    kb.dma("sp", xresf[n * P:(n + 1) * P, tt * TT:(tt + 1) * TT], x[:], reads=[x.tok], writes=[self.dt("xo", tt, n)])
                xt_ = self.dt("x", tt)
                xt_.w = []
                kb.join(xt_, [self.dt("xo", tt, n) for n in range(NK)])
            kb.barrier()

    def phase_final(self):
        cfg, kb = self.cfg, self.kb
        NK, NT = cfg.NK, cfg.NT
        with ExitStack() as st:
            hb = Tile(kb, st, [P, NK, TT], F32R, name="hbf")
            tmp = Rot(kb, st, 3, [P, TT], F32, name="tmpf")
            rstd = Tile(kb, st, [P, TT], F32, name="rstdf")
            for tt in range(NT):
                self.load_xtile(hb, self.xres, tt)
                self.norm_tile(hb, tmp, rstd, self.fngt, None)
                dv = self.yT.rearrange("(k p) t -> p k t", p=P)[:, :, tt * TT:(tt + 1) * TT]
                kb.dma("sp", dv, hb.f32(slice(None)), reads=[hb.tok], writes=[self.dt("y", tt)])
            kb.barrier()

    def mixer_decl(self):
        cfg = self.cfg
        NK, T, DEPTH, NSEG = cfg.NK, cfg.T, cfg.DEPTH, cfg.NSEG
        self.win = self.inp("win", [DEPTH, cfg.NZ, P, NK * P], F32R)
        self.zT = self.scratch("zT", [cfg.NZ * P, T])
        self.wpa = self.inp("wpa", [DEPTH, NK, P, 8 * P], F32R)
        self.wpb = self.inp("wpb", [DEPTH, NK, P, 8 * P], F32R)
        self.wpc = self.inp("wpc", [DEPTH, 2 * NK, P, 8 * P], F32R)
        self.wo = self.inp("wo", [DEPTH, NK, P, NK * P], F32R)
        self.ya = self.scratch("ya", [RW, T], F32R)
        self.yb = self.scratch("yb", [SW, T], F32R)
        self.yc = self.scratch("yc", [S5W, T], F32R)
        self.keepT = self.inp("keepT", [P, TT])
        self.rwp = self.scratch("rwp", [2, 5, RW, T], F32R)
        self.rwbon = self.scratch("rwbon", [RW, T])
        self.rwgc = self.scratch("rwgc", [2, RW, T // 64])
        self.rwsm = self.inp("rwsm", [P, 4, TT])
        self.rwcm = self.inp("rwcm", [P, T])
        self.rwcol = self.inp("rwcol", [DEPTH, P, 5, 8])
        self.rwmu = self.inp("rwmu", [DEPTH, P, 26])
        self.rww0 = self.inp("rww0", [DEPTH, P, 2, 2, 8])
        self.rww2 = self.inp("rww2", [DEPTH, P, 2, RW])
        self.rwg2 = self.inp("rwg2", [DEPTH, P, RW])
        self.hblk = self.inp("hblk", [2, P, P])
        self.rwtri = self.inp("rwtri", [3, P, 64])
        self.rws0 = self.inp("rws0", [DEPTH, 2, 64, 16, 64], F32R)
        self.rwo = self.outp("rwo", [DEPTH, 2, NSEG, 64, 16, 64])
        self.rwoT = self.scratch("rwoT", [2, RW, T])
        self.xcs = self.scratch("xcs", [SXBC, T])
        self.tri = self.inp("tri", [2, P, P])
        self.cmT = self.inp("cmT", [P, 4, TT])
        self.ssdcw = self.inp("ssdcw", [DEPTH, P, 12, 6])
        self.ssdcol = self.inp("ssdcol", [DEPTH, 64, 3])
        self.ssdD = self.inp("ssdD", [DEPTH, P, 8])
        self.ssdg = self.inp("ssdg", [DEPTH, P, 8])
        self.ssds0 = self.inp("ssds0", [DEPTH, 2, 2, P, 512])
        self.ssdkeep = self.inp("ssdkeep", [P, 1])
        self.ssdo = self.outp("ssdo", [DEPTH, 2, NSEG, 2, P, 512])
        self.s5lam = self.inp("s5lam", [DEPTH, 2, P, 3, 32])
        self.s5b = self.inp("s5b", [DEPTH, 2, P, 2, 32, 16])
        self.s5c = self.inp("s5c", [DEPTH, 2, 2, 32, P, P], F32R)
        self.s5d = self.inp("s5d", [DEPTH, P, 8])
        self.s5s0 = self.inp("s5s0", [DEPTH, 2, P, 2, 32])
        self.s5o = self.outp("s5o", [DEPTH, P, 2, 2, 32, NSEG])

    def mixer_consts(self):
        kb = self.kb
        self.epsc = Tile(kb, self.gs, [P, 4], F32, name="epsc")
        kb.op("dve", lambda e: e.memset(self.epsc[:, 0:1], EPS), writes=[self.epsc.tok])
        kb.op("dve", lambda e: e.memset(self.epsc[:, 1:2], float(np.pi / 2)), writes=[self.epsc.tok])
        kb.op("dve", lambda e: e.memset(self.epsc[:, 2:3], 1e-12), writes=[self.epsc.tok])
        kb.op("dve", lambda e: e.memset(self.epsc[:, 3:4], 0.0), writes=[self.epsc.tok])
        self.ident = Tile(kb, self.gs, [P, P], F32, name="ident")
        self.identd = self.inp("identd", [P, P])
        kb.dma("sp", self.ident[:], self.identd[:, :], writes=[self.ident.tok])
        self.keep = Tile(kb, self.gs, [P, TT], F32, name="keep")
        kb.dma("sp", self.keep[:], self.keepT[:, :], writes=[self.keep.tok])

    def phase_mix(self, l):
        cfg = self.cfg
        self.phase_inproj(l)
        if cfg.mix[0]:
            self.phase_rwkv(l)
        if cfg.mix[1]:
            self.phase_ssd(l)
        if cfg.mix[2]:
            self.phase_s5(l)
        self.phase_merge(l)

    def phase_inproj(self, l):
        cfg, kb = self.cfg, self.kb
        NK, NT = cfg.NK, cfg.NT
        A = self.modA[:, l, NK:2 * NK]
        SH = self.mod[:, l, 3 * NK:4 * NK]
        with ExitStack() as st:
            hb = Tile(kb, st, [P, NK, TT], F32R, name="hbi")
            wr = Rot(kb, st, 4, [P, NK * P], F32R, name="wi")
            tmp = Rot(kb, st, 3, [P, TT], F32, name="tmpi")
            stg = Rot(kb, st, 4, [P, TT], F32, name="stg")
            rstd = Tile(kb, st, [P, TT], F32, name="rstdi")
            for tt in range(NT):
                self.load_xtile(hb, self.xres, tt)
                self.norm_tile(hb, tmp, rstd, A, SH)
                for m in range(cfg.NZ):
                    wt = wr.next()
                    kb.dma("pool", wt[:], self.win[l, m], writes=[wt.tok])
                    pb, pt = self.bank()
                    for kc in range(NK):
                        kb.op("pe", lambda e: e.matmul(pb, wt[:, kc * P:(kc + 1) * P], hb[:, kc], start=(kc == 0), stop=(kc == NK - 1)),
                              reads=[wt.tok, hb.tok], writes=[pt])
                    s = stg.next()
                    if m >= cfg.ZG and m < cfg.ZDT:
                        kb.op("act", lambda e: e.activation(out=s[:], in_=pb, func=AF.Sigmoid), reads=[pt], writes=[s.tok])
                    elif m >= cfg.ZB and m < cfg.ZB + 8:
                        kb.op("act", lambda e: e.activation(out=s[:], in_=pb, func=AF.Silu), reads=[pt], writes=[s.tok])
                    elif m % 2:
                        kb.op("act", lambda e: e.activation(out=s[:], in_=pb, func=AF.Copy), reads=[pt], writes=[s.tok])
                    else:
                        kb.op("dve", lambda e: e.tensor_copy(out=s[:], in_=pb), reads=[pt], writes=[s.tok])
                    kb.dma("sp", self.zT[m * P:(m + 1) * P, tt * TT:(tt + 1) * TT], s[:], reads=[s.tok], writes=[self.dt("z", m, tt)])
            kb.barrier()

    def phase_merge(self, l):
        cfg, kb = self.cfg, self.kb
        NK, NT = cfg.NK, cfg.NT
        G = self.modG[:, l, NK:2 * NK]
        xresf = self.xres.bitcast(F32)
        with ExitStack() as st:
            ys = [Tile(kb, st, [P, 8, TT], F32R, name="ym%d" % i) for i in range(3)]
            mg = Tile(kb, st, [P, NK, TT], F32R, ntok=NK, name="mg")
            wr = Rot(kb, st, 4, [P, max(NK, 8) * P], F32R, name="wm")
            gt = Rot(kb, st, 4, [P, TT], F32, name="gt")
            tmp = Rot(kb, st, 6, [P, TT], F32, name="tmpm")
            xc = Rot(kb, st, 3, [P, TT], F32, name="xcm")
            srcs = [self.ya, self.yb, self.yc]
            for tt in range(NT):
                for i in range(3):
                    if cfg.mix[i]:
                        sv = srcs[i].rearrange("(k p) t -> p k t", p=P)[:, :, tt * TT:(tt + 1) * TT]
                        kb.dma("pool", ys[i][:], sv, writes=[ys[i].tok])
                for n in range(NK):
                    terms = []
                    for i, wsrc in ((0, self.wpa), (1, self.wpb)):
                        if not cfg.mix[i]:
                            continue
                        wt = wr.next()
                        kb.dma("pool", wt[:, 0:8 * P], wsrc[l, n], writes=[wt.tok])
                        pb, pt = self.bank()
                        for k in range(8):
                            kb.op("pe", lambda e: e.matmul(pb, wt[:, k * P:(k + 1) * P], ys[i][:, k], start=(k == 0), stop=(k == 7)),
                                  reads=[wt.tok, ys[i].tok], writes=[pt])
                        g = gt.next()
                        kb.dma("sp", g[:], self.zT[(cfg.ZG + i * NK + n) * P:(cfg.ZG + i * NK + n + 1) * P, tt * TT:(tt + 1) * TT], writes=[g.tok])
                        t = tmp.next()
                        kb.op("dve", lambda e: e.tensor_tensor(out=t[:], in0=pb, in1=g[:], op=ALU.mult), reads=[pt, g.tok], writes=[t.tok])
                        terms.append(t)
                    if cfg.mix[2]:
                        pbs = []
                        for hf in range(2):
                            wt = wr.next()
                            kb.dma("pool", wt[:, 0:8 * P], self.wpc[l, hf * NK + n], writes=[wt.tok])
                            pb, pt = self.bank()
                            for k in range(8):
                                kb.op("pe", lambda e: e.matmul(pb, wt[:, k * P:(k + 1) * P], ys[2][:, k], start=(k == 0), stop=(k == 7)),
                                      reads=[wt.tok, ys[2].tok], writes=[pt])
                            pbs.append((pb, pt))
                        g = gt.next()
                        kb.dma("sp", g[:], self.zT[(cfg.ZG + 2 * NK + n) * P:(cfg.ZG + 2 * NK + n + 1) * P, tt * TT:(tt + 1) * TT], writes=[g.tok])
                        sg = tmp.next()
                        kb.op("act", lambda e: e.activation(out=sg[:], in_=pbs[1][0], func=AF.Sigmoid), reads=[pbs[1][1]], writes=[sg.tok])
                        t = tmp.next()
                        kb.op("dve", lambda e: e.tensor_tensor(out=t[:], in0=pbs[0][0], in1=sg[:], op=ALU.mult), reads=[pbs[0][1], sg.tok], writes=[t.tok])
                        kb.op("dve", lambda e: e.tensor_tensor(out=t[:], in0=t[:], in1=g[:], op=ALU.mult), reads=[t.tok, g.tok], writes=[t.tok])
                        terms.append(t)
                    if not terms:
                        kb.op("dve", lambda e: e.memset(mg[:, n], 0.0), writes=[mg.toks[n]])
                    elif len(terms) == 1:
                        kb.op("dve", lambda e: e.tensor_copy(out=mg[:, n], in_=terms[0][:]), reads=[terms[0].tok], writes=[mg.toks[n]])
                    else:
                        for a_ in terms[2:]:
                            kb.op("dve", lambda e: e.tensor_tensor(out=terms[0][:], in0=terms[0][:], in1=a_[:], op=ALU.add),
                                  reads=[terms[0].tok, a_.tok], writes=[terms[0].tok])
                        kb.op("dve", lambda e: e.tensor_tensor(out=mg[:, n], in0=terms[0][:], in1=terms[1][:], op=ALU.add),
                              reads=[terms[0].tok, terms[1].tok], writes=[mg.toks[n]])
                for n in range(NK):
                    wt = wr.next()
                    kb.dma("pool", wt[:, 0:NK * P], self.wo[l, n], writes=[wt.tok])
                    pb, pt = self.bank()
                    for k in range(NK):
                        kb.op("pe", lambda e: e.matmul(pb, wt[:, k * P:(k + 1) * P], mg[:, k], start=(k == 0), stop=(k == NK - 1)),
                              reads=[wt.tok, mg.toks[k]], writes=[pt])
                    x = xc.next()
                    kb.dma("sp", x[:], xresf[n * P:(n + 1) * P, tt * TT:(tt + 1) * TT], reads=[self.dt("x", tt)], writes=[x.tok])
                    kb.op("dve", lambda e: e.scalar_tensor_tensor(out=x[:], in0=pb, scalar=G[:, n:n + 1], in1=x[:], op0=ALU.mult, op1=ALU.add),
                          reads=[pt, x.tok, self.modG.tok], writes=[x.tok])
                    kb.dma("sp", xresf[n * P:(n + 1) * P, tt * TT:(tt + 1) * TT], x[:], reads=[x.tok], writes=[self.dt("xo", tt, n)])
            kb.barrier()

    def phase_s5(self, l):
        cfg, kb = self.cfg, self.kb
        T, NT, NSEG = cfg.T, cfg.NT, cfg.NSEG
        V = lambda e: e
        with ExitStack() as st:
            def tl(shape, name, dtype=F32):
                return Tile(kb, st, shape, dtype, name=name)
            so = tl([P, 2, 2, 32, NSEG], "s5so")
            dcol = tl([P, 8], "s5dc")
            kb.dma("sp", dcol[:], self.s5d[l], writes=[dcol.tok])
            yacc = tl([P, T], "yacc")
            uch = tl([P, T], "uch", F32R)
            prm = []
            for d in range(2):
                lam = tl([P, 3, 32], "lam")
                kb.dma("sp", lam[:], self.s5lam[l, d], writes=[lam.tok])
                bq = tl([P, 2, 32, 16], "bq")
                kb.dma("sp", bq[:], self.s5b[l, d], writes=[bq.tok])
                s0 = tl([P, 2, 32], "s0")
                kb.dma("sp", s0[:], self.s5s0[l, d], writes=[s0.tok])
                w = tl([P, 16, 32], "s5w")
                def W(i):
                    return w[:, i, :]
                def tt_(o, a, b, op):
                    kb.op("dve", lambda e: e.tensor_tensor(out=o, in0=a, in1=b, op=op), reads=[w.tok, lam.tok], writes=[w.tok])
                def ts_(o, a, s1, op0, s2=None, op1=None):
                    if op1 is None:
                        kb.op("dve", lambda e: e.tensor_scalar(out=o, in0=a, scalar1=s1, scalar2=None, op0=op0), reads=[w.tok, lam.tok], writes=[w.tok])
                    else:
                        kb.op("dve", lambda e: e.tensor_scalar(out=o, in0=a, scalar1=s1, scalar2=s2, op0=op0, op1=op1), reads=[w.tok, lam.tok], writes=[w.tok])
                def ac_(o, a, f, bias=None, scale=1.0):
                    if bias is None:
                        kb.op("act", lambda e: e.activation(out=o, in_=a, func=f, scale=scale), reads=[w.tok, lam.tok], writes=[w.tok])
                    else:
                        kb.op("act", lambda e: e.activation(out=o, in_=a, func=f, bias=bias, scale=scale), reads=[w.tok, lam.tok, self.epsc.tok], writes=[w.tok])
                lre, lim, ldt = lam[:, 0, :], lam[:, 1, :], lam[:, 2, :]
                ac_(W(0), ldt, AF.Exp)
                tt_(W(1), lre, W(0), ALU.mult)
                ac_(W(1), W(1), AF.Exp)
                tt_(W(2), lim, W(0), ALU.mult)
                ac_(W(3), W(2), AF.Sin, scale=1.0 / 16)
                ac_(W(4), W(2), AF.Sin, bias=self.epsc[:, 1:2], scale=1.0 / 16)
                for _ in range(4):
                    tt_(W(5), W(3), W(4), ALU.mult)
                    tt_(W(6), W(4), W(4), ALU.mult)
                    tt_(W(7), W(3), W(3), ALU.mult)
                    tt_(W(4), W(6), W(7), ALU.subtract)
                    ts_(W(3), W(5), 2.0, ALU.mult)
                tt_(W(5), W(1), W(4), ALU.mult)
                tt_(W(6), W(1), W(3), ALU.mult)
                ts_(W(7), W(5), -1.0, ALU.add)
                tt_(W(8), lre, lre, ALU.mult)
                tt_(W(9), lim, lim, ALU.mult)
                tt_(W(8), W(8), W(9), ALU.add)
                kb.op("dve", lambda e: e.reciprocal(out=W(8), in_=W(8)), reads=[w.tok], writes=[w.tok])
                tt_(W(9), W(7), lre, ALU.mult)
                tt_(W(10), W(6), lim, ALU.mult)
                tt_(W(9), W(9), W(10), ALU.add)
                tt_(W(9), W(9), W(8), ALU.mult)
                tt_(W(10), W(6), lre, ALU.mult)
                tt_(W(11), W(7), lim, ALU.mult)
                tt_(W(10), W(10), W(11), ALU.subtract)
                tt_(W(10), W(10), W(8), ALU.mult)
                bb = tl([P, 2, 32, 16], "bb")
                tb = tl([P, 32, 16], "tb")
                qre_b = W(9).to_broadcast([P, 32, 16]) if False else None
                def bc(i):
                    return w[:, i, :].unsqueeze(2).to_broadcast([P, 32, 16])
                kb.op("dve", lambda e: e.tensor_tensor(out=bb[:, 0], in0=bq[:, 0], in1=bc(9), op=ALU.mult), reads=[bq.tok, w.tok], writes=[bb.tok])
                kb.op("dve", lambda e: e.tensor_tensor(out=tb[:], in0=bq[:, 1], in1=bc(10), op=ALU.mult), reads=[bq.tok, w.tok], writes=[tb.tok])
                kb.op("dve", lambda e: e.tensor_tensor(out=bb[:, 0], in0=bb[:, 0], in1=tb[:], op=ALU.subtract), reads=[bb.tok, tb.tok], writes=[bb.tok])
                kb.op("dve", lambda e: e.tensor_tensor(out=bb[:, 1], in0=bq[:, 1], in1=bc(9), op=ALU.mult), reads=[bq.tok, w.tok], writes=[bb.tok])
                kb.op("dve", lambda e: e.tensor_tensor(out=tb[:], in0=bq[:, 0], in1=bc(10), op=ALU.mult), reads=[bq.tok, w.tok], writes=[tb.tok])
                kb.op("dve", lambda e: e.tensor_tensor(out=bb[:, 1], in0=bb[:, 1], in1=tb[:], op=ALU.add), reads=[bb.tok, tb.tok], writes=[bb.tok])
                pw = tl([P, 10, 2, 32], "pw")
                kb.op("dve", lambda e: e.tensor_copy(out=pw[:, 0, 0], in_=W(4)), reads=[w.tok], writes=[pw.tok])
                kb.op("dve", lambda e: e.tensor_copy(out=pw[:, 0, 1], in_=W(3)), reads=[w.tok], writes=[pw.tok])
                for k in range(1, 10):
                    c_, s_ = pw[:, k - 1, 0], pw[:, k - 1, 1]
                    kb.op("dve", lambda e: e.tensor_tensor(out=W(11), in0=c_, in1=c_, op=ALU.mult), reads=[pw.tok, w.tok], writes=[w.tok])
                    kb.op("dve", lambda e: e.tensor_tensor(out=W(12), in0=s_, in1=s_, op=ALU.mult), reads=[pw.tok, w.tok], writes=[w.tok])
                    kb.op("dve", lambda e: e.tensor_tensor(out=pw[:, k, 0], in0=W(11), in1=W(12), op=ALU.subtract), reads=[w.tok, pw.tok], writes=[pw.tok])
                    kb.op("dve", lambda e: e.tensor_tensor(out=W(11), in0=c_, in1=s_, op=ALU.mult), reads=[pw.tok, w.tok], writes=[w.tok])
                    kb.op("dve", lambda e: e.tensor_scalar(out=pw[:, k, 1], in0=W(11), scalar1=2.0, scalar2=None, op0=ALU.mult), reads=[w.tok, pw.tok], writes=[pw.tok])
                prm.append(dict(w=w, bb=bb, pw=pw, s0=s0))
            class Res:
                pass
            RS = []
            for d in range(2):
                r_ = Res()
                r_.pool = "A" if d == 0 else "B"
                r_.Fc, r_.Fs, r_.Fsn, r_.d0 = tl([P, TT], "Fc"), tl([P, TT], "Fs"), tl([P, TT], "Fsn"), tl([P, TT], "d0")
                r_.Fcn = tl([P, TT], "Fcn")
                r_.E = [tl([P, P], "Eb%d" % i) for i in range(2)]
                r_.Bp = [tl([P, P], "Bp%d" % i, F32R) for i in range(2)]
                r_.Cp = [tl([P, P], "Cp%d" % i, F32R) for i in range(2)]
                for e_ in r_.E:
                    kb.op("dve", lambda e: e.memset(e_[:], 0.0), writes=[e_.tok])
                r_.wk = Rot(kb, st, 6, [P, TT], F32, name="s5wk")
                r_.pk = Rot(kb, st, 8, [P, TT], F32R, name="s5pk")
                r_.sreR = Rot(kb, st, 2, [P, TT], F32R, name="sre")
                r_.nsiR = Rot(kb, st, 2, [P, TT], F32R, name="nsi")
                r_.cin = tl([P, 4], "cin")
                r_.tmpc = tl([P, 4], "tmpc")
                r_.pend = []
                RS.append(r_)

            def build_stream(q, d, Y):
                R_ = RS[d]
                pr = prm[d]
                w, bb, pw, s0 = pr["w"], pr["bb"], pr["pw"], pr["s0"]
                Fc, Fs, Fsn, d0, E, Bp, Cp = R_.Fc, R_.Fs, R_.Fsn, R_.d0, R_.E, R_.Bp, R_.Cp
                Fcn = R_.Fcn
                cin, tmpc, wk, pk, pend = R_.cin, R_.tmpc, R_.wk, R_.pk, R_.pend
                L = []

                def OP(eng, fn, reads=(), writes=()):
                    L.append(lambda: kb.op(eng, fn, reads=reads, writes=writes))

                def DMA(q_, out, in_, reads=(), writes=()):
                    L.append(lambda: kb.dma(q_, out, in_, reads=reads, writes=writes))

                def flush():
                    def f():
                        while pend:
                            pend.pop(0)()
                    L.append(f)

                flush()
                for ri in range(2):
                    for g2 in range(2):
                        g8 = 2 * (q % 4) + g2
                        OP("dve", lambda e, ri=ri, g2=g2, g8=g8: e.tensor_copy(out=E[ri][g2 * 64:(g2 + 1) * 64, g8 * 16:(g8 + 1) * 16],
                                                                             in_=bb[g2 * 64:(g2 + 1) * 64, ri, q, :]), reads=[bb.tok], writes=[E[ri].tok])
                    pb, pt = self.bank(R_.pool)
                    OP("pe", lambda e, ri=ri, pb=pb: e.transpose(pb[:, 0:P], E[ri][:], self.ident[:]), reads=[E[ri].tok, self.ident.tok], writes=[pt])
                    OP("act", lambda e, ri=ri, pb=pb: e.activation(out=Bp[ri][:], in_=pb[:, 0:P], func=AF.Copy), reads=[pt], writes=[Bp[ri].tok])
                    for g2 in range(2):
                        g8 = 2 * (q % 4) + g2
                        OP("dve", lambda e, ri=ri, g2=g2, g8=g8: e.memset(E[ri][g2 * 64:(g2 + 1) * 64, g8 * 16:(g8 + 1) * 16], 0.0), writes=[E[ri].tok])
                    DMA("pool", Cp[ri][:], self.s5c[l, d, ri, q], writes=[Cp[ri].tok])
                OP("dve", lambda e: e.memset(Fc[:, 0:1], 1.0), writes=[Fc.tok])
                OP("dve", lambda e: e.memset(Fs[:, 0:1], 0.0), writes=[Fs.tok])
                for k in range(9):
                    n_ = 1 << k
                    pc_, ps_ = pw[:, k, 0, q:q + 1], pw[:, k, 1, q:q + 1]
                    t1, t2 = wk.next(), wk.next()
                    OP("dve", lambda e, t1=t1, n_=n_, ps_=ps_: e.tensor_scalar(out=t1[:, 0:n_], in0=Fs[:, 0:n_], scalar1=ps_, scalar2=None, op0=ALU.mult),
                       reads=[Fs.tok, pw.tok], writes=[t1.tok])
                    OP("dve", lambda e, t2=t2, n_=n_, ps_=ps_: e.tensor_scalar(out=t2[:, 0:n_], in0=Fc[:, 0:n_], scalar1=ps_, scalar2=None, op0=ALU.mult),
                       reads=[Fc.tok, pw.tok], writes=[t2.tok])
                    OP("dve", lambda e, t1=t1, n_=n_, pc_=pc_: e.scalar_tensor_tensor(out=Fc[:, n_:2 * n_], in0=Fc[:, 0:n_], scalar=pc_, in1=t1[:, 0:n_],
                                                                                  op0=ALU.mult, op1=ALU.subtract), reads=[Fc.tok, pw.tok, t1.tok], writes=[Fc.tok])
                    OP("dve", lambda e, t2=t2, n_=n_, pc_=pc_: e.scalar_tensor_tensor(out=Fs[:, n_:2 * n_], in0=Fs[:, 0:n_], scalar=pc_, in1=t2[:, 0:n_],
                                                                                  op0=ALU.mult, op1=ALU.add), reads=[Fs.tok, pw.tok, t2.tok], writes=[Fs.tok])
                OP("dve", lambda e: e.tensor_scalar(out=Fsn[:], in0=Fs[:], scalar1=-1.0, scalar2=None, op0=ALU.mult), reads=[Fs.tok], writes=[Fsn.tok])
                OP("dve", lambda e: e.tensor_scalar(out=Fcn[:], in0=Fc[:], scalar1=-1.0, scalar2=None, op0=ALU.mult), reads=[Fc.tok], writes=[Fcn.tok])
                OP("dve", lambda e: e.tensor_scalar(out=d0[:], in0=self.keep[:], scalar1=w[:, 1, q:q + 1], scalar2=None, op0=ALU.mult),
                   reads=[self.keep.tok, w.tok], writes=[d0.tok])

                def crot(src_re, src_im, cc_, ss_, rd):
                    OP("dve", lambda e: e.tensor_scalar(out=tmpc[:, 0:1], in0=src_im, scalar1=ss_, scalar2=None, op0=ALU.mult), reads=rd + [pw.tok], writes=[tmpc.tok])
                    OP("dve", lambda e: e.scalar_tensor_tensor(out=cin[:, 2:3], in0=src_re, scalar=cc_, in1=tmpc[:, 0:1], op0=ALU.mult, op1=ALU.subtract),
                       reads=rd + [tmpc.tok, pw.tok], writes=[cin.tok])
                    OP("dve", lambda e: e.tensor_scalar(out=tmpc[:, 1:2], in0=src_re, scalar1=ss_, scalar2=None, op0=ALU.mult), reads=rd + [pw.tok], writes=[tmpc.tok])
                    OP("dve", lambda e: e.scalar_tensor_tensor(out=cin[:, 3:4], in0=src_im, scalar=cc_, in1=tmpc[:, 1:2], op0=ALU.mult, op1=ALU.add),
                       reads=rd + [tmpc.tok, pw.tok], writes=[cin.tok])
                crot(s0[:, 0, q:q + 1], s0[:, 1, q:q + 1], pw[:, 0, 0, q:q + 1], pw[:, 0, 1, q:q + 1], [s0.tok])
                for tg in range(NT):
                    if d == 0:
                        usl = uch[:, tg * TT:(tg + 1) * TT]
                        ysl = yacc[:, tg * TT:(tg + 1) * TT]
                    else:
                        hi = T - tg * TT
                        usl = uch[:, hi - TT:hi][:, ::-1]
                        ysl = yacc[:, hi - TT:hi][:, ::-1]
                    pbr, ptr = self.bank(R_.pool)
                    pbi, pti = self.bank(R_.pool)
                    OP("pe", lambda e, pbr=pbr, usl=usl: e.matmul(pbr, Bp[0][:], usl, start=True, stop=True), reads=[Bp[0].tok, uch.tok], writes=[ptr])
                    OP("pe", lambda e, pbi=pbi, usl=usl: e.matmul(pbi, Bp[1][:], usl, start=True, stop=True), reads=[Bp[1].tok, uch.tok], writes=[pti])
                    a1, a2, a3, a4 = wk.next(), wk.next(), wk.next(), wk.next()
                    OP("dve", lambda e, a1=a1, pbr=pbr: e.tensor_tensor(out=a1[:], in0=pbr, in1=Fc[:], op=ALU.mult), reads=[ptr, Fc.tok], writes=[a1.tok])
                    OP("dve", lambda e, a2=a2, pbi=pbi: e.tensor_tensor(out=a2[:], in0=pbi, in1=Fs[:], op=ALU.mult), reads=[pti, Fs.tok], writes=[a2.tok])
                    OP("dve", lambda e, a3=a3, pbi=pbi: e.tensor_tensor(out=a3[:], in0=pbi, in1=Fc[:], op=ALU.mult), reads=[pti, Fc.tok], writes=[a3.tok])
                    OP("dve", lambda e, a4=a4, pbr=pbr: e.tensor_tensor(out=a4[:], in0=pbr, in1=Fs[:], op=ALU.mult), reads=[ptr, Fs.tok], writes=[a4.tok])
                    OP("dve", lambda e, a1=a1, a2=a2: e.tensor_tensor(out=a1[:], in0=a1[:], in1=a2[:], op=ALU.add), reads=[a1.tok, a2.tok], writes=[a1.tok])
                    OP("dve", lambda e, a3=a3, a4=a4: e.tensor_tensor(out=a3[:], in0=a3[:], in1=a4[:], op=ALU.subtract), reads=[a3.tok, a4.tok], writes=[a3.tok])
                    OP("dve", lambda e, a1=a1, a2=a2: e.tensor_tensor_scan(out=a2[:], data0=d0[:], data1=a1[:], initial=cin[:, 2:3], op0=ALU.mult, op1=ALU.add),
                       reads=[d0.tok, a1.tok, cin.tok], writes=[a2.tok])
                    OP("dve", lambda e, a3=a3, a4=a4: e.tensor_tensor_scan(out=a4[:], data0=d0[:], data1=a3[:], initial=cin[:, 3:4], op0=ALU.mult, op1=ALU.add),
                       reads=[d0.tok, a3.tok, cin.tok], writes=[a4.tok])
                    flush()
                    if tg + 1 < NT:
                        crot(a2[:, TT - 1:TT], a4[:, TT - 1:TT], pw[:, 9, 0, q:q + 1], pw[:, 9, 1, q:q + 1], [a2.tok, a4.tok])
                    p1, p2, p3, p4 = pk.next(), pk.next(), pk.next(), pk.next()
                    OP("pool", lambda e, p1=p1, a2=a2: e.tensor_tensor(out=p1[:], in0=a2[:], in1=Fc[:], op=ALU.mult), reads=[a2.tok, Fc.tok], writes=[p1.tok])
                    OP("pool", lambda e, p2=p2, a4=a4: e.tensor_tensor(out=p2[:], in0=a4[:], in1=Fsn[:], op=ALU.mult), reads=[a4.tok, Fsn.tok], writes=[p2.tok])
                    OP("pool", lambda e, p3=p3, a2=a2: e.tensor_tensor(out=p3[:], in0=a2[:], in1=Fsn[:], op=ALU.mult), reads=[a2.tok, Fsn.tok], writes=[p3.tok])
                    OP("pool", lambda e, p4=p4, a4=a4: e.tensor_tensor(out=p4[:], in0=a4[:], in1=Fcn[:], op=ALU.mult), reads=[a4.tok, Fcn.tok], writes=[p4.tok])
                    nsg = TT // 256
                    sc_ = (slice(None), slice(255, None, 256))
                    pby, pty = self.bank(R_.pool)
                    for mi, (cp_, pp_) in enumerate(((Cp[0], p1), (Cp[0], p2), (Cp[1], p3), (Cp[1], p4))):
                        OP("pe", lambda e, pby=pby, cp_=cp_, pp_=pp_, mi=mi: e.matmul(pby, cp_[:], pp_[:], start=(mi == 0), stop=(mi == 3)),
                           reads=[cp_.tok, pp_.tok], writes=[pty])
                    OP("dve", lambda e, p1=p1, p2=p2, tg=tg: e.tensor_tensor(out=so[:, d, 0, q, tg * nsg:(tg + 1) * nsg], in0=p1.f32(sc_), in1=p2.f32(sc_), op=ALU.add),
                       reads=[p1.tok, p2.tok], writes=[so.tok])
                    OP("dve", lambda e, p3=p3, p4=p4, tg=tg: e.scalar_tensor_tensor(out=so[:, d, 1, q, tg * nsg:(tg + 1) * nsg], in0=p3.f32(sc_), scalar=-1.0, in1=p4.f32(sc_),
                                                                                op0=ALU.mult, op1=ALU.subtract), reads=[p3.tok, p4.tok], writes=[so.tok])
                    L.append(lambda pby=pby, pty=pty, ysl=ysl: pend.append(
                        lambda: kb.op("dve", lambda e: e.tensor_tensor(out=ysl, in0=pby, in1=ysl, op=ALU.add), reads=[pty, yacc.tok], writes=[yacc.tok])))
                return L

            for Y in range(8):
                kb.dma("pool", uch[:], self.zT.bitcast(F32R)[(cfg.ZC + Y) * P:(cfg.ZC + Y + 1) * P, :], writes=[uch.tok])
                kb.op("dve", lambda e: e.tensor_scalar(out=yacc[:], in0=uch.f32(slice(None)), scalar1=dcol[:, Y:Y + 1], scalar2=None, op0=ALU.mult),
                      reads=[uch.tok, dcol.tok], writes=[yacc.tok])
                for q in range(4 * Y, 4 * Y + 4):
                    LA, LB = build_stream(q, 0, Y), build_stream(q, 1, Y)
                    for i in range(max(len(LA), len(LB))):
                        if i < len(LA):
                            LA[i]()
                        if i < len(LB):
                            LB[i]()
                for r_ in RS:
                    while r_.pend:
                        r_.pend.pop(0)()
                for tg in range(NT):
                    ysl = yacc[:, tg * TT:(tg + 1) * TT]
                    a1, a2 = RS[0].wk.next(), RS[0].wk.next()
                    kb.op("act", lambda e: e.activation(out=a1[:], in_=ysl, func=AF.Square), reads=[yacc.tok], writes=[a1.tok])
                    kb.op("dve", lambda e: e.tensor_scalar(out=a1[:], in0=a1[:], scalar1=0.044715, scalar2=1.0, op0=ALU.mult, op1=ALU.add),
                          reads=[a1.tok], writes=[a1.tok])
                    kb.op("dve", lambda e: e.tensor_tensor(out=a1[:], in0=a1[:], in1=ysl, op=ALU.mult), reads=[a1.tok, yacc.tok], writes=[a1.tok])
                    kb.op("act", lambda e: e.activation(out=a2[:], in_=a1[:], func=AF.Tanh, scale=0.7978845608028654), reads=[a1.tok], writes=[a2.tok])
                    kb.op("dve", lambda e: e.tensor_scalar(out=a2[:], in0=a2[:], scalar1=0.5, scalar2=0.5, op0=ALU.mult, op1=ALU.add),
                          reads=[a2.tok], writes=[a2.tok])
                    kb.op("dve", lambda e: e.tensor_tensor(out=a2[:], in0=a2[:], in1=ysl, op=ALU.mult), reads=[a2.tok, yacc.tok], writes=[a2.tok])
                    kb.dma("sp", self.yc.bitcast(F32)[Y * P:(Y + 1) * P, tg * TT:(tg + 1) * TT], a2[:], reads=[a2.tok], writes=[self.dt("yc", Y, tg)])
            kb.dma("sp", self.s5o[l], so[:], reads=[so.tok], writes=[self.dt("s5o", l)])
            kb.barrier()

    def phase_rwkv(self, l):
        cfg, kb = self.cfg, self.kb
        T, NT, NSEG = cfg.T, cfg.NT, cfg.NSEG
        NCH = T // 64
        HS = (slice(0, 64), slice(64, 128))
        with ExitStack() as st:
            def tl(shape, name, dtype=F32):
                return Tile(kb, st, shape, dtype, name=name)
            sm = tl([P, 4, TT], "rwsm"); kb.dma("sp", sm[:], self.rwsm[:, :, :], writes=[sm.tok])
            cmk = tl([P, T], "rwcm"); kb.dma("sp", cmk[:], self.rwcm[:, :], writes=[cmk.tok])
            col = tl([P, 5, 8], "rwcol"); kb.dma("sp", col[:], self.rwcol[l], writes=[col.tok])
            mu = tl([P, 26], "rwmu"); kb.dma("sp", mu[:], self.rwmu[l], writes=[mu.tok])
            om = tl([P, 26], "rwom")
            kb.op("dve", lambda e: e.tensor_scalar(out=om[:], in0=mu[:], scalar1=-1.0, scalar2=1.0, op0=ALU.mult, op1=ALU.add), reads=[mu.tok], writes=[om.tok])
            oka = tl([P, 8], "rwoka")
            kb.op("dve", lambda e: e.tensor_scalar(out=oka[:], in0=col[:, 1, :], scalar1=-1.0, scalar2=1.0, op0=ALU.mult, op1=ALU.add), reads=[col.tok], writes=[oka.tok])
            w0 = tl([P, 2, 2, 8], "rww0"); kb.dma("sp", w0[:], self.rww0[l], writes=[w0.tok])
            w2 = tl([P, 2, RW], "rww2"); kb.dma("sp", w2[:], self.rww2[l], writes=[w2.tok])
            hb1 = tl([P, P], "hb1"); kb.dma("sp", hb1[:], self.hblk[0], writes=[hb1.tok])
            xraw = tl([P, T], "xraw")
            sacc = tl([P, T], "sacc")
            stmp = Rot(kb, st, 3, [P, TT], F32, name="stmp")

            def load_shifted(ch, dst):
                kb.dma("sp", xraw[:], self.zT[(cfg.ZA + ch) * P:(cfg.ZA + ch + 1) * P, :], writes=[xraw.tok])
                kb.op("dve", lambda e: e.memset(sacc[:], 0.0), writes=[sacc.tok])
                for oi, o in enumerate((-1, 1, -64, 64)):
                    for tg in range(NT):
                        lo, hi = tg * TT, (tg + 1) * TT
                        slo, shi = max(lo + o, 0), min(hi + o, T)
                        dlo, dhi = slo - o, shi - o
                        n_ = dhi - dlo
                        t = stmp.next()
                        kb.op("dve", lambda e: e.tensor_tensor(out=t[:, 0:n_], in0=xraw[:, slo:shi], in1=sm[:, oi, dlo - lo:dhi - lo], op=ALU.mult),
                              reads=[xraw.tok, sm.tok], writes=[t.tok])
                        kb.op("dve", lambda e: e.tensor_tensor(out=sacc[:, dlo:dhi], in0=sacc[:, dlo:dhi], in1=t[:, 0:n_], op=ALU.add),
                              reads=[sacc.tok, t.tok], writes=[sacc.tok])
                kb.op("dve", lambda e: e.tensor_scalar(out=xraw[:], in0=xraw[:], scalar1=om[:, ch:ch + 1], scalar2=None, op0=ALU.mult), reads=[xraw.tok, om.tok], writes=[xraw.tok])
                kb.op("dve", lambda e: e.scalar_tensor_tensor(out=dst[:], in0=sacc[:], scalar=mu[:, ch:ch + 1], in1=xraw[:], op0=ALU.mult, op1=ALU.add),
                      reads=[sacc.tok, mu.tok, xraw.tok], writes=[dst.tok])

            tw = tl([P, T], "rwtw")
            load_shifted(24, tw)
            kb.op("act", lambda e: e.activation(out=tw[0:64, :], in_=tw[0:64, :], func=AF.Tanh), reads=[tw.tok], writes=[tw.tok])
            rr, kk_, vv_ = tl([P, T], "rwr"), tl([P, T], "rwk"), tl([P, T], "rwv")
            kkn = tl([P, T], "rwkkn")
            ad, ldc, cum = tl([P, T], "rwad"), tl([P, T], "rwld"), tl([P, T], "rwcum")
            t1, t2 = tl([P, T], "rwt1"), tl([P, T], "rwt2")
            outr = Rot(kb, st, 3, [P, T], F32, name="rwout")
            gct = tl([P, NCH], "rwgct")
            for j in range(8):
                load_shifted(j, rr)
                load_shifted(8 + j, kk_)
                load_shifted(16 + j, vv_)
                kb.op("dve", lambda e: e.tensor_scalar(out=kkn[:], in0=kk_[:], scalar1=col[:, 0, j:j + 1], scalar2=None, op0=ALU.mult), reads=[kk_.tok, col.tok], writes=[kkn.tok])
                kb.op("act", lambda e: e.activation(out=t1[:], in_=kkn[:], func=AF.Square), reads=[kkn.tok], writes=[t1.tok])
                for tg in range(NT):
                    tc_ = slice(tg * TT, (tg + 1) * TT)
                    pb, pt = self.bank()
                    kb.op("pe", lambda e: e.matmul(pb, hb1[:], t1[:, tc_], start=True, stop=True), reads=[hb1.tok, t1.tok], writes=[pt])
                    kb.op("act", lambda e: e.activation(out=t2[:, tc_], in_=pb, func=AF.Sqrt, bias=self.epsc[:, 2:3], scale=1.0), reads=[pt, self.epsc.tok], writes=[t2.tok])
                kb.op("dve", lambda e: e.reciprocal(out=t2[:], in_=t2[:]), reads=[t2.tok], writes=[t2.tok])
                kb.op("dve", lambda e: e.tensor_tensor(out=kkn[:], in0=kkn[:], in1=t2[:], op=ALU.mult), reads=[kkn.tok, t2.tok], writes=[kkn.tok])
                kb.op("dve", lambda e: e.scalar_tensor_tensor(out=t1[:], in0=rr[:], scalar=col[:, 2, j:j + 1], in1=kk_[:], op0=ALU.mult, op1=ALU.mult),
                      reads=[rr.tok, col.tok, kk_.tok], writes=[t1.tok])
                bo = outr.next()
                for tg in range(NT):
                    tc_ = slice(tg * TT, (tg + 1) * TT)
                    pb, pt = self.bank()
                    kb.op("pe", lambda e: e.matmul(pb, hb1[:], t1[:, tc_], start=True, stop=True), reads=[hb1.tok, t1.tok], writes=[pt])
                    kb.op("dve", lambda e: e.tensor_tensor(out=bo[:, tc_], in0=pb, in1=vv_[:, tc_], op=ALU.mult), reads=[pt, vv_.tok], writes=[bo.tok])
                kb.dma("sp", self.rwbon[j * P:(j + 1) * P, :], bo[:], reads=[bo.tok], writes=[self.dt("rwbon", j)])
                for d in range(2):
                    R = (lambda ap: ap) if d == 0 else (lambda ap: ap[:, ::-1])
                    for tg in range(NT):
                        tc_ = slice(tg * TT, (tg + 1) * TT)
                        pb, pt = self.bank()
                        kb.op("pe", lambda e: e.matmul(pb, w2[0:64, d, j * P:(j + 1) * P], tw[0:64, tc_], start=True, stop=True), reads=[w2.tok, tw.tok], writes=[pt])
                        kb.op("act", lambda e: e.activation(out=ldc[:, tc_], in_=pb, func=AF.Sigmoid, bias=w0[:, d, 0, j:j + 1], scale=1.0), reads=[pt, w0.tok], writes=[ldc.tok])
                        pb2, pt2 = self.bank()
                        kb.op("pe", lambda e: e.matmul(pb2, w2[64:128, d, j * P:(j + 1) * P], tw[64:128, tc_], start=True, stop=True), reads=[w2.tok, tw.tok], writes=[pt2])
                        kb.op("act", lambda e: e.activation(out=ad[:, tc_], in_=pb2, func=AF.Sigmoid, bias=w0[:, d, 1, j:j + 1], scale=1.0), reads=[pt2, w0.tok], writes=[ad.tok])
                    kb.op("dve", lambda e: e.tensor_scalar(out=ldc[:], in0=ldc[:], scalar1=-0.6065306597126334, scalar2=None, op0=ALU.mult), reads=[ldc.tok], writes=[ldc.tok])
                    kb.op("dve", lambda e: e.tensor_tensor_scan(out=cum[:], data0=cmk[:], data1=R(ldc[:]), initial=0.0, op0=ALU.mult, op1=ALU.add),
                          reads=[cmk.tok, ldc.tok], writes=[cum.tok])
                    o_rt = outr.next()
                    kb.op("act", lambda e: e.activation(out=t1[:], in_=cum[:], func=AF.Exp), reads=[cum.tok], writes=[t1.tok])
                    kb.op("dve", lambda e: e.tensor_tensor(out=o_rt[:], in0=t1[:], in1=R(rr[:]), op=ALU.mult), reads=[t1.tok, rr.tok], writes=[o_rt.tok])
                    kb.dma("sp", self.rwp[d, 3, j * P:(j + 1) * P, :].bitcast(F32), o_rt[:], reads=[o_rt.tok], writes=[self.dt("rwp", d, 3, j)])
                    kb.op("dve", lambda e: e.tensor_copy(out=gct[:], in_=t1[:, 63::64]), reads=[t1.tok], writes=[gct.tok])
                    kb.dma("sp", self.rwgc[d, j * P:(j + 1) * P, :], gct[:], reads=[gct.tok], writes=[self.dt("rwgc", d, j)])
                    o_at = outr.next()
                    kb.op("dve", lambda e: e.tensor_tensor(out=t2[:], in0=cum[:], in1=R(ldc[:]), op=ALU.subtract), reads=[cum.tok, ldc.tok], writes=[t2.tok])
                    kb.op("act", lambda e: e.activation(out=t2[:], in_=t2[:], func=AF.Exp), reads=[t2.tok], writes=[t2.tok])
                    kb.op("dve", lambda e: e.scalar_tensor_tensor(out=o_at[:], in0=t2[:], scalar=-1.0, in1=R(kkn[:]), op0=ALU.mult, op1=ALU.mult),
                          reads=[t2.tok, kkn.tok], writes=[o_at.tok])
                    kb.dma("sp", self.rwp[d, 0, j * P:(j + 1) * P, :].bitcast(F32), o_at[:], reads=[o_at.tok], writes=[self.dt("rwp", d, 0, j)])
                    kb.op("act", lambda e: e.activation(out=t1[:], in_=cum[:], func=AF.Exp, scale=-1.0), reads=[cum.tok], writes=[t1.tok])
                    o_bt = outr.next()
                    kb.op("dve", lambda e: e.tensor_tensor(out=t2[:], in0=R(kkn[:]), in1=R(ad[:]), op=ALU.mult), reads=[kkn.tok, ad.tok], writes=[t2.tok])
                    kb.op("dve", lambda e: e.tensor_tensor(out=o_bt[:], in0=t2[:], in1=t1[:], op=ALU.mult), reads=[t2.tok, t1.tok], writes=[o_bt.tok])
                    kb.dma("sp", self.rwp[d, 1, j * P:(j + 1) * P, :].bitcast(F32), o_bt[:], reads=[o_bt.tok], writes=[self.dt("rwp", d, 1, j)])
                    o_kt = outr.next()
                    kb.op("dve", lambda e: e.tensor_scalar(out=t2[:], in0=R(ad[:]), scalar1=col[:, 1, j:j + 1], scalar2=oka[:, j:j + 1], op0=ALU.mult, op1=ALU.add),
                          reads=[ad.tok, col.tok, oka.tok], writes=[t2.tok])
                    kb.op("dve", lambda e: e.tensor_tensor(out=t2[:], in0=t2[:], in1=R(kk_[:]), op=ALU.mult), reads=[t2.tok, kk_.tok], writes=[t2.tok])
                    kb.op("dve", lambda e: e.tensor_tensor(out=o_kt[:], in0=t2[:], in1=t1[:], op=ALU.mult), reads=[t2.tok, t1.tok], writes=[o_kt.tok])
                    kb.dma("sp", self.rwp[d, 2, j * P:(j + 1) * P, :].bitcast(F32), o_kt[:], reads=[o_kt.tok], writes=[self.dt("rwp", d, 2, j)])
                    o_v = outr.next()
                    kb.op("act", lambda e: e.activation(out=o_v[:], in_=R(vv_[:]), func=AF.Copy), reads=[vv_.tok], writes=[o_v.tok])
                    kb.dma("sp", self.rwp[d, 4, j * P:(j + 1) * P, :].bitcast(F32), o_v[:], reads=[o_v.tok], writes=[self.dt("rwp", d, 4, j)])
            kb.barrier()
        with ExitStack() as st:
            def tl(shape, name, dtype=F32):
                return Tile(kb, st, shape, dtype, name=name)
            H = 64
            trm = tl([H, 3, 64], "rwtri")
            for i in range(3):
                kb.dma("sp", trm[:, i, :], self.rwtri[i][0:64, :], writes=[trm.tok])
            kcol = tl([P, 1], "rwkcol"); kb.dma("sp", kcol[:], self.ssdkeep[:, :], writes=[kcol.tok])
            Sst = tl([H, 16, 64], "rwS", F32R)
            gcs = tl([H, 16, NCH], "rwgcs")
            slab = [Rot(kb, st, 2, [H, 16, 64], F32R, name="rwsl%d" % q) for q in range(5)]
            tk = [Rot(kb, st, 2, [H, 16, 64], F32R, name="rwtk%d" % q) for q in range(3)]
            Am = [Rot(kb, st, 2, [H, 16, 64], F32R, name="rwA%d" % q) for q in range(3)]
            Ak = Rot(kb, st, 2, [H, 16, 64], F32R, name="rwAk")
            AkT = Rot(kb, st, 2, [H, 16, 64], F32R, name="rwAkT")
            Tm = Rot(kb, st, 2, [H, 16, 64], F32R, name="rwTm")
            Wt = Rot(kb, st, 2, [H, 16, 64], F32R, name="rwWt")
            Ut = Rot(kb, st, 2, [H, 16, 64], F32R, name="rwUt")
            ost = Rot(kb, st, 2, [H, 16, 64], F32, name="rwost")

            def bc16(ap2):
                return ap2.unsqueeze(1).to_broadcast([H, 16, 64])

            def mm16(psb, ptk, lhs_fn, rhs_fn, reads, first=True, last=True):
                for b_ in range(16):
                    kb.op("pe", lambda e: e.matmul(psb[0:H, b_ * 64:(b_ + 1) * 64], lhs_fn(b_), rhs_fn(b_), start=first, stop=last), reads=reads, writes=ptk)

            def mm16g(psb, ptk, terms):
                n = len(terms)
                for b_ in range(16):
                    for ti, (lt, rt__) in enumerate(terms):
                        kb.op("pe", lambda e: e.matmul(psb[0:H, b_ * 64:(b_ + 1) * 64], lt[:, b_, :], rt__[:, b_, :], start=(ti == 0), stop=(ti == n - 1)),
                              reads=[lt.tok, rt__.tok], writes=ptk)

            def v3(pb):
                return pb[0:H, :].rearrange("p (b t) -> p b t", t=64)

            def dview(ap2d):
                return ap2d.rearrange("(b k) x -> k b x", k=64)

            for d in range(2):
                kb.dma("pool", Sst[:], self.rws0[l, d], writes=[Sst.tok])
                kb.dma("sp", gcs[:], dview(self.rwgc[d]), reads=[self.dt("rwgc", d, 0)], writes=[gcs.tok])
                for c in range(NCH):
                    cs = slice(c * 64, (c + 1) * 64)
                    sl = [r_.next() for r_ in slab]
                    for q in range(5):
                        kb.dma("pool", sl[q][:], dview(self.rwp[d, q])[:, :, cs], writes=[sl[q].tok])
                    at_, bt_, kt_, rt_, vs_ = sl
                    tks = []
                    for q, src in enumerate((bt_, kt_, vs_)):
                        pb, pt = self.bank2()
                        for b_ in range(16):
                            kb.op("pe", lambda e: e.transpose(pb[0:H, b_ * 64:(b_ + 1) * 64], src[:, b_, :].bitcast(F32), self.ident[0:H, 0:H]), reads=[src.tok, self.ident.tok], writes=pt)
                        t_ = tk[q].next()
                        if q % 2:
                            kb.op("act", lambda e: e.activation(out=t_[:], in_=v3(pb), func=AF.Copy), reads=pt, writes=[t_.tok])
                        else:
                            kb.op("dve", lambda e: e.tensor_copy(out=t_[:], in_=v3(pb)), reads=pt, writes=[t_.tok])
                        tks.append(t_)
                    Btk, Ktk, Vtk = tks

                    def amat(lhs, rhs, mask_i, dst):
                        pb, pt = self.bank2()
                        mm16(pb, pt, lambda b_: lhs[:, b_, :], lambda b_: rhs[:, b_, :], [lhs.tok, rhs.tok])
                        kb.op("dve", lambda e: e.tensor_tensor(out=dst[:], in0=v3(pb), in1=bc16(trm[:, mask_i, :]), op=ALU.mult), reads=pt + [trm.tok], writes=[dst.tok])
                    A0, A0T = Ak.next(), AkT.next()
                    amat(bt_, at_, 0, A0)
                    amat(at_, bt_, 1, A0T)
                    Aak, Arb, Ark = Am[0].next(), Am[1].next(), Am[2].next()
                    amat(kt_, at_, 0, Aak)
                    amat(bt_, rt_, 2, Arb)
                    amat(kt_, rt_, 2, Ark)
                    Tc = Tm.next()
                    kb.op("dve", lambda e: e.tensor_tensor(out=Tc[:], in0=A0.f32(slice(None)), in1=bc16(self.ident[0:H, 0:H]), op=ALU.add), reads=[A0.tok, self.ident.tok], writes=[Tc.tok])
                    Ap, ApT = A0, A0T
                    for lev in range(1, 6):
                        An, AnT = Ak.next(), AkT.next()
                        pb, pt = self.bank2()
                        mm16(pb, pt, lambda b_: ApT[:, b_, :], lambda b_: Ap[:, b_, :], [Ap.tok, ApT.tok])
                        pb2, pt2 = self.bank2()
                        mm16(pb2, pt2, lambda b_: Ap[:, b_, :], lambda b_: ApT[:, b_, :], [Ap.tok, ApT.tok])
                        kb.op("dve", lambda e: e.tensor_copy(out=An[:], in_=v3(pb)), reads=pt, writes=[An.tok])
                        kb.op("act", lambda e: e.activation(out=AnT[:], in_=v3(pb2), func=AF.Copy), reads=pt2, writes=[AnT.tok])
                        pb3, pt3 = self.bank2()
                        mm16(pb3, pt3, lambda b_: AnT[:, b_, :], lambda b_: Tc[:, b_, :], [AnT.tok, Tc.tok])
                        Tn = Tm.next()
                        kb.op("dve", lambda e: e.tensor_tensor(out=Tn[:], in0=v3(pb3), in1=Tc.f32(slice(None)), op=ALU.add), reads=pt3 + [Tc.tok], writes=[Tn.tok])
                        Tc, Ap, ApT = Tn, An, AnT
                    pbw, ptw = self.bank2()
                    mm16g(pbw, ptw, [(at_, Sst), (Aak, Vtk)])
                    W_ = Wt.next()
                    kb.op("act", lambda e: e.activation(out=W_[:], in_=v3(pbw), func=AF.Copy), reads=ptw, writes=[W_.tok])
                    pbu, ptu = self.bank2()
                    mm16(pbu, ptu, lambda b_: Tc[:, b_, :], lambda b_: W_[:, b_, :], [Tc.tok, W_.tok])
                    U_ = Ut.next()
                    kb.op("dve", lambda e: e.tensor_copy(out=U_[:], in_=v3(pbu)), reads=ptu, writes=[U_.tok])
                    pbo, pto = self.bank2()
                    mm16g(pbo, pto, [(Sst, rt_), (U_, Arb), (Vtk, Ark)])
                    pbs, pts = self.bank2()
                    mm16g(pbs, pts, [(Btk, U_), (Ktk, Vtk)])
                    o_ = ost.next()
                    if d == 0:
                        kb.op("act", lambda e: e.activation(out=o_[:], in_=v3(pbo), func=AF.Copy), reads=pto, writes=[o_.tok])
                        tcs = cs
                    else:
                        kb.op("dve", lambda e: e.tensor_copy(out=o_[:, :, ::-1], in_=v3(pbo)), reads=pto, writes=[o_.tok])
                        tcs = slice(T - (c + 1) * 64, T - c * 64)
                    kb.dma("sp", dview(self.rwoT[d])[:, :, tcs], o_[:], reads=[o_.tok], writes=[self.dt("rwoT", d, c)])
                    kb.op("dve", lambda e: e.tensor_tensor(out=Sst[:], in0=v3(pbs), in1=Sst.f32(slice(None)), op=ALU.add), reads=pts + [Sst.tok], writes=[Sst.tok])
                    kb.op("dve", lambda e: e.tensor_tensor(out=Sst[:], in0=Sst.f32(slice(None)), in1=gcs[:, :, c:c + 1].to_broadcast([H, 16, 64]), op=ALU.mult),
                          reads=[Sst.tok, gcs.tok], writes=[Sst.tok])
                    if c % 4 == 3:
                        seg = c // 4
                        kb.dma("sp", self.rwo[l, d, seg], Sst.f32(slice(None)), reads=[Sst.tok], writes=[self.dt("rwo", l, d, seg)])
                        kb.op("dve", lambda e: e.tensor_scalar(out=Sst[:], in0=Sst.f32(slice(None)), scalar1=kcol[0:H, 0:1], scalar2=None, op0=ALU.mult), reads=[Sst.tok, kcol.tok], writes=[Sst.tok])
            kb.barrier()
        with ExitStack() as st:
            def tl(shape, name, dtype=F32):
                return Tile(kb, st, shape, dtype, name=name)
            hb64 = tl([P, P], "hb64"); kb.dma("sp", hb64[:], self.hblk[1], writes=[hb64.tok])
            col = tl([P, 5, 8], "rwcol2"); kb.dma("sp", col[:], self.rwcol[l], writes=[col.tok])
            g2 = tl([P, RW], "rwg2"); kb.dma("sp", g2[:], self.rwg2[l], writes=[g2.tok])
            sgl = tl([P, T], "rwsgl")
            kb.dma("sp", sgl[:], self.zT[(cfg.ZA + 25) * P:(cfg.ZA + 26) * P, :], writes=[sgl.tok])
            sm = tl([P, 4, TT], "rwsm2"); kb.dma("sp", sm[:], self.rwsm[:, :, :], writes=[sm.tok])
            mu = tl([P, 26], "rwmu2"); kb.dma("sp", mu[:], self.rwmu[l], writes=[mu.tok])
            om = tl([P, 1], "rwom2")
            kb.op("dve", lambda e: e.tensor_scalar(out=om[:], in0=mu[:, 25:26], scalar1=-1.0, scalar2=1.0, op0=ALU.mult, op1=ALU.add), reads=[mu.tok], writes=[om.tok])
            sacc = tl([P, T], "rwsacc2")
            pt_ = Rot(kb, st, 8, [P, TT], F32, name="rwpt")
            kb.op("dve", lambda e: e.memset(sacc[:], 0.0), writes=[sacc.tok])
            for oi, o in enumerate((-1, 1, -64, 64)):
                for tg in range(NT):
                    lo, hi = tg * TT, (tg + 1) * TT
                    slo, shi = max(lo + o, 0), min(hi + o, T)
                    dlo, dhi = slo - o, shi - o
                    n_ = dhi - dlo
                    t = pt_.next()
                    kb.op("dve", lambda e: e.tensor_tensor(out=t[:, 0:n_], in0=sgl[:, slo:shi], in1=sm[:, oi, dlo - lo:dhi - lo], op=ALU.mult), reads=[sgl.tok, sm.tok], writes=[t.tok])
                    kb.op("dve", lambda e: e.tensor_tensor(out=sacc[:, dlo:dhi], in0=sacc[:, dlo:dhi], in1=t[:, 0:n_], op=ALU.add), reads=[sacc.tok, t.tok], writes=[sacc.tok])
            kb.op("dve", lambda e: e.tensor_scalar(out=sgl[:], in0=sgl[:], scalar1=om[:, 0:1], scalar2=None, op0=ALU.mult), reads=[sgl.tok, om.tok], writes=[sgl.tok])
            kb.op("dve", lambda e: e.scalar_tensor_tensor(out=sgl[:], in0=sacc[:], scalar=mu[:, 25:26], in1=sgl[:], op0=ALU.mult, op1=ALU.add), reads=[sacc.tok, mu.tok, sgl.tok], writes=[sgl.tok])
            kb.op("act", lambda e: e.activation(out=sgl[:], in_=sgl[:], func=AF.Sigmoid), reads=[sgl.tok], writes=[sgl.tok])
            lnb = tl([P, 1], "rwlneps")
            kb.op("dve", lambda e: e.memset(lnb[:], 64e-5), writes=[lnb.tok])
            for j in range(8):
                for tg in range(NT):
                    tc_ = slice(tg * TT, (tg + 1) * TT)
                    of_, ob_ = pt_.next(), pt_.next()
                    kb.dma("sp", of_[:], self.rwoT[0, j * P:(j + 1) * P, tc_], writes=[of_.tok])
                    kb.dma("sp", ob_[:], self.rwoT[1, j * P:(j + 1) * P, tc_], writes=[ob_.tok])
                    kb.op("dve", lambda e: e.tensor_tensor(out=of_[:], in0=of_[:], in1=ob_[:], op=ALU.add), reads=[of_.tok, ob_.tok], writes=[of_.tok])
                    pb, pt = self.bank()
                    kb.op("pe", lambda e: e.matmul(pb, hb64[:], of_[:], start=True, stop=True), reads=[hb64.tok, of_.tok], writes=[pt])
                    oc = pt_.next()
                    kb.op("dve", lambda e: e.tensor_tensor(out=oc[:], in0=of_[:], in1=pb, op=ALU.subtract), reads=[of_.tok, pt], writes=[oc.tok])
                    sq = pt_.next()
                    kb.op("act", lambda e: e.activation(out=sq[:], in_=oc[:], func=AF.Square), reads=[oc.tok], writes=[sq.tok])
                    pb2, pt2 = self.bank()
                    kb.op("pe", lambda e: e.matmul(pb2, hb64[:], sq[:], start=True, stop=True), reads=[hb64.tok, sq.tok], writes=[pt2])
                    kb.op("act", lambda e: e.activation(out=sq[:], in_=pb2, func=AF.Sqrt, bias=lnb[:, 0:1], scale=1.0), reads=[pt2, lnb.tok], writes=[sq.tok])
                    kb.op("dve", lambda e: e.reciprocal(out=sq[:], in_=sq[:]), reads=[sq.tok], writes=[sq.tok])
                    kb.op("dve", lambda e: e.scalar_tensor_tensor(out=oc[:], in0=oc[:], scalar=col[:, 3, j:j + 1], in1=sq[:], op0=ALU.mult, op1=ALU.mult), reads=[oc.tok, col.tok, sq.tok], writes=[oc.tok])
                    bn = pt_.next()
                    kb.dma("sp", bn[:], self.rwbon[j * P:(j + 1) * P, tc_], reads=[self.dt("rwbon", j)], writes=[bn.tok])
                    kb.op("dve", lambda e: e.scalar_tensor_tensor(out=oc[:], in0=oc[:], scalar=col[:, 4, j:j + 1], in1=bn[:], op0=ALU.add, op1=ALU.add), reads=[oc.tok, col.tok, bn.tok], writes=[oc.tok])
                    pb3, pt3 = self.bank()
                    kb.op("pe", lambda e: e.matmul(pb3, g2[:, j * P:(j + 1) * P], sgl[:, tc_], start=True, stop=True), reads=[g2.tok, sgl.tok], writes=[pt3])
                    kb.op("dve", lambda e: e.tensor_tensor(out=oc[:], in0=pb3, in1=oc[:], op=ALU.mult), reads=[pt3, oc.tok], writes=[oc.tok])
                    kb.dma("sp", self.ya.bitcast(F32)[j * P:(j + 1) * P, tc_], oc[:], reads=[oc.tok], writes=[self.dt("ya", j, tg)])
            kb.barrier()

    def phase_ssd(self, l):
        cfg, kb = self.cfg, self.kb
        T, NT, NSEG = cfg.T, cfg.NT, cfg.NSEG
        NC = T // P
        with ExitStack() as st:
            cm = Tile(kb, st, [P, 4, TT], F32, name="cm")
            kb.dma("sp", cm[:], self.cmT[:, :, :], writes=[cm.tok])
            cw = Tile(kb, st, [P, 12, 6], F32, name="cw")
            kb.dma("sp", cw[:], self.ssdcw[l], writes=[cw.tok])
            xin = Rot(kb, st, 2, [P, T], F32, name="xin")
            acc = Rot(kb, st, 2, [P, T], F32, name="cacc")
            tmp = Rot(kb, st, 3, [P, TT], F32, name="ctmp")
            for ch in range(12):
                x = xin.next()
                a = acc.next()
                kb.dma("sp", x[:], self.zT[(cfg.ZB + 8 + ch) * P:(cfg.ZB + 9 + ch) * P, :], writes=[x.tok])
                kb.op("dve", lambda e: e.tensor_scalar(out=a[:], in0=x[:], scalar1=cw[:, ch, 2:3], scalar2=cw[:, ch, 5:6], op0=ALU.mult, op1=ALU.add),
                      reads=[x.tok, cw.tok], writes=[a.tok])
                for oi, o in enumerate((-2, -1, 1, 2)):
                    for tg in range(NT):
                        lo, hi = tg * TT, (tg + 1) * TT
                        slo, shi = max(lo + o, 0), min(hi + o, T)
                        dlo, dhi = slo - o, shi - o
                        t = tmp.next()
                        n_ = dhi - dlo
                        kb.op("dve", lambda e: e.tensor_tensor(out=t[:, 0:n_], in0=x[:, slo:shi], in1=cm[:, oi, dlo - lo:dhi - lo], op=ALU.mult),
                              reads=[x.tok, cm.tok], writes=[t.tok])
                        kb.op("dve", lambda e: e.scalar_tensor_tensor(out=a[:, dlo:dhi], in0=t[:, 0:n_], scalar=cw[:, ch, (o + 2):(o + 3)], in1=a[:, dlo:dhi],
                                                                    op0=ALU.mult, op1=ALU.add), reads=[t.tok, a.tok, cw.tok], writes=[a.tok])
                kb.op("act", lambda e: e.activation(out=a[:], in_=a[:], func=AF.Silu), reads=[a.tok], writes=[a.tok])
                kb.dma("sp", self.xcs[ch * P:(ch + 1) * P, :], a[:], reads=[a.tok], writes=[self.dt("xcs", ch)])
            kb.barrier()
        with ExitStack() as st:
            def tl(shape, name, dtype=F32):
                return Tile(kb, st, shape, dtype, name=name)
            yacc = tl([P, 8, T], "ssdy")
            kb.op("dve", lambda e: e.memset(yacc[:], 0.0), writes=[yacc.tok])
            tri = tl([P, 2, P], "tri")
            for d in range(2):
                kb.dma("sp", tri[:, d, :], self.tri[d], writes=[tri.tok])
            dd_T = tl([64, T], "ddT")
            col = tl([64, 3], "ssdcol")
            kb.dma("sp", col[:], self.ssdcol[l], writes=[col.tok])
            for r in range(4):
                kb.dma("sp", dd_T[r * 16:(r + 1) * 16, :], self.zT[cfg.ZDT * P:cfg.ZDT * P + 16, :], writes=[dd_T.tok])
            mcol = tl([64, 1], "mcol")
            kb.op("act", lambda e: e.activation(out=mcol[:], in_=col[:, 1:2], func=AF.Exp), reads=[col.tok], writes=[mcol.tok])
            kb.op("dve", lambda e: e.tensor_tensor(out=mcol[:], in0=mcol[:], in1=col[:, 2:3], op=ALU.mult), reads=[mcol.tok, col.tok], writes=[mcol.tok])
            kb.op("act", lambda e: e.activation(out=dd_T[:], in_=dd_T[:], func=AF.Exp, bias=col[:, 0:1], scale=1.0), reads=[dd_T.tok, col.tok], writes=[dd_T.tok])
            kb.op("dve", lambda e: e.tensor_scalar(out=dd_T[:], in0=dd_T[:], scalar1=1.0, scalar2=None, op0=ALU.add), reads=[dd_T.tok], writes=[dd_T.tok])
            kb.op("act", lambda e: e.activation(out=dd_T[:], in_=dd_T[:], func=AF.Ln), reads=[dd_T.tok], writes=[dd_T.tok])
            kb.op("dve", lambda e: e.tensor_scalar(out=dd_T[:], in0=dd_T[:], scalar1=mcol[:, 0:1], scalar2=None, op0=ALU.mult), reads=[dd_T.tok, mcol.tok], writes=[dd_T.tok])
            kcol = tl([P, 1], "kcol")
            kb.dma("sp", kcol[:], self.ssdkeep[:, :], writes=[kcol.tok])
            S = [tl([P, 512], "ssdS%d" % g) for g in range(2)]
            xsl = Rot(kb, st, 2, [P, 8, P], F32, name="xsl")
            bcl = Rot(kb, st, 2, [P, 4, P], F32, name="bcl")
            xtok = Rot(kb, st, 2, [P, 16, 64], F32, name="xtok")
            btok = Rot(kb, st, 2, [P, 2, P], F32, name="btok")
            ddk = Rot(kb, st, 2, [P, 64], F32, name="ddk")
            sm = Rot(kb, st, 6, [P, 16], F32, name="ssm")
            gm = Rot(kb, st, 2, [P, 2, P], F32, name="gm")
            Mt = Rot(kb, st, 2, [P, 16, P], F32, name="Mt")
            xdt = Rot(kb, st, 2, [P, 16, 64], F32, name="xdt")
            xw = Rot(kb, st, 2, [P, 16, 64], F32, name="xw")
            ecr = Rot(kb, st, 3, [P, P], F32, name="ecr")
            yt = Rot(kb, st, 3, [P, P], F32, name="yt")
            for d in range(2):
                trd = tri[:, d, :]
                for g in range(2):
                    kb.dma("sp", S[g][:], self.ssds0[l, d, g], writes=[S[g].tok])
                order = range(NC) if d == 0 else range(NC - 1, -1, -1)
                for c in order:
                    cols = slice(c * P, (c + 1) * P)
                    xs_, bc_ = xsl.next(), bcl.next()
                    kb.dma("sp", xs_[:], self.xcs[0:1024, :].rearrange("(j p) t -> p j t", p=P)[:, :, cols], reads=[self.dt("xcs", 0)], writes=[xs_.tok])
                    kb.dma("sp", bc_[:], self.xcs[1024:1536, :].rearrange("(j p) t -> p j t", p=P)[:, :, cols], writes=[bc_.tok])
                    xt_, bt_, dk = xtok.next(), btok.next(), ddk.next()
                    for hb_ in range(2):
                        pb2, pt2 = self.bank2()
                        for jj in range(4):
                            j = hb_ * 4 + jj
                            kb.op("pe", lambda e: e.transpose(pb2[:, jj * P:(jj + 1) * P], xs_[:, j, :], self.ident[:]), reads=[xs_.tok, self.ident.tok], writes=[pt2[0], pt2[1]])
                        kb.op("act", lambda e: e.activation(out=xt_[:, hb_ * 8:(hb_ + 1) * 8, :], in_=pb2[:, 0:512].rearrange("p (h q) -> p h q", q=64), func=AF.Copy),
                              reads=[pt2[0], pt2[1]], writes=[xt_.tok])
                    pb, pt = self.bank()
                    for g in range(2):
                        kb.op("pe", lambda e: e.transpose(pb[:, g * P:(g + 1) * P], bc_[:, g, :], self.ident[:]), reads=[bc_.tok, self.ident.tok], writes=[pt])
                    kb.op("pe", lambda e: e.transpose(pb[:, 256:320], dd_T[:, cols], self.ident[0:64, 0:64]), reads=[dd_T.tok, self.ident.tok], writes=[pt])
                    kb.op("dve", lambda e: e.tensor_copy(out=bt_[:], in_=pb[:, 0:256].rearrange("p (g n) -> p g n", n=P)), reads=[pt], writes=[bt_.tok])
                    kb.op("dve", lambda e: e.tensor_copy(out=dk[:], in_=pb[:, 256:320]), reads=[pt], writes=[dk.tok])
                    dtc = dk[:, 32 * d:32 * d + 16]
                    dac = dk[:, 32 * d + 16:32 * d + 32]
                    pbc, ptc = self.bank()
                    kb.op("pe", lambda e: e.matmul(pbc[:, 0:16], trd, dac, start=True, stop=True), reads=[tri.tok, dk.tok], writes=[ptc])
                    kb.op("pe", lambda e: e.matmul(pbc[:, 16:32], self.ones[:], dac, start=True, stop=True), reads=[self.ones.tok, dk.tok], writes=[ptc])
                    cumc, wts, edec = sm.next(), sm.next(), sm.next()
                    kb.op("dve", lambda e: e.tensor_copy(out=cumc[:], in_=pbc[:, 0:16]), reads=[ptc], writes=[cumc.tok])
                    kb.op("dve", lambda e: e.tensor_tensor(out=wts[:], in0=pbc[:, 16:32], in1=cumc[:], op=ALU.subtract), reads=[ptc, cumc.tok], writes=[wts.tok])
                    kb.op("act", lambda e: e.activation(out=wts[:], in_=wts[:], func=AF.Exp), reads=[wts.tok], writes=[wts.tok])
                    kb.op("pool", lambda e: e.tensor_tensor(out=wts[:], in0=wts[:], in1=dtc, op=ALU.mult), reads=[wts.tok, dk.tok], writes=[wts.tok])
                    kb.op("act", lambda e: e.activation(out=edec[:], in_=pbc[:, 16:32], func=AF.Exp), reads=[ptc], writes=[edec.tok])
                    pbg, ptg = self.bank()
                    for g in range(2):
                        kb.op("pe", lambda e: e.matmul(pbg[:, g * P:(g + 1) * P], bc_[:, g, :], bc_[:, 2 + g, :], start=True, stop=True), reads=[bc_.tok], writes=[ptg])
                    gm_ = gm.next()
                    kb.op("dve", lambda e: e.tensor_tensor(out=gm_[:], in0=pbg[:, 0:256].rearrange("p (g t) -> p g t", t=P),
                                                           in1=tri[:, d:d + 1, :].to_broadcast([P, 2, P]), op=ALU.mult), reads=[ptg, tri.tok], writes=[gm_.tok])
                    M_ = Mt.next()
                    for hq in range(2):
                        pb2, pt2 = self.bank2()
                        for hh in range(8):
                            h = hq * 8 + hh
                            kb.op("pe", lambda e: e.matmul(pb2[:, hh * P:(hh + 1) * P], dac[:, h:h + 1].to_broadcast([P, P]), trd, start=True, stop=True),
                                  reads=[dk.tok, tri.tok], writes=[pt2[0], pt2[1]])
                        for hh in range(8):
                            h = hq * 8 + hh
                            kb.op("dve", lambda e: e.tensor_scalar(out=M_[:, h, :], in0=pb2[:, hh * P:(hh + 1) * P], scalar1=cumc[:, h:h + 1], scalar2=0.0,
                                                                   op0=ALU.subtract, op1=ALU.min), reads=[pt2[0], pt2[1], cumc.tok], writes=[M_.tok])
                    kb.op("act", lambda e: e.activation(out=M_[:], in_=M_[:], func=AF.Exp), reads=[M_.tok], writes=[M_.tok])
                    for g in range(2):
                        kb.op("pool", lambda e: e.tensor_tensor(out=M_[:, 8 * g:8 * g + 8, :], in0=M_[:, 8 * g:8 * g + 8, :],
                                                               in1=gm_[:, g:g + 1, :].to_broadcast([P, 8, P]), op=ALU.mult), reads=[M_.tok, gm_.tok], writes=[M_.tok])
                    xd_, xw_ = xdt.next(), xw.next()
                    kb.op("pool", lambda e: e.tensor_tensor(out=xd_[:], in0=xt_[:], in1=dtc.unsqueeze(2).to_broadcast([P, 16, 64]), op=ALU.mult),
                          reads=[xt_.tok, dk.tok], writes=[xd_.tok])
                    kb.op("pool", lambda e: e.tensor_tensor(out=xw_[:], in0=xt_[:], in1=wts[:].unsqueeze(2).to_broadcast([P, 16, 64]), op=ALU.mult),
                          reads=[xt_.tok, wts.tok], writes=[xw_.tok])
                    for j in range(8):
                        g = j // 4
                        pb, pt = self.bank()
                        for h2 in range(2):
                            kb.op("pe", lambda e: e.matmul(pb[h2 * 64:(h2 + 1) * 64, 0:P], dac[:, 2 * j + h2:2 * j + h2 + 1].to_broadcast([P, 64]), trd, start=True, stop=True),
                                  reads=[dk.tok, tri.tok], writes=[pt])
                        kb.op("pe", lambda e: e.matmul(pb[:, P:2 * P], S[g][:, (j % 4) * P:(j % 4 + 1) * P], bc_[:, 2 + g, :], start=True, stop=True),
                              reads=[S[g].tok, bc_.tok], writes=[pt])
                        for h2 in range(2):
                            kb.op("pe", lambda e: e.matmul(pb[h2 * 64:(h2 + 1) * 64, 2 * P:3 * P], xd_[:, 2 * j + h2, :], M_[:, 2 * j + h2, :], start=True, stop=True),
                                  reads=[xd_.tok, M_.tok], writes=[pt])
                        ec = ecr.next()
                        kb.op("act", lambda e: e.activation(out=ec[:], in_=pb[:, 0:P], func=AF.Exp), reads=[pt], writes=[ec.tok])
                        y_ = yt.next()
                        kb.op("dve", lambda e: e.tensor_tensor(out=y_[:], in0=pb[:, P:2 * P], in1=ec[:], op=ALU.mult), reads=[pt, ec.tok], writes=[y_.tok])
                        kb.op("dve", lambda e: e.tensor_tensor(out=y_[:], in0=pb[:, 2 * P:3 * P], in1=y_[:], op=ALU.add), reads=[pt, y_.tok], writes=[y_.tok])
                        kb.op("pool", lambda e: e.tensor_tensor(out=yacc[:, j, cols], in0=yacc[:, j, cols], in1=y_[:], op=ALU.add), reads=[yacc.tok, y_.tok], writes=[yacc.tok])
                    seg_end = (c % 2 == 1) if d == 0 else (c % 2 == 0)
                    for g in range(2):
                        pb, pt = self.bank()
                        kb.op("pe", lambda e: e.matmul(pb, bt_[:, g, :], xw_[:, 8 * g:8 * g + 8, :], start=True, stop=True), reads=[bt_.tok, xw_.tok], writes=[pt])
                        kb.op("pool", lambda e: e.tensor_tensor(out=S[g][:].rearrange("p (h q) -> p h q", q=64), in0=S[g][:].rearrange("p (h q) -> p h q", q=64),
                                                               in1=edec[:, 8 * g:8 * g + 8].unsqueeze(2).to_broadcast([P, 8, 64]), op=ALU.mult),
                              reads=[S[g].tok, edec.tok], writes=[S[g].tok])
                        kb.op("dve", lambda e: e.tensor_tensor(out=S[g][:], in0=pb, in1=S[g][:], op=ALU.add), reads=[pt, S[g].tok], writes=[S[g].tok])
                        if seg_end:
                            seg = c // 2
                            kb.dma("sp", self.ssdo[l, d, seg, g], S[g][:], reads=[S[g].tok], writes=[self.dt("ssdo", l, d, seg, g)])
                            kb.op("dve", lambda e: e.tensor_scalar(out=S[g][:], in0=S[g][:], scalar1=kcol[:, 0:1], scalar2=None, op0=ALU.mult),
                                  reads=[S[g].tok, kcol.tok], writes=[S[g].tok])
            Dc = tl([P, 8], "ssdDc")
            gc = tl([P, 8], "ssdgc")
            kb.dma("sp", Dc[:], self.ssdD[l], writes=[Dc.tok])
            kb.dma("sp", gc[:], self.ssdg[l], writes=[gc.tok])
            ld = Rot(kb, st, 4, [P, TT], F32, name="sld")
            rs = tl([P, TT], "srs")
            for tg in range(NT):
                tc_ = slice(tg * TT, (tg + 1) * TT)
                pbn, ptn = self.bank()
                for j in range(8):
                    xj, zj = ld.next(), ld.next()
                    kb.dma("sp", xj[:], self.xcs[j * P:(j + 1) * P, tc_], writes=[xj.tok])
                    kb.dma("sp", zj[:], self.zT[(cfg.ZB + j) * P:(cfg.ZB + j + 1) * P, tc_], writes=[zj.tok])
                    kb.op("dve", lambda e: e.scalar_tensor_tensor(out=xj[:], in0=xj[:], scalar=Dc[:, j:j + 1], in1=yacc[:, j, tc_], op0=ALU.mult, op1=ALU.add),
                          reads=[xj.tok, Dc.tok, yacc.tok], writes=[xj.tok])
                    kb.op("dve", lambda e: e.tensor_tensor(out=yacc[:, j, tc_], in0=xj[:], in1=zj[:], op=ALU.mult), reads=[xj.tok, zj.tok], writes=[yacc.tok])
                    kb.op("act", lambda e: e.activation(out=zj[:], in_=yacc[:, j, tc_], func=AF.Square), reads=[yacc.tok], writes=[zj.tok])
                    kb.op("pe", lambda e: e.matmul(pbn, self.ones[:], zj[:], start=(j == 0), stop=(j == 7)), reads=[zj.tok, self.ones.tok], writes=[ptn])
                t = ld.next()
                kb.op("act", lambda e: e.activation(out=t[:], in_=pbn, func=AF.Sqrt, bias=self.epsc[:, 0:1], scale=1.0 / SW), reads=[ptn, self.epsc.tok], writes=[t.tok])
                kb.op("dve", lambda e: e.reciprocal(out=rs[:], in_=t[:]), reads=[t.tok], writes=[rs.tok])
                for j in range(8):
                    o = ld.next()
                    kb.op("dve", lambda e: e.scalar_tensor_tensor(out=o[:], in0=yacc[:, j, tc_], scalar=gc[:, j:j + 1], in1=rs[:], op0=ALU.mult, op1=ALU.mult),
                          reads=[yacc.tok, gc.tok, rs.tok], writes=[o.tok])
                    kb.dma("sp", self.yb.bitcast(F32)[j * P:(j + 1) * P, tc_], o[:], reads=[o.tok], writes=[self.dt("yb", j, tg)])
            kb.barrier()


def prep_common(cfg, inp):
    NK, NJ, DEPTH = cfg.NK, cfg.NJ, cfg.DEPTH
    d = {}
    d["wmodn"] = inp["w_mod"]
    d["bmod"] = np.stack([colv(inp["b_mod"][l]) for l in range(DEPTH)])
    d["normg"] = np.stack([np.concatenate([colv(inp["norm_g"][l, i]) for i in range(3)], axis=1) for l in range(DEPTH)])
    d["fng"] = colv(inp["final_norm_g"])
    d["w1"] = np.stack([np.stack([blk(inp["ffn_w_in"][l, w]) for w in range(2)]) for l in range(DEPTH)])
    d["w2"] = np.stack([np.stack([blk(inp["ffn_w_out"][l, w]) for w in range(2)]) for l in range(DEPTH)])
    return d


def gp_layout(a):
    g, p = a.shape[0], a.shape[1]
    rest = a.shape[2:]
    b = a.reshape((32, 2, 64) + rest)
    b = np.moveaxis(b, 0, 2)
    return np.ascontiguousarray(b.reshape((128, 32) + rest))


def prep_mixer(cfg, inp):
    NK, DEPTH = cfg.NK, cfg.DEPTH
    d = {}
    wins = []
    for l in range(DEPTH):
        W = inp["w_in"][l]
        Wn = np.concatenate([W[:, 0:3328], W[:, 3328:3328 + 2560], W[:, 5904:6928], W[:, 6928:], W[:, 5888:5904],
                             np.zeros((W.shape[0], 112), np.float32)], axis=1)
        wins.append(blk(Wn))
    d["win"] = np.stack(wins)
    d["wpa"] = np.stack([blk(inp["w_proj_a"][l]) for l in range(DEPTH)])
    d["wpb"] = np.stack([blk(inp["w_proj_b"][l]) for l in range(DEPTH)])
    d["wpc"] = np.stack([blk(inp["w_proj_c"][l]) for l in range(DEPTH)])
    d["wo"] = np.stack([blk(inp["w_out"][l]) for l in range(DEPTH)])
    d["identd"] = np.eye(P, dtype=np.float32)
    lam = np.zeros((DEPTH, 2, P, 3, 32), np.float32)
    sb = np.zeros((DEPTH, 2, P, 2, 32, 16), np.float32)
    sc = np.zeros((DEPTH, 2, 2, 32, P, P), np.float32)
    for l in range(DEPTH):
        for dd in range(2):
            lam[l, dd, :, 0] = gp_layout(inp["s5_lambda_re"][l, dd])
            lam[l, dd, :, 1] = gp_layout(inp["s5_lambda_im"][l, dd])
            lam[l, dd, :, 2] = gp_layout(np.repeat(inp["s5_log_dt"][l, dd][:, None], 64, axis=1))
            sb[l, dd, :, 0] = gp_layout(inp["s5_b_re"][l, dd])
            sb[l, dd, :, 1] = gp_layout(inp["s5_b_im"][l, dd])
            for ri, key in enumerate(("s5_c_re", "s5_c_im")):
                C = inp[key][l, dd]
                for q in range(32):
                    for g2 in range(2):
                        g8 = 2 * (q % 4) + g2
                        sc[l, dd, ri, q, g2 * 64:(g2 + 1) * 64, g8 * 16:(g8 + 1) * 16] = C[2 * q + g2].T
    d["s5lam"], d["s5b"], d["s5c"] = lam, sb, sc
    d["s5d"] = np.stack([colv(inp["s5_d"][l]) for l in range(DEPTH)])
    rc = np.zeros((DEPTH, P, 5, 8), np.float32)
    rw0 = np.zeros((DEPTH, P, 2, 2, 8), np.float32)
    rw2 = np.zeros((DEPTH, P, 2, RW), np.float32)
    for l in range(DEPTH):
        rc[l, :, 0] = colv(inp["rwkv_k_k"][l])
        rc[l, :, 1] = colv(inp["rwkv_k_a"][l])
        rc[l, :, 2] = colv(inp["rwkv_r_k"][l].reshape(-1))
        rc[l, :, 3] = colv(inp["rwkv_ln_g"][l])
        rc[l, :, 4] = colv(inp["rwkv_ln_b"][l])
        for dd in range(2):
            rw0[l, :, dd, 0] = colv(inp["rwkv_w0"][l, dd])
            rw0[l, :, dd, 1] = colv(inp["rwkv_a0"][l, dd])
            rw2[l, 0:64, dd] = inp["rwkv_w2"][l, dd]
            rw2[l, 64:128, dd] = inp["rwkv_a2"][l, dd]
    d["rwcol"], d["rww0"], d["rww2"] = rc, rw0, rw2
    d["rwmu"] = np.stack([colv(inp["rwkv_mu"][l]) for l in range(DEPTH)])
    d["rwg2"] = np.ascontiguousarray(inp["rwkv_g2"])
    hb = np.zeros((2, P, P), np.float32)
    hb[0, 0:64, 0:64] = 1.0
    hb[0, 64:, 64:] = 1.0
    hb[1] = hb[0] / 64.0
    d["hblk"] = hb
    i_ = np.arange(64)[:, None]
    t_ = np.arange(64)[None, :]
    rt = np.stack([(i_ < t_), (i_ > t_), (i_ <= t_)]).astype(np.float32)
    d["rwtri"] = np.concatenate([rt, rt], axis=1)
    cmk = np.ones((P, cfg.T), np.float32)
    cmk[:, 0::64] = 0.0
    d["rwcm"] = cmk
    tri = np.zeros((2, P, P), np.float32)
    tri[0] = np.triu(np.ones((P, P), np.float32))
    tri[1] = np.tril(np.ones((P, P), np.float32))
    d["tri"] = tri
    cw = np.zeros((DEPTH, P, 12, 6), np.float32)
    scol = np.zeros((DEPTH, 64, 3), np.float32)
    for l in range(DEPTH):
        w = inp["ssd_conv_w"][l]
        for j in range(5):
            cw[l, :, :, j] = colv(w[j])
        cw[l, :, :, 5] = colv(inp["ssd_conv_b"][l])
        for dd in range(2):
            scol[l, 32 * dd:32 * dd + 16, 0] = inp["ssd_dt_bias"][l, dd]
            scol[l, 32 * dd + 16:32 * dd + 32, 0] = inp["ssd_dt_bias"][l, dd]
            scol[l, 32 * dd + 16:32 * dd + 32, 1] = inp["ssd_a_log"][l, dd]
            scol[l, 32 * dd:32 * dd + 16, 2] = 1.0
            scol[l, 32 * dd + 16:32 * dd + 32, 2] = -1.0
    d["ssdcw"], d["ssdcol"] = cw, scol
    d["ssdD"] = np.stack([colv(np.repeat(inp["ssd_d"][l], 64)) for l in range(DEPTH)])
    d["ssdg"] = np.stack([colv(inp["ssd_norm_g"][l]) for l in range(DEPTH)])
    return d


def core_mixer_inputs(cfg, inp, c, d):
    DEPTH = cfg.DEPTH
    prompt = c < cfg.NPC
    keep = np.ones((P, TT), np.float32)
    if prompt:
        keep[:, 0::256] = 0.0
    d["keepT"] = keep
    s0 = np.zeros((DEPTH, 2, P, 2, 32), np.float32)
    if not prompt:
        b = c - cfg.NPC
        for l in range(DEPTH):
            for dd in range(2):
                s0[l, dd, :, 0] = gp_layout(inp["state_s5_re"][b, l, dd])
                s0[l, dd, :, 1] = gp_layout(inp["state_s5_im"][b, l, dd])
    d["s5s0"] = s0
    cm = np.ones((P, 4, TT), np.float32)
    if prompt:
        for oi, o in enumerate((-2, -1, 1, 2)):
            for t in range(TT):
                if (t + o) // 256 != t // 256:
                    cm[:, oi, t] = 0.0
    d["cmT"] = cm
    smk = np.zeros((P, 4, TT), np.float32)
    tt_ = np.arange(TT)
    if prompt:
        smk[:, 0] = np.where(tt_ % 256 != 0, 0.5, 0.0)
        smk[:, 1] = np.where(tt_ % 256 != 255, 0.5, 0.0)
    else:
        smk[:, 0] = np.where(tt_ % 64 != 0, 0.25, 0.0)
        smk[:, 1] = np.where(tt_ % 64 != 63, 0.25, 0.0)
        smk[:, 2] = 0.25
        smk[:, 3] = 0.25
    d["rwsm"] = smk
    rs0 = np.zeros((DEPTH, 2, 64, 16, 64), np.float32)
    if not prompt:
        b = c - cfg.NPC
        sr = inp["state_rwkv"][b]
        for l in range(DEPTH):
            for dd in range(2):
                rs0[l, dd] = sr[l, dd].transpose(2, 0, 1)
    d["rws0"] = rs0
    d["ssdkeep"] = np.full((P, 1), 0.0 if prompt else 1.0, np.float32)
    ss0 = np.zeros((DEPTH, 2, 2, P, 512), np.float32)
    if not prompt:
        b = c - cfg.NPC
        st_ = inp["state_ssd"][b]
        for l in range(DEPTH):
            for dd in range(2):
                for g in range(2):
                    ss0[l, dd, g] = st_[l, dd, 8 * g:8 * g + 8].transpose(2, 0, 1).reshape(P, 512)
    d["ssds0"] = ss0


def core_inputs(cfg, inp, common, c):
    d = dict(common)
    T = cfg.T
    if c < cfg.NPC:
        ns = T // 256
        x = inp["x_prompt"][c * ns:(c + 1) * ns].reshape(T, cfg.DM)
        cond = inp["c_ctx"]
    else:
        b = c - cfg.NPC
        x = inp["x_sample"][b]
        cond = inp["c"][b]
    d["xT"] = np.ascontiguousarray(x.T)
    d["cond"] = colv(cond)
    core_mixer_inputs(cfg, inp, c, d)
    return d


def run(cfg, inp):
    b = Builder(cfg)
    nc = b.build()
    common = prep_common(cfg, inp)
    common.update(prep_mixer(cfg, inp))
    n = cfg.NPC + cfg.NSC
    maps = [core_inputs(cfg, inp, common, c) for c in range(n)]
    for m in maps:
        for k in list(m.keys()):
            if k not in b.din:
                del m[k]
            else:
                m[k] = np.ascontiguousarray(m[k], dtype=np.float32)
    res = run_bass_kernel_spmd(nc, maps, core_ids=list(range(n)))
    return res.results


def assemble(cfg, inp, results):
    T, DEPTH, NSEG = cfg.T, cfg.DEPTH, cfg.NSEG
    ns = T // 256
    yp = np.concatenate([results[c]["yT"].T.reshape(ns, 256, cfg.DM) for c in range(cfg.NPC)], axis=0)
    ys = np.stack([results[cfg.NPC + b]["yT"].T for b in range(cfg.NSC)], axis=0)
    nb = cfg.NPC * NSEG
    st_rwkv = np.zeros((nb, DEPTH, 2, 16, 64, 64), np.float32)
    st_ssd = np.zeros((nb, DEPTH, 2, 16, 64, 128), np.float32)
    s5 = [np.zeros((nb, DEPTH, 2, 64, 64), np.float32) for _ in range(2)]
    for c in range(cfg.NPC):
        r = results[c]
        if "s5o" in r:
            o = r["s5o"]
            o = o.reshape(DEPTH, 2, 64, 2, 2, 32, NSEG)
            o = o.transpose(6, 0, 3, 4, 5, 1, 2)
            o = o.reshape(NSEG, DEPTH, 2, 2, 64, 64).copy()
            o[:, :, 1] = o[::-1, :, 1]
            for ri in range(2):
                s5[ri][c * NSEG:(c + 1) * NSEG] = o[:, :, :, ri]
        if "rwo" in r:
            o = r["rwo"]
            o = o.transpose(2, 0, 1, 4, 5, 3).copy()
            o[:, :, 1] = o[::-1, :, 1]
            st_rwkv[c * NSEG:(c + 1) * NSEG] = o
        if "ssdo" in r:
            o = r["ssdo"]
            o = o.reshape(DEPTH, 2, NSEG, 2, P, 8, 64).transpose(2, 0, 1, 3, 5, 6, 4)
            st_ssd[c * NSEG:(c + 1) * NSEG] = o.reshape(NSEG, DEPTH, 2, 16, 64, 128)
    return yp, ys, st_rwkv, st_ssd, s5[0], s5[1]


def kernel(**inputs):
    cfg = Cfg()
    inp = {k: np.asarray(v) for k, v in inputs.items()}
    results = run(cfg, inp)
    return assemble(cfg, inp, results)
```

```python
import numpy as np
from contextlib import ExitStack
import concourse.bass as bass
import concourse.mybir as mybir
from concourse.bass_utils import run_bass_kernel_spmd

F32 = mybir.dt.float32
F32R = mybir.dt.float32r
AF = mybir.ActivationFunctionType
ALU = mybir.AluOpType
P = 128
TT = 512
NDS = 40

RW = 1024
RH = 64
SW = 1024
SXBC = 1536
S5W = 1024
EPS = 1e-6


class Cfg:
    def __init__(self, DM=2048, DFF=5504, DEPTH=2, T=2048, NPC=4, NSC=4, mix=(1, 1, 1)):
        self.DM, self.DFF, self.DEPTH, self.T, self.NPC, self.NSC = DM, DFF, DEPTH, T, NPC, NSC
        self.NK = DM // P
        self.NJ = DFF // P
        self.NT = T // TT
        self.NSEG = T // 256
        self.mix = mix
        self.ZA, self.ZB, self.ZC = 0, 26, 46
        self.ZG = 54
        self.ZDT = 54 + 3 * self.NK
        self.NZ = self.ZDT + 1


class Tok:
    __slots__ = ("w", "r")

    def __init__(self):
        self.w = []
        self.r = {}


class KB:
    def __init__(self, nc):
        self.nc = nc
        self.E = {"pe": nc.tensor, "dve": nc.vector, "act": nc.scalar, "pool": nc.gpsimd, "sp": nc.sync}
        self.tick = {e: 0 for e in self.E}
        self.seen = {e: {} for e in self.E}
        self.stack = ExitStack()
        self.sem = {e: self.stack.enter_context(nc.semaphore("s_" + e)) for e in self.E}
        self.dsem = [self.stack.enter_context(nc.semaphore("d%d" % i)) for i in range(NDS)]
        self.dval = [0] * NDS
        self.drr = 0
        self.nid = 0
        self.ninstr = 0

    def name(self, s):
        self.nid += 1
        return "%s_%d" % (s, self.nid)

    def _wait(self, e, dep):
        kind, key, val = dep
        if kind == "e" and key == e and e == "pe":
            return
        k = (kind, key)
        if self.seen[e].get(k, 0) >= val:
            return
        self.seen[e][k] = val
        sem = self.sem[key] if kind == "e" else self.dsem[key]
        self.E[e].wait_ge(sem, val)
        self.ninstr += 1

    def _sync(self, e, reads, writes):
        for t in reads:
            for d in t.w:
                self._wait(e, d)
        for t in writes:
            for d in t.w:
                self._wait(e, d)
            for k, v in t.r.items():
                self._wait(e, (k[0], k[1], v))

    def _mark(self, me, reads, writes):
        for t in writes:
            t.w = [me]
            t.r = {}
        for t in reads:
            if not (len(t.w) == 1 and t.w[0] is me):
                k = (me[0], me[1])
                if t.r.get(k, 0) < me[2]:
                    t.r[k] = me[2]

    def join(self, dst, srcs):
        w = list(dst.w)
        for s_ in srcs:
            w.extend(s_.w)
        dst.w = w

    def op(self, e, fn, reads=(), writes=()):
        self._sync(e, reads, writes)
        ins = fn(self.E[e])
        self.tick[e] += 1
        ins.then_inc(self.sem[e], 1)
        self.ninstr += 1
        self._mark(("e", e, self.tick[e]), reads, writes)
        return ins

    def dma(self, q, out, in_, reads=(), writes=(), **kw):
        self._sync(q, reads, writes)
        s = self.drr
        self.drr = (self.drr + 1) % NDS
        if self.dval[s]:
            self._wait(q, ("d", s, self.dval[s]))
        ins = self.E[q].dma_start(out=out, in_=in_, **kw)
        self.dval[s] += 16
        ins.then_inc(self.dsem[s], 16)
        self.ninstr += 1
        self._mark(("d", s, self.dval[s]), reads, writes)
        return ins

    def barrier(self):
        for e in self.E:
            for e2 in self.E:
                if e2 != e and self.tick[e2]:
                    self._wait(e, ("e", e2, self.tick[e2]))
            for s in range(NDS):
                if self.dval[s]:
                    self._wait(e, ("d", s, self.dval[s]))


class Tile:
    def __init__(self, kb, stack, shape, dtype, ntok=1, name="t"):
        self.t = stack.enter_context(kb.nc.sbuf_tensor(kb.name(name), list(shape), dtype))
        self.toks = [Tok() for _ in range(ntok)]
        self.dtype = dtype

    @property
    def tok(self):
        return self.toks[0]

    def __getitem__(self, k):
        return self.t[k]

    def f32(self, k):
        return self.t[k].bitcast(F32)


class Rot:
    def __init__(self, kb, stack, n, shape, dtype, name="r"):
        self.tiles = [Tile(kb, stack, shape, dtype, name=name) for _ in range(n)]
        self.i = 0

    def next(self):
        t = self.tiles[self.i]
        self.i = (self.i + 1) % len(self.tiles)
        return t


def blk(W):
    K, M = W.shape
    return np.ascontiguousarray(W.reshape(K // P, P, M // P, P).transpose(2, 1, 0, 3)).reshape(M // P, P, (K // P) * P)


def colv(v):
    return np.ascontiguousarray(v.reshape(-1, P).T)


class Builder:
    def __init__(self, cfg):
        self.cfg = cfg
        self.nc = bass.Bass("TRN2", target_bir_lowering=False)
        self.kb = KB(self.nc)
        self.din = {}
        self.dout = {}
        self.dtok = {}

    def inp(self, name, shape, dtype=F32):
        self.din[name] = self.nc.dram_tensor(name, list(shape), dtype, kind="ExternalInput").ap()
        return self.din[name]

    def outp(self, name, shape, dtype=F32):
        self.dout[name] = self.nc.dram_tensor(name, list(shape), dtype, kind="ExternalOutput").ap()
        return self.dout[name]

    def scratch(self, name, shape, dtype=F32):
        if getattr(self.cfg, "debug", False) and dtype == F32:
            return self.outp(name, shape, dtype)
        return self.nc.dram_tensor(name, list(shape), dtype, kind="Internal").ap()

    def dt(self, *key):
        if key not in self.dtok:
            self.dtok[key] = Tok()
        return self.dtok[key]

    def build(self):
        cfg, nc, kb = self.cfg, self.nc, self.kb
        NK, NJ, T, NT, DEPTH = cfg.NK, cfg.NJ, cfg.T, cfg.NT, cfg.DEPTH
        self.xT = self.inp("xT", [cfg.DM, T], F32R)
        self.cond = self.inp("cond", [P, NK])
        self.wmodn = self.inp("wmodn", [DEPTH, cfg.DM, 9 * cfg.DM])
        self.bmod = self.inp("bmod", [DEPTH, P, 9 * NK])
        self.normg = self.inp("normg", [DEPTH, P, 3 * NK])
        self.fng = self.inp("fng", [P, NK])
        self.w1 = self.inp("w1", [DEPTH, 2, 2 * NJ, P, NK * P], F32R)
        self.w2 = self.inp("w2", [DEPTH, 2, NK, P, NJ * P], F32R)
        self.yT = self.outp("yT", [cfg.DM, T])
        self.xres = self.scratch("xres", [cfg.DM, T], F32R)
        self.mixer_decl()

        with ExitStack() as gs:
            self.gs = gs
            self.ps = [kb.stack.enter_context(nc.psum_tensor(kb.name("ps"), [P, 1024], F32)) for _ in range(4)]
            self.pstok = [[Tok(), Tok()] for _ in range(4)]
            self.psi = 0
            self.ones = Tile(kb, gs, [P, P], F32, name="ones")
            kb.op("dve", lambda e: e.memset(self.ones[:], 1.0), writes=[self.ones.tok])
            self.mod = Tile(kb, gs, [P, DEPTH, 9 * NK], F32, name="mod")
            self.modA = Tile(kb, gs, [P, DEPTH, 3 * NK], F32, name="modA")
            self.modG = Tile(kb, gs, [P, DEPTH, 3 * NK], F32, name="modG")
            self.fngt = Tile(kb, gs, [P, NK], F32, name="fng")
            kb.dma("sp", self.fngt[:], self.fng[:, :], writes=[self.fngt.tok])
            self.mixer_consts()
            self.phase_mod()
            src = self.xT
            for l in range(DEPTH):
                self.phase_ffn(l, 0, src)
                src = self.xres
                self.phase_mix(l)
                self.phase_ffn(l, 1, src)
            self.phase_final()
            kb.barrier()
        kb.stack.close()
        return nc

    def bank(self):
        i = self.psi
        self.psi = (self.psi + 1) % 8
        return self.ps[i // 2][:, (i % 2) * 512:(i % 2) * 512 + 512], self.pstok[i // 2][i % 2]

    def bank2(self):
        if self.psi % 2:
            self.psi = (self.psi + 1) % 8
        i = self.psi
        self.psi = (self.psi + 2) % 8
        return self.ps[i // 2], self.pstok[i // 2]

    def phase_mod(self):
        cfg, kb = self.cfg, self.kb
        NK, DEPTH = cfg.NK, cfg.DEPTH
        NM = 9 * NK
        NCOL = NM * P
        CG = 512 if NCOL % 512 == 0 else 256
        with ExitStack() as st:
            cnd = Tile(kb, st, [P, NK], F32, name="cnd")
            sc = Tile(kb, st, [P, NK], F32, name="scnd")
            bm = Tile(kb, st, [P, DEPTH, NM], F32, name="bm")
            rows = Rot(kb, st, 3, [1, CG], F32, name="mrow")
            ng = Tile(kb, st, [P, DEPTH, 3 * NK], F32, name="ng")
            wr = Rot(kb, st, 2, [P, NK, CG], F32, name="wm")
            kb.dma("sp", cnd[:], self.cond[:, :], writes=[cnd.tok])
            for l in range(DEPTH):
                kb.dma("sp", ng[:, l, :], self.normg[l], writes=[ng.tok])
                kb.dma("sp", bm[:, l, :], self.bmod[l], writes=[bm.tok])
            kb.op("act", lambda e: e.activation(out=sc[:], in_=cnd[:], func=AF.Silu), reads=[cnd.tok], writes=[sc.tok])
            MC = CG // P
            for l in range(DEPTH):
                wv = self.wmodn[l].rearrange("(k p) c -> p k c", p=P)
                for cg in range(NCOL // CG):
                    w = wr.next()
                    kb.dma("sp", w[:], wv[:, :, cg * CG:(cg + 1) * CG], writes=[w.tok])
                    pb, pt = self.bank()
                    for kc in range(NK):
                        kb.op("pe", lambda e: e.matmul(pb[0:1, 0:CG], sc[:, kc:kc + 1], w[:, kc, :], start=(kc == 0), stop=(kc == NK - 1)),
                              reads=[w.tok, sc.tok], writes=[pt])
                    row = rows.next()
                    kb.op("act", lambda e: e.activation(out=row[:], in_=pb[0:1, 0:CG], func=AF.Copy), reads=[pt], writes=[row.tok])
                    pb2, pt2 = self.bank()
                    for mm in range(MC):
                        kb.op("pe", lambda e: e.matmul(pb2[:, mm:mm + 1], row[0:1, mm * P:(mm + 1) * P], self.ones[0:1, 0:1], start=True, stop=True),
                              reads=[row.tok, self.ones.tok], writes=[pt2])
                    m0 = cg * MC
                    kb.op("dve", lambda e: e.tensor_tensor(out=self.mod[:, l, m0:m0 + MC], in0=pb2[:, 0:MC], in1=bm[:, l, m0:m0 + MC], op=ALU.add),
                          reads=[pt2, bm.tok], writes=[self.mod.tok])
                for i in range(3):
                    scs = self.mod[:, l, (3 * i + 1) * NK:(3 * i + 2) * NK]
                    kb.op("dve", lambda e: e.scalar_tensor_tensor(
                        out=self.modA[:, l, i * NK:(i + 1) * NK], in0=scs, scalar=1.0, in1=ng[:, l, i * NK:(i + 1) * NK],
                        op0=ALU.add, op1=ALU.mult), reads=[self.mod.tok, ng.tok], writes=[self.modA.tok])
                    gs_ = self.mod[:, l, (3 * i + 2) * NK:(3 * i + 3) * NK]
                    kb.op("dve", lambda e: e.tensor_scalar(
                        out=self.modG[:, l, i * NK:(i + 1) * NK], in0=gs_, scalar1=(1.0 if i == 1 else 0.5), scalar2=None,
                        op0=ALU.mult), reads=[self.mod.tok], writes=[self.modG.tok])
            kb.barrier()

    def norm_tile(self, hb, tmp, rstd, A, SH, out_dtype_r=True):
        cfg, kb = self.cfg, self.kb
        NK = cfg.NK
        pb, pt = self.bank()
        for kc in range(NK):
            t = tmp.next()
            kb.op("act", lambda e, t=t, kc=kc: e.activation(out=t[:], in_=hb.f32((slice(None), kc)), func=AF.Square),
                  reads=[hb.tok], writes=[t.tok])
            kb.op("pe", lambda e, t=t, kc=kc: e.matmul(pb, self.ones[:], t[:], start=(kc == 0), stop=(kc == NK - 1)),
                  reads=[t.tok, self.ones.tok], writes=[pt])
        t = tmp.next()
        kb.op("act", lambda e: e.activation(out=t[:], in_=pb, func=AF.Sqrt, bias=self.epsc[:, 0:1], scale=1.0 / cfg.DM),
              reads=[pt, self.epsc.tok], writes=[t.tok])
        kb.op("dve", lambda e: e.reciprocal(out=rstd[:], in_=t[:]), reads=[t.tok], writes=[rstd.tok])
        for kc in range(NK):
            t = tmp.next()
            kb.op("dve", lambda e, t=t, kc=kc: e.scalar_tensor_tensor(out=t[:], in0=hb.f32((slice(None), kc)), scalar=A[:, kc:kc + 1],
                                                                    in1=rstd[:], op0=ALU.mult, op1=ALU.mult),
                  reads=[hb.tok, rstd.tok, self.modA.tok, self.fngt.tok], writes=[t.tok])
            if SH is not None:
                kb.op("act", lambda e, t=t, kc=kc: e.activation(out=hb[:, kc], in_=t[:], func=AF.Identity, bias=SH[:, kc:kc + 1], scale=1.0),
                      reads=[t.tok, self.mod.tok], writes=[hb.tok])
            else:
                kb.op("act", lambda e, t=t, kc=kc: e.activation(out=hb[:, kc], in_=t[:], func=AF.Copy),
                      reads=[t.tok], writes=[hb.tok])

    def load_xtile(self, hb, src, tt):
        kb, cfg = self.kb, self.cfg
        sv = src.rearrange("(k p) t -> p k t", p=P)[:, :, tt * TT:(tt + 1) * TT]
        kb.dma("pool", hb[:], sv, reads=[self.dt("x", tt)], writes=[hb.tok])

    def phase_ffn(self, l, w, src):
        cfg, kb = self.cfg, self.kb
        NK, NJ, NT = cfg.NK, cfg.NJ, cfg.NT
        JH = (NJ + 1) // 2
        WSZ = max(NK, JH) * P
        A = self.modA[:, l, (2 * w) * NK:(2 * w + 1) * NK]
        SH = self.mod[:, l, (6 * w) * NK:(6 * w + 1) * NK]
        G = self.modG[:, l, (2 * w) * NK:(2 * w + 1) * NK]
        with ExitStack() as st:
            hb = Tile(kb, st, [P, NK, TT], F32R, name="hb")
            act = Tile(kb, st, [P, NJ, TT], F32R, ntok=NJ, name="act")
            wr = Rot(kb, st, 4, [P, WSZ], F32R, name="wf")
            tmp = Rot(kb, st, 3, [P, TT], F32, name="tmp")
            xc = Rot(kb, st, 3, [P, TT], F32, name="xc")
            rstd = Tile(kb, st, [P, TT], F32, name="rstd")
            srcf = src.bitcast(F32)
            xresf = self.xres.bitcast(F32)
            for tt in range(NT):
                self.load_xtile(hb, src, tt)
                self.norm_tile(hb, tmp, rstd, A, SH)
                for j in range(NJ):
                    pbs = []
                    for half in range(2):
                        wt = wr.next()
                        kb.dma("pool", wt[:, 0:NK * P], self.w1[l, w, half * NJ + j], writes=[wt.tok])
                        pb, pt = self.bank()
                        for kc in range(NK):
                            kb.op("pe", lambda e, wt=wt, kc=kc, pb=pb: e.matmul(pb, wt[:, kc * P:(kc + 1) * P], hb[:, kc],
                                                                              start=(kc == 0), stop=(kc == NK - 1)),
                                  reads=[wt.tok, hb.tok], writes=[pt])
                        pbs.append((pb, pt))
                    t = tmp.next()
                    kb.op("act", lambda e, t=t: e.activation(out=t[:], in_=pbs[0][0], func=AF.Silu), reads=[pbs[0][1]], writes=[t.tok])
                    kb.op("dve", lambda e, t=t, j=j: e.tensor_tensor(out=act[:, j], in0=pbs[1][0], in1=t[:], op=ALU.mult),
                          reads=[pbs[1][1], t.tok], writes=[act.toks[j]])
                for n in range(NK):
                    pb, pt = self.bank()
                    for hf in range(2):
                        j0, j1 = (0, JH) if hf == 0 else (JH, NJ)
                        wt = wr.next()
                        kb.dma("pool", wt[:, 0:(j1 - j0) * P], self.w2[l, w, n][:, j0 * P:j1 * P], writes=[wt.tok])
                        for j in range(j0, j1):
                            kb.op("pe", lambda e, wt=wt, j=j, j0=j0: e.matmul(pb, wt[:, (j - j0) * P:(j - j0 + 1) * P], act[:, j],
                                                                            start=(j == 0), stop=(j == NJ - 1)),
                                  reads=[wt.tok, act.toks[j]], writes=[pt])
                    x = xc.next()
                    kb.dma("sp", x[:], srcf[n * P:(n + 1) * P, tt * TT:(tt + 1) * TT], reads=[self.dt("x", tt)], writes=[x.tok])
                    kb.op("dve", lambda e, x=x, n=n: e.scalar_tensor_tensor(out=x[:], in0=pb, scalar=G[:, n:n + 1], in1=x[:],
                                                                          op0=ALU.mult, op1=ALU.add),
                          reads=[pt, x.tok, self.modG.tok], writes=[x.tok])
                    kb.dma("sp", xresf[n * P:(n + 1) * P, tt * TT:(tt + 1) * TT], x[:], reads=[x.tok], writes=[self.dt("xo", tt, n)])
                xt_ = self.dt("x", tt)
                xt_.w = []
                kb.join(xt_, [self.dt("xo", tt, n) for n in range(NK)])
            kb.barrier()

    def phase_final(self):
        cfg, kb = self.cfg, self.kb
        NK, NT = cfg.NK, cfg.NT
        with ExitStack() as st:
            hb = Tile(kb, st, [P, NK, TT], F32R, name="hbf")
            tmp = Rot(kb, st, 3, [P, TT], F32, name="tmpf")
            rstd = Tile(kb, st, [P, TT], F32, name="rstdf")
            for tt in range(NT):
                self.load_xtile(hb, self.xres, tt)
                self.norm_tile(hb, tmp, rstd, self.fngt, None)
                dv = self.yT.rearrange("(k p) t -> p k t", p=P)[:, :, tt * TT:(tt + 1) * TT]
                kb.dma("sp", dv, hb.f32(slice(None)), reads=[hb.tok], writes=[self.dt("y", tt)])
            kb.barrier()

    def mixer_decl(self):
        cfg = self.cfg
        NK, T, DEPTH, NSEG = cfg.NK, cfg.T, cfg.DEPTH, cfg.NSEG
        self.win = self.inp("win", [DEPTH, cfg.NZ, P, NK * P], F32R)
        self.zT = self.scratch("zT", [cfg.NZ * P, T])
        self.wpa = self.inp("wpa", [DEPTH, NK, P, 8 * P], F32R)
        self.wpb = self.inp("wpb", [DEPTH, NK, P, 8 * P], F32R)
        self.wpc = self.inp("wpc", [DEPTH, 2 * NK, P, 8 * P], F32R)
        self.wo = self.inp("wo", [DEPTH, NK, P, NK * P], F32R)
        self.ya = self.scratch("ya", [RW, T], F32R)
        self.yb = self.scratch("yb", [SW, T], F32R)
        self.yc = self.scratch("yc", [S5W, T], F32R)
        self.keepT = self.inp("keepT", [P, TT])
        self.rwp = self.scratch("rwp", [2, 5, RW, T], F32R)
        self.rwbon = self.scratch("rwbon", [RW, T])
        self.rwgc = self.scratch("rwgc", [2, RW, T // 64])
        self.rwsm = self.inp("rwsm", [P, 4, TT])
        self.rwcm = self.inp("rwcm", [P, T])
        self.rwcol = self.inp("rwcol", [DEPTH, P, 5, 8])
        self.rwmu = self.inp("rwmu", [DEPTH, P, 26])
        self.rww0 = self.inp("rww0", [DEPTH, P, 2, 2, 8])
        self.rww2 = self.inp("rww2", [DEPTH, P, 2, RW])
        self.rwg2 = self.inp("rwg2", [DEPTH, P, RW])
        self.hblk = self.inp("hblk", [2, P, P])
        self.rwtri = self.inp("rwtri", [3, P, 64])
        self.rws0 = self.inp("rws0", [DEPTH, 2, 64, 16, 64], F32R)
        self.rwo = self.outp("rwo", [DEPTH, 2, NSEG, 64, 16, 64])
        self.rwoT = self.scratch("rwoT", [2, RW, T])
        self.xcs = self.scratch("xcs", [SXBC, T])
        self.tri = self.inp("tri", [2, P, P])
        self.cmT = self.inp("cmT", [P, 4, TT])
        self.ssdcw = self.inp("ssdcw", [DEPTH, P, 12, 6])
        self.ssdcol = self.inp("ssdcol", [DEPTH, 64, 3])
        self.ssdD = self.inp("ssdD", [DEPTH, P, 8])
        self.ssdg = self.inp("ssdg", [DEPTH, P, 8])
        self.ssds0 = self.inp("ssds0", [DEPTH, 2, 2, P, 512])
        self.ssdkeep = self.inp("ssdkeep", [P, 1])
        self.ssdo = self.outp("ssdo", [DEPTH, 2, NSEG, 2, P, 512])
        self.s5lam = self.inp("s5lam", [DEPTH, 2, P, 3, 32])
        self.s5b = self.inp("s5b", [DEPTH, 2, P, 2, 32, 16])
        self.s5c = self.inp("s5c", [DEPTH, 2, 2, 32, P, P], F32R)
        self.s5d = self.inp("s5d", [DEPTH, P, 8])
        self.s5s0 = self.inp("s5s0", [DEPTH, 2, P, 2, 32])
        self.s5o = self.outp("s5o", [DEPTH, P, 2, 2, 32, NSEG])

    def mixer_consts(self):
        kb = self.kb
        self.epsc = Tile(kb, self.gs, [P, 4], F32, name="epsc")
        kb.op("dve", lambda e: e.memset(self.epsc[:, 0:1], EPS), writes=[self.epsc.tok])
        kb.op("dve", lambda e: e.memset(self.epsc[:, 1:2], float(np.pi / 2)), writes=[self.epsc.tok])
        kb.op("dve", lambda e: e.memset(self.epsc[:, 2:3], 1e-12), writes=[self.epsc.tok])
        kb.op("dve", lambda e: e.memset(self.epsc[:, 3:4], 0.0), writes=[self.epsc.tok])
        self.ident = Tile(kb, self.gs, [P, P], F32, name="ident")
        self.identd = self.inp("identd", [P, P])
        kb.dma("sp", self.ident[:], self.identd[:, :], writes=[self.ident.tok])
        self.keep = Tile(kb, self.gs, [P, TT], F32, name="keep")
        kb.dma("sp", self.keep[:], self.keepT[:, :], writes=[self.keep.tok])

    def phase_mix(self, l):
        cfg = self.cfg
        self.phase_inproj(l)
        if cfg.mix[0]:
            self.phase_rwkv(l)
        if cfg.mix[1]:
            self.phase_ssd(l)
        if cfg.mix[2]:
            self.phase_s5(l)
        self.phase_merge(l)

    def phase_inproj(self, l):
        cfg, kb = self.cfg, self.kb
        NK, NT = cfg.NK, cfg.NT
        A = self.modA[:, l, NK:2 * NK]
        SH = self.mod[:, l, 3 * NK:4 * NK]
        with ExitStack() as st:
            hb = Tile(kb, st, [P, NK, TT], F32R, name="hbi")
            wr = Rot(kb, st, 4, [P, NK * P], F32R, name="wi")
            tmp = Rot(kb, st, 3, [P, TT], F32, name="tmpi")
            stg = Rot(kb, st, 4, [P, TT], F32, name="stg")
            rstd = Tile(kb, st, [P, TT], F32, name="rstdi")
            for tt in range(NT):
                self.load_xtile(hb, self.xres, tt)
                self.norm_tile(hb, tmp, rstd, A, SH)
                for m in range(cfg.NZ):
                    wt = wr.next()
                    kb.dma("pool", wt[:], self.win[l, m], writes=[wt.tok])
                    pb, pt = self.bank()
                    for kc in range(NK):
                        kb.op("pe", lambda e: e.matmul(pb, wt[:, kc * P:(kc + 1) * P], hb[:, kc], start=(kc == 0), stop=(kc == NK - 1)),
                              reads=[wt.tok, hb.tok], writes=[pt])
                    s = stg.next()
                    if m >= cfg.ZG and m < cfg.ZDT:
                        kb.op("act", lambda e: e.activation(out=s[:], in_=pb, func=AF.Sigmoid), reads=[pt], writes=[s.tok])
                    elif m >= cfg.ZB and m < cfg.ZB + 8:
                        kb.op("act", lambda e: e.activation(out=s[:], in_=pb, func=AF.Silu), reads=[pt], writes=[s.tok])
                    elif m % 2:
                        kb.op("act", lambda e: e.activation(out=s[:], in_=pb, func=AF.Copy), reads=[pt], writes=[s.tok])
                    else:
                        kb.op("dve", lambda e: e.tensor_copy(out=s[:], in_=pb), reads=[pt], writes=[s.tok])
                    kb.dma("sp", self.zT[m * P:(m + 1) * P, tt * TT:(tt + 1) * TT], s[:], reads=[s.tok], writes=[self.dt("z", m, tt)])
            kb.barrier()

    def phase_merge(self, l):
        cfg, kb = self.cfg, self.kb
        NK, NT = cfg.NK, cfg.NT
        G = self.modG[:, l, NK:2 * NK]
        xresf = self.xres.bitcast(F32)
        with ExitStack() as st:
            ys = [Tile(kb, st, [P, 8, TT], F32R, name="ym%d" % i) for i in range(3)]
            mg = Tile(kb, st, [P, NK, TT], F32R, ntok=NK, name="mg")
            wr = Rot(kb, st, 4, [P, max(NK, 8) * P], F32R, name="wm")
            gt = Rot(kb, st, 4, [P, TT], F32, name="gt")
            tmp = Rot(kb, st, 6, [P, TT], F32, name="tmpm")
            xc = Rot(kb, st, 3, [P, TT], F32, name="xcm")
            srcs = [self.ya, self.yb, self.yc]
            for tt in range(NT):
                for i in range(3):
                    if cfg.mix[i]:
                        sv = srcs[i].rearrange("(k p) t -> p k t", p=P)[:, :, tt * TT:(tt + 1) * TT]
                        kb.dma("pool", ys[i][:], sv, writes=[ys[i].tok])
                for n in range(NK):
                    terms = []
                    for i, wsrc in ((0, self.wpa), (1, self.wpb)):
                        if not cfg.mix[i]:
                            continue
                        wt = wr.next()
                        kb.dma("pool", wt[:, 0:8 * P], wsrc[l, n], writes=[wt.tok])
                        pb, pt = self.bank()
                        for k in range(8):
                            kb.op("pe", lambda e: e.matmul(pb, wt[:, k * P:(k + 1) * P], ys[i][:, k], start=(k == 0), stop=(k == 7)),
                                  reads=[wt.tok, ys[i].tok], writes=[pt])
                        g = gt.next()
                        kb.dma("sp", g[:], self.zT[(cfg.ZG + i * NK + n) * P:(cfg.ZG + i * NK + n + 1) * P, tt * TT:(tt + 1) * TT], writes=[g.tok])
                        t = tmp.next()
                        kb.op("dve", lambda e: e.tensor_tensor(out=t[:], in0=pb, in1=g[:], op=ALU.mult), reads=[pt, g.tok], writes=[t.tok])
                        terms.append(t)
                    if cfg.mix[2]:
                        pbs = []
                        for hf in range(2):
                            wt = wr.next()
                            kb.dma("pool", wt[:, 0:8 * P], self.wpc[l, hf * NK + n], writes=[wt.tok])
                            pb, pt = self.bank()
                            for k in range(8):
                                kb.op("pe", lambda e: e.matmul(pb, wt[:, k * P:(k + 1) * P], ys[2][:, k], start=(k == 0), stop=(k == 7)),
                                      reads=[wt.tok, ys[2].tok], writes=[pt])
                            pbs.append((pb, pt))
                        g = gt.next()
                        kb.dma("sp", g[:], self.zT[(cfg.ZG + 2 * NK + n) * P:(cfg.ZG + 2 * NK + n + 1) * P, tt * TT:(tt + 1) * TT], writes=[g.tok])
                        sg = tmp.next()
                        kb.op("act", lambda e: e.activation(out=sg[:], in_=pbs[1][0], func=AF.Sigmoid), reads=[pbs[1][1]], writes=[sg.tok])
                        t = tmp.next()
                        kb.op("dve", lambda e: e.tensor_tensor(out=t[:], in0=pbs[0][0], in1=sg[:], op=ALU.mult), reads=[pbs[0][1], sg.tok], writes=[t.tok])
                        kb.op("dve", lambda e: e.tensor_tensor(out=t[:], in0=t[:], in1=g[:], op=ALU.mult), reads=[t.tok, g.tok], writes=[t.tok])
                        terms.append(t)
                    if not terms:
                        kb.op("dve", lambda e: e.memset(mg[:, n], 0.0), writes=[mg.toks[n]])
                    elif len(terms) == 1:
                        kb.op("dve", lambda e: e.tensor_copy(out=mg[:, n], in_=terms[0][:]), reads=[terms[0].tok], writes=[mg.toks[n]])
                    else:
                        for a_ in terms[2:]:
                            kb.op("dve", lambda e: e.tensor_tensor(out=terms[0][:], in0=terms[0][:], in1=a_[:], op=ALU.add),
                                  reads=[terms[0].tok, a_.tok], writes=[terms[0].tok])
                        kb.op("dve", lambda e: e.tensor_tensor(out=mg[:, n], in0=terms[0][:], in1=terms[1][:], op=ALU.add),
                              reads=[terms[0].tok, terms[1].tok], writes=[mg.toks[n]])
                for n in range(NK):
                    wt = wr.next()
                    kb.dma("pool", wt[:, 0:NK * P], self.wo[l, n], writes=[wt.tok])
                    pb, pt = self.bank()
                    for k in range(NK):
                        kb.op("pe", lambda e: e.matmul(pb, wt[:, k * P:(k + 1) * P], mg[:, k], start=(k == 0), stop=(k == NK - 1)),
                              reads=[wt.tok, mg.toks[k]], writes=[pt])
                    x = xc.next()
                    kb.dma("sp", x[:], xresf[n * P:(n + 1) * P, tt * TT:(tt + 1) * TT], reads=[self.dt("x", tt)], writes=[x.tok])
                    kb.op("dve", lambda e: e.scalar_tensor_tensor(out=x[:], in0=pb, scalar=G[:, n:n + 1], in1=x[:], op0=ALU.mult, op1=ALU.add),
                          reads=[pt, x.tok, self.modG.tok], writes=[x.tok])
                    kb.dma("sp", xresf[n * P:(n + 1) * P, tt * TT:(tt + 1) * TT], x[:], reads=[x.tok], writes=[self.dt("xo", tt, n)])
            kb.barrier()

    def phase_s5(self, l):
        cfg, kb = self.cfg, self.kb
        T, NT, NSEG = cfg.T, cfg.NT, cfg.NSEG
        V = lambda e: e
        with ExitStack() as st:
            def tl(shape, name, dtype=F32):
                return Tile(kb, st, shape, dtype, name=name)
            so = tl([P, 2, 2, 32, NSEG], "s5so")
            dcol = tl([P, 8], "s5dc")
            kb.dma("sp", dcol[:], self.s5d[l], writes=[dcol.tok])
            yacc = tl([P, T], "yacc")
            uch = tl([P, T], "uch", F32R)
            prm = []
            for d in range(2):
                lam = tl([P, 3, 32], "lam")
                kb.dma("sp", lam[:], self.s5lam[l, d], writes=[lam.tok])
                bq = tl([P, 2, 32, 16], "bq")
                kb.dma("sp", bq[:], self.s5b[l, d], writes=[bq.tok])
                s0 = tl([P, 2, 32], "s0")
                kb.dma("sp", s0[:], self.s5s0[l, d], writes=[s0.tok])
                w = tl([P, 16, 32], "s5w")
                def W(i):
                    return w[:, i, :]
                def tt_(o, a, b, op):
                    kb.op("dve", lambda e: e.tensor_tensor(out=o, in0=a, in1=b, op=op), reads=[w.tok, lam.tok], writes=[w.tok])
                def ts_(o, a, s1, op0, s2=None, op1=None):
                    if op1 is None:
                        kb.op("dve", lambda e: e.tensor_scalar(out=o, in0=a, scalar1=s1, scalar2=None, op0=op0), reads=[w.tok, lam.tok], writes=[w.tok])
                    else:
                        kb.op("dve", lambda e: e.tensor_scalar(out=o, in0=a, scalar1=s1, scalar2=s2, op0=op0, op1=op1), reads=[w.tok, lam.tok], writes=[w.tok])
                def ac_(o, a, f, bias=None, scale=1.0):
                    if bias is None:
                        kb.op("act", lambda e: e.activation(out=o, in_=a, func=f, scale=scale), reads=[w.tok, lam.tok], writes=[w.tok])
                    else:
                        kb.op("act", lambda e: e.activation(out=o, in_=a, func=f, bias=bias, scale=scale), reads=[w.tok, lam.tok, self.epsc.tok], writes=[w.tok])
                lre, lim, ldt = lam[:, 0, :], lam[:, 1, :], lam[:, 2, :]
                ac_(W(0), ldt, AF.Exp)
                tt_(W(1), lre, W(0), ALU.mult)
                ac_(W(1), W(1), AF.Exp)
                tt_(W(2), lim, W(0), ALU.mult)
                ac_(W(3), W(2), AF.Sin, scale=1.0 / 16)
                ac_(W(4), W(2), AF.Sin, bias=self.epsc[:, 1:2], scale=1.0 / 16)
                for _ in range(4):
                    tt_(W(5), W(3), W(4), ALU.mult)
                    tt_(W(6), W(4), W(4), ALU.mult)
                    tt_(W(7), W(3), W(3), ALU.mult)
                    tt_(W(4), W(6), W(7), ALU.subtract)
                    ts_(W(3), W(5), 2.0, ALU.mult)
                tt_(W(5), W(1), W(4), ALU.mult)
                tt_(W(6), W(1), W(3), ALU.mult)
                ts_(W(7), W(5), -1.0, ALU.add)
                tt_(W(8), lre, lre, ALU.mult)
                tt_(W(9), lim, lim, ALU.mult)
                tt_(W(8), W(8), W(9), ALU.add)
                kb.op("dve", lambda e: e.reciprocal(out=W(8), in_=W(8)), reads=[w.tok], writes=[w.tok])
                tt_(W(9), W(7), lre, ALU.mult)
                tt_(W(10), W(6), lim, ALU.mult)
                tt_(W(9), W(9), W(10), ALU.add)
                tt_(W(9), W(9), W(8), ALU.mult)
                tt_(W(10), W(6), lre, ALU.mult)
                tt_(W(11), W(7), lim, ALU.mult)
                tt_(W(10), W(10), W(11), ALU.subtract)
                tt_(W(10), W(10), W(8), ALU.mult)
                bb = tl([P, 2, 32, 16], "bb")
                tb = tl([P, 32, 16], "tb")
                qre_b = W(9).to_broadcast([P, 32, 16]) if False else None
                def bc(i):
                    return w[:, i, :].unsqueeze(2).to_broadcast([P, 32, 16])
                kb.op("dve", lambda e: e.tensor_tensor(out=bb[:, 0], in0=bq[:, 0], in1=bc(9), op=ALU.mult), reads=[bq.tok, w.tok], writes=[bb.tok])
                kb.op("dve", lambda e: e.tensor_tensor(out=tb[:], in0=bq[:, 1], in1=bc(10), op=ALU.mult), reads=[bq.tok, w.tok], writes=[tb.tok])
                kb.op("dve", lambda e: e.tensor_tensor(out=bb[:, 0], in0=bb[:, 0], in1=tb[:], op=ALU.subtract), reads=[bb.tok, tb.tok], writes=[bb.tok])
                kb.op("dve", lambda e: e.tensor_tensor(out=bb[:, 1], in0=bq[:, 1], in1=bc(9), op=ALU.mult), reads=[bq.tok, w.tok], writes=[bb.tok])
                kb.op("dve", lambda e: e.tensor_tensor(out=tb[:], in0=bq[:, 0], in1=bc(10), op=ALU.mult), reads=[bq.tok, w.tok], writes=[tb.tok])
                kb.op("dve", lambda e: e.tensor_tensor(out=bb[:, 1], in0=bb[:, 1], in1=tb[:], op=ALU.add), reads=[bb.tok, tb.tok], writes=[bb.tok])
                pw = tl([P, 10, 2, 32], "pw")
                kb.op("dve", lambda e: e.tensor_copy(out=pw[:, 0, 0], in_=W(4)), reads=[w.tok], writes=[pw.tok])
                kb.op("dve", lambda e: e.tensor_copy(out=pw[:, 0, 1], in_=W(3)), reads=[w.tok], writes=[pw.tok])
                for k in range(1, 10):
                    c_, s_ = pw[:, k - 1, 0], pw[:, k - 1, 1]
                    kb.op("dve", lambda e: e.tensor_tensor(out=W(11), in0=c_, in1=c_, op=ALU.mult), reads=[pw.tok, w.tok], writes=[w.tok])
                    kb.op("dve", lambda e: e.tensor_tensor(out=W(12), in0=s_, in1=s_, op=ALU.mult), reads=[pw.tok, w.tok], writes=[w.tok])
                    kb.op("dve", lambda e: e.tensor_tensor(out=pw[:, k, 0], in0=W(11), in1=W(12), op=ALU.subtract), reads=[w.tok, pw.tok], writes=[pw.tok])
                    kb.op("dve", lambda e: e.tensor_tensor(out=W(11), in0=c_, in1=s_, op=ALU.mult), reads=[pw.tok, w.tok], writes=[w.tok])
                    kb.op("dve", lambda e: e.tensor_scalar(out=pw[:, k, 1], in0=W(11), scalar1=2.0, scalar2=None, op0=ALU.mult), reads=[w.tok, pw.tok], writes=[pw.tok])
                prm.append(dict(w=w, bb=bb, pw=pw, s0=s0))
            Fc, Fs = tl([P, TT], "Fc"), tl([P, TT], "Fs")
            d0 = tl([P, TT], "d0")
            E = [tl([P, P], "Eb%d" % i) for i in range(2)]
            Bp = [tl([P, P], "Bp%d" % i, F32R) for i in range(2)]
            Cp = [tl([P, P], "Cp%d" % i, F32R) for i in range(2)]
            for e_ in E:
                kb.op("dve", lambda e: e.memset(e_[:], 0.0), writes=[e_.tok])
            wk = Rot(kb, st, 8, [P, TT], F32, name="s5wk")
            sreR = Rot(kb, st, 2, [P, TT], F32R, name="sre")
            nsiR = Rot(kb, st, 2, [P, TT], F32R, name="nsi")
            pend = []
            pk = Rot(kb, st, 8, [P, TT], F32R, name="s5pk")
            Fsn = tl([P, TT], "Fsn")
            Fcn = tl([P, TT], "Fcn")
            cin = tl([P, 4], "cin")
            tmpc = tl([P, 4], "tmpc")
            for Y in range(8):
                kb.dma("pool", uch[:], self.zT.bitcast(F32R)[(cfg.ZC + Y) * P:(cfg.ZC + Y + 1) * P, :], writes=[uch.tok])
                kb.op("dve", lambda e: e.tensor_scalar(out=yacc[:], in0=uch.f32(slice(None)), scalar1=dcol[:, Y:Y + 1], scalar2=None, op0=ALU.mult),
                      reads=[uch.tok, dcol.tok], writes=[yacc.tok])
                for q in range(4 * Y, 4 * Y + 4):
                    for d in range(2):
                        pr = prm[d]
                        w, bb, pw, s0 = pr["w"], pr["bb"], pr["pw"], pr["s0"]
                        for ri in range(2):
                            for g2 in range(2):
                                g8 = 2 * (q % 4) + g2
                                kb.op("dve", lambda e: e.tensor_copy(out=E[ri][g2 * 64:(g2 + 1) * 64, g8 * 16:(g8 + 1) * 16],
                                                                   in_=bb[g2 * 64:(g2 + 1) * 64, ri, q, :]),
                                      reads=[bb.tok], writes=[E[ri].tok])
                            pb, pt = self.bank()
                            kb.op("pe", lambda e: e.transpose(pb[:, 0:P], E[ri][:], self.ident[:]), reads=[E[ri].tok, self.ident.tok], writes=[pt])
                            kb.op("act", lambda e: e.activation(out=Bp[ri][:], in_=pb[:, 0:P], func=AF.Copy), reads=[pt], writes=[Bp[ri].tok])
                            for g2 in range(2):
                                g8 = 2 * (q % 4) + g2
                                kb.op("dve", lambda e: e.memset(E[ri][g2 * 64:(g2 + 1) * 64, g8 * 16:(g8 + 1) * 16], 0.0),
                                      reads=[], writes=[E[ri].tok])
                            kb.dma("pool", Cp[ri][:], self.s5c[l, d, ri, q], writes=[Cp[ri].tok])
                        kb.op("dve", lambda e: e.memset(Fc[:, 0:1], 1.0), writes=[Fc.tok])
                        kb.op("dve", lambda e: e.memset(Fs[:, 0:1], 0.0), writes=[Fs.tok])
                        for k in range(9):
                            n_ = 1 << k
                            pc_, ps_ = pw[:, k, 0, q:q + 1], pw[:, k, 1, q:q + 1]
                            t1 = wk.next()
                            kb.op("dve", lambda e: e.tensor_scalar(out=t1[:, 0:n_], in0=Fs[:, 0:n_], scalar1=ps_, scalar2=None, op0=ALU.mult),
                                  reads=[Fs.tok, pw.tok], writes=[t1.tok])
                            t2 = wk.next()
                            kb.op("dve", lambda e: e.tensor_scalar(out=t2[:, 0:n_], in0=Fc[:, 0:n_], scalar1=ps_, scalar2=None, op0=ALU.mult),
                                  reads=[Fc.tok, pw.tok], writes=[t2.tok])
                            kb.op("dve", lambda e: e.scalar_tensor_tensor(out=Fc[:, n_:2 * n_], in0=Fc[:, 0:n_], scalar=pc_, in1=t1[:, 0:n_],
                                                                        op0=ALU.mult, op1=ALU.subtract),
                                  reads=[Fc.tok, pw.tok, t1.tok], writes=[Fc.tok])
                            kb.op("dve", lambda e: e.scalar_tensor_tensor(out=Fs[:, n_:2 * n_], in0=Fs[:, 0:n_], scalar=pc_, in1=t2[:, 0:n_],
                                                                        op0=ALU.mult, op1=ALU.add),
                                  reads=[Fs.tok, pw.tok, t2.tok], writes=[Fs.tok])
                        kb.op("dve", lambda e: e.tensor_scalar(out=Fsn[:], in0=Fs[:], scalar1=-1.0, scalar2=None, op0=ALU.mult), reads=[Fs.tok], writes=[Fsn.tok])
                        kb.op("dve", lambda e: e.tensor_scalar(out=Fcn[:], in0=Fc[:], scalar1=-1.0, scalar2=None, op0=ALU.mult), reads=[Fc.tok], writes=[Fcn.tok])
                        kb.op("dve", lambda e: e.tensor_scalar(out=d0[:], in0=self.keep[:], scalar1=w[:, 1, q:q + 1], scalar2=None, op0=ALU.mult),
                              reads=[self.keep.tok, w.tok], writes=[d0.tok])
                        def crot(src_re, src_im, cc_, ss_, rd):
                            kb.op("dve", lambda e: e.tensor_scalar(out=tmpc[:, 0:1], in0=src_im, scalar1=ss_, scalar2=None, op0=ALU.mult), reads=rd + [pw.tok], writes=[tmpc.tok])
                            kb.op("dve", lambda e: e.scalar_tensor_tensor(out=cin[:, 2:3], in0=src_re, scalar=cc_, in1=tmpc[:, 0:1], op0=ALU.mult, op1=ALU.subtract),
                                  reads=rd + [tmpc.tok, pw.tok], writes=[cin.tok])
                            kb.op("dve", lambda e: e.tensor_scalar(out=tmpc[:, 1:2], in0=src_re, scalar1=ss_, scalar2=None, op0=ALU.mult), reads=rd + [pw.tok], writes=[tmpc.tok])
                            kb.op("dve", lambda e: e.scalar_tensor_tensor(out=cin[:, 3:4], in0=src_im, scalar=cc_, in1=tmpc[:, 1:2], op0=ALU.mult, op1=ALU.add),
                                  reads=rd + [tmpc.tok, pw.tok], writes=[cin.tok])
                        crot(s0[:, 0, q:q + 1], s0[:, 1, q:q + 1], pw[:, 0, 0, q:q + 1], pw[:, 0, 1, q:q + 1], [s0.tok])
                        def emit_bu(tg_):
                            if d == 0:
                                usl_ = uch[:, tg_ * TT:(tg_ + 1) * TT]
                            else:
                                hi_ = T - tg_ * TT
                                usl_ = uch[:, hi_ - TT:hi_][:, ::-1]
                            pbr_, ptr_ = self.bank()
                            kb.op("pe", lambda e: e.matmul(pbr_, Bp[0][:], usl_, start=True, stop=True), reads=[Bp[0].tok, uch.tok], writes=[ptr_])
                            pbi_, pti_ = self.bank()
                            kb.op("pe", lambda e: e.matmul(pbi_, Bp[1][:], usl_, start=True, stop=True), reads=[Bp[1].tok, uch.tok], writes=[pti_])
                            return pbr_, ptr_, pbi_, pti_
                        nxt = emit_bu(0)
                        for tg in range(NT):
                            if d == 0:
                                ysl = yacc[:, tg * TT:(tg + 1) * TT]
                            else:
                                hi = T - tg * TT
                                ysl = yacc[:, hi - TT:hi][:, ::-1]
                            pbr, ptr, pbi, pti = nxt
                            a1, a2, a3, a4 = wk.next(), wk.next(), wk.next(), wk.next()
                            kb.op("dve", lambda e: e.tensor_tensor(out=a1[:], in0=pbr, in1=Fc[:], op=ALU.mult), reads=[ptr, Fc.tok], writes=[a1.tok])
                            kb.op("dve", lambda e: e.tensor_tensor(out=a2[:], in0=pbi, in1=Fs[:], op=ALU.mult), reads=[pti, Fs.tok], writes=[a2.tok])
                            kb.op("dve", lambda e: e.tensor_tensor(out=a3[:], in0=pbi, in1=Fc[:], op=ALU.mult), reads=[pti, Fc.tok], writes=[a3.tok])
                            kb.op("dve", lambda e: e.tensor_tensor(out=a4[:], in0=pbr, in1=Fs[:], op=ALU.mult), reads=[ptr, Fs.tok], writes=[a4.tok])
                            kb.op("dve", lambda e: e.tensor_tensor(out=a1[:], in0=a1[:], in1=a2[:], op=ALU.add), reads=[a1.tok, a2.tok], writes=[a1.tok])
                            kb.op("dve", lambda e: e.tensor_tensor(out=a3[:], in0=a3[:], in1=a4[:], op=ALU.subtract), reads=[a3.tok, a4.tok], writes=[a3.tok])
                            kb.op("dve", lambda e: e.tensor_tensor_scan(out=a2[:], data0=d0[:], data1=a1[:], initial=cin[:, 2:3], op0=ALU.mult, op1=ALU.add),
                                  reads=[d0.tok, a1.tok, cin.tok], writes=[a2.tok])
                            kb.op("dve", lambda e: e.tensor_tensor_scan(out=a4[:], data0=d0[:], data1=a3[:], initial=cin[:, 3:4], op0=ALU.mult, op1=ALU.add),
                                  reads=[d0.tok, a3.tok, cin.tok], writes=[a4.tok])
                            while pend:
                                pend.pop(0)()
                            if tg + 1 < NT:
                                crot(a2[:, TT - 1:TT], a4[:, TT - 1:TT], pw[:, 9, 0, q:q + 1], pw[:, 9, 1, q:q + 1], [a2.tok, a4.tok])
                            if tg + 1 < NT:
                                nxt = emit_bu(tg + 1)
                            p1, p2, p3, p4 = pk.next(), pk.next(), pk.next(), pk.next()
                            kb.op("pool", lambda e: e.tensor_tensor(out=p1[:], in0=a2[:], in1=Fc[:], op=ALU.mult), reads=[a2.tok, Fc.tok], writes=[p1.tok])
                            kb.op("pool", lambda e: e.tensor_tensor(out=p2[:], in0=a4[:], in1=Fsn[:], op=ALU.mult), reads=[a4.tok, Fsn.tok], writes=[p2.tok])
                            kb.op("pool", lambda e: e.tensor_tensor(out=p3[:], in0=a2[:], in1=Fsn[:], op=ALU.mult), reads=[a2.tok, Fsn.tok], writes=[p3.tok])
                            kb.op("pool", lambda e: e.tensor_tensor(out=p4[:], in0=a4[:], in1=Fcn[:], op=ALU.mult), reads=[a4.tok, Fcn.tok], writes=[p4.tok])
                            nsg = TT // 256
                            sc_ = (slice(None), slice(255, None, 256))
                            pby, pty = self.bank()
                            for mi, (cp_, pp_) in enumerate(((Cp[0], p1), (Cp[0], p2), (Cp[1], p3), (Cp[1], p4))):
                                kb.op("pe", lambda e: e.matmul(pby, cp_[:], pp_[:], start=(mi == 0), stop=(mi == 3)), reads=[cp_.tok, pp_.tok], writes=[pty])
                            kb.op("dve", lambda e: e.tensor_tensor(out=so[:, d, 0, q, tg * nsg:(tg + 1) * nsg], in0=p1.f32(sc_), in1=p2.f32(sc_), op=ALU.add),
                                  reads=[p1.tok, p2.tok], writes=[so.tok])
                            kb.op("dve", lambda e: e.scalar_tensor_tensor(out=so[:, d, 1, q, tg * nsg:(tg + 1) * nsg], in0=p3.f32(sc_), scalar=-1.0, in1=p4.f32(sc_),
                                                                        op0=ALU.mult, op1=ALU.subtract), reads=[p3.tok, p4.tok], writes=[so.tok])
                            pend.append(lambda pby=pby, pty=pty, ysl=ysl: kb.op("dve", lambda e: e.tensor_tensor(out=ysl, in0=pby, in1=ysl, op=ALU.add),
                                                                                 reads=[pty, yacc.tok], writes=[yacc.tok]))
                while pend:
                    pend.pop(0)()
                for tg in range(NT):
                    ysl = yacc[:, tg * TT:(tg + 1) * TT]
                    a1, a2 = wk.next(), wk.next()
                    kb.op("act", lambda e: e.activation(out=a1[:], in_=ysl, func=AF.Square), reads=[yacc.tok], writes=[a1.tok])
                    kb.op("dve", lambda e: e.tensor_scalar(out=a1[:], in0=a1[:], scalar1=0.044715, scalar2=1.0, op0=ALU.mult, op1=ALU.add),
                          reads=[a1.tok], writes=[a1.tok])
                    kb.op("dve", lambda e: e.tensor_tensor(out=a1[:], in0=a1[:], in1=ysl, op=ALU.mult), reads=[a1.tok, yacc.tok], writes=[a1.tok])
                    kb.op("act", lambda e: e.activation(out=a2[:], in_=a1[:], func=AF.Tanh, scale=0.7978845608028654), reads=[a1.tok], writes=[a2.tok])
                    kb.op("dve", lambda e: e.tensor_scalar(out=a2[:], in0=a2[:], scalar1=0.5, scalar2=0.5, op0=ALU.mult, op1=ALU.add),
                          reads=[a2.tok], writes=[a2.tok])
                    kb.op("dve", lambda e: e.tensor_tensor(out=a2[:], in0=a2[:], in1=ysl, op=ALU.mult), reads=[a2.tok, yacc.tok], writes=[a2.tok])
                    kb.dma("sp", self.yc.bitcast(F32)[Y * P:(Y + 1) * P, tg * TT:(tg + 1) * TT], a2[:], reads=[a2.tok], writes=[self.dt("yc", Y, tg)])
            kb.dma("sp", self.s5o[l], so[:], reads=[so.tok], writes=[self.dt("s5o", l)])
            kb.barrier()

    def phase_rwkv(self, l):
        cfg, kb = self.cfg, self.kb
        T, NT, NSEG = cfg.T, cfg.NT, cfg.NSEG
        NCH = T // 64
        HS = (slice(0, 64), slice(64, 128))
        with ExitStack() as st:
            def tl(shape, name, dtype=F32):
                return Tile(kb, st, shape, dtype, name=name)
            sm = tl([P, 4, TT], "rwsm"); kb.dma("sp", sm[:], self.rwsm[:, :, :], writes=[sm.tok])
            cmk = tl([P, T], "rwcm"); kb.dma("sp", cmk[:], self.rwcm[:, :], writes=[cmk.tok])
            col = tl([P, 5, 8], "rwcol"); kb.dma("sp", col[:], self.rwcol[l], writes=[col.tok])
            mu = tl([P, 26], "rwmu"); kb.dma("sp", mu[:], self.rwmu[l], writes=[mu.tok])
            om = tl([P, 26], "rwom")
            kb.op("dve", lambda e: e.tensor_scalar(out=om[:], in0=mu[:], scalar1=-1.0, scalar2=1.0, op0=ALU.mult, op1=ALU.add), reads=[mu.tok], writes=[om.tok])
            oka = tl([P, 8], "rwoka")
            kb.op("dve", lambda e: e.tensor_scalar(out=oka[:], in0=col[:, 1, :], scalar1=-1.0, scalar2=1.0, op0=ALU.mult, op1=ALU.add), reads=[col.tok], writes=[oka.tok])
            w0 = tl([P, 2, 2, 8], "rww0"); kb.dma("sp", w0[:], self.rww0[l], writes=[w0.tok])
            w2 = tl([P, 2, RW], "rww2"); kb.dma("sp", w2[:], self.rww2[l], writes=[w2.tok])
            hb1 = tl([P, P], "hb1"); kb.dma("sp", hb1[:], self.hblk[0], writes=[hb1.tok])
            xraw = tl([P, T], "xraw")
            sacc = tl([P, T], "sacc")
            stmp = Rot(kb, st, 3, [P, TT], F32, name="stmp")

            def load_shifted(ch, dst):
                kb.dma("sp", xraw[:], self.zT[(cfg.ZA + ch) * P:(cfg.ZA + ch + 1) * P, :], writes=[xraw.tok])
                kb.op("dve", lambda e: e.memset(sacc[:], 0.0), writes=[sacc.tok])
                for oi, o in enumerate((-1, 1, -64, 64)):
                    for tg in range(NT):
                        lo, hi = tg * TT, (tg + 1) * TT
                        slo, shi = max(lo + o, 0), min(hi + o, T)
                        dlo, dhi = slo - o, shi - o
                        n_ = dhi - dlo
                        t = stmp.next()
                        kb.op("dve", lambda e: e.tensor_tensor(out=t[:, 0:n_], in0=xraw[:, slo:shi], in1=sm[:, oi, dlo - lo:dhi - lo], op=ALU.mult),
                              reads=[xraw.tok, sm.tok], writes=[t.tok])
                        kb.op("dve", lambda e: e.tensor_tensor(out=sacc[:, dlo:dhi], in0=sacc[:, dlo:dhi], in1=t[:, 0:n_], op=ALU.add),
                              reads=[sacc.tok, t.tok], writes=[sacc.tok])
                kb.op("dve", lambda e: e.tensor_scalar(out=xraw[:], in0=xraw[:], scalar1=om[:, ch:ch + 1], scalar2=None, op0=ALU.mult), reads=[xraw.tok, om.tok], writes=[xraw.tok])
                kb.op("dve", lambda e: e.scalar_tensor_tensor(out=dst[:], in0=sacc[:], scalar=mu[:, ch:ch + 1], in1=xraw[:], op0=ALU.mult, op1=ALU.add),
                      reads=[sacc.tok, mu.tok, xraw.tok], writes=[dst.tok])

            tw = tl([P, T], "rwtw")
            load_shifted(24, tw)
            kb.op("act", lambda e: e.activation(out=tw[0:64, :], in_=tw[0:64, :], func=AF.Tanh), reads=[tw.tok], writes=[tw.tok])
            rr, kk_, vv_ = tl([P, T], "rwr"), tl([P, T], "rwk"), tl([P, T], "rwv")
            kkn = tl([P, T], "rwkkn")
            ad, ldc, cum = tl([P, T], "rwad"), tl([P, T], "rwld"), tl([P, T], "rwcum")
            t1, t2 = tl([P, T], "rwt1"), tl([P, T], "rwt2")
            outr = Rot(kb, st, 3, [P, T], F32, name="rwout")
            gct = tl([P, NCH], "rwgct")
            for j in range(8):
                load_shifted(j, rr)
                load_shifted(8 + j, kk_)
                load_shifted(16 + j, vv_)
                kb.op("dve", lambda e: e.tensor_scalar(out=kkn[:], in0=kk_[:], scalar1=col[:, 0, j:j + 1], scalar2=None, op0=ALU.mult), reads=[kk_.tok, col.tok], writes=[kkn.tok])
                kb.op("act", lambda e: e.activation(out=t1[:], in_=kkn[:], func=AF.Square), reads=[kkn.tok], writes=[t1.tok])
                for tg in range(NT):
                    tc_ = slice(tg * TT, (tg + 1) * TT)
                    pb, pt = self.bank()
                    kb.op("pe", lambda e: e.matmul(pb, hb1[:], t1[:, tc_], start=True, stop=True), reads=[hb1.tok, t1.tok], writes=[pt])
                    kb.op("act", lambda e: e.activation(out=t2[:, tc_], in_=pb, func=AF.Sqrt, bias=self.epsc[:, 2:3], scale=1.0), reads=[pt, self.epsc.tok], writes=[t2.tok])
                kb.op("dve", lambda e: e.reciprocal(out=t2[:], in_=t2[:]), reads=[t2.tok], writes=[t2.tok])
                kb.op("dve", lambda e: e.tensor_tensor(out=kkn[:], in0=kkn[:], in1=t2[:], op=ALU.mult), reads=[kkn.tok, t2.tok], writes=[kkn.tok])
                kb.op("dve", lambda e: e.scalar_tensor_tensor(out=t1[:], in0=rr[:], scalar=col[:, 2, j:j + 1], in1=kk_[:], op0=ALU.mult, op1=ALU.mult),
                      reads=[rr.tok, col.tok, kk_.tok], writes=[t1.tok])
                bo = outr.next()
                for tg in range(NT):
                    tc_ = slice(tg * TT, (tg + 1) * TT)
                    pb, pt = self.bank()
                    kb.op("pe", lambda e: e.matmul(pb, hb1[:], t1[:, tc_], start=True, stop=True), reads=[hb1.tok, t1.tok], writes=[pt])
                    kb.op("dve", lambda e: e.tensor_tensor(out=bo[:, tc_], in0=pb, in1=vv_[:, tc_], op=ALU.mult), reads=[pt, vv_.tok], writes=[bo.tok])
                kb.dma("sp", self.rwbon[j * P:(j + 1) * P, :], bo[:], reads=[bo.tok], writes=[self.dt("rwbon", j)])
                for d in range(2):
                    R = (lambda ap: ap) if d == 0 else (lambda ap: ap[:, ::-1])
                    for tg in range(NT):
                        tc_ = slice(tg * TT, (tg + 1) * TT)
                        pb, pt = self.bank()
                        kb.op("pe", lambda e: e.matmul(pb, w2[0:64, d, j * P:(j + 1) * P], tw[0:64, tc_], start=True, stop=True), reads=[w2.tok, tw.tok], writes=[pt])
                        kb.op("act", lambda e: e.activation(out=ldc[:, tc_], in_=pb, func=AF.Sigmoid, bias=w0[:, d, 0, j:j + 1], scale=1.0), reads=[pt, w0.tok], writes=[ldc.tok])
                        pb2, pt2 = self.bank()
                        kb.op("pe", lambda e: e.matmul(pb2, w2[64:128, d, j * P:(j + 1) * P], tw[64:128, tc_], start=True, stop=True), reads=[w2.tok, tw.tok], writes=[pt2])
                        kb.op("act", lambda e: e.activation(out=ad[:, tc_], in_=pb2, func=AF.Sigmoid, bias=w0[:, d, 1, j:j + 1], scale=1.0), reads=[pt2, w0.tok], writes=[ad.tok])
                    kb.op("dve", lambda e: e.tensor_scalar(out=ldc[:], in0=ldc[:], scalar1=-0.6065306597126334, scalar2=None, op0=ALU.mult), reads=[ldc.tok], writes=[ldc.tok])
                    kb.op("dve", lambda e: e.tensor_tensor_scan(out=cum[:], data0=cmk[:], data1=R(ldc[:]), initial=0.0, op0=ALU.mult, op1=ALU.add),
                          reads=[cmk.tok, ldc.tok], writes=[cum.tok])
                    o_rt = outr.next()
                    kb.op("act", lambda e: e.activation(out=t1[:], in_=cum[:], func=AF.Exp), reads=[cum.tok], writes=[t1.tok])
                    kb.op("dve", lambda e: e.tensor_tensor(out=o_rt[:], in0=t1[:], in1=R(rr[:]), op=ALU.mult), reads=[t1.tok, rr.tok], writes=[o_rt.tok])
                    kb.dma("sp", self.rwp[d, 3, j * P:(j + 1) * P, :].bitcast(F32), o_rt[:], reads=[o_rt.tok], writes=[self.dt("rwp", d, 3, j)])
                    kb.op("dve", lambda e: e.tensor_copy(out=gct[:], in_=t1[:, 63::64]), reads=[t1.tok], writes=[gct.tok])
                    kb.dma("sp", self.rwgc[d, j * P:(j + 1) * P, :], gct[:], reads=[gct.tok], writes=[self.dt("rwgc", d, j)])
                    o_at = outr.next()
                    kb.op("dve", lambda e: e.tensor_tensor(out=t2[:], in0=cum[:], in1=R(ldc[:]), op=ALU.subtract), reads=[cum.tok, ldc.tok], writes=[t2.tok])
                    kb.op("act", lambda e: e.activation(out=t2[:], in_=t2[:], func=AF.Exp), reads=[t2.tok], writes=[t2.tok])
                    kb.op("dve", lambda e: e.scalar_tensor_tensor(out=o_at[:], in0=t2[:], scalar=-1.0, in1=R(kkn[:]), op0=ALU.mult, op1=ALU.mult),
                          reads=[t2.tok, kkn.tok], writes=[o_at.tok])
                    kb.dma("sp", self.rwp[d, 0, j * P:(j + 1) * P, :].bitcast(F32), o_at[:], reads=[o_at.tok], writes=[self.dt("rwp", d, 0, j)])
                    kb.op("act", lambda e: e.activation(out=t1[:], in_=cum[:], func=AF.Exp, scale=-1.0), reads=[cum.tok], writes=[t1.tok])
                    o_bt = outr.next()
                    kb.op("dve", lambda e: e.tensor_tensor(out=t2[:], in0=R(kkn[:]), in1=R(ad[:]), op=ALU.mult), reads=[kkn.tok, ad.tok], writes=[t2.tok])
                    kb.op("dve", lambda e: e.tensor_tensor(out=o_bt[:], in0=t2[:], in1=t1[:], op=ALU.mult), reads=[t2.tok, t1.tok], writes=[o_bt.tok])
                    kb.dma("sp", self.rwp[d, 1, j * P:(j + 1) * P, :].bitcast(F32), o_bt[:], reads=[o_bt.tok], writes=[self.dt("rwp", d, 1, j)])
                    o_kt = outr.next()
                    kb.op("dve", lambda e: e.tensor_scalar(out=t2[:], in0=R(ad[:]), scalar1=col[:, 1, j:j + 1], scalar2=oka[:, j:j + 1], op0=ALU.mult, op1=ALU.add),
                          reads=[ad.tok, col.tok, oka.tok], writes=[t2.tok])
                    kb.op("dve", lambda e: e.tensor_tensor(out=t2[:], in0=t2[:], in1=R(kk_[:]), op=ALU.mult), reads=[t2.tok, kk_.tok], writes=[t2.tok])
                    kb.op("dve", lambda e: e.tensor_tensor(out=o_kt[:], in0=t2[:], in1=t1[:], op=ALU.mult), reads=[t2.tok, t1.tok], writes=[o_kt.tok])
                    kb.dma("sp", self.rwp[d, 2, j * P:(j + 1) * P, :].bitcast(F32), o_kt[:], reads=[o_kt.tok], writes=[self.dt("rwp", d, 2, j)])
                    o_v = outr.next()
                    kb.op("act", lambda e: e.activation(out=o_v[:], in_=R(vv_[:]), func=AF.Copy), reads=[vv_.tok], writes=[o_v.tok])
                    kb.dma("sp", self.rwp[d, 4, j * P:(j + 1) * P, :].bitcast(F32), o_v[:], reads=[o_v.tok], writes=[self.dt("rwp", d, 4, j)])
            kb.barrier()
        with ExitStack() as st:
            def tl(shape, name, dtype=F32):
                return Tile(kb, st, shape, dtype, name=name)
            H = 64
            trm = tl([H, 3, 64], "rwtri")
            for i in range(3):
                kb.dma("sp", trm[:, i, :], self.rwtri[i][0:64, :], writes=[trm.tok])
            kcol = tl([P, 1], "rwkcol"); kb.dma("sp", kcol[:], self.ssdkeep[:, :], writes=[kcol.tok])
            Sst = tl([H, 16, 64], "rwS", F32R)
            gcs = tl([H, 16, NCH], "rwgcs")
            slab = [Rot(kb, st, 2, [H, 16, 64], F32R, name="rwsl%d" % q) for q in range(5)]
            tk = [Rot(kb, st, 2, [H, 16, 64], F32R, name="rwtk%d" % q) for q in range(3)]
            Am = [Rot(kb, st, 2, [H, 16, 64], F32R, name="rwA%d" % q) for q in range(3)]
            Ak = Rot(kb, st, 2, [H, 16, 64], F32R, name="rwAk")
            AkT = Rot(kb, st, 2, [H, 16, 64], F32R, name="rwAkT")
            Tm = Rot(kb, st, 2, [H, 16, 64], F32R, name="rwTm")
            Wt = Rot(kb, st, 2, [H, 16, 64], F32R, name="rwWt")
            Ut = Rot(kb, st, 2, [H, 16, 64], F32R, name="rwUt")
            ost = Rot(kb, st, 2, [H, 16, 64], F32, name="rwost")

            def bc16(ap2):
                return ap2.unsqueeze(1).to_broadcast([H, 16, 64])

            def mm16(psb, ptk, lhs_fn, rhs_fn, reads, first=True, last=True):
                for b_ in range(16):
                    kb.op("pe", lambda e: e.matmul(psb[0:H, b_ * 64:(b_ + 1) * 64], lhs_fn(b_), rhs_fn(b_), start=first, stop=last), reads=reads, writes=ptk)

            def mm16g(psb, ptk, terms):
                n = len(terms)
                for b_ in range(16):
                    for ti, (lt, rt__) in enumerate(terms):
                        kb.op("pe", lambda e: e.matmul(psb[0:H, b_ * 64:(b_ + 1) * 64], lt[:, b_, :], rt__[:, b_, :], start=(ti == 0), stop=(ti == n - 1)),
                              reads=[lt.tok, rt__.tok], writes=ptk)

            def v3(pb):
                return pb[0:H, :].rearrange("p (b t) -> p b t", t=64)

            def dview(ap2d):
                return ap2d.rearrange("(b k) x -> k b x", k=64)

            for d in range(2):
                kb.dma("pool", Sst[:], self.rws0[l, d], writes=[Sst.tok])
                kb.dma("sp", gcs[:], dview(self.rwgc[d]), reads=[self.dt("rwgc", d, 0)], writes=[gcs.tok])
                for c in range(NCH):
                    cs = slice(c * 64, (c + 1) * 64)
                    sl = [r_.next() for r_ in slab]
                    for q in range(5):
                        kb.dma("pool", sl[q][:], dview(self.rwp[d, q])[:, :, cs], writes=[sl[q].tok])
                    at_, bt_, kt_, rt_, vs_ = sl
                    tks = []
                    for q, src in enumerate((bt_, kt_, vs_)):
                        pb, pt = self.bank2()
                        for b_ in range(16):
                            kb.op("pe", lambda e: e.transpose(pb[0:H, b_ * 64:(b_ + 1) * 64], src[:, b_, :].bitcast(F32), self.ident[0:H, 0:H]), reads=[src.tok, self.ident.tok], writes=pt)
                        t_ = tk[q].next()
                        if q % 2:
                            kb.op("act", lambda e: e.activation(out=t_[:], in_=v3(pb), func=AF.Copy), reads=pt, writes=[t_.tok])
                        else:
                            kb.op("dve", lambda e: e.tensor_copy(out=t_[:], in_=v3(pb)), reads=pt, writes=[t_.tok])
                        tks.append(t_)
                    Btk, Ktk, Vtk = tks

                    def amat(lhs, rhs, mask_i, dst):
                        pb, pt = self.bank2()
                        mm16(pb, pt, lambda b_: lhs[:, b_, :], lambda b_: rhs[:, b_, :], [lhs.tok, rhs.tok])
                        kb.op("dve", lambda e: e.tensor_tensor(out=dst[:], in0=v3(pb), in1=bc16(trm[:, mask_i, :]), op=ALU.mult), reads=pt + [trm.tok], writes=[dst.tok])
                    A0, A0T = Ak.next(), AkT.next()
                    amat(bt_, at_, 0, A0)
                    amat(at_, bt_, 1, A0T)
                    Aak, Arb, Ark = Am[0].next(), Am[1].next(), Am[2].next()
                    amat(kt_, at_, 0, Aak)
                    amat(bt_, rt_, 2, Arb)
                    amat(kt_, rt_, 2, Ark)
                    Tc = Tm.next()
                    kb.op("dve", lambda e: e.tensor_tensor(out=Tc[:], in0=A0.f32(slice(None)), in1=bc16(self.ident[0:H, 0:H]), op=ALU.add), reads=[A0.tok, self.ident.tok], writes=[Tc.tok])
                    Ap, ApT = A0, A0T
                    for lev in range(1, 6):
                        An, AnT = Ak.next(), AkT.next()
                        pb, pt = self.bank2()
                        mm16(pb, pt, lambda b_: ApT[:, b_, :], lambda b_: Ap[:, b_, :], [Ap.tok, ApT.tok])
                        pb2, pt2 = self.bank2()
                        mm16(pb2, pt2, lambda b_: Ap[:, b_, :], lambda b_: ApT[:, b_, :], [Ap.tok, ApT.tok])
                        kb.op("dve", lambda e: e.tensor_copy(out=An[:], in_=v3(pb)), reads=pt, writes=[An.tok])
                        kb.op("act", lambda e: e.activation(out=AnT[:], in_=v3(pb2), func=AF.Copy), reads=pt2, writes=[AnT.tok])
                        pb3, pt3 = self.bank2()
                        mm16(pb3, pt3, lambda b_: AnT[:, b_, :], lambda b_: Tc[:, b_, :], [AnT.tok, Tc.tok])
                        Tn = Tm.next()
                        kb.op("dve", lambda e: e.tensor_tensor(out=Tn[:], in0=v3(pb3), in1=Tc.f32(slice(None)), op=ALU.add), reads=pt3 + [Tc.tok], writes=[Tn.tok])
                        Tc, Ap, ApT = Tn, An, AnT
                    pbw, ptw = self.bank2()
                    mm16g(pbw, ptw, [(at_, Sst), (Aak, Vtk)])
                    W_ = Wt.next()
                    kb.op("act", lambda e: e.activation(out=W_[:], in_=v3(pbw), func=AF.Copy), reads=ptw, writes=[W_.tok])
                    pbu, ptu = self.bank2()
                    mm16(pbu, ptu, lambda b_: Tc[:, b_, :], lambda b_: W_[:, b_, :], [Tc.tok, W_.tok])
                    U_ = Ut.next()
                    kb.op("dve", lambda e: e.tensor_copy(out=U_[:], in_=v3(pbu)), reads=ptu, writes=[U_.tok])
                    pbo, pto = self.bank2()
                    mm16g(pbo, pto, [(Sst, rt_), (U_, Arb), (Vtk, Ark)])
                    pbs, pts = self.bank2()
                    mm16g(pbs, pts, [(Btk, U_), (Ktk, Vtk)])
                    o_ = ost.next()
                    if d == 0:
                        kb.op("act", lambda e: e.activation(out=o_[:], in_=v3(pbo), func=AF.Copy), reads=pto, writes=[o_.tok])
                        tcs = cs
                    else:
                        kb.op("dve", lambda e: e.tensor_copy(out=o_[:, :, ::-1], in_=v3(pbo)), reads=pto, writes=[o_.tok])
                        tcs = slice(T - (c + 1) * 64, T - c * 64)
                    kb.dma("sp", dview(self.rwoT[d])[:, :, tcs], o_[:], reads=[o_.tok], writes=[self.dt("rwoT", d, c)])
                    kb.op("dve", lambda e: e.tensor_tensor(out=Sst[:], in0=v3(pbs), in1=Sst.f32(slice(None)), op=ALU.add), reads=pts + [Sst.tok], writes=[Sst.tok])
                    kb.op("dve", lambda e: e.tensor_tensor(out=Sst[:], in0=Sst.f32(slice(None)), in1=gcs[:, :, c:c + 1].to_broadcast([H, 16, 64]), op=ALU.mult),
                          reads=[Sst.tok, gcs.tok], writes=[Sst.tok])
                    if c % 4 == 3:
                        seg = c // 4
                        kb.dma("sp", self.rwo[l, d, seg], Sst.f32(slice(None)), reads=[Sst.tok], writes=[self.dt("rwo", l, d, seg)])
                        kb.op("dve", lambda e: e.tensor_scalar(out=Sst[:], in0=Sst.f32(slice(None)), scalar1=kcol[0:H, 0:1], scalar2=None, op0=ALU.mult), reads=[Sst.tok, kcol.tok], writes=[Sst.tok])
            kb.barrier()
        with ExitStack() as st:
            def tl(shape, name, dtype=F32):
                return Tile(kb, st, shape, dtype, name=name)
            hb64 = tl([P, P], "hb64"); kb.dma("sp", hb64[:], self.hblk[1], writes=[hb64.tok])
            col = tl([P, 5, 8], "rwcol2"); kb.dma("sp", col[:], self.rwcol[l], writes=[col.tok])
            g2 = tl([P, RW], "rwg2"); kb.dma("sp", g2[:], self.rwg2[l], writes=[g2.tok])
            sgl = tl([P, T], "rwsgl")
            kb.dma("sp", sgl[:], self.zT[(cfg.ZA + 25) * P:(cfg.ZA + 26) * P, :], writes=[sgl.tok])
            sm = tl([P, 4, TT], "rwsm2"); kb.dma("sp", sm[:], self.rwsm[:, :, :], writes=[sm.tok])
            mu = tl([P, 26], "rwmu2"); kb.dma("sp", mu[:], self.rwmu[l], writes=[mu.tok])
            om = tl([P, 1], "rwom2")
            kb.op("dve", lambda e: e.tensor_scalar(out=om[:], in0=mu[:, 25:26], scalar1=-1.0, scalar2=1.0, op0=ALU.mult, op1=ALU.add), reads=[mu.tok], writes=[om.tok])
            sacc = tl([P, T], "rwsacc2")
            pt_ = Rot(kb, st, 8, [P, TT], F32, name="rwpt")
            kb.op("dve", lambda e: e.memset(sacc[:], 0.0), writes=[sacc.tok])
            for oi, o in enumerate((-1, 1, -64, 64)):
                for tg in range(NT):
                    lo, hi = tg * TT, (tg + 1) * TT
                    slo, shi = max(lo + o, 0), min(hi + o, T)
                    dlo, dhi = slo - o, shi - o
                    n_ = dhi - dlo
                    t = pt_.next()
                    kb.op("dve", lambda e: e.tensor_tensor(out=t[:, 0:n_], in0=sgl[:, slo:shi], in1=sm[:, oi, dlo - lo:dhi - lo], op=ALU.mult), reads=[sgl.tok, sm.tok], writes=[t.tok])
                    kb.op("dve", lambda e: e.tensor_tensor(out=sacc[:, dlo:dhi], in0=sacc[:, dlo:dhi], in1=t[:, 0:n_], op=ALU.add), reads=[sacc.tok, t.tok], writes=[sacc.tok])
            kb.op("dve", lambda e: e.tensor_scalar(out=sgl[:], in0=sgl[:], scalar1=om[:, 0:1], scalar2=None, op0=ALU.mult), reads=[sgl.tok, om.tok], writes=[sgl.tok])
            kb.op("dve", lambda e: e.scalar_tensor_tensor(out=sgl[:], in0=sacc[:], scalar=mu[:, 25:26], in1=sgl[:], op0=ALU.mult, op1=ALU.add), reads=[sacc.tok, mu.tok, sgl.tok], writes=[sgl.tok])
            kb.op("act", lambda e: e.activation(out=sgl[:], in_=sgl[:], func=AF.Sigmoid), reads=[sgl.tok], writes=[sgl.tok])
            lnb = tl([P, 1], "rwlneps")
            kb.op("dve", lambda e: e.memset(lnb[:], 64e-5), writes=[lnb.tok])
            for j in range(8):
                for tg in range(NT):
                    tc_ = slice(tg * TT, (tg + 1) * TT)
                    of_, ob_ = pt_.next(), pt_.next()
                    kb.dma("sp", of_[:], self.rwoT[0, j * P:(j + 1) * P, tc_], writes=[of_.tok])
                    kb.dma("sp", ob_[:], self.rwoT[1, j * P:(j + 1) * P, tc_], writes=[ob_.tok])
                    kb.op("dve", lambda e: e.tensor_tensor(out=of_[:], in0=of_[:], in1=ob_[:], op=ALU.add), reads=[of_.tok, ob_.tok], writes=[of_.tok])
                    pb, pt = self.bank()
                    kb.op("pe", lambda e: e.matmul(pb, hb64[:], of_[:], start=True, stop=True), reads=[hb64.tok, of_.tok], writes=[pt])
                    oc = pt_.next()
                    kb.op("dve", lambda e: e.tensor_tensor(out=oc[:], in0=of_[:], in1=pb, op=ALU.subtract), reads=[of_.tok, pt], writes=[oc.tok])
                    sq = pt_.next()
                    kb.op("act", lambda e: e.activation(out=sq[:], in_=oc[:], func=AF.Square), reads=[oc.tok], writes=[sq.tok])
                    pb2, pt2 = self.bank()
                    kb.op("pe", lambda e: e.matmul(pb2, hb64[:], sq[:], start=True, stop=True), reads=[hb64.tok, sq.tok], writes=[pt2])
                    kb.op("act", lambda e: e.activation(out=sq[:], in_=pb2, func=AF.Sqrt, bias=lnb[:, 0:1], scale=1.0), reads=[pt2, lnb.tok], writes=[sq.tok])
                    kb.op("dve", lambda e: e.reciprocal(out=sq[:], in_=sq[:]), reads=[sq.tok], writes=[sq.tok])
                    kb.op("dve", lambda e: e.scalar_tensor_tensor(out=oc[:], in0=oc[:], scalar=col[:, 3, j:j + 1], in1=sq[:], op0=ALU.mult, op1=ALU.mult), reads=[oc.tok, col.tok, sq.tok], writes=[oc.tok])
                    bn = pt_.next()
                    kb.dma("sp", bn[:], self.rwbon[j * P:(j + 1) * P, tc_], reads=[self.dt("rwbon", j)], writes=[bn.tok])
                    kb.op("dve", lambda e: e.scalar_tensor_tensor(out=oc[:], in0=oc[:], scalar=col[:, 4, j:j + 1], in1=bn[:], op0=ALU.add, op1=ALU.add), reads=[oc.tok, col.tok, bn.tok], writes=[oc.tok])
                    pb3, pt3 = self.bank()
                    kb.op("pe", lambda e: e.matmul(pb3, g2[:, j * P:(j + 1) * P], sgl[:, tc_], start=True, stop=True), reads=[g2.tok, sgl.tok], writes=[pt3])
                    kb.op("dve", lambda e: e.tensor_tensor(out=oc[:], in0=pb3, in1=oc[:], op=ALU.mult), reads=[pt3, oc.tok], writes=[oc.tok])
                    kb.dma("sp", self.ya.bitcast(F32)[j * P:(j + 1) * P, tc_], oc[:], reads=[oc.tok], writes=[self.dt("ya", j, tg)])
            kb.barrier()

    def phase_ssd(self, l):
        cfg, kb = self.cfg, self.kb
        T, NT, NSEG = cfg.T, cfg.NT, cfg.NSEG
        NC = T // P
        with ExitStack() as st:
            cm = Tile(kb, st, [P, 4, TT], F32, name="cm")
            kb.dma("sp", cm[:], self.cmT[:, :, :], writes=[cm.tok])
            cw = Tile(kb, st, [P, 12, 6], F32, name="cw")
            kb.dma("sp", cw[:], self.ssdcw[l], writes=[cw.tok])
            xin = Rot(kb, st, 2, [P, T], F32, name="xin")
            acc = Rot(kb, st, 2, [P, T], F32, name="cacc")
            tmp = Rot(kb, st, 3, [P, TT], F32, name="ctmp")
            for ch in range(12):
                x = xin.next()
                a = acc.next()
                kb.dma("sp", x[:], self.zT[(cfg.ZB + 8 + ch) * P:(cfg.ZB + 9 + ch) * P, :], writes=[x.tok])
                kb.op("dve", lambda e: e.tensor_scalar(out=a[:], in0=x[:], scalar1=cw[:, ch, 2:3], scalar2=cw[:, ch, 5:6], op0=ALU.mult, op1=ALU.add),
                      reads=[x.tok, cw.tok], writes=[a.tok])
                for oi, o in enumerate((-2, -1, 1, 2)):
                    for tg in range(NT):
                        lo, hi = tg * TT, (tg + 1) * TT
                        slo, shi = max(lo + o, 0), min(hi + o, T)
                        dlo, dhi = slo - o, shi - o
                        t = tmp.next()
                        n_ = dhi - dlo
                        kb.op("dve", lambda e: e.tensor_tensor(out=t[:, 0:n_], in0=x[:, slo:shi], in1=cm[:, oi, dlo - lo:dhi - lo], op=ALU.mult),
                              reads=[x.tok, cm.tok], writes=[t.tok])
                        kb.op("dve", lambda e: e.scalar_tensor_tensor(out=a[:, dlo:dhi], in0=t[:, 0:n_], scalar=cw[:, ch, (o + 2):(o + 3)], in1=a[:, dlo:dhi],
                                                                    op0=ALU.mult, op1=ALU.add), reads=[t.tok, a.tok, cw.tok], writes=[a.tok])
                kb.op("act", lambda e: e.activation(out=a[:], in_=a[:], func=AF.Silu), reads=[a.tok], writes=[a.tok])
                kb.dma("sp", self.xcs[ch * P:(ch + 1) * P, :], a[:], reads=[a.tok], writes=[self.dt("xcs", ch)])
            kb.barrier()
        with ExitStack() as st:
            def tl(shape, name, dtype=F32):
                return Tile(kb, st, shape, dtype, name=name)
            yacc = tl([P, 8, T], "ssdy")
            kb.op("dve", lambda e: e.memset(yacc[:], 0.0), writes=[yacc.tok])
            tri = tl([P, 2, P], "tri")
            for d in range(2):
                kb.dma("sp", tri[:, d, :], self.tri[d], writes=[tri.tok])
            dd_T = tl([64, T], "ddT")
            col = tl([64, 3], "ssdcol")
            kb.dma("sp", col[:], self.ssdcol[l], writes=[col.tok])
            for r in range(4):
                kb.dma("sp", dd_T[r * 16:(r + 1) * 16, :], self.zT[cfg.ZDT * P:cfg.ZDT * P + 16, :], writes=[dd_T.tok])
            mcol = tl([64, 1], "mcol")
            kb.op("act", lambda e: e.activation(out=mcol[:], in_=col[:, 1:2], func=AF.Exp), reads=[col.tok], writes=[mcol.tok])
            kb.op("dve", lambda e: e.tensor_tensor(out=mcol[:], in0=mcol[:], in1=col[:, 2:3], op=ALU.mult), reads=[mcol.tok, col.tok], writes=[mcol.tok])
            kb.op("act", lambda e: e.activation(out=dd_T[:], in_=dd_T[:], func=AF.Exp, bias=col[:, 0:1], scale=1.0), reads=[dd_T.tok, col.tok], writes=[dd_T.tok])
            kb.op("dve", lambda e: e.tensor_scalar(out=dd_T[:], in0=dd_T[:], scalar1=1.0, scalar2=None, op0=ALU.add), reads=[dd_T.tok], writes=[dd_T.tok])
            kb.op("act", lambda e: e.activation(out=dd_T[:], in_=dd_T[:], func=AF.Ln), reads=[dd_T.tok], writes=[dd_T.tok])
            kb.op("dve", lambda e: e.tensor_scalar(out=dd_T[:], in0=dd_T[:], scalar1=mcol[:, 0:1], scalar2=None, op0=ALU.mult), reads=[dd_T.tok, mcol.tok], writes=[dd_T.tok])
            kcol = tl([P, 1], "kcol")
            kb.dma("sp", kcol[:], self.ssdkeep[:, :], writes=[kcol.tok])
            S = [tl([P, 512], "ssdS%d" % g) for g in range(2)]
            xsl = Rot(kb, st, 2, [P, 8, P], F32, name="xsl")
            bcl = Rot(kb, st, 2, [P, 4, P], F32, name="bcl")
            xtok = Rot(kb, st, 2, [P, 16, 64], F32, name="xtok")
            btok = Rot(kb, st, 2, [P, 2, P], F32, name="btok")
            ddk = Rot(kb, st, 2, [P, 64], F32, name="ddk")
            sm = Rot(kb, st, 6, [P, 16], F32, name="ssm")
            gm = Rot(kb, st, 2, [P, 2, P], F32, name="gm")
            Mt = Rot(kb, st, 2, [P, 16, P], F32, name="Mt")
            xdt = Rot(kb, st, 2, [P, 16, 64], F32, name="xdt")
            xw = Rot(kb, st, 2, [P, 16, 64], F32, name="xw")
            ecr = Rot(kb, st, 3, [P, P], F32, name="ecr")
            yt = Rot(kb, st, 3, [P, P], F32, name="yt")
            for d in range(2):
                trd = tri[:, d, :]
                for g in range(2):
                    kb.dma("sp", S[g][:], self.ssds0[l, d, g], writes=[S[g].tok])
                order = range(NC) if d == 0 else range(NC - 1, -1, -1)
                for c in order:
                    cols = slice(c * P, (c + 1) * P)
                    xs_, bc_ = xsl.next(), bcl.next()
                    kb.dma("sp", xs_[:], self.xcs[0:1024, :].rearrange("(j p) t -> p j t", p=P)[:, :, cols], reads=[self.dt("xcs", 0)], writes=[xs_.tok])
                    kb.dma("sp", bc_[:], self.xcs[1024:1536, :].rearrange("(j p) t -> p j t", p=P)[:, :, cols], writes=[bc_.tok])
                    xt_, bt_, dk = xtok.next(), btok.next(), ddk.next()
                    for hb_ in range(2):
                        pb2, pt2 = self.bank2()
                        for jj in range(4):
                            j = hb_ * 4 + jj
                            kb.op("pe", lambda e: e.transpose(pb2[:, jj * P:(jj + 1) * P], xs_[:, j, :], self.ident[:]), reads=[xs_.tok, self.ident.tok], writes=[pt2[0], pt2[1]])
                        kb.op("act", lambda e: e.activation(out=xt_[:, hb_ * 8:(hb_ + 1) * 8, :], in_=pb2[:, 0:512].rearrange("p (h q) -> p h q", q=64), func=AF.Copy),
                              reads=[pt2[0], pt2[1]], writes=[xt_.tok])
                    pb, pt = self.bank()
                    for g in range(2):
                        kb.op("pe", lambda e: e.transpose(pb[:, g * P:(g + 1) * P], bc_[:, g, :], self.ident[:]), reads=[bc_.tok, self.ident.tok], writes=[pt])
                    kb.op("pe", lambda e: e.transpose(pb[:, 256:320], dd_T[:, cols], self.ident[0:64, 0:64]), reads=[dd_T.tok, self.ident.tok], writes=[pt])
                    kb.op("dve", lambda e: e.tensor_copy(out=bt_[:], in_=pb[:, 0:256].rearrange("p (g n) -> p g n", n=P)), reads=[pt], writes=[bt_.tok])
                    kb.op("dve", lambda e: e.tensor_copy(out=dk[:], in_=pb[:, 256:320]), reads=[pt], writes=[dk.tok])
                    dtc = dk[:, 32 * d:32 * d + 16]
                    dac = dk[:, 32 * d + 16:32 * d + 32]
                    pbc, ptc = self.bank()
                    kb.op("pe", lambda e: e.matmul(pbc[:, 0:16], trd, dac, start=True, stop=True), reads=[tri.tok, dk.tok], writes=[ptc])
                    kb.op("pe", lambda e: e.matmul(pbc[:, 16:32], self.ones[:], dac, start=True, stop=True), reads=[self.ones.tok, dk.tok], writes=[ptc])
                    cumc, wts, edec = sm.next(), sm.next(), sm.next()
                    kb.op("dve", lambda e: e.tensor_copy(out=cumc[:], in_=pbc[:, 0:16]), reads=[ptc], writes=[cumc.tok])
                    kb.op("dve", lambda e: e.tensor_tensor(out=wts[:], in0=pbc[:, 16:32], in1=cumc[:], op=ALU.subtract), reads=[ptc, cumc.tok], writes=[wts.tok])
                    kb.op("act", lambda e: e.activation(out=wts[:], in_=wts[:], func=AF.Exp), reads=[wts.tok], writes=[wts.tok])
                    kb.op("pool", lambda e: e.tensor_tensor(out=wts[:], in0=wts[:], in1=dtc, op=ALU.mult), reads=[wts.tok, dk.tok], writes=[wts.tok])
                    kb.op("act", lambda e: e.activation(out=edec[:], in_=pbc[:, 16:32], func=AF.Exp), reads=[ptc], writes=[edec.tok])
                    pbg, ptg = self.bank()
                    for g in range(2):
                        kb.op("pe", lambda e: e.matmul(pbg[:, g * P:(g + 1) * P], bc_[:, g, :], bc_[:, 2 + g, :], start=True, stop=True), reads=[bc_.tok], writes=[ptg])
                    gm_ = gm.next()
                    kb.op("dve", lambda e: e.tensor_tensor(out=gm_[:], in0=pbg[:, 0:256].rearrange("p (g t) -> p g t", t=P),
                                                           in1=tri[:, d:d + 1, :].to_broadcast([P, 2, P]), op=ALU.mult), reads=[ptg, tri.tok], writes=[gm_.tok])
                    M_ = Mt.next()
                    for hq in range(2):
                        pb2, pt2 = self.bank2()
                        for hh in range(8):
                            h = hq * 8 + hh
                            kb.op("pe", lambda e: e.matmul(pb2[:, hh * P:(hh + 1) * P], dac[:, h:h + 1].to_broadcast([P, P]), trd, start=True, stop=True),
                                  reads=[dk.tok, tri.tok], writes=[pt2[0], pt2[1]])
                        for hh in range(8):
                            h = hq * 8 + hh
                            kb.op("dve", lambda e: e.tensor_scalar(out=M_[:, h, :], in0=pb2[:, hh * P:(hh + 1) * P], scalar1=cumc[:, h:h + 1], scalar2=0.0,
                                                                   op0=ALU.subtract, op1=ALU.min), reads=[pt2[0], pt2[1], cumc.tok], writes=[M_.tok])
                    kb.op("act", lambda e: e.activation(out=M_[:], in_=M_[:], func=AF.Exp), reads=[M_.tok], writes=[M_.tok])
                    for g in range(2):
                        kb.op("pool", lambda e: e.tensor_tensor(out=M_[:, 8 * g:8 * g + 8, :], in0=M_[:, 8 * g:8 * g + 8, :],
                                                               in1=gm_[:, g:g + 1, :].to_broadcast([P, 8, P]), op=ALU.mult), reads=[M_.tok, gm_.tok], writes=[M_.tok])
                    xd_, xw_ = xdt.next(), xw.next()
                    kb.op("pool", lambda e: e.tensor_tensor(out=xd_[:], in0=xt_[:], in1=dtc.unsqueeze(2).to_broadcast([P, 16, 64]), op=ALU.mult),
                          reads=[xt_.tok, dk.tok], writes=[xd_.tok])
                    kb.op("pool", lambda e: e.tensor_tensor(out=xw_[:], in0=xt_[:], in1=wts[:].unsqueeze(2).to_broadcast([P, 16, 64]), op=ALU.mult),
                          reads=[xt_.tok, wts.tok], writes=[xw_.tok])
                    for j in range(8):
                        g = j // 4
                        pb, pt = self.bank()
                        for h2 in range(2):
                            kb.op("pe", lambda e: e.matmul(pb[h2 * 64:(h2 + 1) * 64, 0:P], dac[:, 2 * j + h2:2 * j + h2 + 1].to_broadcast([P, 64]), trd, start=True, stop=True),
                                  reads=[dk.tok, tri.tok], writes=[pt])
                        kb.op("pe", lambda e: e.matmul(pb[:, P:2 * P], S[g][:, (j % 4) * P:(j % 4 + 1) * P], bc_[:, 2 + g, :], start=True, stop=True),
                              reads=[S[g].tok, bc_.tok], writes=[pt])
                        for h2 in range(2):
                            kb.op("pe", lambda e: e.matmul(pb[h2 * 64:(h2 + 1) * 64, 2 * P:3 * P], xd_[:, 2 * j + h2, :], M_[:, 2 * j + h2, :], start=True, stop=True),
                                  reads=[xd_.tok, M_.tok], writes=[pt])
                        ec = ecr.next()
                        kb.op("act", lambda e: e.activation(out=ec[:], in_=pb[:, 0:P], func=AF.Exp), reads=[pt], writes=[ec.tok])
                        y_ = yt.next()
                        kb.op("dve", lambda e: e.tensor_tensor(out=y_[:], in0=pb[:, P:2 * P], in1=ec[:], op=ALU.mult), reads=[pt, ec.tok], writes=[y_.tok])
                        kb.op("dve", lambda e: e.tensor_tensor(out=y_[:], in0=pb[:, 2 * P:3 * P], in1=y_[:], op=ALU.add), reads=[pt, y_.tok], writes=[y_.tok])
                        kb.op("pool", lambda e: e.tensor_tensor(out=yacc[:, j, cols], in0=yacc[:, j, cols], in1=y_[:], op=ALU.add), reads=[yacc.tok, y_.tok], writes=[yacc.tok])
                    seg_end = (c % 2 == 1) if d == 0 else (c % 2 == 0)
                    for g in range(2):
                        pb, pt = self.bank()
                        kb.op("pe", lambda e: e.matmul(pb, bt_[:, g, :], xw_[:, 8 * g:8 * g + 8, :], start=True, stop=True), reads=[bt_.tok, xw_.tok], writes=[pt])
                        kb.op("pool", lambda e: e.tensor_tensor(out=S[g][:].rearrange("p (h q) -> p h q", q=64), in0=S[g][:].rearrange("p (h q) -> p h q", q=64),
                                                               in1=edec[:, 8 * g:8 * g + 8].unsqueeze(2).to_broadcast([P, 8, 64]), op=ALU.mult),
                              reads=[S[g].tok, edec.tok], writes=[S[g].tok])
                        kb.op("dve", lambda e: e.tensor_tensor(out=S[g][:], in0=pb, in1=S[g][:], op=ALU.add), reads=[pt, S[g].tok], writes=[S[g].tok])
                        if seg_end:
                            seg = c // 2
                            kb.dma("sp", self.ssdo[l, d, seg, g], S[g][:], reads=[S[g].tok], writes=[self.dt("ssdo", l, d, seg, g)])
                            kb.op("dve", lambda e: e.tensor_scalar(out=S[g][:], in0=S[g][:], scalar1=kcol[:, 0:1], scalar2=None, op0=ALU.mult),
                                  reads=[S[g].tok, kcol.tok], writes=[S[g].tok])
            Dc = tl([P, 8], "ssdDc")
            gc = tl([P, 8], "ssdgc")
            kb.dma("sp", Dc[:], self.ssdD[l], writes=[Dc.tok])
            kb.dma("sp", gc[:], self.ssdg[l], writes=[gc.tok])
            ld = Rot(kb, st, 4, [P, TT], F32, name="sld")
            rs = tl([P, TT], "srs")
            for tg in range(NT):
                tc_ = slice(tg * TT, (tg + 1) * TT)
                pbn, ptn = self.bank()
                for j in range(8):
                    xj, zj = ld.next(), ld.next()
                    kb.dma("sp", xj[:], self.xcs[j * P:(j + 1) * P, tc_], writes=[xj.tok])
                    kb.dma("sp", zj[:], self.zT[(cfg.ZB + j) * P:(cfg.ZB + j + 1) * P, tc_], writes=[zj.tok])
                    kb.op("dve", lambda e: e.scalar_tensor_tensor(out=xj[:], in0=xj[:], scalar=Dc[:, j:j + 1], in1=yacc[:, j, tc_], op0=ALU.mult, op1=ALU.add),
                          reads=[xj.tok, Dc.tok, yacc.tok], writes=[xj.tok])
                    kb.op("dve", lambda e: e.tensor_tensor(out=yacc[:, j, tc_], in0=xj[:], in1=zj[:], op=ALU.mult), reads=[xj.tok, zj.tok], writes=[yacc.tok])
                    kb.op("act", lambda e: e.activation(out=zj[:], in_=yacc[:, j, tc_], func=AF.Square), reads=[yacc.tok], writes=[zj.tok])
                    kb.op("pe", lambda e: e.matmul(pbn, self.ones[:], zj[:], start=(j == 0), stop=(j == 7)), reads=[zj.tok, self.ones.tok], writes=[ptn])
                t = ld.next()
                kb.op("act", lambda e: e.activation(out=t[:], in_=pbn, func=AF.Sqrt, bias=self.epsc[:, 0:1], scale=1.0 / SW), reads=[ptn, self.epsc.tok], writes=[t.tok])
                kb.op("dve", lambda e: e.reciprocal(out=rs[:], in_=t[:]), reads=[t.tok], writes=[rs.tok])
                for j in range(8):
                    o = ld.next()
                    kb.op("dve", lambda e: e.scalar_tensor_tensor(out=o[:], in0=yacc[:, j, tc_], scalar=gc[:, j:j + 1], in1=rs[:], op0=ALU.mult, op1=ALU.mult),
                          reads=[yacc.tok, gc.tok, rs.tok], writes=[o.tok])
                    kb.dma("sp", self.yb.bitcast(F32)[j * P:(j + 1) * P, tc_], o[:], reads=[o.tok], writes=[self.dt("yb", j, tg)])
            kb.barrier()


def prep_common(cfg, inp):
    NK, NJ, DEPTH = cfg.NK, cfg.NJ, cfg.DEPTH
    d = {}
    d["wmodn"] = inp["w_mod"]
    d["bmod"] = np.stack([colv(inp["b_mod"][l]) for l in range(DEPTH)])
    d["normg"] = np.stack([np.concatenate([colv(inp["norm_g"][l, i]) for i in range(3)], axis=1) for l in range(DEPTH)])
    d["fng"] = colv(inp["final_norm_g"])
    d["w1"] = np.stack([np.stack([blk(inp["ffn_w_in"][l, w]) for w in range(2)]) for l in range(DEPTH)])
    d["w2"] = np.stack([np.stack([blk(inp["ffn_w_out"][l, w]) for w in range(2)]) for l in range(DEPTH)])
    return d


def gp_layout(a):
    g, p = a.shape[0], a.shape[1]
    rest = a.shape[2:]
    b = a.reshape((32, 2, 64) + rest)
    b = np.moveaxis(b, 0, 2)
    return np.ascontiguousarray(b.reshape((128, 32) + rest))


def prep_mixer(cfg, inp):
    NK, DEPTH = cfg.NK, cfg.DEPTH
    d = {}
    wins = []
    for l in range(DEPTH):
        W = inp["w_in"][l]
        Wn = np.concatenate([W[:, 0:3328], W[:, 3328:3328 + 2560], W[:, 5904:6928], W[:, 6928:], W[:, 5888:5904],
                             np.zeros((W.shape[0], 112), np.float32)], axis=1)
        wins.append(blk(Wn))
    d["win"] = np.stack(wins)
    d["wpa"] = np.stack([blk(inp["w_proj_a"][l]) for l in range(DEPTH)])
    d["wpb"] = np.stack([blk(inp["w_proj_b"][l]) for l in range(DEPTH)])
    d["wpc"] = np.stack([blk(inp["w_proj_c"][l]) for l in range(DEPTH)])
    d["wo"] = np.stack([blk(inp["w_out"][l]) for l in range(DEPTH)])
    d["identd"] = np.eye(P, dtype=np.float32)
    lam = np.zeros((DEPTH, 2, P, 3, 32), np.float32)
    sb = np.zeros((DEPTH, 2, P, 2, 32, 16), np.float32)
    sc = np.zeros((DEPTH, 2, 2, 32, P, P), np.float32)
    for l in range(DEPTH):
        for dd in range(2):
            lam[l, dd, :, 0] = gp_layout(inp["s5_lambda_re"][l, dd])
            lam[l, dd, :, 1] = gp_layout(inp["s5_lambda_im"][l, dd])
            lam[l, dd, :, 2] = gp_layout(np.repeat(inp["s5_log_dt"][l, dd][:, None], 64, axis=1))
            sb[l, dd, :, 0] = gp_layout(inp["s5_b_re"][l, dd])
            sb[l, dd, :, 1] = gp_layout(inp["s5_b_im"][l, dd])
            for ri, key in enumerate(("s5_c_re", "s5_c_im")):
                C = inp[key][l, dd]
                for q in range(32):
                    for g2 in range(2):
                        g8 = 2 * (q % 4) + g2
                        sc[l, dd, ri, q, g2 * 64:(g2 + 1) * 64, g8 * 16:(g8 + 1) * 16] = C[2 * q + g2].T
    d["s5lam"], d["s5b"], d["s5c"] = lam, sb, sc
    d["s5d"] = np.stack([colv(inp["s5_d"][l]) for l in range(DEPTH)])
    rc = np.zeros((DEPTH, P, 5, 8), np.float32)
    rw0 = np.zeros((DEPTH, P, 2, 2, 8), np.float32)
    rw2 = np.zeros((DEPTH, P, 2, RW), np.float32)
    for l in range(DEPTH):
        rc[l, :, 0] = colv(inp["rwkv_k_k"][l])
        rc[l, :, 1] = colv(inp["rwkv_k_a"][l])
        rc[l, :, 2] = colv(inp["rwkv_r_k"][l].reshape(-1))
        rc[l, :, 3] = colv(inp["rwkv_ln_g"][l])
        rc[l, :, 4] = colv(inp["rwkv_ln_b"][l])
        for dd in range(2):
            rw0[l, :, dd, 0] = colv(inp["rwkv_w0"][l, dd])
            rw0[l, :, dd, 1] = colv(inp["rwkv_a0"][l, dd])
            rw2[l, 0:64, dd] = inp["rwkv_w2"][l, dd]
            rw2[l, 64:128, dd] = inp["rwkv_a2"][l, dd]
    d["rwcol"], d["rww0"], d["rww2"] = rc, rw0, rw2
    d["rwmu"] = np.stack([colv(inp["rwkv_mu"][l]) for l in range(DEPTH)])
    d["rwg2"] = np.ascontiguousarray(inp["rwkv_g2"])
    hb = np.zeros((2, P, P), np.float32)
    hb[0, 0:64, 0:64] = 1.0
    hb[0, 64:, 64:] = 1.0
    hb[1] = hb[0] / 64.0
    d["hblk"] = hb
    i_ = np.arange(64)[:, None]
    t_ = np.arange(64)[None, :]
    rt = np.stack([(i_ < t_), (i_ > t_), (i_ <= t_)]).astype(np.float32)
    d["rwtri"] = np.concatenate([rt, rt], axis=1)
    cmk = np.ones((P, cfg.T), np.float32)
    cmk[:, 0::64] = 0.0
    d["rwcm"] = cmk
    tri = np.zeros((2, P, P), np.float32)
    tri[0] = np.triu(np.ones((P, P), np.float32))
    tri[1] = np.tril(np.ones((P, P), np.float32))
    d["tri"] = tri
    cw = np.zeros((DEPTH, P, 12, 6), np.float32)
    scol = np.zeros((DEPTH, 64, 3), np.float32)
    for l in range(DEPTH):
        w = inp["ssd_conv_w"][l]
        for j in range(5):
            cw[l, :, :, j] = colv(w[j])
        cw[l, :, :, 5] = colv(inp["ssd_conv_b"][l])
        for dd in range(2):
            scol[l, 32 * dd:32 * dd + 16, 0] = inp["ssd_dt_bias"][l, dd]
            scol[l, 32 * dd + 16:32 * dd + 32, 0] = inp["ssd_dt_bias"][l, dd]
            scol[l, 32 * dd + 16:32 * dd + 32, 1] = inp["ssd_a_log"][l, dd]
            scol[l, 32 * dd:32 * dd + 16, 2] = 1.0
            scol[l, 32 * dd + 16:32 * dd + 32, 2] = -1.0
    d["ssdcw"], d["ssdcol"] = cw, scol
    d["ssdD"] = np.stack([colv(np.repeat(inp["ssd_d"][l], 64)) for l in range(DEPTH)])
    d["ssdg"] = np.stack([colv(inp["ssd_norm_g"][l]) for l in range(DEPTH)])
    return d


def core_mixer_inputs(cfg, inp, c, d):
    DEPTH = cfg.DEPTH
    prompt = c < cfg.NPC
    keep = np.ones((P, TT), np.float32)
    if prompt:
        keep[:, 0::256] = 0.0
    d["keepT"] = keep
    s0 = np.zeros((DEPTH, 2, P, 2, 32), np.float32)
    if not prompt:
        b = c - cfg.NPC
        for l in range(DEPTH):
            for dd in range(2):
                s0[l, dd, :, 0] = gp_layout(inp["state_s5_re"][b, l, dd])
                s0[l, dd, :, 1] = gp_layout(inp["state_s5_im"][b, l, dd])
    d["s5s0"] = s0
    cm = np.ones((P, 4, TT), np.float32)
    if prompt:
        for oi, o in enumerate((-2, -1, 1, 2)):
            for t in range(TT):
                if (t + o) // 256 != t // 256:
                    cm[:, oi, t] = 0.0
    d["cmT"] = cm
    smk = np.zeros((P, 4, TT), np.float32)
    tt_ = np.arange(TT)
    if prompt:
        smk[:, 0] = np.where(tt_ % 256 != 0, 0.5, 0.0)
        smk[:, 1] = np.where(tt_ % 256 != 255, 0.5, 0.0)
    else:
        smk[:, 0] = np.where(tt_ % 64 != 0, 0.25, 0.0)
        smk[:, 1] = np.where(tt_ % 64 != 63, 0.25, 0.0)
        smk[:, 2] = 0.25
        smk[:, 3] = 0.25
    d["rwsm"] = smk
    rs0 = np.zeros((DEPTH, 2, 64, 16, 64), np.float32)
    if not prompt:
        b = c - cfg.NPC
        sr = inp["state_rwkv"][b]
        for l in range(DEPTH):
            for dd in range(2):
                rs0[l, dd] = sr[l, dd].transpose(2, 0, 1)
    d["rws0"] = rs0
    d["ssdkeep"] = np.full((P, 1), 0.0 if prompt else 1.0, np.float32)
    ss0 = np.zeros((DEPTH, 2, 2, P, 512), np.float32)
    if not prompt:
        b = c - cfg.NPC
        st_ = inp["state_ssd"][b]
        for l in range(DEPTH):
            for dd in range(2):
                for g in range(2):
                    ss0[l, dd, g] = st_[l, dd, 8 * g:8 * g + 8].transpose(2, 0, 1).reshape(P, 512)
    d["ssds0"] = ss0


def core_inputs(cfg, inp, common, c):
    d = dict(common)
    T = cfg.T
    if c < cfg.NPC:
        ns = T // 256
        x = inp["x_prompt"][c * ns:(c + 1) * ns].reshape(T, cfg.DM)
        cond = inp["c_ctx"]
    else:
        b = c - cfg.NPC
        x = inp["x_sample"][b]
        cond = inp["c"][b]
    d["xT"] = np.ascontiguousarray(x.T)
    d["cond"] = colv(cond)
    core_mixer_inputs(cfg, inp, c, d)
    return d


def run(cfg, inp):
    b = Builder(cfg)
    nc = b.build()
    common = prep_common(cfg, inp)
    common.update(prep_mixer(cfg, inp))
    n = cfg.NPC + cfg.NSC
    maps = [core_inputs(cfg, inp, common, c) for c in range(n)]
    for m in maps:
        for k in list(m.keys()):
            if k not in b.din:
                del m[k]
            else:
                m[k] = np.ascontiguousarray(m[k], dtype=np.float32)
    res = run_bass_kernel_spmd(nc, maps, core_ids=list(range(n)))
    return res.results


def assemble(cfg, inp, results):
    T, DEPTH, NSEG = cfg.T, cfg.DEPTH, cfg.NSEG
    ns = T // 256
    yp = np.concatenate([results[c]["yT"].T.reshape(ns, 256, cfg.DM) for c in range(cfg.NPC)], axis=0)
    ys = np.stack([results[cfg.NPC + b]["yT"].T for b in range(cfg.NSC)], axis=0)
    nb = cfg.NPC * NSEG
    st_rwkv = np.zeros((nb, DEPTH, 2, 16, 64, 64), np.float32)
    st_ssd = np.zeros((nb, DEPTH, 2, 16, 64, 128), np.float32)
    s5 = [np.zeros((nb, DEPTH, 2, 64, 64), np.float32) for _ in range(2)]
    for c in range(cfg.NPC):
        r = results[c]
        if "s5o" in r:
            o = r["s5o"]
            o = o.reshape(DEPTH, 2, 64, 2, 2, 32, NSEG)
            o = o.transpose(6, 0, 3, 4, 5, 1, 2)
            o = o.reshape(NSEG, DEPTH, 2, 2, 64, 64).copy()
            o[:, :, 1] = o[::-1, :, 1]
            for ri in range(2):
                s5[ri][c * NSEG:(c + 1) * NSEG] = o[:, :, :, ri]
        if "rwo" in r:
            o = r["rwo"]
            o = o.transpose(2, 0, 1, 4, 5, 3).copy()
            o[:, :, 1] = o[::-1, :, 1]
            st_rwkv[c * NSEG:(c + 1) * NSEG] = o
        if "ssdo" in r:
            o = r["ssdo"]
            o = o.reshape(DEPTH, 2, NSEG, 2, P, 8, 64).transpose(2, 0, 1, 3, 5, 6, 4)
            st_ssd[c * NSEG:(c + 1) * NSEG] = o.reshape(NSEG, DEPTH, 2, 16, 64, 128)
    return yp, ys, st_rwkv, st_ssd, s5[0], s5[1]


def kernel(**inputs):
    cfg = Cfg()
    inp = {k: np.asarray(v) for k, v in inputs.items()}
    results = run(cfg, inp)
    return assemble(cfg, inp, results)
```

```python
import numpy as np
from contextlib import ExitStack
import concourse.bass as bass
import concourse.mybir as mybir
from concourse.bass_utils import run_bass_kernel_spmd

F32 = mybir.dt.float32
F32R = mybir.dt.float32r
AF = mybir.ActivationFunctionType
ALU = mybir.AluOpType
P = 128
TT = 512
NDS = 40

RW = 1024
RH = 64
SW = 1024
SXBC = 1536
S5W = 1024
EPS = 1e-6


class Cfg:
    def __init__(self, DM=2048, DFF=5504, DEPTH=2, T=2048, NPC=4, NSC=4, mix=(1, 1, 1)):
        self.DM, self.DFF, self.DEPTH, self.T, self.NPC, self.NSC = DM, DFF, DEPTH, T, NPC, NSC
        self.NK = DM // P
        self.NJ = DFF // P
        self.NT = T // TT
        self.NSEG = T // 256
        self.mix = mix
        self.ZA, self.ZB, self.ZC = 0, 26, 46
        self.ZG = 54
        self.ZDT = 54 + 3 * self.NK
        self.NZ = self.ZDT + 1


class Tok:
    __slots__ = ("w", "r")

    def __init__(self):
        self.w = []
        self.r = {}


class KB:
    def __init__(self, nc):
        self.nc = nc
        self.E = {"pe": nc.tensor, "dve": nc.vector, "act": nc.scalar, "pool": nc.gpsimd, "sp": nc.sync}
        self.tick = {e: 0 for e in self.E}
        self.seen = {e: {} for e in self.E}
        self.stack = ExitStack()
        self.sem = {e: self.stack.enter_context(nc.semaphore("s_" + e)) for e in self.E}
        self.dsem = [self.stack.enter_context(nc.semaphore("d%d" % i)) for i in range(NDS)]
        self.dval = [0] * NDS
        self.drr = 0
        self.nid = 0
        self.ninstr = 0

    def name(self, s):
        self.nid += 1
        return "%s_%d" % (s, self.nid)

    def _wait(self, e, dep):
        kind, key, val = dep
        if kind == "e" and key == e and e == "pe":
            return
        k = (kind, key)
        if self.seen[e].get(k, 0) >= val:
            return
        self.seen[e][k] = val
        sem = self.sem[key] if kind == "e" else self.dsem[key]
        self.E[e].wait_ge(sem, val)
        self.ninstr += 1

    def _sync(self, e, reads, writes):
        for t in reads:
            for d in t.w:
                self._wait(e, d)
        for t in writes:
            for d in t.w:
                self._wait(e, d)
            for k, v in t.r.items():
                self._wait(e, (k[0], k[1], v))

    def _mark(self, me, reads, writes):
        for t in writes:
            t.w = [me]
            t.r = {}
        for t in reads:
            if not (len(t.w) == 1 and t.w[0] is me):
                k = (me[0], me[1])
                if t.r.get(k, 0) < me[2]:
                    t.r[k] = me[2]

    def join(self, dst, srcs):
        w = list(dst.w)
        for s_ in srcs:
            w.extend(s_.w)
        dst.w = w

    def op(self, e, fn, reads=(), writes=()):
        self._sync(e, reads, writes)
        ins = fn(self.E[e])
        self.tick[e] += 1
        ins.then_inc(self.sem[e], 1)
        self.ninstr += 1
        self._mark(("e", e, self.tick[e]), reads, writes)
        return ins

    def dma(self, q, out, in_, reads=(), writes=(), **kw):
        self._sync(q, reads, writes)
        s = self.drr
        self.drr = (self.drr + 1) % NDS
        if self.dval[s]:
            self._wait(q, ("d", s, self.dval[s]))
        ins = self.E[q].dma_start(out=out, in_=in_, **kw)
        self.dval[s] += 16
        ins.then_inc(self.dsem[s], 16)
        self.ninstr += 1
        self._mark(("d", s, self.dval[s]), reads, writes)
        return ins

    def barrier(self):
        for e in self.E:
            for e2 in self.E:
                if e2 != e and self.tick[e2]:
                    self._wait(e, ("e", e2, self.tick[e2]))
            for s in range(NDS):
                if self.dval[s]:
                    self._wait(e, ("d", s, self.dval[s]))


class Tile:
    def __init__(self, kb, stack, shape, dtype, ntok=1, name="t"):
        self.t = stack.enter_context(kb.nc.sbuf_tensor(kb.name(name), list(shape), dtype))
        self.toks = [Tok() for _ in range(ntok)]
        self.dtype = dtype

    @property
    def tok(self):
        return self.toks[0]

    def __getitem__(self, k):
        return self.t[k]

    def f32(self, k):
        return self.t[k].bitcast(F32)


class Rot:
    def __init__(self, kb, stack, n, shape, dtype, name="r"):
        self.tiles = [Tile(kb, stack, shape, dtype, name=name) for _ in range(n)]
        self.i = 0

    def next(self):
        t = self.tiles[self.i]
        self.i = (self.i + 1) % len(self.tiles)
        return t


def blk(W):
    K, M = W.shape
    return np.ascontiguousarray(W.reshape(K // P, P, M // P, P).transpose(2, 1, 0, 3)).reshape(M // P, P, (K // P) * P)


def colv(v):
    return np.ascontiguousarray(v.reshape(-1, P).T)


class Builder:
    def __init__(self, cfg):
        self.cfg = cfg
        self.nc = bass.Bass("TRN2", target_bir_lowering=False)
        self.kb = KB(self.nc)
        self.din = {}
        self.dout = {}
        self.dtok = {}

    def inp(self, name, shape, dtype=F32):
        self.din[name] = self.nc.dram_tensor(name, list(shape), dtype, kind="ExternalInput").ap()
        return self.din[name]

    def outp(self, name, shape, dtype=F32):
        self.dout[name] = self.nc.dram_tensor(name, list(shape), dtype, kind="ExternalOutput").ap()
        return self.dout[name]

    def scratch(self, name, shape, dtype=F32):
        if getattr(self.cfg, "debug", False) and dtype == F32:
            return self.outp(name, shape, dtype)
        return self.nc.dram_tensor(name, list(shape), dtype, kind="Internal").ap()

    def dt(self, *key):
        if key not in self.dtok:
            self.dtok[key] = Tok()
        return self.dtok[key]

    def build(self):
        cfg, nc, kb = self.cfg, self.nc, self.kb
        NK, NJ, T, NT, DEPTH = cfg.NK, cfg.NJ, cfg.T, cfg.NT, cfg.DEPTH
        self.xT = self.inp("xT", [cfg.DM, T], F32R)
        self.cond = self.inp("cond", [P, NK])
        self.wmodn = self.inp("wmodn", [DEPTH, cfg.DM, 9 * cfg.DM])
        self.bmod = self.inp("bmod", [DEPTH, P, 9 * NK])
        self.normg = self.inp("normg", [DEPTH, P, 3 * NK])
        self.fng = self.inp("fng", [P, NK])
        self.w1 = self.inp("w1", [DEPTH, 2, 2 * NJ, P, NK * P], F32R)
        self.w2 = self.inp("w2", [DEPTH, 2, NK, P, NJ * P], F32R)
        self.yT = self.outp("yT", [cfg.DM, T])
        self.xres = self.scratch("xres", [cfg.DM, T], F32R)
        self.mixer_decl()

        with ExitStack() as gs:
            self.gs = gs
            self.ps = [kb.stack.enter_context(nc.psum_tensor(kb.name("ps"), [P, 1024], F32)) for _ in range(4)]
            self.pstok = [[Tok(), Tok()] for _ in range(4)]
            self.psi = 0
            self.ones = Tile(kb, gs, [P, P], F32, name="ones")
            kb.op("dve", lambda e: e.memset(self.ones[:], 1.0), writes=[self.ones.tok])
            self.mod = Tile(kb, gs, [P, DEPTH, 9 * NK], F32, name="mod")
            self.modA = Tile(kb, gs, [P, DEPTH, 3 * NK], F32, name="modA")
            self.modG = Tile(kb, gs, [P, DEPTH, 3 * NK], F32, name="modG")
            self.fngt = Tile(kb, gs, [P, NK], F32, name="fng")
            kb.dma("sp", self.fngt[:], self.fng[:, :], writes=[self.fngt.tok])
            self.mixer_consts()
            self.phase_mod()
            src = self.xT
            for l in range(DEPTH):
                self.phase_ffn(l, 0, src)
                src = self.xres
                self.phase_mix(l)
                self.phase_ffn(l, 1, src)
            self.phase_final()
            kb.barrier()
        kb.stack.close()
        return nc

    def bank(self):
        i = self.psi
        self.psi = (self.psi + 1) % 8
        return self.ps[i // 2][:, (i % 2) * 512:(i % 2) * 512 + 512], self.pstok[i // 2][i % 2]

    def bank2(self):
        if self.psi % 2:
            self.psi = (self.psi + 1) % 8
        i = self.psi
        self.psi = (self.psi + 2) % 8
        return self.ps[i // 2], self.pstok[i // 2]

    def phase_mod(self):
        cfg, kb = self.cfg, self.kb
        NK, DEPTH = cfg.NK, cfg.DEPTH
        NM = 9 * NK
        NCOL = NM * P
        CG = 512 if NCOL % 512 == 0 else 256
        with ExitStack() as st:
            cnd = Tile(kb, st, [P, NK], F32, name="cnd")
            sc = Tile(kb, st, [P, NK], F32, name="scnd")
            bm = Tile(kb, st, [P, DEPTH, NM], F32, name="bm")
            rows = Rot(kb, st, 3, [1, CG], F32, name="mrow")
            ng = Tile(kb, st, [P, DEPTH, 3 * NK], F32, name="ng")
            wr = Rot(kb, st, 2, [P, NK, CG], F32, name="wm")
            kb.dma("sp", cnd[:], self.cond[:, :], writes=[cnd.tok])
            for l in range(DEPTH):
                kb.dma("sp", ng[:, l, :], self.normg[l], writes=[ng.tok])
                kb.dma("sp", bm[:, l, :], self.bmod[l], writes=[bm.tok])
            kb.op("act", lambda e: e.activation(out=sc[:], in_=cnd[:], func=AF.Silu), reads=[cnd.tok], writes=[sc.tok])
            MC = CG // P
            for l in range(DEPTH):
                wv = self.wmodn[l].rearrange("(k p) c -> p k c", p=P)
                for cg in range(NCOL // CG):
                    w = wr.next()
                    kb.dma("sp", w[:], wv[:, :, cg * CG:(cg + 1) * CG], writes=[w.tok])
                    pb, pt = self.bank()
                    for kc in range(NK):
                        kb.op("pe", lambda e: e.matmul(pb[0:1, 0:CG], sc[:, kc:kc + 1], w[:, kc, :], start=(kc == 0), stop=(kc == NK - 1)),
                              reads=[w.tok, sc.tok], writes=[pt])
                    row = rows.next()
                    kb.op("act", lambda e: e.activation(out=row[:], in_=pb[0:1, 0:CG], func=AF.Copy), reads=[pt], writes=[row.tok])
                    pb2, pt2 = self.bank()
                    for mm in range(MC):
                        kb.op("pe", lambda e: e.matmul(pb2[:, mm:mm + 1], row[0:1, mm * P:(mm + 1) * P], self.ones[0:1, 0:1], start=True, stop=True),
                              reads=[row.tok, self.ones.tok], writes=[pt2])
                    m0 = cg * MC
                    kb.op("dve", lambda e: e.tensor_tensor(out=self.mod[:, l, m0:m0 + MC], in0=pb2[:, 0:MC], in1=bm[:, l, m0:m0 + MC], op=ALU.add),
                          reads=[pt2, bm.tok], writes=[self.mod.tok])
                for i in range(3):
                    scs = self.mod[:, l, (3 * i + 1) * NK:(3 * i + 2) * NK]
                    kb.op("dve", lambda e: e.scalar_tensor_tensor(
                        out=self.modA[:, l, i * NK:(i + 1) * NK], in0=scs, scalar=1.0, in1=ng[:, l, i * NK:(i + 1) * NK],
                        op0=ALU.add, op1=ALU.mult), reads=[self.mod.tok, ng.tok], writes=[self.modA.tok])
                    gs_ = self.mod[:, l, (3 * i + 2) * NK:(3 * i + 3) * NK]
                    kb.op("dve", lambda e: e.tensor_scalar(
                        out=self.modG[:, l, i * NK:(i + 1) * NK], in0=gs_, scalar1=(1.0 if i == 1 else 0.5), scalar2=None,
                        op0=ALU.mult), reads=[self.mod.tok], writes=[self.modG.tok])
            kb.barrier()

    def norm_tile(self, hb, tmp, rstd, A, SH, out_dtype_r=True):
        cfg, kb = self.cfg, self.kb
        NK = cfg.NK
        pb, pt = self.bank()
        for kc in range(NK):
            t = tmp.next()
            kb.op("act", lambda e, t=t, kc=kc: e.activation(out=t[:], in_=hb.f32((slice(None), kc)), func=AF.Square),
                  reads=[hb.tok], writes=[t.tok])
            kb.op("pe", lambda e, t=t, kc=kc: e.matmul(pb, self.ones[:], t[:], start=(kc == 0), stop=(kc == NK - 1)),
                  reads=[t.tok, self.ones.tok], writes=[pt])
        t = tmp.next()
        kb.op("act", lambda e: e.activation(out=t[:], in_=pb, func=AF.Sqrt, bias=self.epsc[:, 0:1], scale=1.0 / cfg.DM),
              reads=[pt, self.epsc.tok], writes=[t.tok])
        kb.op("dve", lambda e: e.reciprocal(out=rstd[:], in_=t[:]), reads=[t.tok], writes=[rstd.tok])
        for kc in range(NK):
            t = tmp.next()
            kb.op("dve", lambda e, t=t, kc=kc: e.scalar_tensor_tensor(out=t[:], in0=hb.f32((slice(None), kc)), scalar=A[:, kc:kc + 1],
                                                                    in1=rstd[:], op0=ALU.mult, op1=ALU.mult),
                  reads=[hb.tok, rstd.tok, self.modA.tok, self.fngt.tok], writes=[t.tok])
            if SH is not None:
                kb.op("act", lambda e, t=t, kc=kc: e.activation(out=hb[:, kc], in_=t[:], func=AF.Identity, bias=SH[:, kc:kc + 1], scale=1.0),
                      reads=[t.tok, self.mod.tok], writes=[hb.tok])
            else:
                kb.op("act", lambda e, t=t, kc=kc: e.activation(out=hb[:, kc], in_=t[:], func=AF.Copy),
                      reads=[t.tok], writes=[hb.tok])

    def load_xtile(self, hb, src, tt):
        kb, cfg = self.kb, self.cfg
        sv = src.rearrange("(k p) t -> p k t", p=P)[:, :, tt * TT:(tt + 1) * TT]
        kb.dma("pool", hb[:], sv, reads=[self.dt("x", tt)], writes=[hb.tok])

    def phase_ffn(self, l, w, src):
        cfg, kb = self.cfg, self.kb
        NK, NJ, NT = cfg.NK, cfg.NJ, cfg.NT
        JH = (NJ + 1) // 2
        WSZ = max(NK, JH) * P
        A = self.modA[:, l, (2 * w) * NK:(2 * w + 1) * NK]
        SH = self.mod[:, l, (6 * w) * NK:(6 * w + 1) * NK]
        G = self.modG[:, l, (2 * w) * NK:(2 * w + 1) * NK]
        with ExitStack() as st:
            hb = Tile(kb, st, [P, NK, TT], F32R, name="hb")
            act = Tile(kb, st, [P, NJ, TT], F32R, ntok=NJ, name="act")
            wr = Rot(kb, st, 4, [P, WSZ], F32R, name="wf")
            tmp = Rot(kb, st, 3, [P, TT], F32, name="tmp")
            xc = Rot(kb, st, 3, [P, TT], F32, name="xc")
            rstd = Tile(kb, st, [P, TT], F32, name="rstd")
            srcf = src.bitcast(F32)
            xresf = self.xres.bitcast(F32)
            for tt in range(NT):
                self.load_xtile(hb, src, tt)
                self.norm_tile(hb, tmp, rstd, A, SH)
                for j in range(NJ):
                    pbs = []
                    for half in range(2):
                        wt = wr.next()
                        kb.dma("pool", wt[:, 0:NK * P], self.w1[l, w, half * NJ + j], writes=[wt.tok])
                        pb, pt = self.bank()
                        for kc in range(NK):
                            kb.op("pe", lambda e, wt=wt, kc=kc, pb=pb: e.matmul(pb, wt[:, kc * P:(kc + 1) * P], hb[:, kc],
                                                                              start=(kc == 0), stop=(kc == NK - 1)),
                                  reads=[wt.tok, hb.tok], writes=[pt])
                        pbs.append((pb, pt))
                    t = tmp.next()
                    kb.op("act", lambda e, t=t: e.activation(out=t[:], in_=pbs[0][0], func=AF.Silu), reads=[pbs[0][1]], writes=[t.tok])
                    kb.op("dve", lambda e, t=t, j=j: e.tensor_tensor(out=act[:, j], in0=pbs[1][0], in1=t[:], op=ALU.mult),
                          reads=[pbs[1][1], t.tok], writes=[act.toks[j]])
                for n in range(NK):
                    pb, pt = self.bank()
                    for hf in range(2):
                        j0, j1 = (0, JH) if hf == 0 else (JH, NJ)
                        wt = wr.next()
                        kb.dma("pool", wt[:, 0:(j1 - j0) * P], self.w2[l, w, n][:, j0 * P:j1 * P], writes=[wt.tok])
                        for j in range(j0, j1):
                            kb.op("pe", lambda e, wt=wt, j=j, j0=j0: e.matmul(pb, wt[:, (j - j0) * P:(j - j0 + 1) * P], act[:, j],
                                                                            start=(j == 0), stop=(j == NJ - 1)),
                                  reads=[wt.tok, act.toks[j]], writes=[pt])
                    x = xc.next()
                    kb.dma("sp", x[:], srcf[n * P:(n + 1) * P, tt * TT:(tt + 1) * TT], reads=[self.dt("x", tt)], writes=[x.tok])
                    kb.op("dve", lambda e, x=x, n=n: e.scalar_tensor_tensor(out=x[:], in0=pb, scalar=G[:, n:n + 1], in1=x[:],
                                                                          op0=ALU.mult, op1=ALU.add),
                          reads=[pt, x.tok, self.modG.tok], writes=[x.tok])
                    kb.dma("sp", xresf[n * P:(n + 1) * P, tt * TT:(tt + 1) * TT], x[:], reads=[x.tok], writes=[self.dt("xo", tt, n)])
                xt_ = self.dt("x", tt)
                xt_.w = []
                kb.join(xt_, [self.dt("xo", tt, n) for n in range(NK)])
            kb.barrier()

    def phase_final(self):
        cfg, kb = self.cfg, self.kb
        NK, NT = cfg.NK, cfg.NT
        with ExitStack() as st:
            hb = Tile(kb, st, [P, NK, TT], F32R, name="hbf")
            tmp = Rot(kb, st, 3, [P, TT], F32, name="tmpf")
            rstd = Tile(kb, st, [P, TT], F32, name="rstdf")
            for tt in range(NT):
                self.load_xtile(hb, self.xres, tt)
                self.norm_tile(hb, tmp, rstd, self.fngt, None)
                dv = self.yT.rearrange("(k p) t -> p k t", p=P)[:, :, tt * TT:(tt + 1) * TT]
                kb.dma("sp", dv, hb.f32(slice(None)), reads=[hb.tok], writes=[self.dt("y", tt)])
            kb.barrier()

    def mixer_decl(self):
        cfg = self.cfg
        NK, T, DEPTH, NSEG = cfg.NK, cfg.T, cfg.DEPTH, cfg.NSEG
        self.win = self.inp("win", [DEPTH, cfg.NZ, P, NK * P], F32R)
        self.zT = self.scratch("zT", [cfg.NZ * P, T])
        self.wpa = self.inp("wpa", [DEPTH, NK, P, 8 * P], F32R)
        self.wpb = self.inp("wpb", [DEPTH, NK, P, 8 * P], F32R)
        self.wpc = self.inp("wpc", [DEPTH, 2 * NK, P, 8 * P], F32R)
        self.wo = self.inp("wo", [DEPTH, NK, P, NK * P], F32R)
        self.ya = self.scratch("ya", [RW, T], F32R)
        self.yb = self.scratch("yb", [SW, T], F32R)
        self.yc = self.scratch("yc", [S5W, T], F32R)
        self.keepT = self.inp("keepT", [P, TT])
        self.rwp = self.scratch("rwp", [2, 5, RW, T], F32R)
        self.rwbon = self.scratch("rwbon", [RW, T])
        self.rwgc = self.scratch("rwgc", [2, RW, T // 64])
        self.rwsm = self.inp("rwsm", [P, 4, TT])
        self.rwcm = self.inp("rwcm", [P, T])
        self.rwcol = self.inp("rwcol", [DEPTH, P, 5, 8])
        self.rwmu = self.inp("rwmu", [DEPTH, P, 26])
        self.rww0 = self.inp("rww0", [DEPTH, P, 2, 2, 8])
        self.rww2 = self.inp("rww2", [DEPTH, P, 2, RW])
        self.rwg2 = self.inp("rwg2", [DEPTH, P, RW])
        self.hblk = self.inp("hblk", [2, P, P])
        self.rwtri = self.inp("rwtri", [3, P, 64])
        self.rws0 = self.inp("rws0", [DEPTH, 2, 64, 16, 64], F32R)
        self.rwo = self.outp("rwo", [DEPTH, 2, NSEG, 64, 16, 64])
        self.rwoT = self.scratch("rwoT", [2, RW, T])
        self.xcs = self.scratch("xcs", [SXBC, T])
        self.tri = self.inp("tri", [2, P, P])
        self.cmT = self.inp("cmT", [P, 4, TT])
        self.ssdcw = self.inp("ssdcw", [DEPTH, P, 12, 6])
        self.ssdcol = self.inp("ssdcol", [DEPTH, 64, 3])
        self.ssdD = self.inp("ssdD", [DEPTH, P, 8])
        self.ssdg = self.inp("ssdg", [DEPTH, P, 8])
        self.ssds0 = self.inp("ssds0", [DEPTH, 2, 2, P, 512])
        self.ssdkeep = self.inp("ssdkeep", [P, 1])
        self.ssdo = self.outp("ssdo", [DEPTH, 2, NSEG, 2, P, 512])
        self.s5lam = self.inp("s5lam", [DEPTH, 2, P, 3, 32])
        self.s5b = self.inp("s5b", [DEPTH, 2, P, 2, 32, 16])
        self.s5c = self.inp("s5c", [DEPTH, 2, 2, 32, P, P], F32R)
        self.s5d = self.inp("s5d", [DEPTH, P, 8])
        self.s5s0 = self.inp("s5s0", [DEPTH, 2, P, 2, 32])
        self.s5o = self.outp("s5o", [DEPTH, P, 2, 2, 32, NSEG])

    def mixer_consts(self):
        kb = self.kb
        self.epsc = Tile(kb, self.gs, [P, 4], F32, name="epsc")
        kb.op("dve", lambda e: e.memset(self.epsc[:, 0:1], EPS), writes=[self.epsc.tok])
        kb.op("dve", lambda e: e.memset(self.epsc[:, 1:2], float(np.pi / 2)), writes=[self.epsc.tok])
        kb.op("dve", lambda e: e.memset(self.epsc[:, 2:3], 1e-12), writes=[self.epsc.tok])
        kb.op("dve", lambda e: e.memset(self.epsc[:, 3:4], 0.0), writes=[self.epsc.tok])
        self.ident = Tile(kb, self.gs, [P, P], F32, name="ident")
        self.identd = self.inp("identd", [P, P])
        kb.dma("sp", self.ident[:], self.identd[:, :], writes=[self.ident.tok])
        self.keep = Tile(kb, self.gs, [P, TT], F32, name="keep")
        kb.dma("sp", self.keep[:], self.keepT[:, :], writes=[self.keep.tok])

    def phase_mix(self, l):
        cfg = self.cfg
        self.phase_inproj(l)
        if cfg.mix[0]:
            self.phase_rwkv(l)
        if cfg.mix[1]:
            self.phase_ssd(l)
        if cfg.mix[2]:
            self.phase_s5(l)
        self.phase_merge(l)

    def phase_inproj(self, l):
        cfg, kb = self.cfg, self.kb
        NK, NT = cfg.NK, cfg.NT
        A = self.modA[:, l, NK:2 * NK]
        SH = self.mod[:, l, 3 * NK:4 * NK]
        with ExitStack() as st:
            hb = Tile(kb, st, [P, NK, TT], F32R, name="hbi")
            wr = Rot(kb, st, 4, [P, NK * P], F32R, name="wi")
            tmp = Rot(kb, st, 3, [P, TT], F32, name="tmpi")
            stg = Rot(kb, st, 4, [P, TT], F32, name="stg")
            rstd = Tile(kb, st, [P, TT], F32, name="rstdi")
            for tt in range(NT):
                self.load_xtile(hb, self.xres, tt)
                self.norm_tile(hb, tmp, rstd, A, SH)
                for m in range(cfg.NZ):
                    wt = wr.next()
                    kb.dma("pool", wt[:], self.win[l, m], writes=[wt.tok])
                    pb, pt = self.bank()
                    for kc in range(NK):
                        kb.op("pe", lambda e: e.matmul(pb, wt[:, kc * P:(kc + 1) * P], hb[:, kc], start=(kc == 0), stop=(kc == NK - 1)),
                              reads=[wt.tok, hb.tok], writes=[pt])
                    s = stg.next()
                    if m >= cfg.ZG and m < cfg.ZDT:
                        kb.op("act", lambda e: e.activation(out=s[:], in_=pb, func=AF.Sigmoid), reads=[pt], writes=[s.tok])
                    elif m >= cfg.ZB and m < cfg.ZB + 8:
                        kb.op("act", lambda e: e.activation(out=s[:], in_=pb, func=AF.Silu), reads=[pt], writes=[s.tok])
                    elif m % 2:
                        kb.op("act", lambda e: e.activation(out=s[:], in_=pb, func=AF.Copy), reads=[pt], writes=[s.tok])
                    else:
                        kb.op("dve", lambda e: e.tensor_copy(out=s[:], in_=pb), reads=[pt], writes=[s.tok])
                    kb.dma("sp", self.zT[m * P:(m + 1) * P, tt * TT:(tt + 1) * TT], s[:], reads=[s.tok], writes=[self.dt("z", m, tt)])
            kb.barrier()

    def phase_merge(self, l):
        cfg, kb = self.cfg, self.kb
        NK, NT = cfg.NK, cfg.NT
        G = self.modG[:, l, NK:2 * NK]
        xresf = self.xres.bitcast(F32)
        with ExitStack() as st:
            ys = [Tile(kb, st, [P, 8, TT], F32R, name="ym%d" % i) for i in range(3)]
            mg = Tile(kb, st, [P, NK, TT], F32R, ntok=NK, name="mg")
            wr = Rot(kb, st, 4, [P, max(NK, 8) * P], F32R, name="wm")
            gt = Rot(kb, st, 4, [P, TT], F32, name="gt")
            tmp = Rot(kb, st, 6, [P, TT], F32, name="tmpm")
            xc = Rot(kb, st, 3, [P, TT], F32, name="xcm")
            srcs = [self.ya, self.yb, self.yc]
            for tt in range(NT):
                for i in range(3):
                    if cfg.mix[i]:
                        sv = srcs[i].rearrange("(k p) t -> p k t", p=P)[:, :, tt * TT:(tt + 1) * TT]
                        kb.dma("pool", ys[i][:], sv, writes=[ys[i].tok])
                for n in range(NK):
                    terms = []
                    for i, wsrc in ((0, self.wpa), (1, self.wpb)):
                        if not cfg.mix[i]:
                            continue
                        wt = wr.next()
                        kb.dma("pool", wt[:, 0:8 * P], wsrc[l, n], writes=[wt.tok])
                        pb, pt = self.bank()
                        for k in range(8):
                            kb.op("pe", lambda e: e.matmul(pb, wt[:, k * P:(k + 1) * P], ys[i][:, k], start=(k == 0), stop=(k == 7)),
                                  reads=[wt.tok, ys[i].tok], writes=[pt])
                        g = gt.next()
                        kb.dma("sp", g[:], self.zT[(cfg.ZG + i * NK + n) * P:(cfg.ZG + i * NK + n + 1) * P, tt * TT:(tt + 1) * TT], writes=[g.tok])
                        t = tmp.next()
                        kb.op("dve", lambda e: e.tensor_tensor(out=t[:], in0=pb, in1=g[:], op=ALU.mult), reads=[pt, g.tok], writes=[t.tok])
                        terms.append(t)
                    if cfg.mix[2]:
                        pbs = []
                        for hf in range(2):
                            wt = wr.next()
                            kb.dma("pool", wt[:, 0:8 * P], self.wpc[l, hf * NK + n], writes=[wt.tok])
                            pb, pt = self.bank()
                            for k in range(8):
                                kb.op("pe", lambda e: e.matmul(pb, wt[:, k * P:(k + 1) * P], ys[2][:, k], start=(k == 0), stop=(k == 7)),
                                      reads=[wt.tok, ys[2].tok], writes=[pt])
                            pbs.append((pb, pt))
                        g = gt.next()
                        kb.dma("sp", g[:], self.zT[(cfg.ZG + 2 * NK + n) * P:(cfg.ZG + 2 * NK + n + 1) * P, tt * TT:(tt + 1) * TT], writes=[g.tok])
                        sg = tmp.next()
                        kb.op("act", lambda e: e.activation(out=sg[:], in_=pbs[1][0], func=AF.Sigmoid), reads=[pbs[1][1]], writes=[sg.tok])
                        t = tmp.next()
                        kb.op("dve", lambda e: e.tensor_tensor(out=t[:], in0=pbs[0][0], in1=sg[:], op=ALU.mult), reads=[pbs[0][1], sg.tok], writes=[t.tok])
                        kb.op("dve", lambda e: e.tensor_tensor(out=t[:], in0=t[:], in1=g[:], op=ALU.mult), reads=[t.tok, g.tok], writes=[t.tok])
                        terms.append(t)
                    if not terms:
                        kb.op("dve", lambda e: e.memset(mg[:, n], 0.0), writes=[mg.toks[n]])
                    elif len(terms) == 1:
                        kb.op("dve", lambda e: e.tensor_copy(out=mg[:, n], in_=terms[0][:]), reads=[terms[0].tok], writes=[mg.toks[n]])
                    else:
                        for a_ in terms[2:]:
                            kb.op("dve", lambda e: e.tensor_tensor(out=terms[0][:], in0=terms[0][:], in1=a_[:], op=ALU.add),
                                  reads=[terms[0].tok, a_.tok], writes=[terms[0].tok])
                        kb.op("dve", lambda e: e.tensor_tensor(out=mg[:, n], in0=terms[0][:], in1=terms[1][:], op=ALU.add),
                              reads=[terms[0].tok, terms[1].tok], writes=[mg.toks[n]])
                for n in range(NK):
                    wt = wr.next()
                    kb.dma("pool", wt[:, 0:NK * P], self.wo[l, n], writes=[wt.tok])
                    pb, pt = self.bank()
                    for k in range(NK):
                        kb.op("pe", lambda e: e.matmul(pb, wt[:, k * P:(k + 1) * P], mg[:, k], start=(k == 0), stop=(k == NK - 1)),
                              reads=[wt.tok, mg.toks[k]], writes=[pt])
                    x = xc.next()
                    kb.dma("sp", x[:], xresf[n * P:(n + 1) * P, tt * TT:(tt + 1) * TT], reads=[self.dt("x", tt)], writes=[x.tok])
                    kb.op("dve", lambda e: e.scalar_tensor_tensor(out=x[:], in0=pb, scalar=G[:, n:n + 1], in1=x[:], op0=ALU.mult, op1=ALU.add),
                          reads=[pt, x.tok, self.modG.tok], writes=[x.tok])
                    kb.dma("sp", xresf[n * P:(n + 1) * P, tt * TT:(tt + 1) * TT], x[:], reads=[x.tok], writes=[self.dt("xo", tt, n)])
            kb.barrier()

    def phase_s5(self, l):
        cfg, kb = self.cfg, self.kb
        T, NT, NSEG = cfg.T, cfg.NT, cfg.NSEG
        V = lambda e: e
        with ExitStack() as st:
            def tl(shape, name, dtype=F32):
                return Tile(kb, st, shape, dtype, name=name)
            so = tl([P, 2, 2, 32, NSEG], "s5so")
            dcol = tl([P, 8], "s5dc")
            kb.dma("sp", dcol[:], self.s5d[l], writes=[dcol.tok])
            yacc = tl([P, T], "yacc")
            uch = tl([P, T], "uch", F32R)
            prm = []
            for d in range(2):
                lam = tl([P, 3, 32], "lam")
                kb.dma("sp", lam[:], self.s5lam[l, d], writes=[lam.tok])
                bq = tl([P, 2, 32, 16], "bq")
                kb.dma("sp", bq[:], self.s5b[l, d], writes=[bq.tok])
                s0 = tl([P, 2, 32], "s0")
                kb.dma("sp", s0[:], self.s5s0[l, d], writes=[s0.tok])
                w = tl([P, 16, 32], "s5w")
                def W(i):
                    return w[:, i, :]
                def tt_(o, a, b, op):
                    kb.op("dve", lambda e: e.tensor_tensor(out=o, in0=a, in1=b, op=op), reads=[w.tok, lam.tok], writes=[w.tok])
                def ts_(o, a, s1, op0, s2=None, op1=None):
                    if op1 is None:
                        kb.op("dve", lambda e: e.tensor_scalar(out=o, in0=a, scalar1=s1, scalar2=None, op0=op0), reads=[w.tok, lam.tok], writes=[w.tok])
                    else:
                        kb.op("dve", lambda e: e.tensor_scalar(out=o, in0=a, scalar1=s1, scalar2=s2, op0=op0, op1=op1), reads=[w.tok, lam.tok], writes=[w.tok])
                def ac_(o, a, f, bias=None, scale=1.0):
                    if bias is None:
                        kb.op("act", lambda e: e.activation(out=o, in_=a, func=f, scale=scale), reads=[w.tok, lam.tok], writes=[w.tok])
                    else:
                        kb.op("act", lambda e: e.activation(out=o, in_=a, func=f, bias=bias, scale=scale), reads=[w.tok, lam.tok, self.epsc.tok], writes=[w.tok])
                lre, lim, ldt = lam[:, 0, :], lam[:, 1, :], lam[:, 2, :]
                ac_(W(0), ldt, AF.Exp)
                tt_(W(1), lre, W(0), ALU.mult)
                ac_(W(1), W(1), AF.Exp)
                tt_(W(2), lim, W(0), ALU.mult)
                ac_(W(3), W(2), AF.Sin, scale=1.0 / 16)
                ac_(W(4), W(2), AF.Sin, bias=self.epsc[:, 1:2], scale=1.0 / 16)
                for _ in range(4):
                    tt_(W(5), W(3), W(4), ALU.mult)
                    tt_(W(6), W(4), W(4), ALU.mult)
                    tt_(W(7), W(3), W(3), ALU.mult)
                    tt_(W(4), W(6), W(7), ALU.subtract)
                    ts_(W(3), W(5), 2.0, ALU.mult)
                tt_(W(5), W(1), W(4), ALU.mult)
                tt_(W(6), W(1), W(3), ALU.mult)
                ts_(W(7), W(5), -1.0, ALU.add)
                tt_(W(8), lre, lre, ALU.mult)
                tt_(W(9), lim, lim, ALU.mult)
                tt_(W(8), W(8), W(9), ALU.add)
                kb.op("dve", lambda e: e.reciprocal(out=W(8), in_=W(8)), reads=[w.tok], writes=[w.tok])
                tt_(W(9), W(7), lre, ALU.mult)
                tt_(W(10), W(6), lim, ALU.mult)
                tt_(W(9), W(9), W(10), ALU.add)
                tt_(W(9), W(9), W(8), ALU.mult)
                tt_(W(10), W(6), lre, ALU.mult)
                tt_(W(11), W(7), lim, ALU.mult)
                tt_(W(10), W(10), W(11), ALU.subtract)
                tt_(W(10), W(10), W(8), ALU.mult)
                bb = tl([P, 2, 32, 16], "bb")
                tb = tl([P, 32, 16], "tb")
                qre_b = W(9).to_broadcast([P, 32, 16]) if False else None
                def bc(i):
                    return w[:, i, :].unsqueeze(2).to_broadcast([P, 32, 16])
                kb.op("dve", lambda e: e.tensor_tensor(out=bb[:, 0], in0=bq[:, 0], in1=bc(9), op=ALU.mult), reads=[bq.tok, w.tok], writes=[bb.tok])
                kb.op("dve", lambda e: e.tensor_tensor(out=tb[:], in0=bq[:, 1], in1=bc(10), op=ALU.mult), reads=[bq.tok, w.tok], writes=[tb.tok])
                kb.op("dve", lambda e: e.tensor_tensor(out=bb[:, 0], in0=bb[:, 0], in1=tb[:], op=ALU.subtract), reads=[bb.tok, tb.tok], writes=[bb.tok])
                kb.op("dve", lambda e: e.tensor_tensor(out=bb[:, 1], in0=bq[:, 1], in1=bc(9), op=ALU.mult), reads=[bq.tok, w.tok], writes=[bb.tok])
                kb.op("dve", lambda e: e.tensor_tensor(out=tb[:], in0=bq[:, 0], in1=bc(10), op=ALU.mult), reads=[bq.tok, w.tok], writes=[tb.tok])
                kb.op("dve", lambda e: e.tensor_tensor(out=bb[:, 1], in0=bb[:, 1], in1=tb[:], op=ALU.add), reads=[bb.tok, tb.tok], writes=[bb.tok])
                pw = tl([P, 10, 2, 32], "pw")
                kb.op("dve", lambda e: e.tensor_copy(out=pw[:, 0, 0], in_=W(4)), reads=[w.tok], writes=[pw.tok])
                kb.op("dve", lambda e: e.tensor_copy(out=pw[:, 0, 1], in_=W(3)), reads=[w.tok], writes=[pw.tok])
                for k in range(1, 10):
                    c_, s_ = pw[:, k - 1, 0], pw[:, k - 1, 1]
                    kb.op("dve", lambda e: e.tensor_tensor(out=W(11), in0=c_, in1=c_, op=ALU.mult), reads=[pw.tok, w.tok], writes=[w.tok])
                    kb.op("dve", lambda e: e.tensor_tensor(out=W(12), in0=s_, in1=s_, op=ALU.mult), reads=[pw.tok, w.tok], writes=[w.tok])
                    kb.op("dve", lambda e: e.tensor_tensor(out=pw[:, k, 0], in0=W(11), in1=W(12), op=ALU.subtract), reads=[w.tok, pw.tok], writes=[pw.tok])
                    kb.op("dve", lambda e: e.tensor_tensor(out=W(11), in0=c_, in1=s_, op=ALU.mult), reads=[pw.tok, w.tok], writes=[w.tok])
                    kb.op("dve", lambda e: e.tensor_scalar(out=pw[:, k, 1], in0=W(11), scalar1=2.0, scalar2=None, op0=ALU.mult), reads=[w.tok, pw.tok], writes=[pw.tok])
                prm.append(dict(w=w, bb=bb, pw=pw, s0=s0))
            Fc, Fs = tl([P, TT], "Fc"), tl([P, TT], "Fs")
            d0 = tl([P, TT], "d0")
            E4 = [[tl([P, P], "Eb%d_%d" % (pos, i)) for i in range(2)] for pos in range(4)]
            E = [e_ for pair in E4 for e_ in pair]
            Bp = [tl([P, P], "Bp%d" % i, F32R) for i in range(2)]
            Cp = [tl([P, P], "Cp%d" % i, F32R) for i in range(2)]
            for e_ in E:
                kb.op("dve", lambda e: e.memset(e_[:], 0.0), writes=[e_.tok])
            wk = Rot(kb, st, 8, [P, TT], F32, name="s5wk")
            sreR = Rot(kb, st, 2, [P, TT], F32R, name="sre")
            nsiR = Rot(kb, st, 2, [P, TT], F32R, name="nsi")
            pend = []
            pk = Rot(kb, st, 8, [P, TT], F32R, name="s5pk")
            Fsn = tl([P, TT], "Fsn")
            Fcn = tl([P, TT], "Fcn")
            cin = tl([P, 4], "cin")
            tmpc = tl([P, 4], "tmpc")
            for Y in range(8):
                kb.dma("pool", uch[:], self.zT.bitcast(F32R)[(cfg.ZC + Y) * P:(cfg.ZC + Y + 1) * P, :], writes=[uch.tok])
                kb.op("dve", lambda e: e.tensor_scalar(out=yacc[:], in0=uch.f32(slice(None)), scalar1=dcol[:, Y:Y + 1], scalar2=None, op0=ALU.mult),
                      reads=[uch.tok, dcol.tok], writes=[yacc.tok])
                for q in range(4 * Y, 4 * Y + 4):
                    for d in range(2):
                        pr = prm[d]
                        w, bb, pw, s0 = pr["w"], pr["bb"], pr["pw"], pr["s0"]
                        E = E4[q % 4]
                        for ri in range(2):
                            for g2 in range(2):
                                g8 = 2 * (q % 4) + g2
                                kb.op("dve", lambda e: e.tensor_copy(out=E[ri][g2 * 64:(g2 + 1) * 64, g8 * 16:(g8 + 1) * 16],
                                                                   in_=bb[g2 * 64:(g2 + 1) * 64, ri, q, :]),
                                      reads=[bb.tok], writes=[E[ri].tok])
                            pb, pt = self.bank()
                            kb.op("pe", lambda e: e.transpose(pb[:, 0:P], E[ri][:], self.ident[:]), reads=[E[ri].tok, self.ident.tok], writes=[pt])
                            kb.op("act", lambda e: e.activation(out=Bp[ri][:], in_=pb[:, 0:P], func=AF.Copy), reads=[pt], writes=[Bp[ri].tok])
                            kb.dma("pool", Cp[ri][:], self.s5c[l, d, ri, q], writes=[Cp[ri].tok])
                        kb.op("dve", lambda e: e.memset(Fc[:, 0:1], 1.0), writes=[Fc.tok])
                        kb.op("dve", lambda e: e.memset(Fs[:, 0:1], 0.0), writes=[Fs.tok])
                        for k in range(9):
                            n_ = 1 << k
                            pc_, ps_ = pw[:, k, 0, q:q + 1], pw[:, k, 1, q:q + 1]
                            t1 = wk.next()
                            kb.op("dve", lambda e: e.tensor_scalar(out=t1[:, 0:n_], in0=Fs[:, 0:n_], scalar1=ps_, scalar2=None, op0=ALU.mult),
                                  reads=[Fs.tok, pw.tok], writes=[t1.tok])
                            t2 = wk.next()
                            kb.op("dve", lambda e: e.tensor_scalar(out=t2[:, 0:n_], in0=Fc[:, 0:n_], scalar1=ps_, scalar2=None, op0=ALU.mult),
                                  reads=[Fc.tok, pw.tok], writes=[t2.tok])
                            kb.op("dve", lambda e: e.scalar_tensor_tensor(out=Fc[:, n_:2 * n_], in0=Fc[:, 0:n_], scalar=pc_, in1=t1[:, 0:n_],
                                                                        op0=ALU.mult, op1=ALU.subtract),
                                  reads=[Fc.tok, pw.tok, t1.tok], writes=[Fc.tok])
                            kb.op("dve", lambda e: e.scalar_tensor_tensor(out=Fs[:, n_:2 * n_], in0=Fs[:, 0:n_], scalar=pc_, in1=t2[:, 0:n_],
                                                                        op0=ALU.mult, op1=ALU.add),
                                  reads=[Fs.tok, pw.tok, t2.tok], writes=[Fs.tok])
                        kb.op("dve", lambda e: e.tensor_scalar(out=Fsn[:], in0=Fs[:], scalar1=-1.0, scalar2=None, op0=ALU.mult), reads=[Fs.tok], writes=[Fsn.tok])
                        kb.op("dve", lambda e: e.tensor_scalar(out=Fcn[:], in0=Fc[:], scalar1=-1.0, scalar2=None, op0=ALU.mult), reads=[Fc.tok], writes=[Fcn.tok])
                        kb.op("dve", lambda e: e.tensor_scalar(out=d0[:], in0=self.keep[:], scalar1=w[:, 1, q:q + 1], scalar2=None, op0=ALU.mult),
                              reads=[self.keep.tok, w.tok], writes=[d0.tok])
                        def crot(src_re, src_im, cc_, ss_, rd):
                            kb.op("dve", lambda e: e.tensor_scalar(out=tmpc[:, 0:1], in0=src_im, scalar1=ss_, scalar2=None, op0=ALU.mult), reads=rd + [pw.tok], writes=[tmpc.tok])
                            kb.op("dve", lambda e: e.scalar_tensor_tensor(out=cin[:, 2:3], in0=src_re, scalar=cc_, in1=tmpc[:, 0:1], op0=ALU.mult, op1=ALU.subtract),
                                  reads=rd + [tmpc.tok, pw.tok], writes=[cin.tok])
                            kb.op("dve", lambda e: e.tensor_scalar(out=tmpc[:, 1:2], in0=src_re, scalar1=ss_, scalar2=None, op0=ALU.mult), reads=rd + [pw.tok], writes=[tmpc.tok])
                            kb.op("dve", lambda e: e.scalar_tensor_tensor(out=cin[:, 3:4], in0=src_im, scalar=cc_, in1=tmpc[:, 1:2], op0=ALU.mult, op1=ALU.add),
                                  reads=rd + [tmpc.tok, pw.tok], writes=[cin.tok])
                        crot(s0[:, 0, q:q + 1], s0[:, 1, q:q + 1], pw[:, 0, 0, q:q + 1], pw[:, 0, 1, q:q + 1], [s0.tok])
                        def emit_bu(tg_):
                            if d == 0:
                                usl_ = uch[:, tg_ * TT:(tg_ + 1) * TT]
                            else:
                                hi_ = T - tg_ * TT
                                usl_ = uch[:, hi_ - TT:hi_][:, ::-1]
                            pbr_, ptr_ = self.bank()
                            kb.op("pe", lambda e: e.matmul(pbr_, Bp[0][:], usl_, start=True, stop=True), reads=[Bp[0].tok, uch.tok], writes=[ptr_])
                            pbi_, pti_ = self.bank()
                            kb.op("pe", lambda e: e.matmul(pbi_, Bp[1][:], usl_, start=True, stop=True), reads=[Bp[1].tok, uch.tok], writes=[pti_])
                            return pbr_, ptr_, pbi_, pti_
                        nxt = emit_bu(0)
                        for tg in range(NT):
                            if d == 0:
                                ysl = yacc[:, tg * TT:(tg + 1) * TT]
                            else:
                                hi = T - tg * TT
                                ysl = yacc[:, hi - TT:hi][:, ::-1]
                            pbr, ptr, pbi, pti = nxt
                            a1, a2, a3, a4 = wk.next(), wk.next(), wk.next(), wk.next()
                            kb.op("dve", lambda e: e.tensor_tensor(out=a1[:], in0=pbr, in1=Fc[:], op=ALU.mult), reads=[ptr, Fc.tok], writes=[a1.tok])
                            kb.op("dve", lambda e: e.tensor_tensor(out=a2[:], in0=pbi, in1=Fs[:], op=ALU.mult), reads=[pti, Fs.tok], writes=[a2.tok])
                            kb.op("dve", lambda e: e.tensor_tensor(out=a3[:], in0=pbi, in1=Fc[:], op=ALU.mult), reads=[pti, Fc.tok], writes=[a3.tok])
                            kb.op("dve", lambda e: e.tensor_tensor(out=a4[:], in0=pbr, in1=Fs[:], op=ALU.mult), reads=[ptr, Fs.tok], writes=[a4.tok])
                            kb.op("dve", lambda e: e.tensor_tensor(out=a1[:], in0=a1[:], in1=a2[:], op=ALU.add), reads=[a1.tok, a2.tok], writes=[a1.tok])
                            kb.op("dve", lambda e: e.tensor_tensor(out=a3[:], in0=a3[:], in1=a4[:], op=ALU.subtract), reads=[a3.tok, a4.tok], writes=[a3.tok])
                            kb.op("dve", lambda e: e.tensor_tensor_scan(out=a2[:], data0=d0[:], data1=a1[:], initial=cin[:, 2:3], op0=ALU.mult, op1=ALU.add),
                                  reads=[d0.tok, a1.tok, cin.tok], writes=[a2.tok])
                            kb.op("dve", lambda e: e.tensor_tensor_scan(out=a4[:], data0=d0[:], data1=a3[:], initial=cin[:, 3:4], op0=ALU.mult, op1=ALU.add),
                                  reads=[d0.tok, a3.tok, cin.tok], writes=[a4.tok])
                            while pend:
                                pend.pop(0)()
                            if tg + 1 < NT:
                                crot(a2[:, TT - 1:TT], a4[:, TT - 1:TT], pw[:, 9, 0, q:q + 1], pw[:, 9, 1, q:q + 1], [a2.tok, a4.tok])
                            if tg + 1 < NT:
                                nxt = emit_bu(tg + 1)
                            p1, p2, p3, p4 = pk.next(), pk.next(), pk.next(), pk.next()
                            kb.op("pool", lambda e: e.tensor_tensor(out=p1[:], in0=a2[:], in1=Fc[:], op=ALU.mult), reads=[a2.tok, Fc.tok], writes=[p1.tok])
                            kb.op("pool", lambda e: e.tensor_tensor(out=p2[:], in0=a4[:], in1=Fsn[:], op=ALU.mult), reads=[a4.tok, Fsn.tok], writes=[p2.tok])
                            kb.op("pool", lambda e: e.tensor_tensor(out=p3[:], in0=a2[:], in1=Fsn[:], op=ALU.mult), reads=[a2.tok, Fsn.tok], writes=[p3.tok])
                            kb.op("pool", lambda e: e.tensor_tensor(out=p4[:], in0=a4[:], in1=Fcn[:], op=ALU.mult), reads=[a4.tok, Fcn.tok], writes=[p4.tok])
                            nsg = TT // 256
                            sc_ = (slice(None), slice(255, None, 256))
                            pby, pty = self.bank()
                            for mi, (cp_, pp_) in enumerate(((Cp[0], p1), (Cp[0], p2), (Cp[1], p3), (Cp[1], p4))):
                                kb.op("pe", lambda e: e.matmul(pby, cp_[:], pp_[:], start=(mi == 0), stop=(mi == 3)), reads=[cp_.tok, pp_.tok], writes=[pty])
                            kb.op("dve", lambda e: e.tensor_tensor(out=so[:, d, 0, q, tg * nsg:(tg + 1) * nsg], in0=p1.f32(sc_), in1=p2.f32(sc_), op=ALU.add),
                                  reads=[p1.tok, p2.tok], writes=[so.tok])
                            kb.op("dve", lambda e: e.scalar_tensor_tensor(out=so[:, d, 1, q, tg * nsg:(tg + 1) * nsg], in0=p3.f32(sc_), scalar=-1.0, in1=p4.f32(sc_),
                                                                        op0=ALU.mult, op1=ALU.subtract), reads=[p3.tok, p4.tok], writes=[so.tok])
                            pend.append(lambda pby=pby, pty=pty, ysl=ysl: kb.op("dve", lambda e: e.tensor_tensor(out=ysl, in0=pby, in1=ysl, op=ALU.add),
                                                                                 reads=[pty, yacc.tok], writes=[yacc.tok]))
                while pend:
                    pend.pop(0)()
                for tg in range(NT):
                    ysl = yacc[:, tg * TT:(tg + 1) * TT]
                    a1, a2 = wk.next(), wk.next()
                    kb.op("act", lambda e: e.activation(out=a1[:], in_=ysl, func=AF.Square), reads=[yacc.tok], writes=[a1.tok])
                    kb.op("dve", lambda e: e.tensor_scalar(out=a1[:], in0=a1[:], scalar1=0.044715, scalar2=1.0, op0=ALU.mult, op1=ALU.add),
                          reads=[a1.tok], writes=[a1.tok])
                    kb.op("dve", lambda e: e.tensor_tensor(out=a1[:], in0=a1[:], in1=ysl, op=ALU.mult), reads=[a1.tok, yacc.tok], writes=[a1.tok])
                    kb.op("act", lambda e: e.activation(out=a2[:], in_=a1[:], func=AF.Tanh, scale=0.7978845608028654), reads=[a1.tok], writes=[a2.tok])
                    kb.op("dve", lambda e: e.tensor_scalar(out=a2[:], in0=a2[:], scalar1=0.5, scalar2=0.5, op0=ALU.mult, op1=ALU.add),
                          reads=[a2.tok], writes=[a2.tok])
                    kb.op("dve", lambda e: e.tensor_tensor(out=a2[:], in0=a2[:], in1=ysl, op=ALU.mult), reads=[a2.tok, yacc.tok], writes=[a2.tok])
                    kb.dma("sp", self.yc.bitcast(F32)[Y * P:(Y + 1) * P, tg * TT:(tg + 1) * TT], a2[:], reads=[a2.tok], writes=[self.dt("yc", Y, tg)])
            kb.dma("sp", self.s5o[l], so[:], reads=[so.tok], writes=[self.dt("s5o", l)])
            kb.barrier()

    def phase_rwkv(self, l):
        cfg, kb = self.cfg, self.kb
        T, NT, NSEG = cfg.T, cfg.NT, cfg.NSEG
        NCH = T // 64
        HS = (slice(0, 64), slice(64, 128))
        with ExitStack() as st:
            def tl(shape, name, dtype=F32):
                return Tile(kb, st, shape, dtype, name=name)
            sm = tl([P, 4, TT], "rwsm"); kb.dma("sp", sm[:], self.rwsm[:, :, :], writes=[sm.tok])
            cmk = tl([P, T], "rwcm"); kb.dma("sp", cmk[:], self.rwcm[:, :], writes=[cmk.tok])
            col = tl([P, 5, 8], "rwcol"); kb.dma("sp", col[:], self.rwcol[l], writes=[col.tok])
            mu = tl([P, 26], "rwmu"); kb.dma("sp", mu[:], self.rwmu[l], writes=[mu.tok])
            om = tl([P, 26], "rwom")
            kb.op("dve", lambda e: e.tensor_scalar(out=om[:], in0=mu[:], scalar1=-1.0, scalar2=1.0, op0=ALU.mult, op1=ALU.add), reads=[mu.tok], writes=[om.tok])
            oka = tl([P, 8], "rwoka")
            kb.op("dve", lambda e: e.tensor_scalar(out=oka[:], in0=col[:, 1, :], scalar1=-1.0, scalar2=1.0, op0=ALU.mult, op1=ALU.add), reads=[col.tok], writes=[oka.tok])
            w0 = tl([P, 2, 2, 8], "rww0"); kb.dma("sp", w0[:], self.rww0[l], writes=[w0.tok])
            w2 = tl([P, 2, RW], "rww2"); kb.dma("sp", w2[:], self.rww2[l], writes=[w2.tok])
            hb1 = tl([P, P], "hb1"); kb.dma("sp", hb1[:], self.hblk[0], writes=[hb1.tok])
            xraw = tl([P, T], "xraw")
            sacc = tl([P, T], "sacc")
            stmp = Rot(kb, st, 3, [P, TT], F32, name="stmp")

            def load_shifted(ch, dst):
                kb.dma("sp", xraw[:], self.zT[(cfg.ZA + ch) * P:(cfg.ZA + ch + 1) * P, :], writes=[xraw.tok])
                kb.op("dve", lambda e: e.memset(sacc[:], 0.0), writes=[sacc.tok])
                for oi, o in enumerate((-1, 1, -64, 64)):
                    for tg in range(NT):
                        lo, hi = tg * TT, (tg + 1) * TT
                        slo, shi = max(lo + o, 0), min(hi + o, T)
                        dlo, dhi = slo - o, shi - o
                        n_ = dhi - dlo
                        t = stmp.next()
                        kb.op("dve", lambda e: e.tensor_tensor(out=t[:, 0:n_], in0=xraw[:, slo:shi], in1=sm[:, oi, dlo - lo:dhi - lo], op=ALU.mult),
                              reads=[xraw.tok, sm.tok], writes=[t.tok])
                        kb.op("dve", lambda e: e.tensor_tensor(out=sacc[:, dlo:dhi], in0=sacc[:, dlo:dhi], in1=t[:, 0:n_], op=ALU.add),
                              reads=[sacc.tok, t.tok], writes=[sacc.tok])
                kb.op("dve", lambda e: e.tensor_scalar(out=xraw[:], in0=xraw[:], scalar1=om[:, ch:ch + 1], scalar2=None, op0=ALU.mult), reads=[xraw.tok, om.tok], writes=[xraw.tok])
                kb.op("dve", lambda e: e.scalar_tensor_tensor(out=dst[:], in0=sacc[:], scalar=mu[:, ch:ch + 1], in1=xraw[:], op0=ALU.mult, op1=ALU.add),
                      reads=[sacc.tok, mu.tok, xraw.tok], writes=[dst.tok])

            tw = tl([P, T], "rwtw")
            load_shifted(24, tw)
            kb.op("act", lambda e: e.activation(out=tw[0:64, :], in_=tw[0:64, :], func=AF.Tanh), reads=[tw.tok], writes=[tw.tok])
            rr, kk_, vv_ = tl([P, T], "rwr"), tl([P, T], "rwk"), tl([P, T], "rwv")
            kkn = tl([P, T], "rwkkn")
            ad, ldc, cum = tl([P, T], "rwad"), tl([P, T], "rwld"), tl([P, T], "rwcum")
            t1, t2 = tl([P, T], "rwt1"), tl([P, T], "rwt2")
            outr = Rot(kb, st, 3, [P, T], F32, name="rwout")
            gct = tl([P, NCH], "rwgct")
            for j in range(8):
                load_shifted(j, rr)
                load_shifted(8 + j, kk_)
                load_shifted(16 + j, vv_)
                kb.op("dve", lambda e: e.tensor_scalar(out=kkn[:], in0=kk_[:], scalar1=col[:, 0, j:j + 1], scalar2=None, op0=ALU.mult), reads=[kk_.tok, col.tok], writes=[kkn.tok])
                kb.op("act", lambda e: e.activation(out=t1[:], in_=kkn[:], func=AF.Square), reads=[kkn.tok], writes=[t1.tok])
                for tg in range(NT):
                    tc_ = slice(tg * TT, (tg + 1) * TT)
                    pb, pt = self.bank()
                    kb.op("pe", lambda e: e.matmul(pb, hb1[:], t1[:, tc_], start=True, stop=True), reads=[hb1.tok, t1.tok], writes=[pt])
                    kb.op("act", lambda e: e.activation(out=t2[:, tc_], in_=pb, func=AF.Sqrt, bias=self.epsc[:, 2:3], scale=1.0), reads=[pt, self.epsc.tok], writes=[t2.tok])
                kb.op("dve", lambda e: e.reciprocal(out=t2[:], in_=t2[:]), reads=[t2.tok], writes=[t2.tok])
                kb.op("dve", lambda e: e.tensor_tensor(out=kkn[:], in0=kkn[:], in1=t2[:], op=ALU.mult), reads=[kkn.tok, t2.tok], writes=[kkn.tok])
                kb.op("dve", lambda e: e.scalar_tensor_tensor(out=t1[:], in0=rr[:], scalar=col[:, 2, j:j + 1], in1=kk_[:], op0=ALU.mult, op1=ALU.mult),
                      reads=[rr.tok, col.tok, kk_.tok], writes=[t1.tok])
                bo = outr.next()
                for tg in range(NT):
                    tc_ = slice(tg * TT, (tg + 1) * TT)
                    pb, pt = self.bank()
                    kb.op("pe", lambda e: e.matmul(pb, hb1[:], t1[:, tc_], start=True, stop=True), reads=[hb1.tok, t1.tok], writes=[pt])
                    kb.op("dve", lambda e: e.tensor_tensor(out=bo[:, tc_], in0=pb, in1=vv_[:, tc_], op=ALU.mult), reads=[pt, vv_.tok], writes=[bo.tok])
                kb.dma("sp", self.rwbon[j * P:(j + 1) * P, :], bo[:], reads=[bo.tok], writes=[self.dt("rwbon", j)])
                for d in range(2):
                    R = (lambda ap: ap) if d == 0 else (lambda ap: ap[:, ::-1])
                    for tg in range(NT):
                        tc_ = slice(tg * TT, (tg + 1) * TT)
                        pb, pt = self.bank()
                        kb.op("pe", lambda e: e.matmul(pb, w2[0:64, d, j * P:(j + 1) * P], tw[0:64, tc_], start=True, stop=True), reads=[w2.tok, tw.tok], writes=[pt])
                        kb.op("act", lambda e: e.activation(out=ldc[:, tc_], in_=pb, func=AF.Sigmoid, bias=w0[:, d, 0, j:j + 1], scale=1.0), reads=[pt, w0.tok], writes=[ldc.tok])
                        pb2, pt2 = self.bank()
                        kb.op("pe", lambda e: e.matmul(pb2, w2[64:128, d, j * P:(j + 1) * P], tw[64:128, tc_], start=True, stop=True), reads=[w2.tok, tw.tok], writes=[pt2])
                        kb.op("act", lambda e: e.activation(out=ad[:, tc_], in_=pb2, func=AF.Sigmoid, bias=w0[:, d, 1, j:j + 1], scale=1.0), reads=[pt2, w0.tok], writes=[ad.tok])
                    kb.op("dve", lambda e: e.tensor_scalar(out=ldc[:], in0=ldc[:], scalar1=-0.6065306597126334, scalar2=None, op0=ALU.mult), reads=[ldc.tok], writes=[ldc.tok])
                    kb.op("dve", lambda e: e.tensor_tensor_scan(out=cum[:], data0=cmk[:], data1=R(ldc[:]), initial=0.0, op0=ALU.mult, op1=ALU.add),
                          reads=[cmk.tok, ldc.tok], writes=[cum.tok])
                    o_rt = outr.next()
                    kb.op("act", lambda e: e.activation(out=t1[:], in_=cum[:], func=AF.Exp), reads=[cum.tok], writes=[t1.tok])
                    kb.op("dve", lambda e: e.tensor_tensor(out=o_rt[:], in0=t1[:], in1=R(rr[:]), op=ALU.mult), reads=[t1.tok, rr.tok], writes=[o_rt.tok])
                    kb.dma("sp", self.rwp[d, 3, j * P:(j + 1) * P, :].bitcast(F32), o_rt[:], reads=[o_rt.tok], writes=[self.dt("rwp", d, 3, j)])
                    kb.op("dve", lambda e: e.tensor_copy(out=gct[:], in_=t1[:, 63::64]), reads=[t1.tok], writes=[gct.tok])
                    kb.dma("sp", self.rwgc[d, j * P:(j + 1) * P, :], gct[:], reads=[gct.tok], writes=[self.dt("rwgc", d, j)])
                    o_at = outr.next()
                    kb.op("dve", lambda e: e.tensor_tensor(out=t2[:], in0=cum[:], in1=R(ldc[:]), op=ALU.subtract), reads=[cum.tok, ldc.tok], writes=[t2.tok])
                    kb.op("act", lambda e: e.activation(out=t2[:], in_=t2[:], func=AF.Exp), reads=[t2.tok], writes=[t2.tok])
                    kb.op("dve", lambda e: e.scalar_tensor_tensor(out=o_at[:], in0=t2[:], scalar=-1.0, in1=R(kkn[:]), op0=ALU.mult, op1=ALU.mult),
                          reads=[t2.tok, kkn.tok], writes=[o_at.tok])
                    kb.dma("sp", self.rwp[d, 0, j * P:(j + 1) * P, :].bitcast(F32), o_at[:], reads=[o_at.tok], writes=[self.dt("rwp", d, 0, j)])
                    kb.op("act", lambda e: e.activation(out=t1[:], in_=cum[:], func=AF.Exp, scale=-1.0), reads=[cum.tok], writes=[t1.tok])
                    o_bt = outr.next()
                    kb.op("dve", lambda e: e.tensor_tensor(out=t2[:], in0=R(kkn[:]), in1=R(ad[:]), op=ALU.mult), reads=[kkn.tok, ad.tok], writes=[t2.tok])
                    kb.op("dve", lambda e: e.tensor_tensor(out=o_bt[:], in0=t2[:], in1=t1[:], op=ALU.mult), reads=[t2.tok, t1.tok], writes=[o_bt.tok])
                    kb.dma("sp", self.rwp[d, 1, j * P:(j + 1) * P, :].bitcast(F32), o_bt[:], reads=[o_bt.tok], writes=[self.dt("rwp", d, 1, j)])
                    o_kt = outr.next()
                    kb.op("dve", lambda e: e.tensor_scalar(out=t2[:], in0=R(ad[:]), scalar1=col[:, 1, j:j + 1], scalar2=oka[:, j:j + 1], op0=ALU.mult, op1=ALU.add),
                          reads=[ad.tok, col.tok, oka.tok], writes=[t2.tok])
                    kb.op("dve", lambda e: e.tensor_tensor(out=t2[:], in0=t2[:], in1=R(kk_[:]), op=ALU.mult), reads=[t2.tok, kk_.tok], writes=[t2.tok])
                    kb.op("dve", lambda e: e.tensor_tensor(out=o_kt[:], in0=t2[:], in1=t1[:], op=ALU.mult), reads=[t2.tok, t1.tok], writes=[o_kt.tok])
                    kb.dma("sp", self.rwp[d, 2, j * P:(j + 1) * P, :].bitcast(F32), o_kt[:], reads=[o_kt.tok], writes=[self.dt("rwp", d, 2, j)])
                    o_v = outr.next()
                    kb.op("act", lambda e: e.activation(out=o_v[:], in_=R(vv_[:]), func=AF.Copy), reads=[vv_.tok], writes=[o_v.tok])
                    kb.dma("sp", self.rwp[d, 4, j * P:(j + 1) * P, :].bitcast(F32), o_v[:], reads=[o_v.tok], writes=[self.dt("rwp", d, 4, j)])
            kb.barrier()
        with ExitStack() as st:
            def tl(shape, name, dtype=F32):
                return Tile(kb, st, shape, dtype, name=name)
            H = 64
            trm = tl([H, 3, 64], "rwtri")
            for i in range(3):
                kb.dma("sp", trm[:, i, :], self.rwtri[i][0:64, :], writes=[trm.tok])
            kcol = tl([P, 1], "rwkcol"); kb.dma("sp", kcol[:], self.ssdkeep[:, :], writes=[kcol.tok])
            Sst = tl([H, 16, 64], "rwS", F32R)
            gcs = tl([H, 16, NCH], "rwgcs")
            slab = [Rot(kb, st, 2, [H, 16, 64], F32R, name="rwsl%d" % q) for q in range(5)]
            tk = [Rot(kb, st, 2, [H, 16, 64], F32R, name="rwtk%d" % q) for q in range(3)]
            Am = [Rot(kb, st, 2, [H, 16, 64], F32R, name="rwA%d" % q) for q in range(3)]
            Ak = Rot(kb, st, 2, [H, 16, 64], F32R, name="rwAk")
            AkT = Rot(kb, st, 2, [H, 16, 64], F32R, name="rwAkT")
            Tm = Rot(kb, st, 2, [H, 16, 64], F32R, name="rwTm")
            Wt = Rot(kb, st, 2, [H, 16, 64], F32R, name="rwWt")
            Ut = Rot(kb, st, 2, [H, 16, 64], F32R, name="rwUt")
            ost = Rot(kb, st, 2, [H, 16, 64], F32, name="rwost")

            def bc16(ap2):
                return ap2.unsqueeze(1).to_broadcast([H, 16, 64])

            def mm16(psb, ptk, lhs_fn, rhs_fn, reads, first=True, last=True):
                for b_ in range(16):
                    kb.op("pe", lambda e: e.matmul(psb[0:H, b_ * 64:(b_ + 1) * 64], lhs_fn(b_), rhs_fn(b_), start=first, stop=last), reads=reads, writes=ptk)

            def mm16g(psb, ptk, terms):
                n = len(terms)
                for b_ in range(16):
                    for ti, (lt, rt__) in enumerate(terms):
                        kb.op("pe", lambda e: e.matmul(psb[0:H, b_ * 64:(b_ + 1) * 64], lt[:, b_, :], rt__[:, b_, :], start=(ti == 0), stop=(ti == n - 1)),
                              reads=[lt.tok, rt__.tok], writes=ptk)

            def v3(pb):
                return pb[0:H, :].rearrange("p (b t) -> p b t", t=64)

            def dview(ap2d):
                return ap2d.rearrange("(b k) x -> k b x", k=64)

            for d in range(2):
                kb.dma("pool", Sst[:], self.rws0[l, d], writes=[Sst.tok])
                kb.dma("sp", gcs[:], dview(self.rwgc[d]), reads=[self.dt("rwgc", d, 0)], writes=[gcs.tok])
                for c in range(NCH):
                    cs = slice(c * 64, (c + 1) * 64)
                    sl = [r_.next() for r_ in slab]
                    for q in range(5):
                        kb.dma("pool", sl[q][:], dview(self.rwp[d, q])[:, :, cs], writes=[sl[q].tok])
                    at_, bt_, kt_, rt_, vs_ = sl
                    tks = []
                    for q, src in enumerate((bt_, kt_, vs_)):
                        pb, pt = self.bank2()
                        for b_ in range(16):
                            kb.op("pe", lambda e: e.transpose(pb[0:H, b_ * 64:(b_ + 1) * 64], src[:, b_, :].bitcast(F32), self.ident[0:H, 0:H]), reads=[src.tok, self.ident.tok], writes=pt)
                        t_ = tk[q].next()
                        if q % 2:
                            kb.op("act", lambda e: e.activation(out=t_[:], in_=v3(pb), func=AF.Copy), reads=pt, writes=[t_.tok])
                        else:
                            kb.op("dve", lambda e: e.tensor_copy(out=t_[:], in_=v3(pb)), reads=pt, writes=[t_.tok])
                        tks.append(t_)
                    Btk, Ktk, Vtk = tks

                    def amat(lhs, rhs, mask_i, dst):
                        pb, pt = self.bank2()
                        mm16(pb, pt, lambda b_: lhs[:, b_, :], lambda b_: rhs[:, b_, :], [lhs.tok, rhs.tok])
                        kb.op("dve", lambda e: e.tensor_tensor(out=dst[:], in0=v3(pb), in1=bc16(trm[:, mask_i, :]), op=ALU.mult), reads=pt + [trm.tok], writes=[dst.tok])
                    A0, A0T = Ak.next(), AkT.next()
                    amat(bt_, at_, 0, A0)
                    amat(at_, bt_, 1, A0T)
                    Aak, Arb, Ark = Am[0].next(), Am[1].next(), Am[2].next()
                    amat(kt_, at_, 0, Aak)
                    amat(bt_, rt_, 2, Arb)
                    amat(kt_, rt_, 2, Ark)
                    Tc = Tm.next()
                    kb.op("dve", lambda e: e.tensor_tensor(out=Tc[:], in0=A0.f32(slice(None)), in1=bc16(self.ident[0:H, 0:H]), op=ALU.add), reads=[A0.tok, self.ident.tok], writes=[Tc.tok])
                    Ap, ApT = A0, A0T
                    for lev in range(1, 6):
                        An, AnT = Ak.next(), AkT.next()
                        pb, pt = self.bank2()
                        mm16(pb, pt, lambda b_: ApT[:, b_, :], lambda b_: Ap[:, b_, :], [Ap.tok, ApT.tok])
                        pb2, pt2 = self.bank2()
                        mm16(pb2, pt2, lambda b_: Ap[:, b_, :], lambda b_: ApT[:, b_, :], [Ap.tok, ApT.tok])
                        kb.op("dve", lambda e: e.tensor_copy(out=An[:], in_=v3(pb)), reads=pt, writes=[An.tok])
                        kb.op("act", lambda e: e.activation(out=AnT[:], in_=v3(pb2), func=AF.Copy), reads=pt2, writes=[AnT.tok])
                        pb3, pt3 = self.bank2()
                        mm16(pb3, pt3, lambda b_: AnT[:, b_, :], lambda b_: Tc[:, b_, :], [AnT.tok, Tc.tok])
                        Tn = Tm.next()
                        kb.op("dve", lambda e: e.tensor_tensor(out=Tn[:], in0=v3(pb3), in1=Tc.f32(slice(None)), op=ALU.add), reads=pt3 + [Tc.tok], writes=[Tn.tok])
                        Tc, Ap, ApT = Tn, An, AnT
                    pbw, ptw = self.bank2()
                    mm16g(pbw, ptw, [(at_, Sst), (Aak, Vtk)])
                    W_ = Wt.next()
                    kb.op("act", lambda e: e.activation(out=W_[:], in_=v3(pbw), func=AF.Copy), reads=ptw, writes=[W_.tok])
                    pbu, ptu = self.bank2()
                    mm16(pbu, ptu, lambda b_: Tc[:, b_, :], lambda b_: W_[:, b_, :], [Tc.tok, W_.tok])
                    U_ = Ut.next()
                    kb.op("dve", lambda e: e.tensor_copy(out=U_[:], in_=v3(pbu)), reads=ptu, writes=[U_.tok])
                    pbo, pto = self.bank2()
                    mm16g(pbo, pto, [(Sst, rt_), (U_, Arb), (Vtk, Ark)])
                    pbs, pts = self.bank2()
                    mm16g(pbs, pts, [(Btk, U_), (Ktk, Vtk)])
                    o_ = ost.next()
                    if d == 0:
                        kb.op("act", lambda e: e.activation(out=o_[:], in_=v3(pbo), func=AF.Copy), reads=pto, writes=[o_.tok])
                        tcs = cs
                    else:
                        kb.op("dve", lambda e: e.tensor_copy(out=o_[:, :, ::-1], in_=v3(pbo)), reads=pto, writes=[o_.tok])
                        tcs = slice(T - (c + 1) * 64, T - c * 64)
                    kb.dma("sp", dview(self.rwoT[d])[:, :, tcs], o_[:], reads=[o_.tok], writes=[self.dt("rwoT", d, c)])
                    kb.op("dve", lambda e: e.tensor_tensor(out=Sst[:], in0=v3(pbs), in1=Sst.f32(slice(None)), op=ALU.add), reads=pts + [Sst.tok], writes=[Sst.tok])
                    kb.op("dve", lambda e: e.tensor_tensor(out=Sst[:], in0=Sst.f32(slice(None)), in1=gcs[:, :, c:c + 1].to_broadcast([H, 16, 64]), op=ALU.mult),
                          reads=[Sst.tok, gcs.tok], writes=[Sst.tok])
                    if c % 4 == 3:
                        seg = c // 4
                        kb.dma("sp", self.rwo[l, d, seg], Sst.f32(slice(None)), reads=[Sst.tok], writes=[self.dt("rwo", l, d, seg)])
                        kb.op("dve", lambda e: e.tensor_scalar(out=Sst[:], in0=Sst.f32(slice(None)), scalar1=kcol[0:H, 0:1], scalar2=None, op0=ALU.mult), reads=[Sst.tok, kcol.tok], writes=[Sst.tok])
            kb.barrier()
        with ExitStack() as st:
            def tl(shape, name, dtype=F32):
                return Tile(kb, st, shape, dtype, name=name)
            hb64 = tl([P, P], "hb64"); kb.dma("sp", hb64[:], self.hblk[1], writes=[hb64.tok])
            col = tl([P, 5, 8], "rwcol2"); kb.dma("sp", col[:], self.rwcol[l], writes=[col.tok])
            g2 = tl([P, RW], "rwg2"); kb.dma("sp", g2[:], self.rwg2[l], writes=[g2.tok])
            sgl = tl([P, T], "rwsgl")
            kb.dma("sp", sgl[:], self.zT[(cfg.ZA + 25) * P:(cfg.ZA + 26) * P, :], writes=[sgl.tok])
            sm = tl([P, 4, TT], "rwsm2"); kb.dma("sp", sm[:], self.rwsm[:, :, :], writes=[sm.tok])
            mu = tl([P, 26], "rwmu2"); kb.dma("sp", mu[:], self.rwmu[l], writes=[mu.tok])
            om = tl([P, 1], "rwom2")
            kb.op("dve", lambda e: e.tensor_scalar(out=om[:], in0=mu[:, 25:26], scalar1=-1.0, scalar2=1.0, op0=ALU.mult, op1=ALU.add), reads=[mu.tok], writes=[om.tok])
            sacc = tl([P, T], "rwsacc2")
            pt_ = Rot(kb, st, 8, [P, TT], F32, name="rwpt")
            kb.op("dve", lambda e: e.memset(sacc[:], 0.0), writes=[sacc.tok])
            for oi, o in enumerate((-1, 1, -64, 64)):
                for tg in range(NT):
                    lo, hi = tg * TT, (tg + 1) * TT
                    slo, shi = max(lo + o, 0), min(hi + o, T)
                    dlo, dhi = slo - o, shi - o
                    n_ = dhi - dlo
                    t = pt_.next()
                    kb.op("dve", lambda e: e.tensor_tensor(out=t[:, 0:n_], in0=sgl[:, slo:shi], in1=sm[:, oi, dlo - lo:dhi - lo], op=ALU.mult), reads=[sgl.tok, sm.tok], writes=[t.tok])
                    kb.op("dve", lambda e: e.tensor_tensor(out=sacc[:, dlo:dhi], in0=sacc[:, dlo:dhi], in1=t[:, 0:n_], op=ALU.add), reads=[sacc.tok, t.tok], writes=[sacc.tok])
            kb.op("dve", lambda e: e.tensor_scalar(out=sgl[:], in0=sgl[:], scalar1=om[:, 0:1], scalar2=None, op0=ALU.mult), reads=[sgl.tok, om.tok], writes=[sgl.tok])
            kb.op("dve", lambda e: e.scalar_tensor_tensor(out=sgl[:], in0=sacc[:], scalar=mu[:, 25:26], in1=sgl[:], op0=ALU.mult, op1=ALU.add), reads=[sacc.tok, mu.tok, sgl.tok], writes=[sgl.tok])
            kb.op("act", lambda e: e.activation(out=sgl[:], in_=sgl[:], func=AF.Sigmoid), reads=[sgl.tok], writes=[sgl.tok])
            lnb = tl([P, 1], "rwlneps")
            kb.op("dve", lambda e: e.memset(lnb[:], 64e-5), writes=[lnb.tok])
            for j in range(8):
                for tg in range(NT):
                    tc_ = slice(tg * TT, (tg + 1) * TT)
                    of_, ob_ = pt_.next(), pt_.next()
                    kb.dma("sp", of_[:], self.rwoT[0, j * P:(j + 1) * P, tc_], writes=[of_.tok])
                    kb.dma("sp", ob_[:], self.rwoT[1, j * P:(j + 1) * P, tc_], writes=[ob_.tok])
                    kb.op("dve", lambda e: e.tensor_tensor(out=of_[:], in0=of_[:], in1=ob_[:], op=ALU.add), reads=[of_.tok, ob_.tok], writes=[of_.tok])
                    pb, pt = self.bank()
                    kb.op("pe", lambda e: e.matmul(pb, hb64[:], of_[:], start=True, stop=True), reads=[hb64.tok, of_.tok], writes=[pt])
                    oc = pt_.next()
                    kb.op("dve", lambda e: e.tensor_tensor(out=oc[:], in0=of_[:], in1=pb, op=ALU.subtract), reads=[of_.tok, pt], writes=[oc.tok])
                    sq = pt_.next()
                    kb.op("act", lambda e: e.activation(out=sq[:], in_=oc[:], func=AF.Square), reads=[oc.tok], writes=[sq.tok])
                    pb2, pt2 = self.bank()
                    kb.op("pe", lambda e: e.matmul(pb2, hb64[:], sq[:], start=True, stop=True), reads=[hb64.tok, sq.tok], writes=[pt2])
                    kb.op("act", lambda e: e.activation(out=sq[:], in_=pb2, func=AF.Sqrt, bias=lnb[:, 0:1], scale=1.0), reads=[pt2, lnb.tok], writes=[sq.tok])
                    kb.op("dve", lambda e: e.reciprocal(out=sq[:], in_=sq[:]), reads=[sq.tok], writes=[sq.tok])
                    kb.op("dve", lambda e: e.scalar_tensor_tensor(out=oc[:], in0=oc[:], scalar=col[:, 3, j:j + 1], in1=sq[:], op0=ALU.mult, op1=ALU.mult), reads=[oc.tok, col.tok, sq.tok], writes=[oc.tok])
                    bn = pt_.next()
                    kb.dma("sp", bn[:], self.rwbon[j * P:(j + 1) * P, tc_], reads=[self.dt("rwbon", j)], writes=[bn.tok])
                    kb.op("dve", lambda e: e.scalar_tensor_tensor(out=oc[:], in0=oc[:], scalar=col[:, 4, j:j + 1], in1=bn[:], op0=ALU.add, op1=ALU.add), reads=[oc.tok, col.tok, bn.tok], writes=[oc.tok])
                    pb3, pt3 = self.bank()
                    kb.op("pe", lambda e: e.matmul(pb3, g2[:, j * P:(j + 1) * P], sgl[:, tc_], start=True, stop=True), reads=[g2.tok, sgl.tok], writes=[pt3])
                    kb.op("dve", lambda e: e.tensor_tensor(out=oc[:], in0=pb3, in1=oc[:], op=ALU.mult), reads=[pt3, oc.tok], writes=[oc.tok])
                    kb.dma("sp", self.ya.bitcast(F32)[j * P:(j + 1) * P, tc_], oc[:], reads=[oc.tok], writes=[self.dt("ya", j, tg)])
            kb.barrier()

    def phase_ssd(self, l):
        cfg, kb = self.cfg, self.kb
        T, NT, NSEG = cfg.T, cfg.NT, cfg.NSEG
        NC = T // P
        with ExitStack() as st:
            cm = Tile(kb, st, [P, 4, TT], F32, name="cm")
            kb.dma("sp", cm[:], self.cmT[:, :, :], writes=[cm.tok])
            cw = Tile(kb, st, [P, 12, 6], F32, name="cw")
            kb.dma("sp", cw[:], self.ssdcw[l], writes=[cw.tok])
            xin = Rot(kb, st, 2, [P, T], F32, name="xin")
            acc = Rot(kb, st, 2, [P, T], F32, name="cacc")
            tmp = Rot(kb, st, 3, [P, TT], F32, name="ctmp")
            for ch in range(12):
                x = xin.next()
                a = acc.next()
                kb.dma("sp", x[:], self.zT[(cfg.ZB + 8 + ch) * P:(cfg.ZB + 9 + ch) * P, :], writes=[x.tok])
                kb.op("dve", lambda e: e.tensor_scalar(out=a[:], in0=x[:], scalar1=cw[:, ch, 2:3], scalar2=cw[:, ch, 5:6], op0=ALU.mult, op1=ALU.add),
                      reads=[x.tok, cw.tok], writes=[a.tok])
                for oi, o in enumerate((-2, -1, 1, 2)):
                    for tg in range(NT):
                        lo, hi = tg * TT, (tg + 1) * TT
                        slo, shi = max(lo + o, 0), min(hi + o, T)
                        dlo, dhi = slo - o, shi - o
                        t = tmp.next()
                        n_ = dhi - dlo
                        kb.op("dve", lambda e: e.tensor_tensor(out=t[:, 0:n_], in0=x[:, slo:shi], in1=cm[:, oi, dlo - lo:dhi - lo], op=ALU.mult),
                              reads=[x.tok, cm.tok], writes=[t.tok])
                        kb.op("dve", lambda e: e.scalar_tensor_tensor(out=a[:, dlo:dhi], in0=t[:, 0:n_], scalar=cw[:, ch, (o + 2):(o + 3)], in1=a[:, dlo:dhi],
                                                                    op0=ALU.mult, op1=ALU.add), reads=[t.tok, a.tok, cw.tok], writes=[a.tok])
                kb.op("act", lambda e: e.activation(out=a[:], in_=a[:], func=AF.Silu), reads=[a.tok], writes=[a.tok])
                kb.dma("sp", self.xcs[ch * P:(ch + 1) * P, :], a[:], reads=[a.tok], writes=[self.dt("xcs", ch)])
            kb.barrier()
        with ExitStack() as st:
            def tl(shape, name, dtype=F32):
                return Tile(kb, st, shape, dtype, name=name)
            yacc = tl([P, 8, T], "ssdy")
            kb.op("dve", lambda e: e.memset(yacc[:], 0.0), writes=[yacc.tok])
            tri = tl([P, 2, P], "tri")
            for d in range(2):
                kb.dma("sp", tri[:, d, :], self.tri[d], writes=[tri.tok])
            dd_T = tl([64, T], "ddT")
            col = tl([64, 3], "ssdcol")
            kb.dma("sp", col[:], self.ssdcol[l], writes=[col.tok])
            for r in range(4):
                kb.dma("sp", dd_T[r * 16:(r + 1) * 16, :], self.zT[cfg.ZDT * P:cfg.ZDT * P + 16, :], writes=[dd_T.tok])
            mcol = tl([64, 1], "mcol")
            kb.op("act", lambda e: e.activation(out=mcol[:], in_=col[:, 1:2], func=AF.Exp), reads=[col.tok], writes=[mcol.tok])
            kb.op("dve", lambda e: e.tensor_tensor(out=mcol[:], in0=mcol[:], in1=col[:, 2:3], op=ALU.mult), reads=[mcol.tok, col.tok], writes=[mcol.tok])
            kb.op("act", lambda e: e.activation(out=dd_T[:], in_=dd_T[:], func=AF.Exp, bias=col[:, 0:1], scale=1.0), reads=[dd_T.tok, col.tok], writes=[dd_T.tok])
            kb.op("dve", lambda e: e.tensor_scalar(out=dd_T[:], in0=dd_T[:], scalar1=1.0, scalar2=None, op0=ALU.add), reads=[dd_T.tok], writes=[dd_T.tok])
            kb.op("act", lambda e: e.activation(out=dd_T[:], in_=dd_T[:], func=AF.Ln), reads=[dd_T.tok], writes=[dd_T.tok])
            kb.op("dve", lambda e: e.tensor_scalar(out=dd_T[:], in0=dd_T[:], scalar1=mcol[:, 0:1], scalar2=None, op0=ALU.mult), reads=[dd_T.tok, mcol.tok], writes=[dd_T.tok])
            kcol = tl([P, 1], "kcol")
            kb.dma("sp", kcol[:], self.ssdkeep[:, :], writes=[kcol.tok])
            S = [tl([P, 512], "ssdS%d" % g) for g in range(2)]
            xsl = Rot(kb, st, 2, [P, 8, P], F32, name="xsl")
            bcl = Rot(kb, st, 2, [P, 4, P], F32, name="bcl")
            xtok = Rot(kb, st, 2, [P, 16, 64], F32, name="xtok")
            btok = Rot(kb, st, 2, [P, 2, P], F32, name="btok")
            ddk = Rot(kb, st, 2, [P, 64], F32, name="ddk")
            sm = Rot(kb, st, 6, [P, 16], F32, name="ssm")
            gm = Rot(kb, st, 2, [P, 2, P], F32, name="gm")
            Mt = Rot(kb, st, 2, [P, 16, P], F32, name="Mt")
            xdt = Rot(kb, st, 2, [P, 16, 64], F32, name="xdt")
            xw = Rot(kb, st, 2, [P, 16, 64], F32, name="xw")
            ecr = Rot(kb, st, 3, [P, P], F32, name="ecr")
            yt = Rot(kb, st, 3, [P, P], F32, name="yt")
            for d in range(2):
                trd = tri[:, d, :]
                for g in range(2):
                    kb.dma("sp", S[g][:], self.ssds0[l, d, g], writes=[S[g].tok])
                order = range(NC) if d == 0 else range(NC - 1, -1, -1)
                for c in order:
                    cols = slice(c * P, (c + 1) * P)
                    xs_, bc_ = xsl.next(), bcl.next()
                    kb.dma("sp", xs_[:], self.xcs[0:1024, :].rearrange("(j p) t -> p j t", p=P)[:, :, cols], reads=[self.dt("xcs", 0)], writes=[xs_.tok])
                    kb.dma("sp", bc_[:], self.xcs[1024:1536, :].rearrange("(j p) t -> p j t", p=P)[:, :, cols], writes=[bc_.tok])
                    xt_, bt_, dk = xtok.next(), btok.next(), ddk.next()
                    for hb_ in range(2):
                        pb2, pt2 = self.bank2()
                        for jj in range(4):
                            j = hb_ * 4 + jj
                            kb.op("pe", lambda e: e.transpose(pb2[:, jj * P:(jj + 1) * P], xs_[:, j, :], self.ident[:]), reads=[xs_.tok, self.ident.tok], writes=[pt2[0], pt2[1]])
                        kb.op("act", lambda e: e.activation(out=xt_[:, hb_ * 8:(hb_ + 1) * 8, :], in_=pb2[:, 0:512].rearrange("p (h q) -> p h q", q=64), func=AF.Copy),
                              reads=[pt2[0], pt2[1]], writes=[xt_.tok])
                    pb, pt = self.bank()
                    for g in range(2):
                        kb.op("pe", lambda e: e.transpose(pb[:, g * P:(g + 1) * P], bc_[:, g, :], self.ident[:]), reads=[bc_.tok, self.ident.tok], writes=[pt])
                    kb.op("pe", lambda e: e.transpose(pb[:, 256:320], dd_T[:, cols], self.ident[0:64, 0:64]), reads=[dd_T.tok, self.ident.tok], writes=[pt])
                    kb.op("dve", lambda e: e.tensor_copy(out=bt_[:], in_=pb[:, 0:256].rearrange("p (g n) -> p g n", n=P)), reads=[pt], writes=[bt_.tok])
                    kb.op("dve", lambda e: e.tensor_copy(out=dk[:], in_=pb[:, 256:320]), reads=[pt], writes=[dk.tok])
                    dtc = dk[:, 32 * d:32 * d + 16]
                    dac = dk[:, 32 * d + 16:32 * d + 32]
                    pbc, ptc = self.bank()
                    kb.op("pe", lambda e: e.matmul(pbc[:, 0:16], trd, dac, start=True, stop=True), reads=[tri.tok, dk.tok], writes=[ptc])
                    kb.op("pe", lambda e: e.matmul(pbc[:, 16:32], self.ones[:], dac, start=True, stop=True), reads=[self.ones.tok, dk.tok], writes=[ptc])
                    cumc, wts, edec = sm.next(), sm.next(), sm.next()
                    kb.op("dve", lambda e: e.tensor_copy(out=cumc[:], in_=pbc[:, 0:16]), reads=[ptc], writes=[cumc.tok])
                    kb.op("dve", lambda e: e.tensor_tensor(out=wts[:], in0=pbc[:, 16:32], in1=cumc[:], op=ALU.subtract), reads=[ptc, cumc.tok], writes=[wts.tok])
                    kb.op("act", lambda e: e.activation(out=wts[:], in_=wts[:], func=AF.Exp), reads=[wts.tok], writes=[wts.tok])
                    kb.op("pool", lambda e: e.tensor_tensor(out=wts[:], in0=wts[:], in1=dtc, op=ALU.mult), reads=[wts.tok, dk.tok], writes=[wts.tok])
                    kb.op("act", lambda e: e.activation(out=edec[:], in_=pbc[:, 16:32], func=AF.Exp), reads=[ptc], writes=[edec.tok])
                    pbg, ptg = self.bank()
                    for g in range(2):
                        kb.op("pe", lambda e: e.matmul(pbg[:, g * P:(g + 1) * P], bc_[:, g, :], bc_[:, 2 + g, :], start=True, stop=True), reads=[bc_.tok], writes=[ptg])
                    gm_ = gm.next()
                    kb.op("dve", lambda e: e.tensor_tensor(out=gm_[:], in0=pbg[:, 0:256].rearrange("p (g t) -> p g t", t=P),
                                                           in1=tri[:, d:d + 1, :].to_broadcast([P, 2, P]), op=ALU.mult), reads=[ptg, tri.tok], writes=[gm_.tok])
                    M_ = Mt.next()
                    for hq in range(2):
                        pb2, pt2 = self.bank2()
                        for hh in range(8):
                            h = hq * 8 + hh
                            kb.op("pe", lambda e: e.matmul(pb2[:, hh * P:(hh + 1) * P], dac[:, h:h + 1].to_broadcast([P, P]), trd, start=True, stop=True),
                                  reads=[dk.tok, tri.tok], writes=[pt2[0], pt2[1]])
                        for hh in range(8):
                            h = hq * 8 + hh
                            kb.op("dve", lambda e: e.tensor_scalar(out=M_[:, h, :], in0=pb2[:, hh * P:(hh + 1) * P], scalar1=cumc[:, h:h + 1], scalar2=0.0,
                                                                   op0=ALU.subtract, op1=ALU.min), reads=[pt2[0], pt2[1], cumc.tok], writes=[M_.tok])
                    kb.op("act", lambda e: e.activation(out=M_[:], in_=M_[:], func=AF.Exp), reads=[M_.tok], writes=[M_.tok])
                    for g in range(2):
                        kb.op("pool", lambda e: e.tensor_tensor(out=M_[:, 8 * g:8 * g + 8, :], in0=M_[:, 8 * g:8 * g + 8, :],
                                                               in1=gm_[:, g:g + 1, :].to_broadcast([P, 8, P]), op=ALU.mult), reads=[M_.tok, gm_.tok], writes=[M_.tok])
                    xd_, xw_ = xdt.next(), xw.next()
                    kb.op("pool", lambda e: e.tensor_tensor(out=xd_[:], in0=xt_[:], in1=dtc.unsqueeze(2).to_broadcast([P, 16, 64]), op=ALU.mult),
                          reads=[xt_.tok, dk.tok], writes=[xd_.tok])
                    kb.op("pool", lambda e: e.tensor_tensor(out=xw_[:], in0=xt_[:], in1=wts[:].unsqueeze(2).to_broadcast([P, 16, 64]), op=ALU.mult),
                          reads=[xt_.tok, wts.tok], writes=[xw_.tok])
                    for j in range(8):
                        g = j // 4
                        pb, pt = self.bank()
                        for h2 in range(2):
                            kb.op("pe", lambda e: e.matmul(pb[h2 * 64:(h2 + 1) * 64, 0:P], dac[:, 2 * j + h2:2 * j + h2 + 1].to_broadcast([P, 64]), trd, start=True, stop=True),
                                  reads=[dk.tok, tri.tok], writes=[pt])
                        kb.op("pe", lambda e: e.matmul(pb[:, P:2 * P], S[g][:, (j % 4) * P:(j % 4 + 1) * P], bc_[:, 2 + g, :], start=True, stop=True),
                              reads=[S[g].tok, bc_.tok], writes=[pt])
                        for h2 in range(2):
                            kb.op("pe", lambda e: e.matmul(pb[h2 * 64:(h2 + 1) * 64, 2 * P:3 * P], xd_[:, 2 * j + h2, :], M_[:, 2 * j + h2, :], start=True, stop=True),
                                  reads=[xd_.tok, M_.tok], writes=[pt])
                        ec = ecr.next()
                        kb.op("act", lambda e: e.activation(out=ec[:], in_=pb[:, 0:P], func=AF.Exp), reads=[pt], writes=[ec.tok])
                        y_ = yt.next()
                        kb.op("dve", lambda e: e.tensor_tensor(out=y_[:], in0=pb[:, P:2 * P], in1=ec[:], op=ALU.mult), reads=[pt, ec.tok], writes=[y_.tok])
                        kb.op("dve", lambda e: e.tensor_tensor(out=y_[:], in0=pb[:, 2 * P:3 * P], in1=y_[:], op=ALU.add), reads=[pt, y_.tok], writes=[y_.tok])
                        kb.op("pool", lambda e: e.tensor_tensor(out=yacc[:, j, cols], in0=yacc[:, j, cols], in1=y_[:], op=ALU.add), reads=[yacc.tok, y_.tok], writes=[yacc.tok])
                    seg_end = (c % 2 == 1) if d == 0 else (c % 2 == 0)
                    for g in range(2):
                        pb, pt = self.bank()
                        kb.op("pe", lambda e: e.matmul(pb, bt_[:, g, :], xw_[:, 8 * g:8 * g + 8, :], start=True, stop=True), reads=[bt_.tok, xw_.tok], writes=[pt])
                        kb.op("pool", lambda e: e.tensor_tensor(out=S[g][:].rearrange("p (h q) -> p h q", q=64), in0=S[g][:].rearrange("p (h q) -> p h q", q=64),
                                                               in1=edec[:, 8 * g:8 * g + 8].unsqueeze(2).to_broadcast([P, 8, 64]), op=ALU.mult),
                              reads=[S[g].tok, edec.tok], writes=[S[g].tok])
                        kb.op("dve", lambda e: e.tensor_tensor(out=S[g][:], in0=pb, in1=S[g][:], op=ALU.add), reads=[pt, S[g].tok], writes=[S[g].tok])
                        if seg_end:
                            seg = c // 2
                            kb.dma("sp", self.ssdo[l, d, seg, g], S[g][:], reads=[S[g].tok], writes=[self.dt("ssdo", l, d, seg, g)])
                            kb.op("dve", lambda e: e.tensor_scalar(out=S[g][:], in0=S[g][:], scalar1=kcol[:, 0:1], scalar2=None, op0=ALU.mult),
                                  reads=[S[g].tok, kcol.tok], writes=[S[g].tok])
            Dc = tl([P, 8], "ssdDc")
            gc = tl([P, 8], "ssdgc")
            kb.dma("sp", Dc[:], self.ssdD[l], writes=[Dc.tok])
            kb.dma("sp", gc[:], self.ssdg[l], writes=[gc.tok])
            ld = Rot(kb, st, 4, [P, TT], F32, name="sld")
            rs = tl([P, TT], "srs")
            for tg in range(NT):
                tc_ = slice(tg * TT, (tg + 1) * TT)
                pbn, ptn = self.bank()
                for j in range(8):
                    xj, zj = ld.next(), ld.next()
                    kb.dma("sp", xj[:], self.xcs[j * P:(j + 1) * P, tc_], writes=[xj.tok])
                    kb.dma("sp", zj[:], self.zT[(cfg.ZB + j) * P:(cfg.ZB + j + 1) * P, tc_], writes=[zj.tok])
                    kb.op("dve", lambda e: e.scalar_tensor_tensor(out=xj[:], in0=xj[:], scalar=Dc[:, j:j + 1], in1=yacc[:, j, tc_], op0=ALU.mult, op1=ALU.add),
                          reads=[xj.tok, Dc.tok, yacc.tok], writes=[xj.tok])
                    kb.op("dve", lambda e: e.tensor_tensor(out=yacc[:, j, tc_], in0=xj[:], in1=zj[:], op=ALU.mult), reads=[xj.tok, zj.tok], writes=[yacc.tok])
                    kb.op("act", lambda e: e.activation(out=zj[:], in_=yacc[:, j, tc_], func=AF.Square), reads=[yacc.tok], writes=[zj.tok])
                    kb.op("pe", lambda e: e.matmul(pbn, self.ones[:], zj[:], start=(j == 0), stop=(j == 7)), reads=[zj.tok, self.ones.tok], writes=[ptn])
                t = ld.next()
                kb.op("act", lambda e: e.activation(out=t[:], in_=pbn, func=AF.Sqrt, bias=self.epsc[:, 0:1], scale=1.0 / SW), reads=[ptn, self.epsc.tok], writes=[t.tok])
                kb.op("dve", lambda e: e.reciprocal(out=rs[:], in_=t[:]), reads=[t.tok], writes=[rs.tok])
                for j in range(8):
                    o = ld.next()
                    kb.op("dve", lambda e: e.scalar_tensor_tensor(out=o[:], in0=yacc[:, j, tc_], scalar=gc[:, j:j + 1], in1=rs[:], op0=ALU.mult, op1=ALU.mult),
                          reads=[yacc.tok, gc.tok, rs.tok], writes=[o.tok])
                    kb.dma("sp", self.yb.bitcast(F32)[j * P:(j + 1) * P, tc_], o[:], reads=[o.tok], writes=[self.dt("yb", j, tg)])
            kb.barrier()


def prep_common(cfg, inp):
    NK, NJ, DEPTH = cfg.NK, cfg.NJ, cfg.DEPTH
    d = {}
    d["wmodn"] = inp["w_mod"]
    d["bmod"] = np.stack([colv(inp["b_mod"][l]) for l in range(DEPTH)])
    d["normg"] = np.stack([np.concatenate([colv(inp["norm_g"][l, i]) for i in range(3)], axis=1) for l in range(DEPTH)])
    d["fng"] = colv(inp["final_norm_g"])
    d["w1"] = np.stack([np.stack([blk(inp["ffn_w_in"][l, w]) for w in range(2)]) for l in range(DEPTH)])
    d["w2"] = np.stack([np.stack([blk(inp["ffn_w_out"][l, w]) for w in range(2)]) for l in range(DEPTH)])
    return d


def gp_layout(a):
    g, p = a.shape[0], a.shape[1]
    rest = a.shape[2:]
    b = a.reshape((32, 2, 64) + rest)
    b = np.moveaxis(b, 0, 2)
    return np.ascontiguousarray(b.reshape((128, 32) + rest))


def prep_mixer(cfg, inp):
    NK, DEPTH = cfg.NK, cfg.DEPTH
    d = {}
    wins = []
    for l in range(DEPTH):
        W = inp["w_in"][l]
        Wn = np.concatenate([W[:, 0:3328], W[:, 3328:3328 + 2560], W[:, 5904:6928], W[:, 6928:], W[:, 5888:5904],
                             np.zeros((W.shape[0], 112), np.float32)], axis=1)
        wins.append(blk(Wn))
    d["win"] = np.stack(wins)
    d["wpa"] = np.stack([blk(inp["w_proj_a"][l]) for l in range(DEPTH)])
    d["wpb"] = np.stack([blk(inp["w_proj_b"][l]) for l in range(DEPTH)])
    d["wpc"] = np.stack([blk(inp["w_proj_c"][l]) for l in range(DEPTH)])
    d["wo"] = np.stack([blk(inp["w_out"][l]) for l in range(DEPTH)])
    d["identd"] = np.eye(P, dtype=np.float32)
    lam = np.zeros((DEPTH, 2, P, 3, 32), np.float32)
    sb = np.zeros((DEPTH, 2, P, 2, 32, 16), np.float32)
    sc = np.zeros((DEPTH, 2, 2, 32, P, P), np.float32)
    for l in range(DEPTH):
        for dd in range(2):
            lam[l, dd, :, 0] = gp_layout(inp["s5_lambda_re"][l, dd])
            lam[l, dd, :, 1] = gp_layout(inp["s5_lambda_im"][l, dd])
            lam[l, dd, :, 2] = gp_layout(np.repeat(inp["s5_log_dt"][l, dd][:, None], 64, axis=1))
            sb[l, dd, :, 0] = gp_layout(inp["s5_b_re"][l, dd])
            sb[l, dd, :, 1] = gp_layout(inp["s5_b_im"][l, dd])
            for ri, key in enumerate(("s5_c_re", "s5_c_im")):
                C = inp[key][l, dd]
                for q in range(32):
                    for g2 in range(2):
                        g8 = 2 * (q % 4) + g2
                        sc[l, dd, ri, q, g2 * 64:(g2 + 1) * 64, g8 * 16:(g8 + 1) * 16] = C[2 * q + g2].T
    d["s5lam"], d["s5b"], d["s5c"] = lam, sb, sc
    d["s5d"] = np.stack([colv(inp["s5_d"][l]) for l in range(DEPTH)])
    rc = np.zeros((DEPTH, P, 5, 8), np.float32)
    rw0 = np.zeros((DEPTH, P, 2, 2, 8), np.float32)
    rw2 = np.zeros((DEPTH, P, 2, RW), np.float32)
    for l in range(DEPTH):
        rc[l, :, 0] = colv(inp["rwkv_k_k"][l])
        rc[l, :, 1] = colv(inp["rwkv_k_a"][l])
        rc[l, :, 2] = colv(inp["rwkv_r_k"][l].reshape(-1))
        rc[l, :, 3] = colv(inp["rwkv_ln_g"][l])
        rc[l, :, 4] = colv(inp["rwkv_ln_b"][l])
        for dd in range(2):
            rw0[l, :, dd, 0] = colv(inp["rwkv_w0"][l, dd])
            rw0[l, :, dd, 1] = colv(inp["rwkv_a0"][l, dd])
            rw2[l, 0:64, dd] = inp["rwkv_w2"][l, dd]
            rw2[l, 64:128, dd] = inp["rwkv_a2"][l, dd]
    d["rwcol"], d["rww0"], d["rww2"] = rc, rw0, rw2
    d["rwmu"] = np.stack([colv(inp["rwkv_mu"][l]) for l in range(DEPTH)])
    d["rwg2"] = np.ascontiguousarray(inp["rwkv_g2"])
    hb = np.zeros((2, P, P), np.float32)
    hb[0, 0:64, 0:64] = 1.0
    hb[0, 64:, 64:] = 1.0
    hb[1] = hb[0] / 64.0
    d["hblk"] = hb
    i_ = np.arange(64)[:, None]
    t_ = np.arange(64)[None, :]
    rt = np.stack([(i_ < t_), (i_ > t_), (i_ <= t_)]).astype(np.float32)
    d["rwtri"] = np.concatenate([rt, rt], axis=1)
    cmk = np.ones((P, cfg.T), np.float32)
    cmk[:, 0::64] = 0.0
    d["rwcm"] = cmk
    tri = np.zeros((2, P, P), np.float32)
    tri[0] = np.triu(np.ones((P, P), np.float32))
    tri[1] = np.tril(np.ones((P, P), np.float32))
    d["tri"] = tri
    cw = np.zeros((DEPTH, P, 12, 6), np.float32)
    scol = np.zeros((DEPTH, 64, 3), np.float32)
    for l in range(DEPTH):
        w = inp["ssd_conv_w"][l]
        for j in range(5):
            cw[l, :, :, j] = colv(w[j])
        cw[l, :, :, 5] = colv(inp["ssd_conv_b"][l])
        for dd in range(2):
            scol[l, 32 * dd:32 * dd + 16, 0] = inp["ssd_dt_bias"][l, dd]
            scol[l, 32 * dd + 16:32 * dd + 32, 0] = inp["ssd_dt_bias"][l, dd]
            scol[l, 32 * dd + 16:32 * dd + 32, 1] = inp["ssd_a_log"][l, dd]
            scol[l, 32 * dd:32 * dd + 16, 2] = 1.0
            scol[l, 32 * dd + 16:32 * dd + 32, 2] = -1.0
    d["ssdcw"], d["ssdcol"] = cw, scol
    d["ssdD"] = np.stack([colv(np.repeat(inp["ssd_d"][l], 64)) for l in range(DEPTH)])
    d["ssdg"] = np.stack([colv(inp["ssd_norm_g"][l]) for l in range(DEPTH)])
    return d


def core_mixer_inputs(cfg, inp, c, d):
    DEPTH = cfg.DEPTH
    prompt = c < cfg.NPC
    keep = np.ones((P, TT), np.float32)
    if prompt:
        keep[:, 0::256] = 0.0
    d["keepT"] = keep
    s0 = np.zeros((DEPTH, 2, P, 2, 32), np.float32)
    if not prompt:
        b = c - cfg.NPC
        for l in range(DEPTH):
            for dd in range(2):
                s0[l, dd, :, 0] = gp_layout(inp["state_s5_re"][b, l, dd])
                s0[l, dd, :, 1] = gp_layout(inp["state_s5_im"][b, l, dd])
    d["s5s0"] = s0
    cm = np.ones((P, 4, TT), np.float32)
    if prompt:
        for oi, o in enumerate((-2, -1, 1, 2)):
            for t in range(TT):
                if (t + o) // 256 != t // 256:
                    cm[:, oi, t] = 0.0
    d["cmT"] = cm
    smk = np.zeros((P, 4, TT), np.float32)
    tt_ = np.arange(TT)
    if prompt:
        smk[:, 0] = np.where(tt_ % 256 != 0, 0.5, 0.0)
        smk[:, 1] = np.where(tt_ % 256 != 255, 0.5, 0.0)
    else:
        smk[:, 0] = np.where(tt_ % 64 != 0, 0.25, 0.0)
        smk[:, 1] = np.where(tt_ % 64 != 63, 0.25, 0.0)
        smk[:, 2] = 0.25
        smk[:, 3] = 0.25
    d["rwsm"] = smk
    rs0 = np.zeros((DEPTH, 2, 64, 16, 64), np.float32)
    if not prompt:
        b = c - cfg.NPC
        sr = inp["state_rwkv"][b]
        for l in range(DEPTH):
            for dd in range(2):
                rs0[l, dd] = sr[l, dd].transpose(2, 0, 1)
    d["rws0"] = rs0
    d["ssdkeep"] = np.full((P, 1), 0.0 if prompt else 1.0, np.float32)
    ss0 = np.zeros((DEPTH, 2, 2, P, 512), np.float32)
    if not prompt:
        b = c - cfg.NPC
        st_ = inp["state_ssd"][b]
        for l in range(DEPTH):
            for dd in range(2):
                for g in range(2):
                    ss0[l, dd, g] = st_[l, dd, 8 * g:8 * g + 8].transpose(2, 0, 1).reshape(P, 512)
    d["ssds0"] = ss0


def core_inputs(cfg, inp, common, c):
    d = dict(common)
    T = cfg.T
    if c < cfg.NPC:
        ns = T // 256
        x = inp["x_prompt"][c * ns:(c + 1) * ns].reshape(T, cfg.DM)
        cond = inp["c_ctx"]
    else:
        b = c - cfg.NPC
        x = inp["x_sample"][b]
        cond = inp["c"][b]
    d["xT"] = np.ascontiguousarray(x.T)
    d["cond"] = colv(cond)
    core_mixer_inputs(cfg, inp, c, d)
    return d


def run(cfg, inp):
    b = Builder(cfg)
    nc = b.build()
    common = prep_common(cfg, inp)
    common.update(prep_mixer(cfg, inp))
    n = cfg.NPC + cfg.NSC
    maps = [core_inputs(cfg, inp, common, c) for c in range(n)]
    for m in maps:
        for k in list(m.keys()):
            if k not in b.din:
                del m[k]
            else:
                m[k] = np.ascontiguousarray(m[k], dtype=np.float32)
    res = run_bass_kernel_spmd(nc, maps, core_ids=list(range(n)))
    return res.results


def assemble(cfg, inp, results):
    T, DEPTH, NSEG = cfg.T, cfg.DEPTH, cfg.NSEG
    ns = T // 256
    yp = np.concatenate([results[c]["yT"].T.reshape(ns, 256, cfg.DM) for c in range(cfg.NPC)], axis=0)
    ys = np.stack([results[cfg.NPC + b]["yT"].T for b in range(cfg.NSC)], axis=0)
    nb = cfg.NPC * NSEG
    st_rwkv = np.zeros((nb, DEPTH, 2, 16, 64, 64), np.float32)
    st_ssd = np.zeros((nb, DEPTH, 2, 16, 64, 128), np.float32)
    s5 = [np.zeros((nb, DEPTH, 2, 64, 64), np.float32) for _ in range(2)]
    for c in range(cfg.NPC):
        r = results[c]
        if "s5o" in r:
            o = r["s5o"]
            o = o.reshape(DEPTH, 2, 64, 2, 2, 32, NSEG)
            o = o.transpose(6, 0, 3, 4, 5, 1, 2)
            o = o.reshape(NSEG, DEPTH, 2, 2, 64, 64).copy()
            o[:, :, 1] = o[::-1, :, 1]
            for ri in range(2):
                s5[ri][c * NSEG:(c + 1) * NSEG] = o[:, :, :, ri]
        if "rwo" in r:
            o = r["rwo"]
            o = o.transpose(2, 0, 1, 4, 5, 3).copy()
            o[:, :, 1] = o[::-1, :, 1]
            st_rwkv[c * NSEG:(c + 1) * NSEG] = o
        if "ssdo" in r:
            o = r["ssdo"]
            o = o.reshape(DEPTH, 2, NSEG, 2, P, 8, 64).transpose(2, 0, 1, 3, 5, 6, 4)
            st_ssd[c * NSEG:(c + 1) * NSEG] = o.reshape(NSEG, DEPTH, 2, 16, 64, 128)
    return yp, ys, st_rwkv, st_ssd, s5[0], s5[1]


def kernel(**inputs):
    cfg = Cfg()
    inp = {k: np.asarray(v) for k, v in inputs.items()}
    results = run(cfg, inp)
    return assemble(cfg, inp, results)
```
